# Optimizing a Trainium2 kernel written in Bass

```python
import jax, jax.numpy as jnp
from jax import lax
import numpy as np

D_MODEL = 1024
BATCH = 8
SEQ = 4096
DEPTH = 4

HEAD_DIM = 64
MIX_WIDTH = D_MODEL
MLA_HEADS = MIX_WIDTH // 2 // HEAD_DIM
MLA_WIDTH = MLA_HEADS * HEAD_DIM
NOPE_DIM = HEAD_DIM
ROPE_DIM = HEAD_DIM // 2
V_DIM = HEAD_DIM
Q_RANK = 3 * D_MODEL // 8
KV_RANK = 4 * HEAD_DIM
ROPE_THETA = 10000.0
Q_BLOCK = 128
SG_GROUPS = MIX_WIDTH // 4 // HEAD_DIM
SG_WIDTH = SG_GROUPS * HEAD_DIM
CHUNK = 128
CV_GROUPS = MIX_WIDTH // 4 // HEAD_DIM
CV_WIDTH = CV_GROUPS * HEAD_DIM
CONV_WIDTH = 3
D_FF = ((8 * D_MODEL + 3 * 256 - 1) // (3 * 256)) * 256
EPS = 1e-6
OFF_CQ = 0
OFF_CKV = OFF_CQ + Q_RANK
OFF_KR = OFF_CKV + KV_RANK
OFF_SG = OFF_KR + ROPE_DIM
OFF_CV = OFF_SG + 2 * SG_WIDTH
IN_WIDTH = OFF_CV + 3 * CV_WIDTH

kernel_name = "hybrid_mla_sgu_shortconv_sandwich"


def rms_norm(x, g):
    xf = x.astype(jnp.float32)
    y = xf * lax.rsqrt(jnp.mean(xf * xf, axis=-1, keepdims=True) + EPS)
    return y.astype(x.dtype) * g


def group_layer_norm(x, g, b, groups):
    shp = x.shape
    xf = x.astype(jnp.float32).reshape(shp[:-1] + (groups, shp[-1] // groups))
    mu = jnp.mean(xf, axis=-1, keepdims=True)
    var = jnp.mean(jnp.square(xf - mu), axis=-1, keepdims=True)
    y = ((xf - mu) * lax.rsqrt(var + EPS)).reshape(shp)
    return y.astype(x.dtype) * g + b


def rope_tables(positions):
    inv_freq = 1.0 / (ROPE_THETA ** (jnp.arange(0, ROPE_DIM // 2, dtype=jnp.float32) / (ROPE_DIM // 2)))
    ang = positions.astype(jnp.float32)[..., None] * inv_freq
    return jnp.cos(ang), jnp.sin(ang)


def apply_rope(t, cos, sin):
    tf = t.astype(jnp.float32)
    t1, t2 = jnp.split(tf, 2, axis=-1)
    return jnp.concatenate([t1 * cos - t2 * sin, t2 * cos + t1 * sin], axis=-1).astype(t.dtype)


def causal_latent_attention(q_nope, q_rope, k_nope, k_rope, v):
    b, s, h, _ = q_nope.shape
    nb = s // Q_BLOCK
    scale = (NOPE_DIM + ROPE_DIM) ** -0.5
    kpos = jnp.arange(s)

    def to_blocks(t):
        return jnp.moveaxis(t.reshape((b, nb, Q_BLOCK) + t.shape[2:]), 1, 0)

    def one_block(args):
        qn, qr, i = args
        sc = (jnp.einsum('bqhd,bkhd->bhqk', qn, k_nope)
              + jnp.einsum('bqhd,bkd->bhqk', qr, k_rope)).astype(jnp.float32) * scale
        qpos = i * Q_BLOCK + jnp.arange(Q_BLOCK)
        mask = kpos[None, :] <= qpos[:, None]
        sc = jnp.where(mask, sc, jnp.finfo(jnp.float32).min)
        p = jax.nn.softmax(sc, axis=-1).astype(v.dtype)
        return jnp.einsum('bhqk,bkhd->bqhd', p, v)

    out = lax.map(one_block, (to_blocks(q_nope), to_blocks(q_rope), jnp.arange(nb)))
    return jnp.moveaxis(out, 0, 1).reshape(b, s, h * V_DIM)


def mla_branch(z, cos, sin, q_norm_g, w_uq, kv_norm_g, w_ukv):
    b, s, _ = z.shape
    c_q = rms_norm(z[..., OFF_CQ:OFF_CKV], q_norm_g)
    q = (c_q @ w_uq).reshape(b, s, MLA_HEADS, NOPE_DIM + ROPE_DIM)
    q_nope = q[..., :NOPE_DIM]
    q_rope = apply_rope(q[..., NOPE_DIM:], cos[:, :, None, :], sin[:, :, None, :])
    c_kv = rms_norm(z[..., OFF_CKV:OFF_KR], kv_norm_g)
    kv = (c_kv @ w_ukv).reshape(b, s, MLA_HEADS, NOPE_DIM + V_DIM)
    k_nope, v = kv[..., :NOPE_DIM], kv[..., NOPE_DIM:]
    k_rope = apply_rope(z[..., OFF_KR:OFF_SG], cos, sin)
    return causal_latent_attention(q_nope, q_rope, k_nope, k_rope, v)


def sgu_branch(z, sg_ln_g, sg_ln_b, w_sp, b_sp):
    b, s, _ = z.shape
    uv = jax.nn.gelu(z[..., OFF_SG:OFF_CV])
    u, v = uv[..., :SG_WIDTH], uv[..., SG_WIDTH:]
    v = group_layer_norm(v, sg_ln_g, sg_ln_b, SG_GROUPS)
    vc = v.reshape(b, s // CHUNK, CHUNK, SG_GROUPS, HEAD_DIM)
    w_causal = w_sp * jnp.tril(jnp.ones((CHUNK, CHUNK), w_sp.dtype))
    mixed = jnp.einsum('gts,bcsge->bctge', w_causal, vc) + jnp.swapaxes(b_sp, 0, 1)[:, :, None]
    return u * mixed.reshape(b, s, SG_WIDTH)


def conv_branch(z, conv_w):
    gate_b = z[..., OFF_CV:OFF_CV + CV_WIDTH]
    gate_c = z[..., OFF_CV + CV_WIDTH:OFF_CV + 2 * CV_WIDTH]
    h = z[..., OFF_CV + 2 * CV_WIDTH:IN_WIDTH]
    y = gate_c * h
    yp = jnp.pad(y, ((0, 0), (CONV_WIDTH - 1, 0), (0, 0)))
    s = y.shape[1]
    conv = yp[:, 0:s] * conv_w[0] + yp[:, 1:s + 1] * conv_w[1] + yp[:, 2:s + 2] * conv_w[2]
    return gate_b * conv


def setup_inputs(seed: int = 0) -> dict:
    key = jax.random.key(seed)
    ks = jax.random.split(key, 24)
    L, D = DEPTH, D_MODEL

    def nrm(k, shape, fan_in):
        return jax.random.normal(k, shape, jnp.float32) * fan_in ** -0.5

    def gain(k, shape):
        return 1.0 + 0.05 * jax.random.normal(k, shape, jnp.float32)

    x = jax.random.normal(ks[0], (BATCH, SEQ, D), jnp.float32)
    offsets = jax.random.randint(ks[1], (BATCH, 1), 0, 1024, dtype=jnp.int32)
    positions = (offsets + jnp.arange(SEQ, dtype=jnp.int32)[None, :]).astype(jnp.int32)
    return {
        "x": x,
        "positions": positions,
        "mix_pre_g": gain(ks[2], (L, D)),
        "mix_post_g": gain(ks[3], (L, D)),
        "ffn_pre_g": gain(ks[4], (L, D)),
        "ffn_post_g": gain(ks[5], (L, D)),
        "w_in": nrm(ks[6], (L, D, IN_WIDTH), D),
        "q_norm_g": gain(ks[7], (L, Q_RANK)),
        "w_uq": nrm(ks[8], (L, Q_RANK, MLA_HEADS * (NOPE_DIM + ROPE_DIM)), Q_RANK),
        "kv_norm_g": gain(ks[9], (L, KV_RANK)),
        "w_ukv": nrm(ks[10], (L, KV_RANK, MLA_HEADS * (NOPE_DIM + V_DIM)), KV_RANK),
        "sg_ln_g": gain(ks[11], (L, SG_WIDTH)),
        "sg_ln_b": 0.02 * jax.random.normal(ks[12], (L, SG_WIDTH), jnp.float32),
        "w_sp": nrm(ks[13], (L, SG_GROUPS, CHUNK, CHUNK), CHUNK),
        "b_sp": gain(ks[14], (L, SG_GROUPS, CHUNK)),
        "conv_w": nrm(ks[15], (L, CONV_WIDTH, CV_WIDTH), CONV_WIDTH),
        "out_norm_g": gain(ks[16], (L, MIX_WIDTH)),
        "w_out": nrm(ks[17], (L, MIX_WIDTH, D), MIX_WIDTH),
        "w_gate": nrm(ks[18], (L, D, D_FF), D),
        "w_up": nrm(ks[19], (L, D, D_FF), D),
        "w_down": nrm(ks[20], (L, D_FF, D), D_FF),
    }


def reference(x, positions, mix_pre_g, mix_post_g, ffn_pre_g, ffn_post_g, w_in, q_norm_g, w_uq,
              kv_norm_g, w_ukv, sg_ln_g, sg_ln_b, w_sp, b_sp, conv_w, out_norm_g, w_out,
              w_gate, w_up, w_down):
    cos, sin = rope_tables(positions)
    a_end = MLA_WIDTH
    s_end = MLA_WIDTH + SG_WIDTH
    for l in range(DEPTH):
        h = rms_norm(x, mix_pre_g[l])
        z = h @ w_in[l]
        y_a = mla_branch(z, cos, sin, q_norm_g[l], w_uq[l], kv_norm_g[l], w_ukv[l])
        y_b = sgu_branch(z, sg_ln_g[l], sg_ln_b[l], w_sp[l], b_sp[l])
        y_c = conv_branch(z, conv_w[l])
        g = out_norm_g[l]
        mix = jnp.concatenate([rms_norm(y_a, g[:a_end]),
                               rms_norm(y_b, g[a_end:s_end]),
                               rms_norm(y_c, g[s_end:])], axis=-1)
        x = x + rms_norm(mix @ w_out[l], mix_post_g[l])
        h = rms_norm(x, ffn_pre_g[l])
        f = (jax.nn.silu(h @ w_gate[l]) * (h @ w_up[l])) @ w_down[l]
        x = x + rms_norm(f, ffn_post_g[l])
    return x
```

```python
import math
import os
import numpy as np
import concourse.bass as bass
import concourse.mybir as mybir
from concourse.bass_utils import run_bass_kernel_spmd

F32 = mybir.dt.float32
BF16 = mybir.dt.bfloat16
I32 = mybir.dt.int32
AF = mybir.ActivationFunctionType
ALU = mybir.AluOpType
AX = mybir.AxisListType

D = 1024
SEQ = 4096
DEPTH = 4
NSUB = SEQ // 128
NTILE = SEQ // 512
INW = 1952
H = 8
DFF = 2816
NFF = DFF // 128
EPS = 1e-6
SCALE = 96.0 ** -0.5
GW = 33
BCW = 3328
ND = 8


class Buf:
    __slots__ = ("name", "w", "r", "psum")

    def __init__(self, name, psum=False):
        self.name = name
        self.w = None
        self.r = {}
        self.psum = psum


class Sched:
    def __init__(self, nc):
        self.nc = nc
        self.E = {"pe": nc.tensor, "act": nc.scalar, "dve": nc.vector, "pool": nc.gpsimd, "sp": nc.sync}
        self.sem = {k: nc.alloc_semaphore("s_" + k) for k in self.E}
        self.cnt = {k: 0 for k in self.E}
        self.seen = {k: {} for k in self.E}
        self.pend = {k: [] for k in self.E}
        self.dsem = {q: [nc.alloc_semaphore(f"d_{q}{i}") for i in range(ND)] for q in ("sp", "pool")}
        self.dcnt = {q: [0] * ND for q in self.dsem}
        self.drr = {q: 0 for q in self.dsem}

    def _collect(self, eng, reads, writes, skip_own_war=True):
        toks = {}
        own = self.sem.get(eng)

        def need(s, v, war=False):
            if s is own and (eng == "pe" or (war and skip_own_war)):
                return
            if toks.get(s, 0) < v:
                toks[s] = v

        for b in reads:
            if b.w is not None:
                need(*b.w)
            if b.psum:
                for s, v in b.r.items():
                    if s is not own:
                        need(s, v)
        for b in writes:
            if b.w is not None:
                need(*b.w)
            for s, v in b.r.items():
                need(s, v, war=True)
        return toks

    def _emit_waits(self, eng, toks):
        e = self.E[eng]
        seen = self.seen[eng]
        for s, v in toks.items():
            if seen.get(s, 0) < v:
                e.wait_ge(s, v)
                seen[s] = v

    def op(self, eng, fn, reads=(), writes=(), inc=True):
        self._emit_waits(eng, self._collect(eng, reads, writes))
        ins = fn(self.E[eng])
        self.pend[eng].append((tuple(reads), tuple(writes)))
        if inc:
            self.cnt[eng] += 1
            s = self.sem[eng]
            ins.then_inc(s, 1)
            v = self.cnt[eng]
            for rs, ws in self.pend[eng]:
                for b in rs:
                    b.r[s] = v
                for b in ws:
                    b.w = (s, v)
                    b.r = {}
            self.pend[eng] = []
        return ins

    def dma(self, q, fn, reads=(), writes=()):
        assert not self.pend[q]
        i = self.drr[q]
        self.drr[q] = (i + 1) % ND
        s = self.dsem[q][i]
        toks = self._collect(q, reads, writes, skip_own_war=False)
        prev = 16 * self.dcnt[q][i]
        if prev and toks.get(s, 0) < prev:
            toks[s] = prev
        self._emit_waits(q, toks)
        ins = fn(self.E[q])
        ins.then_inc(s, 16)
        self.dcnt[q][i] += 1
        v = 16 * self.dcnt[q][i]
        for b in reads:
            b.r[s] = v
        for b in writes:
            b.w = (s, v)
            b.r = {}
        return ins

    def handoff(self, src, dst):
        for d in dst:
            for b in src:
                if b.w is not None:
                    s, v = b.w
                    if d.r.get(s, 0) < v:
                        d.r[s] = v
                for s, v in b.r.items():
                    if d.r.get(s, 0) < v:
                        d.r[s] = v

    def barrier(self):
        for k in self.E:
            assert not self.pend[k]
        toks = {}
        for k in self.E:
            if self.cnt[k]:
                toks[self.sem[k]] = self.cnt[k]
        for q in self.dsem:
            for i in range(ND):
                if self.dcnt[q][i]:
                    toks[self.dsem[q][i]] = 16 * self.dcnt[q][i]
        for k in self.E:
            t = {s: v for s, v in toks.items() if s is not self.sem[k]}
            self._emit_waits(k, t)

    def finish(self, eng, bufs):
        toks = {}
        for b in bufs:
            if b.w is not None:
                s, v = b.w
                toks[s] = max(toks.get(s, 0), v)
            for s, v in b.r.items():
                toks[s] = max(toks.get(s, 0), v)
        self._emit_waits(eng, toks)


def build_program(L=DEPTH):
    DBG_T = int(os.environ.get('KDBG_TILES', NTILE))
    DBG_STAGE = os.environ.get('KDBG_STAGE', 'full')
    DBG_STOP = float(os.environ.get('KDBG_STOP', 99))
    nc = bass.Bass("TRN2", target_bir_lowering=False)
    S = Sched(nc)

    def dram(name, shape, dt, kind):
        return nc.dram_tensor(name, shape, dt, kind=kind).ap()

    x_d = dram("x", [SEQ, D], F32, "ExternalInput")
    pos_d = dram("pos", [128, NSUB], I32, "ExternalInput")
    invf_d = dram("invf", [128, 16], F32, "ExternalInput")
    w_in_d = dram("w_in", [DEPTH, D, INW], F32, "ExternalInput")
    w_uq_d = dram("w_uq", [DEPTH, 384, 768], F32, "ExternalInput")
    w_ukv_d = dram("w_ukv", [DEPTH, 256, 1024], F32, "ExternalInput")
    w_out_d = dram("w_out", [DEPTH, D, D], F32, "ExternalInput")
    w_gate_d = dram("w_gate", [DEPTH, D, DFF], F32, "ExternalInput")
    w_up_d = dram("w_up", [DEPTH, D, DFF], F32, "ExternalInput")
    w_down_d = dram("w_down", [DEPTH, DFF, D], F32, "ExternalInput")
    gT_d = dram("gT", [128, DEPTH * GW], F32, "ExternalInput")
    bc_d = dram("bc", [DEPTH, 128, BCW], F32, "ExternalInput")
    wsp_d = dram("wspT", [DEPTH, 128, 512], F32, "ExternalInput")
    out_d = dram("out", [SEQ, D], F32, "ExternalOutput")
    mix_d = dram("mixd", [SEQ, D], BF16, "Internal")

    base = (nc.sbuf_base + 31) // 32 * 32
    top = nc.sbuf_top

    def sb(name, shape, dt, off):
        nb = int(np.prod(shape[1:])) * (2 if dt == BF16 else 4)
        assert off % 32 == 0 and base + off + nb <= top, (name, off, nb, top - base)
        return nc.alloc_sbuf_tensor_at(name, list(shape), dt, offset=base + off)

    ident = sb("ident", [128, 128], BF16, 0)
    tri = sb("tri", [128, 128], BF16, 256)
    mhalf = sb("mhalf", [128, 8], F32, 512)
    cosT = sb("cosT", [128, NSUB, 16], F32, 576)
    sinT = sb("sinT", [128, NSUB, 16], F32, 2624)
    gT = sb("gTs", [128, DEPTH * GW], F32, 4672)
    stt_ = sb("stats", [128, 64], F32, 5216)
    invf = sb("invfs", [128, 16], F32, 5472)
    posi = sb("posi", [128, NSUB], I32, 5536)
    posf = sb("posf", [128, NSUB], F32, 5664)
    stv = sb("stv", [128, 32], F32, 5792)
    CEND = 5920
    BIG = CEND
    MID = BIG + 174080
    assert base + MID + 30720 <= top, (base, MID, top)

    KT = sb("KT", [128, H, SEQ], BF16, BIG + 0)
    Vc = sb("Vc", [128, NSUB, H, 65], BF16, BIG + 65536)
    w_in = sb("w_in_s", [128, 8, INW], BF16, BIG + 98816)
    w_uq = sb("w_uq_s", [128, 3, 768], BF16, BIG + 130048)
    w_ukv = sb("w_ukv_s", [128, 2, 1024], BF16, BIG + 134656)
    qT = sb("qT", [128, H, 512], BF16, BIG + 138752)
    mix_tm = sb("mix_tm", [128, 4, D], BF16, BIG + 146944)
    SA = BIG + 155136
    z_sb = sb("z_sb", [128, INW], F32, SA)
    uv = sb("uv", [128, 512], F32, SA + 7808)
    ysh1 = sb("ysh1", [128, 256], F32, SA + 9856)
    ysh2 = sb("ysh2", [128, 256], F32, SA + 10880)
    acc = sb("acc", [128, 256], F32, SA + 11904)
    sq = sb("sq", [128, 256], F32, SA + 12928)
    vn32 = sb("vn32", [128, 256], F32, SA + 13952)
    cqkv = sb("cqkv", [128, 640], BF16, SA + 14976)
    ycv = [sb("ycv0", [128, 256], F32, SA + 16352), sb("ycv1", [128, 256], F32, SA + 17376)]
    assert SA + 18400 <= MID
    ya = sb("ya", [128, 4, 512], F32, SA)
    PT = [sb(f"PT{r}", [128, 512], BF16, SA + 8192 + 1024 * r) for r in range(4)]
    hT = [sb("hT0", [128, 8, 128], BF16, MID + 0), sb("hT1", [128, 8, 128], BF16, MID + 2048)]
    xs = sb("xs", [128, D], F32, MID + 4096)
    xn = sb("xn", [128, D], BF16, MID + 8192)
    sgln = sb("sgln", [128, 512], F32, MID + 10240)
    convw = sb("convw", [128, 768], F32, MID + 12288)
    wspT = sb("wspTs", [128, 4, 128], BF16, MID + 15360)
    cqT = sb("cqT", [128, 5, 128], BF16, MID + 16384)
    q_tm = sb("q_tm", [128, H, 96], BF16, MID + 17664)
    k_tm = sb("k_tm", [128, H, 96], BF16, MID + 19200)
    rp = sb("rp", [128, 9, 32], F32, MID + 20736)
    rt = [sb(f"rt{i}", [128, 9, 16], F32, MID + 21888 + 576 * i) for i in range(4)]
    rr = sb("rr", [128, 9, 32], F32, MID + 24192)
    vn = sb("vn", [128, 256], BF16, MID + 25344)
    rinv = sb("rinv", [128, 4], F32, MID + 25856)
    w_gate = sb("w_gate_s", [128, 8, DFF], BF16, BIG + 0)
    w_up = sb("w_up_s", [128, 8, DFF], BF16, BIG + 45056)
    w_down = sb("w_down_s", [128, NFF, D], BF16, BIG + 90112)
    w_out = sb("w_out_s", [128, 8, D], BF16, BIG + 135168)
    aT = sb("aT", [128, NFF, 512], BF16, BIG + 151552)
    h2T = sb("h2T", [128, 8, 512], BF16, MID + 0)
    xs_f = sb("xs_f", [128, D], F32, MID + 8192)
    xn_f = [sb("xn_f0", [128, D], BF16, MID + 12288), sb("xn_f1", [128, D], BF16, MID + 14336)]
    sg_f = [sb("sg_f0", [128, 512], F32, MID + 12288), sb("sg_f1", [128, 512], F32, MID + 14336)]
    mixT = sb("mixT", [128, 8, 128], BF16, MID + 16384)
    t_f = sb("t_f", [128, D], F32, MID + 18432)
    postg_m = sb("postg_m", [128, D], F32, MID + 22528)
    postg_f = sb("postg_f", [128, D], F32, MID + 26624)

    pT = [nc.alloc_psum_tensor(f"pT{i}", [128, 8, 128], BF16) for i in range(2)]
    bk = [nc.alloc_psum_tensor(f"bk{i}", [128, 512], F32) for i in range(6)]
    BpT = [Buf(f"pT{i}", psum=True) for i in range(2)]
    Bbk = [Buf(f"bk{i}", psum=True) for i in range(6)]
    tcount = [0]

    def next_pT():
        i = tcount[0] % 2
        tcount[0] += 1
        return pT[i], BpT[i]

    B = {}

    def bf(name):
        if name not in B:
            B[name] = Buf(name)
        return B[name]

    Xd = [Buf(f"xd{n}") for n in range(NSUB)]
    Md = [Buf(f"md{i}") for i in range(NTILE)]
    BKT = [Buf(f"kt{n}") for n in range(NSUB)]
    BV = [Buf(f"v{n}") for n in range(NSUB)]

    slot = [0]

    def new_slot():
        k = slot[0] % 16
        slot[0] += 1
        return stt_[:, 4 * k:4 * k + 4], bf(f"slot{k}")

    def rstd_from_ss(sl, Bsl, ncols, Dn):
        if ncols == 2:
            S.op("pool", lambda e: e.tensor_tensor(out=sl[:, 0:1], in0=sl[:, 0:1], in1=sl[:, 1:2], op=ALU.add), reads=[Bsl], writes=[Bsl])
        S.op("pool", lambda e: e.tensor_scalar(out=sl[:, 2:3], in0=sl[:, 0:1], scalar1=1.0 / Dn, scalar2=EPS, op0=ALU.mult, op1=ALU.add), reads=[Bsl], writes=[Bsl])
        S.op("pool", lambda e: e.tensor_tensor(out=sl[:, 3:4], in0=sl[:, 2:3], in1=mhalf[:, 0:1], op=ALU.pow), reads=[Bsl, bf("mhalf")], writes=[Bsl])
        return sl[:, 3:4]

    def transpose_to(src_ap_fn, nchunk, rows, Bsrc, dst_ap, Bdst, gcol0=None, eng="dve"):
        p, Bp = next_pT()
        for c in range(nchunk):
            S.op("pe", lambda e, c=c: e.transpose(out=p[0:rows, c, :], in_=src_ap_fn(c), identity=ident[:]),
                 reads=[Bsrc, bf("ident")], writes=[Bp], inc=(c == nchunk - 1))
        if gcol0 is None:
            if eng == "act":
                S.op("act", lambda e: e.activation(out=dst_ap, in_=p[0:rows, 0:nchunk, :], func=AF.Copy), reads=[Bp], writes=[Bdst])
            else:
                S.op(eng, lambda e: e.tensor_copy(out=dst_ap, in_=p[0:rows, 0:nchunk, :]), reads=[Bp], writes=[Bdst])
        else:
            g = gT[0:rows, gcol0:gcol0 + nchunk].unsqueeze(2).broadcast_to([rows, nchunk, 128])
            S.op(eng, lambda e: e.tensor_tensor(out=dst_ap, in0=p[0:rows, 0:nchunk, :], in1=g, op=ALU.mult), reads=[Bp, bf("gT")], writes=[Bdst])

    S.op("pool", lambda e: e.memset(ident[:], 1.0), writes=[bf("ident")])
    S.op("pool", lambda e: e.affine_select(out=ident[:], in_=ident[:], pattern=[[-1, 128]], compare_op=ALU.is_equal, fill=0.0, base=0, channel_multiplier=1),
         reads=[bf("ident")], writes=[bf("ident")])
    S.op("pool", lambda e: e.memset(tri[:], 1.0), writes=[bf("tri")])
    S.op("pool", lambda e: e.affine_select(out=tri[:], in_=tri[:], pattern=[[1, 128]], compare_op=ALU.is_ge, fill=0.0, base=0, channel_multiplier=-1),
         reads=[bf("tri")], writes=[bf("tri")])
    S.op("pool", lambda e: e.memset(mhalf[:], -0.5), writes=[bf("mhalf")])
    S.dma("sp", lambda e: e.dma_start(out=gT[:], in_=gT_d), writes=[bf("gT")])
    S.dma("sp", lambda e: e.dma_start(out=invf[:], in_=invf_d), writes=[bf("invf")])
    S.dma("sp", lambda e: e.dma_start(out=posi[:], in_=pos_d), writes=[bf("posi")])
    TWO_PI = 2.0 * math.pi
    C1 = float(np.float32(6.28125))
    C2 = float(np.float32(TWO_PI - 6.28125))
    MAGIC = 12582912.0
    ang = sb("ang", [128, NSUB, 16], F32, BIG + 0)
    kk = sb("kk", [128, NSUB, 16], F32, BIG + 2048)
    t2 = sb("t2p", [128, NSUB, 16], F32, BIG + 4096)
    S.op("dve", lambda e: e.tensor_copy(out=posf[:], in_=posi[:]), reads=[bf("posi")], writes=[bf("posf")])
    S.op("dve", lambda e: e.tensor_tensor(out=ang[:], in0=posf[:].unsqueeze(2).broadcast_to([128, NSUB, 16]),
                                          in1=invf[:].unsqueeze(1).broadcast_to([128, NSUB, 16]), op=ALU.mult),
         reads=[bf("posf"), bf("invf")], writes=[bf("ang")])
    S.op("dve", lambda e: e.tensor_scalar(out=t2[:], in0=ang[:], scalar1=1.0 / TWO_PI, scalar2=MAGIC, op0=ALU.mult, op1=ALU.add), reads=[bf("ang")], writes=[bf("t2")])
    S.op("dve", lambda e: e.tensor_scalar(out=kk[:], in0=t2[:], scalar1=-MAGIC, scalar2=None, op0=ALU.add), reads=[bf("t2")], writes=[bf("kk")])
    S.op("dve", lambda e: e.scalar_tensor_tensor(out=ang[:], in0=kk[:], scalar=-C1, in1=ang[:], op0=ALU.mult, op1=ALU.add), reads=[bf("kk"), bf("ang")], writes=[bf("ang")])
    S.op("dve", lambda e: e.scalar_tensor_tensor(out=ang[:], in0=kk[:], scalar=-C2, in1=ang[:], op0=ALU.mult, op1=ALU.add), reads=[bf("kk"), bf("ang")], writes=[bf("ang")])
    S.op("dve", lambda e: e.tensor_scalar(out=ang[:], in0=ang[:], scalar1=math.pi, scalar2=-math.pi, op0=ALU.min, op1=ALU.max), reads=[bf("ang")], writes=[bf("ang")])
    S.op("act", lambda e: e.activation(out=sinT[:], in_=ang[:], func=AF.Sin), reads=[bf("ang")], writes=[bf("sin")])
    S.op("dve", lambda e: e.tensor_scalar(out=t2[:], in0=ang[:], scalar1=-1.0, scalar2=None, op0=ALU.mult), reads=[bf("ang")], writes=[bf("t2")])
    S.op("dve", lambda e: e.tensor_tensor(out=t2[:], in0=t2[:], in1=ang[:], op=ALU.max), reads=[bf("ang"), bf("t2")], writes=[bf("t2")])
    S.op("dve", lambda e: e.tensor_scalar(out=t2[:], in0=t2[:], scalar1=-1.0, scalar2=math.pi / 2, op0=ALU.mult, op1=ALU.add), reads=[bf("t2")], writes=[bf("t2")])
    S.op("act", lambda e: e.activation(out=cosT[:], in_=t2[:], func=AF.Sin), reads=[bf("t2")], writes=[bf("cos")])

    stA = [bf("z0"), bf("z1"), bf("z2"), bf("z3"), bf("uv"), bf("ysh1"), bf("ysh2"), bf("acc"), bf("sq"), bf("vn32"), bf("cqkv")]
    stB = [bf("ya"), bf("PT0"), bf("PT1"), bf("PT2"), bf("PT3")]
    rot = [0]

    def mixer_pass(l):
        xsrc = x_d if l == 0 else out_d
        g0 = l * GW
        S.barrier()
        for k in range(8):
            S.dma("pool", lambda e, k=k: e.dma_start(out=w_in[:, k, :], in_=w_in_d[l, k * 128:(k + 1) * 128, :]), writes=[bf(f"w_in{k}")])
        for k in range(3):
            S.dma("pool", lambda e, k=k: e.dma_start(out=w_uq[:, k, :], in_=w_uq_d[l, k * 128:(k + 1) * 128, :]), writes=[bf("w_uq")])
        for k in range(2):
            S.dma("pool", lambda e, k=k: e.dma_start(out=w_ukv[:, k, :], in_=w_ukv_d[l, k * 128:(k + 1) * 128, :]), writes=[bf("w_ukv")])
        S.dma("pool", lambda e: e.dma_start(out=wspT[:].rearrange("p g t -> p (g t)"), in_=wsp_d[l]), writes=[bf("wspT")])
        S.dma("sp", lambda e: e.dma_start(out=sgln[:], in_=bc_d[l, :, 2048:2560]), writes=[bf("sgln")])
        S.dma("sp", lambda e: e.dma_start(out=convw[:], in_=bc_d[l, :, 2560:3328]), writes=[bf("convw")])
        S.op("dve", lambda e: e.tensor_tensor(out=wspT[:], in0=wspT[:], in1=tri[:].unsqueeze(1).broadcast_to([128, 4, 128]), op=ALU.mult),
             reads=[bf("wspT"), bf("tri")], writes=[bf("wspT")])
        S.op("pool", lambda e: e.memset(Vc[:, :, :, 64:65], 1.0), writes=BV)
        S.op("pool", lambda e: e.memset(ycv[1][:], 0.0), writes=[bf("ycv1")])

        for i in range(DBG_T):
            S.handoff(stB, stA)
            for s in range(4):
                n = 4 * i + s
                t0 = n * 128
                hb = hT[n % 2]
                Bh = bf(f"hT{n % 2}")
                if DBG_STOP <= 0:
                    return
                S.dma("sp", lambda e: e.dma_start(out=xs[:], in_=xsrc[t0:t0 + 128, :]), reads=[Xd[n]], writes=[bf("xs")])
                sl, Bsl = new_slot()
                S.op("act", lambda e: e.activation(out=xn[:], in_=xs[:], func=AF.Square, accum_out=sl[:, 0:1]), reads=[bf("xs")], writes=[bf("xn"), Bsl])
                r = rstd_from_ss(sl, Bsl, 1, D)
                S.op("dve", lambda e: e.tensor_scalar(out=xn[:], in0=xs[:], scalar1=r, scalar2=None, op0=ALU.mult), reads=[bf("xs"), Bsl], writes=[bf("xn")])
                transpose_to(lambda c: xn[:, c * 128:(c + 1) * 128], 8, 128, bf("xn"), hb[:], Bh, gcol0=g0 + 0)
                if DBG_STOP <= 1:
                    return
                cg = [(0, 512), (512, 1024), (1024, 1536), (1536, INW)]
                for k in range(8):
                    for q, (c0, c1) in enumerate(cg):
                        S.op("pe", lambda e, k=k, q=q, c0=c0, c1=c1: e.matmul(bk[q][:, 0:c1 - c0], lhsT=hb[:, k, :], rhs=w_in[:, k, c0:c1], start=(k == 0), stop=(k == 7)),
                             reads=[Bh, bf(f"w_in{k}")], writes=[Bbk[q]], inc=(k == 7 and q == 3))
                for q, (c0, c1) in enumerate(cg):
                    if q % 2 == 0:
                        S.op("act", lambda e, q=q, c0=c0, c1=c1: e.activation(out=z_sb[:, c0:c1], in_=bk[q][:, 0:c1 - c0], func=AF.Copy), reads=[Bbk[q]], writes=[bf(f"z{q}")])
                    else:
                        S.op("dve", lambda e, q=q, c0=c0, c1=c1: e.tensor_copy(out=z_sb[:, c0:c1], in_=bk[q][:, 0:c1 - c0]), reads=[Bbk[q]], writes=[bf(f"z{q}")])
                if DBG_STOP <= 2:
                    return
                slq, Bq = new_slot()
                slk, Bk_ = new_slot()
                S.op("act", lambda e: e.activation(out=cqkv[:, 0:384], in_=z_sb[:, 0:384], func=AF.Square, accum_out=slq[:, 0:1]), reads=[bf("z0")], writes=[bf("cqkv"), Bq])
                S.op("act", lambda e: e.activation(out=cqkv[:, 384:640], in_=z_sb[:, 384:640], func=AF.Square, accum_out=slk[:, 0:1]), reads=[bf("z0"), bf("z1")], writes=[bf("cqkv"), Bk_])
                rq = rstd_from_ss(slq, Bq, 1, 384)
                rk = rstd_from_ss(slk, Bk_, 1, 256)
                S.op("dve", lambda e: e.tensor_scalar(out=cqkv[:, 0:384], in0=z_sb[:, 0:384], scalar1=rq, scalar2=None, op0=ALU.mult), reads=[bf("z0"), Bq], writes=[bf("cqkv")])
                S.op("dve", lambda e: e.tensor_scalar(out=cqkv[:, 384:640], in0=z_sb[:, 384:640], scalar1=rk, scalar2=None, op0=ALU.mult), reads=[bf("z0"), bf("z1"), Bk_], writes=[bf("cqkv")])
                transpose_to(lambda c: cqkv[:, c * 128:(c + 1) * 128], 5, 128, bf("cqkv"), cqT[:], bf("cqT"), gcol0=g0 + 24)
                if DBG_STOP <= 3:
                    return
                for k in range(3):
                    S.op("pe", lambda e, k=k: e.matmul(bk[4][:, 0:480], lhsT=cqT[:, k, :], rhs=w_uq[:, k, 0:480], start=(k == 0), stop=(k == 2)),
                         reads=[bf("cqT"), bf("w_uq")], writes=[Bbk[4]], inc=False)
                    S.op("pe", lambda e, k=k: e.matmul(bk[5][:, 0:288], lhsT=cqT[:, k, :], rhs=w_uq[:, k, 480:768], start=(k == 0), stop=(k == 2)),
                         reads=[bf("cqT"), bf("w_uq")], writes=[Bbk[5]], inc=(k == 2))
                for k in range(2):
                    S.op("pe", lambda e, k=k: e.matmul(bk[0][:, :], lhsT=cqT[:, 3 + k, :], rhs=w_ukv[:, k, 0:512], start=(k == 0), stop=(k == 1)),
                         reads=[bf("cqT"), bf("w_ukv")], writes=[Bbk[0]], inc=False)
                    S.op("pe", lambda e, k=k: e.matmul(bk[1][:, :], lhsT=cqT[:, 3 + k, :], rhs=w_ukv[:, k, 512:1024], start=(k == 0), stop=(k == 1)),
                         reads=[bf("cqT"), bf("w_ukv")], writes=[Bbk[1]], inc=(k == 1))
                if DBG_STOP <= 3.1:
                    return
                q4 = bk[4][:, 0:480].rearrange("p (h c) -> p h c", c=96)
                q5 = bk[5][:, 0:288].rearrange("p (h c) -> p h c", c=96)
                k0 = bk[0][:, :].rearrange("p (h c) -> p h c", c=128)
                k1 = bk[1][:, :].rearrange("p (h c) -> p h c", c=128)
                S.op("act", lambda e: e.activation(out=q_tm[:, 0:5, 0:64], in_=q4[:, :, 0:64], func=AF.Copy), reads=[Bbk[4]], writes=[bf("q_tm")])
                S.op("act", lambda e: e.activation(out=q_tm[:, 5:8, 0:64], in_=q5[:, :, 0:64], func=AF.Copy), reads=[Bbk[5]], writes=[bf("q_tm")])
                S.op("dve", lambda e: e.tensor_copy(out=rp[:, 0:5, :], in_=q4[:, :, 64:96]), reads=[Bbk[4]], writes=[bf("rp")])
                S.op("dve", lambda e: e.tensor_copy(out=rp[:, 5:8, :], in_=q5[:, :, 64:96]), reads=[Bbk[5]], writes=[bf("rp")])
                S.op("dve", lambda e: e.tensor_copy(out=rp[:, 8, :], in_=z_sb[:, 640:672]), reads=[bf("z1")], writes=[bf("rp")])
                if DBG_STOP <= 3.2:
                    return
                cs = cosT[:, n, :].unsqueeze(1).broadcast_to([128, 9, 16])
                sn = sinT[:, n, :].unsqueeze(1).broadcast_to([128, 9, 16])
                S.op("dve", lambda e: e.tensor_tensor(out=rt[0][:], in0=rp[:, :, 0:16], in1=cs, op=ALU.mult), reads=[bf("rp"), bf("cos")], writes=[bf("rt0")])
                S.op("dve", lambda e: e.tensor_tensor(out=rt[1][:], in0=rp[:, :, 16:32], in1=sn, op=ALU.mult), reads=[bf("rp"), bf("sin")], writes=[bf("rt1")])
                S.op("dve", lambda e: e.tensor_tensor(out=rt[2][:], in0=rp[:, :, 16:32], in1=cs, op=ALU.mult), reads=[bf("rp"), bf("cos")], writes=[bf("rt2")])
                S.op("dve", lambda e: e.tensor_tensor(out=rt[3][:], in0=rp[:, :, 0:16], in1=sn, op=ALU.mult), reads=[bf("rp"), bf("sin")], writes=[bf("rt3")])
                S.op("dve", lambda e: e.tensor_tensor(out=rr[:, :, 0:16], in0=rt[0][:], in1=rt[1][:], op=ALU.subtract), reads=[bf("rt0"), bf("rt1")], writes=[bf("rr")])
                S.op("dve", lambda e: e.tensor_tensor(out=rr[:, :, 16:32], in0=rt[2][:], in1=rt[3][:], op=ALU.add), reads=[bf("rt2"), bf("rt3")], writes=[bf("rr")])
                S.op("dve", lambda e: e.tensor_copy(out=q_tm[:, :, 64:96], in_=rr[:, 0:8, :]), reads=[bf("rr")], writes=[bf("q_tm")])
                S.op("dve", lambda e: e.tensor_copy(out=k_tm[:, :, 64:96], in_=rr[:, 8:9, :].broadcast_to([128, 8, 32])), reads=[bf("rr")], writes=[bf("k_tm")])
                if DBG_STOP <= 3.3:
                    return
                S.op("act", lambda e: e.activation(out=k_tm[:, 0:4, 0:64], in_=k0[:, :, 0:64], func=AF.Copy), reads=[Bbk[0]], writes=[bf("k_tm")])
                S.op("dve", lambda e: e.tensor_copy(out=k_tm[:, 4:8, 0:64], in_=k1[:, :, 0:64]), reads=[Bbk[1]], writes=[bf("k_tm")])
                if DBG_STOP <= 3.4:
                    return
                S.op("act", lambda e: e.activation(out=Vc[:, n, 0:4, 0:64], in_=k0[:, :, 64:128], func=AF.Copy), reads=[Bbk[0]], writes=[BV[n]])
                S.op("dve", lambda e: e.tensor_copy(out=Vc[:, n, 4:8, 0:64], in_=k1[:, :, 64:128]), reads=[Bbk[1]], writes=[BV[n]])
                if DBG_STOP <= 4:
                    return
                transpose_to(lambda h: q_tm[:, h, :], 8, 96, bf("q_tm"), qT[0:96, :, s * 128:(s + 1) * 128], bf("qT"), eng="act")
                transpose_to(lambda h: k_tm[:, h, :], 8, 96, bf("k_tm"), KT[0:96, :, t0:t0 + 128], BKT[n], eng="dve")
                if DBG_STOP <= 5:
                    return
                S.op("act", lambda e: e.activation(out=uv[:], in_=z_sb[:, 672:1184], func=AF.Gelu_apprx_tanh), reads=[bf("z1"), bf("z2")], writes=[bf("uv")])
                v3 = uv[:, 256:512].rearrange("p (g e) -> p g e", g=4)
                Bsv = bf("stv")
                S.op("dve", lambda e: e.tensor_reduce(out=stv[:, 0:4], in_=v3, axis=AX.X, op=ALU.add), reads=[bf("uv")], writes=[Bsv])
                S.op("pool", lambda e: e.tensor_tensor(out=sq[:], in0=uv[:, 256:512], in1=uv[:, 256:512], op=ALU.mult), reads=[bf("uv")], writes=[bf("sq")])
                S.op("dve", lambda e: e.tensor_reduce(out=stv[:, 4:8], in_=sq[:].rearrange("p (g e) -> p g e", g=4), axis=AX.X, op=ALU.add), reads=[bf("sq")], writes=[Bsv])
                S.op("pool", lambda e: e.tensor_scalar(out=stv[:, 8:12], in0=stv[:, 0:4], scalar1=1.0 / 64, scalar2=None, op0=ALU.mult), reads=[Bsv], writes=[Bsv])
                S.op("pool", lambda e: e.tensor_tensor(out=stv[:, 12:16], in0=stv[:, 8:12], in1=stv[:, 8:12], op=ALU.mult), reads=[Bsv], writes=[Bsv])
                S.op("pool", lambda e: e.tensor_scalar(out=stv[:, 24:28], in0=stv[:, 4:8], scalar1=1.0 / 64, scalar2=EPS, op0=ALU.mult, op1=ALU.add), reads=[Bsv], writes=[Bsv])
                S.op("pool", lambda e: e.tensor_tensor(out=stv[:, 16:20], in0=stv[:, 24:28], in1=stv[:, 12:16], op=ALU.subtract), reads=[Bsv], writes=[Bsv])
                S.op("pool", lambda e: e.tensor_tensor(out=stv[:, 20:24], in0=stv[:, 16:20], in1=mhalf[:, 0:4], op=ALU.pow), reads=[Bsv, bf("mhalf")], writes=[Bsv])
                vn3 = vn32[:].rearrange("p (g e) -> p g e", g=4)
                S.op("dve", lambda e: e.tensor_tensor(out=vn3, in0=v3, in1=stv[:, 8:12].unsqueeze(2).broadcast_to([128, 4, 64]), op=ALU.subtract), reads=[bf("uv"), Bsv], writes=[bf("vn32")])
                S.op("dve", lambda e: e.tensor_tensor(out=vn3, in0=vn3, in1=stv[:, 20:24].unsqueeze(2).broadcast_to([128, 4, 64]), op=ALU.mult), reads=[bf("vn32"), Bsv], writes=[bf("vn32")])
                S.op("pool", lambda e: e.tensor_tensor(out=vn32[:], in0=vn32[:], in1=sgln[:, 0:256], op=ALU.mult), reads=[bf("vn32"), bf("sgln")], writes=[bf("vn32")])
                S.op("pool", lambda e: e.tensor_tensor(out=vn[:], in0=vn32[:], in1=sgln[:, 256:512], op=ALU.add), reads=[bf("vn32"), bf("sgln")], writes=[bf("vn")])
                for g in range(4):
                    S.op("pe", lambda e, g=g: e.matmul(bk[2][:, g * 64:(g + 1) * 64], lhsT=wspT[:, g, :], rhs=vn[:, g * 64:(g + 1) * 64], start=True, stop=True, skip_group_check=True),
                         reads=[bf("wspT"), bf("vn")], writes=[Bbk[2]], inc=(g == 3))
                for g in range(4):
                    S.op("dve", lambda e, g=g: e.scalar_tensor_tensor(out=sq[:, g * 64:(g + 1) * 64], in0=bk[2][:, g * 64:(g + 1) * 64], scalar=gT[:, g0 + 29 + g:g0 + 30 + g],
                                                                      in1=uv[:, g * 64:(g + 1) * 64], op0=ALU.add, op1=ALU.mult),
                         reads=[Bbk[2], bf("gT"), bf("uv")], writes=[bf("sq")])
                slb, Bb_ = new_slot()
                S.op("act", lambda e: e.activation(out=mix_tm[:, s, 512:768], in_=sq[:], func=AF.Square, accum_out=slb[:, 0:1]), reads=[bf("sq")], writes=[bf(f"mix{s}"), Bb_])
                rb = rstd_from_ss(slb, Bb_, 1, 256)
                S.op("dve", lambda e: e.tensor_scalar(out=mix_tm[:, s, 512:768], in0=sq[:], scalar1=rb, scalar2=None, op0=ALU.mult), reads=[bf("sq"), Bb_], writes=[bf(f"mix{s}")])
                if DBG_STOP <= 6:
                    return
                yc, yp = ycv[n % 2], ycv[(n + 1) % 2]
                Byc, Byp = bf(f"ycv{n % 2}"), bf(f"ycv{(n + 1) % 2}")
                S.op("pool", lambda e: e.tensor_tensor(out=yc[:], in0=z_sb[:, 1440:1696], in1=z_sb[:, 1696:1952], op=ALU.mult), reads=[bf("z2"), bf("z3")], writes=[Byc])
                S.dma("sp", lambda e: e.dma_start(out=ysh1[1:128, :], in_=yc[0:127, :]), reads=[Byc], writes=[bf("ysh1")])
                S.dma("sp", lambda e: e.dma_start(out=ysh1[0:1, :], in_=yp[127:128, :]), reads=[Byp], writes=[bf("ysh1")])
                S.dma("sp", lambda e: e.dma_start(out=ysh2[2:128, :], in_=yc[0:126, :]), reads=[Byc], writes=[bf("ysh2")])
                S.dma("sp", lambda e: e.dma_start(out=ysh2[0:2, :], in_=yp[126:128, :]), reads=[Byp], writes=[bf("ysh2")])
                S.op("pool", lambda e: e.tensor_tensor(out=acc[:], in0=yc[:], in1=convw[:, 512:768], op=ALU.mult), reads=[Byc, bf("convw")], writes=[bf("acc")])
                S.op("pool", lambda e: e.tensor_tensor(out=ysh1[:], in0=ysh1[:], in1=convw[:, 256:512], op=ALU.mult), reads=[bf("ysh1"), bf("convw")], writes=[bf("ysh1")])
                S.op("pool", lambda e: e.tensor_tensor(out=acc[:], in0=acc[:], in1=ysh1[:], op=ALU.add), reads=[bf("acc"), bf("ysh1")], writes=[bf("acc")])
                S.op("pool", lambda e: e.tensor_tensor(out=ysh2[:], in0=ysh2[:], in1=convw[:, 0:256], op=ALU.mult), reads=[bf("ysh2"), bf("convw")], writes=[bf("ysh2")])
                S.op("pool", lambda e: e.tensor_tensor(out=acc[:], in0=acc[:], in1=ysh2[:], op=ALU.add), reads=[bf("acc"), bf("ysh2")], writes=[bf("acc")])
                S.op("pool", lambda e: e.tensor_tensor(out=acc[:], in0=acc[:], in1=z_sb[:, 1184:1440], op=ALU.mult), reads=[bf("acc"), bf("z2")], writes=[bf("acc")])
                slc, Bc_ = new_slot()
                S.op("act", lambda e: e.activation(out=mix_tm[:, s, 768:1024], in_=acc[:], func=AF.Square, accum_out=slc[:, 0:1]), reads=[bf("acc")], writes=[bf(f"mix{s}"), Bc_])
                rc = rstd_from_ss(slc, Bc_, 1, 256)
                S.op("dve", lambda e: e.tensor_scalar(out=mix_tm[:, s, 768:1024], in0=acc[:], scalar1=rc, scalar2=None, op0=ALU.mult), reads=[bf("acc"), Bc_], writes=[bf(f"mix{s}")])

            if DBG_STAGE in ('A', 'A0'):
                continue
            S.handoff(stA, stB)
            nkb = 4 * i + 4
            for h in range(H):
                O = bk[4 + h % 2]
                BO = Bbk[4 + h % 2]
                O3 = O[:, :].rearrange("p (j c) -> p j c", j=4)
                for kb in range(nkb):
                    j0 = max(0, kb - 4 * i)
                    c0 = j0 * 128
                    r = rot[0] % 4
                    rot[0] += 1
                    sbk, Bs = bk[r], Bbk[r]
                    S.op("pe", lambda e, kb=kb, c0=c0, sbk=sbk: e.matmul(sbk[:, c0:512], lhsT=KT[0:96, h, kb * 128:(kb + 1) * 128], rhs=qT[0:96, h, c0:512], start=True, stop=True),
                         reads=[BKT[kb], bf("qT")], writes=[Bs])
                    S.op("act", lambda e, c0=c0, sbk=sbk, r=r: e.activation(out=PT[r][:, c0:512], in_=sbk[:, c0:512], func=AF.Exp, scale=SCALE), reads=[Bs], writes=[bf(f"PT{r}")])
                    if kb >= 4 * i:
                        S.op("pool", lambda e, c0=c0, r=r: e.tensor_tensor(out=PT[r][:, c0:c0 + 128], in0=PT[r][:, c0:c0 + 128], in1=tri[:], op=ALU.mult),
                             reads=[bf(f"PT{r}"), bf("tri")], writes=[bf(f"PT{r}")])
                    for j in range(j0, 4):
                        S.op("pe", lambda e, kb=kb, j=j, r=r: e.matmul(O3[:, j, 0:65], lhsT=PT[r][:, j * 128:(j + 1) * 128], rhs=Vc[:, kb, h, :],
                                                                       start=(kb == 0 and j == 0), stop=(kb == 4 * i + j), skip_group_check=True),
                             reads=[bf(f"PT{r}"), BV[kb]], writes=[BO], inc=(j == 3))
                S.op("dve", lambda e: e.reciprocal(out=rinv[:].unsqueeze(2), in_=O3[:, :, 64:65]), reads=[BO], writes=[bf("rinv")])
                S.op("dve", lambda e: e.tensor_tensor(out=ya[:, :, h * 64:(h + 1) * 64], in0=O3[:, :, 0:64], in1=rinv[:].unsqueeze(2).broadcast_to([128, 4, 64]), op=ALU.mult),
                     reads=[BO, bf("rinv")], writes=[bf("ya")])
            for j in range(4):
                sla, Ba_ = new_slot()
                S.op("act", lambda e, j=j: e.activation(out=mix_tm[:, j, 0:512], in_=ya[:, j, :], func=AF.Square, accum_out=sla[:, 0:1]), reads=[bf("ya")], writes=[bf(f"mix{j}"), Ba_])
                ra = rstd_from_ss(sla, Ba_, 1, 512)
                S.op("dve", lambda e, j=j: e.tensor_scalar(out=mix_tm[:, j, 0:512], in0=ya[:, j, :], scalar1=ra, scalar2=None, op0=ALU.mult), reads=[bf("ya"), Ba_], writes=[bf(f"mix{j}")])
            S.dma("sp", lambda e: e.dma_start(out=mix_d[i * 512:(i + 1) * 512, :].rearrange("(j p) d -> p j d", p=128), in_=mix_tm[:]),
                  reads=[bf("mix0"), bf("mix1"), bf("mix2"), bf("mix3")], writes=[Md[i]])

    def ffn_pass(l):
        xsrc = x_d if l == 0 else out_d
        g0 = l * GW
        S.barrier()
        for k in range(8):
            S.dma("pool", lambda e, k=k: e.dma_start(out=w_out[:, k, :], in_=w_out_d[l, k * 128:(k + 1) * 128, :]), writes=[bf("w_out")])
        S.dma("sp", lambda e: e.dma_start(out=postg_m[:], in_=bc_d[l, :, 0:1024]), writes=[bf("postg_m")])
        S.dma("sp", lambda e: e.dma_start(out=postg_f[:], in_=bc_d[l, :, 1024:2048]), writes=[bf("postg_f")])
        for k in range(8):
            S.dma("pool", lambda e, k=k: e.dma_start(out=w_gate[:, k, :], in_=w_gate_d[l, k * 128:(k + 1) * 128, :]), writes=[bf(f"w_gate{k}")])
            S.dma("pool", lambda e, k=k: e.dma_start(out=w_up[:, k, :], in_=w_up_d[l, k * 128:(k + 1) * 128, :]), writes=[bf(f"w_up{k}")])
        for c in range(NFF):
            S.dma("pool", lambda e, c=c: e.dma_start(out=w_down[:, c, :], in_=w_down_d[l, c * 128:(c + 1) * 128, :]), writes=[bf(f"w_down{c}")])

        def post_norm_residual(postg, Bpostg, n):
            sl, Bsl = new_slot()
            S.op("act", lambda e: e.activation(out=t_f[:, 0:512], in_=bk[4][:, :], func=AF.Square, accum_out=sl[:, 0:1]), reads=[Bbk[4]], writes=[bf("t_f"), Bsl])
            S.op("act", lambda e: e.activation(out=t_f[:, 512:1024], in_=bk[5][:, :], func=AF.Square, accum_out=sl[:, 1:2]), reads=[Bbk[5]], writes=[bf("t_f"), Bsl])
            r = rstd_from_ss(sl, Bsl, 2, D)
            S.op("dve", lambda e: e.scalar_tensor_tensor(out=t_f[:, 0:512], in0=bk[4][:, :], scalar=r, in1=postg[:, 0:512], op0=ALU.mult, op1=ALU.mult),
                 reads=[Bbk[4], Bsl, Bpostg], writes=[bf("t_f")])
            S.op("dve", lambda e: e.scalar_tensor_tensor(out=t_f[:, 512:1024], in0=bk[5][:, :], scalar=r, in1=postg[:, 512:1024], op0=ALU.mult, op1=ALU.mult),
                 reads=[Bbk[5], Bsl, Bpostg], writes=[bf("t_f")])
            S.op("pool", lambda e: e.tensor_tensor(out=xs_f[:], in0=xs_f[:], in1=t_f[:], op=ALU.add), reads=[bf("xs_f"), bf("t_f")], writes=[bf("xs_f")])

        for i in range(DBG_T if DBG_STAGE == 'full' else 0):
            for s in range(4):
                n = 4 * i + s
                t0 = n * 128
                m0, Bm0 = xn_f[0], bf("xn_f0")
                m1, Bm1 = xn_f[1], bf("xn_f1")
                S.dma("sp", lambda e: e.dma_start(out=m0[:], in_=mix_d[t0:t0 + 128, :]), reads=[Md[i]], writes=[Bm0])
                transpose_to(lambda c: m0[:, c * 128:(c + 1) * 128], 8, 128, Bm0, mixT[:], bf("mixT"), gcol0=g0 + 16)
                for k in range(8):
                    S.op("pe", lambda e, k=k: e.matmul(bk[4][:, :], lhsT=mixT[:, k, :], rhs=w_out[:, k, 0:512], start=(k == 0), stop=(k == 7)),
                         reads=[bf("mixT"), bf("w_out")], writes=[Bbk[4]], inc=False)
                    S.op("pe", lambda e, k=k: e.matmul(bk[5][:, :], lhsT=mixT[:, k, :], rhs=w_out[:, k, 512:1024], start=(k == 0), stop=(k == 7)),
                         reads=[bf("mixT"), bf("w_out")], writes=[Bbk[5]], inc=(k == 7))
                S.dma("sp", lambda e: e.dma_start(out=xs_f[:], in_=xsrc[t0:t0 + 128, :]), reads=[Xd[n]], writes=[bf("xs_f")])
                post_norm_residual(postg_m, bf("postg_m"), n)
                S.dma("sp", lambda e: e.dma_start(out=out_d[t0:t0 + 128, :], in_=xs_f[:]), reads=[bf("xs_f")], writes=[Xd[n]])
                sl, Bsl = new_slot()
                S.op("act", lambda e: e.activation(out=m1[:], in_=xs_f[:], func=AF.Square, accum_out=sl[:, 0:1]), reads=[bf("xs_f")], writes=[Bm1, Bsl])
                r = rstd_from_ss(sl, Bsl, 1, D)
                S.op("dve", lambda e: e.tensor_scalar(out=m1[:], in0=xs_f[:], scalar1=r, scalar2=None, op0=ALU.mult), reads=[bf("xs_f"), Bsl], writes=[Bm1])
                transpose_to(lambda c: m1[:, c * 128:(c + 1) * 128], 8, 128, Bm1, h2T[:, :, s * 128:(s + 1) * 128], bf("h2T"), gcol0=g0 + 8)
            for c in range(NFF):
                gb, Bg = bk[c % 2], Bbk[c % 2]
                ub, Bu = bk[2 + c % 2], Bbk[2 + c % 2]
                for k in range(8):
                    S.op("pe", lambda e, k=k, c=c, gb=gb: e.matmul(gb[:, :], lhsT=w_gate[:, k, c * 128:(c + 1) * 128], rhs=h2T[:, k, :], start=(k == 0), stop=(k == 7)),
                         reads=[bf(f"w_gate{k}"), bf("h2T")], writes=[Bg], inc=(k == 7))
                for k in range(8):
                    S.op("pe", lambda e, k=k, c=c, ub=ub: e.matmul(ub[:, :], lhsT=w_up[:, k, c * 128:(c + 1) * 128], rhs=h2T[:, k, :], start=(k == 0), stop=(k == 7)),
                         reads=[bf(f"w_up{k}"), bf("h2T")], writes=[Bu], inc=(k == 7))
                sg, Bsg = sg_f[c % 2], bf(f"xn_f{c % 2}")
                S.op("act", lambda e, gb=gb, sg=sg: e.activation(out=sg[:], in_=gb[:, :], func=AF.Silu), reads=[Bg], writes=[Bsg])
                S.op("dve", lambda e, c=c, ub=ub, sg=sg: e.tensor_tensor(out=aT[:, c, :], in0=ub[:, :], in1=sg[:], op=ALU.mult), reads=[Bu, Bsg], writes=[bf(f"aT{c}")])
            for s in range(4):
                n = 4 * i + s
                t0 = n * 128
                for c in range(NFF):
                    S.op("pe", lambda e, c=c: e.matmul(bk[4][:, :], lhsT=aT[:, c, s * 128:(s + 1) * 128], rhs=w_down[:, c, 0:512], start=(c == 0), stop=(c == NFF - 1)),
                         reads=[bf(f"aT{c}"), bf(f"w_down{c}")], writes=[Bbk[4]], inc=False)
                    S.op("pe", lambda e, c=c: e.matmul(bk[5][:, :], lhsT=aT[:, c, s * 128:(s + 1) * 128], rhs=w_down[:, c, 512:1024], start=(c == 0), stop=(c == NFF - 1)),
                         reads=[bf(f"aT{c}"), bf(f"w_down{c}")], writes=[Bbk[5]], inc=(c == NFF - 1))
                S.dma("sp", lambda e: e.dma_start(out=xs_f[:], in_=out_d[t0:t0 + 128, :]), reads=[Xd[n]], writes=[bf("xs_f")])
                post_norm_residual(postg_f, bf("postg_f"), n)
                S.dma("sp", lambda e: e.dma_start(out=out_d[t0:t0 + 128, :], in_=xs_f[:]), reads=[bf("xs_f")], writes=[Xd[n]])

    for l in range(L if DBG_STAGE != 'P' else 0):
        mixer_pass(l)
        if DBG_STAGE != 'A0':
            ffn_pass(l)
    S.finish("sp", Xd)
    S.barrier()
    return nc


def _layouts(p):
    L = DEPTH
    gT = np.zeros((128, L * GW), np.float32)
    for l in range(L):
        o = l * GW
        gT[:, o + 0:o + 8] = p["mix_pre_g"][l].reshape(8, 128).T
        gT[:, o + 8:o + 16] = p["ffn_pre_g"][l].reshape(8, 128).T
        gT[:, o + 16:o + 24] = p["out_norm_g"][l].reshape(8, 128).T
        gT[:, o + 24:o + 27] = p["q_norm_g"][l].reshape(3, 128).T
        gT[:, o + 27:o + 29] = p["kv_norm_g"][l].reshape(2, 128).T
        gT[:, o + 29:o + 33] = p["b_sp"][l].T
    bc = np.zeros((L, 128, BCW), np.float32)
    bc[:, :, 0:1024] = p["mix_post_g"][:, None, :]
    bc[:, :, 1024:2048] = p["ffn_post_g"][:, None, :]
    bc[:, :, 2048:2304] = p["sg_ln_g"][:, None, :]
    bc[:, :, 2304:2560] = p["sg_ln_b"][:, None, :]
    bc[:, :, 2560:3328] = p["conv_w"].reshape(L, 1, 768)
    wspT = np.ascontiguousarray(np.transpose(p["w_sp"], (0, 3, 1, 2))).reshape(L, 128, 512)
    return gT, bc, wspT


_INVF = (np.float32(1.0) / (np.float32(10000.0) ** (np.arange(16, dtype=np.float32) / np.float32(16)))).astype(np.float32)


def kernel(x, positions, mix_pre_g, mix_post_g, ffn_pre_g, ffn_post_g, w_in, q_norm_g, w_uq, kv_norm_g, w_ukv,
           sg_ln_g, sg_ln_b, w_sp, b_sp, conv_w, out_norm_g, w_out, w_gate, w_up, w_down, _depth=DEPTH, _cores=8):
    p = dict(mix_pre_g=mix_pre_g, mix_post_g=mix_post_g, ffn_pre_g=ffn_pre_g, ffn_post_g=ffn_post_g, q_norm_g=q_norm_g,
             kv_norm_g=kv_norm_g, sg_ln_g=sg_ln_g, sg_ln_b=sg_ln_b, w_sp=w_sp, b_sp=b_sp, conv_w=conv_w, out_norm_g=out_norm_g)
    p = {k: np.asarray(v, np.float32) for k, v in p.items()}
    gT, bc, wspT = _layouts(p)
    f = lambda a: np.ascontiguousarray(np.asarray(a, np.float32))
    shared = {"invf": np.ascontiguousarray(np.broadcast_to(_INVF[None, :], (128, 16))), "w_in": f(w_in), "w_uq": f(w_uq), "w_ukv": f(w_ukv),
              "w_out": f(w_out), "w_gate": f(w_gate), "w_up": f(w_up), "w_down": f(w_down), "gT": gT, "bc": bc, "wspT": wspT}
    x = np.asarray(x, np.float32)
    positions = np.asarray(positions, np.int32)
    nc = build_program(_depth)
    in_maps = []
    for b in range(_cores):
        m = dict(shared)
        m["x"] = np.ascontiguousarray(x[b])
        m["pos"] = np.ascontiguousarray(positions[b].reshape(NSUB, 128).T)
        in_maps.append(m)
    res = run_bass_kernel_spmd(nc, in_maps, core_ids=list(range(_cores)))
    return np.stack([np.asarray(r["out"], np.float32) for r in res.results], axis=0)
```

```python
import math
import os
import numpy as np
import concourse.bass as bass
import concourse.mybir as mybir
from concourse.bass_utils import run_bass_kernel_spmd

F32 = mybir.dt.float32
BF16 = mybir.dt.bfloat16
I32 = mybir.dt.int32
AF = mybir.ActivationFunctionType
ALU = mybir.AluOpType
AX = mybir.AxisListType

D = 1024
SEQ = 4096
DEPTH = 4
NSUB = SEQ // 128
NTILE = SEQ // 512
INW = 1952
H = 8
DFF = 2816
NFF = DFF // 128
EPS = 1e-6
SCALE = 96.0 ** -0.5
GW = 33
BCW = 3328
ND = 8


class Buf:
    __slots__ = ("name", "w", "r", "psum")

    def __init__(self, name, psum=False):
        self.name = name
        self.w = None
        self.r = {}
        self.psum = psum


class Sched:
    def __init__(self, nc):
        self.nc = nc
        self.E = {"pe": nc.tensor, "act": nc.scalar, "dve": nc.vector, "pool": nc.gpsimd, "sp": nc.sync}
        self.sem = {k: nc.alloc_semaphore("s_" + k) for k in self.E}
        self.cnt = {k: 0 for k in self.E}
        self.seen = {k: {} for k in self.E}
        self.pend = {k: [] for k in self.E}
        self.dsem = {q: [nc.alloc_semaphore(f"d_{q}{i}") for i in range(ND)] for q in ("sp", "pool")}
        self.dcnt = {q: [0] * ND for q in self.dsem}
        self.drr = {q: 0 for q in self.dsem}

    def _collect(self, eng, reads, writes, skip_own_war=True):
        toks = {}
        own = self.sem.get(eng)

        def need(s, v, war=False):
            if s is own and (eng == "pe" or (war and skip_own_war)):
                return
            if toks.get(s, 0) < v:
                toks[s] = v

        for b in reads:
            if b.w is not None:
                need(*b.w)
            if b.psum:
                for s, v in b.r.items():
                    if s is not own:
                        need(s, v)
        for b in writes:
            if b.w is not None:
                need(*b.w)
            for s, v in b.r.items():
                need(s, v, war=True)
        return toks

    def _emit_waits(self, eng, toks):
        e = self.E[eng]
        seen = self.seen[eng]
        for s, v in toks.items():
            if seen.get(s, 0) < v:
                e.wait_ge(s, v)
                seen[s] = v

    def op(self, eng, fn, reads=(), writes=(), inc=True):
        self._emit_waits(eng, self._collect(eng, reads, writes))
        ins = fn(self.E[eng])
        self.pend[eng].append((tuple(reads), tuple(writes)))
        if inc:
            self.cnt[eng] += 1
            s = self.sem[eng]
            ins.then_inc(s, 1)
            v = self.cnt[eng]
            for rs, ws in self.pend[eng]:
                for b in rs:
                    b.r[s] = v
                for b in ws:
                    b.w = (s, v)
                    b.r = {}
            self.pend[eng] = []
        return ins

    def dma(self, q, fn, reads=(), writes=()):
        assert not self.pend[q]
        i = self.drr[q]
        self.drr[q] = (i + 1) % ND
        s = self.dsem[q][i]
        toks = self._collect(q, reads, writes, skip_own_war=False)
        prev = 16 * self.dcnt[q][i]
        if prev and toks.get(s, 0) < prev:
            toks[s] = prev
        self._emit_waits(q, toks)
        ins = fn(self.E[q])
        ins.then_inc(s, 16)
        self.dcnt[q][i] += 1
        v = 16 * self.dcnt[q][i]
        for b in reads:
            b.r[s] = v
        for b in writes:
            b.w = (s, v)
            b.r = {}
        return ins

    def handoff(self, src, dst):
        for d in dst:
            for b in src:
                if b.w is not None:
                    s, v = b.w
                    if d.r.get(s, 0) < v:
                        d.r[s] = v
                for s, v in b.r.items():
                    if d.r.get(s, 0) < v:
                        d.r[s] = v

    def barrier(self):
        for k in self.E:
            assert not self.pend[k]
        toks = {}
        for k in self.E:
            if self.cnt[k]:
                toks[self.sem[k]] = self.cnt[k]
        for q in self.dsem:
            for i in range(ND):
                if self.dcnt[q][i]:
                    toks[self.dsem[q][i]] = 16 * self.dcnt[q][i]
        for k in self.E:
            t = {s: v for s, v in toks.items() if s is not self.sem[k]}
            self._emit_waits(k, t)

    def finish(self, eng, bufs):
        toks = {}
        for b in bufs:
            if b.w is not None:
                s, v = b.w
                toks[s] = max(toks.get(s, 0), v)
            for s, v in b.r.items():
                toks[s] = max(toks.get(s, 0), v)
        self._emit_waits(eng, toks)


def build_program(L=DEPTH):
    DBG_T = int(os.environ.get('KDBG_TILES', NTILE))
    DBG_STAGE = os.environ.get('KDBG_STAGE', 'full')
    DBG_STOP = float(os.environ.get('KDBG_STOP', 99))
    nc = bass.Bass("TRN2", target_bir_lowering=False)
    S = Sched(nc)

    def dram(name, shape, dt, kind):
        return nc.dram_tensor(name, shape, dt, kind=kind).ap()

    x_d = dram("x", [SEQ, D], F32, "ExternalInput")
    pos_d = dram("pos", [128, NSUB], I32, "ExternalInput")
    invf_d = dram("invf", [128, 16], F32, "ExternalInput")
    w_in_d = dram("w_in", [DEPTH, D, INW], F32, "ExternalInput")
    w_uq_d = dram("w_uq", [DEPTH, 384, 768], F32, "ExternalInput")
    w_ukv_d = dram("w_ukv", [DEPTH, 256, 1024], F32, "ExternalInput")
    w_out_d = dram("w_out", [DEPTH, D, D], F32, "ExternalInput")
    w_gate_d = dram("w_gate", [DEPTH, D, DFF], F32, "ExternalInput")
    w_up_d = dram("w_up", [DEPTH, D, DFF], F32, "ExternalInput")
    w_down_d = dram("w_down", [DEPTH, DFF, D], F32, "ExternalInput")
    gT_d = dram("gT", [128, DEPTH * GW], F32, "ExternalInput")
    bc_d = dram("bc", [DEPTH, 128, BCW], F32, "ExternalInput")
    wsp_d = dram("wspT", [DEPTH, 128, 512], F32, "ExternalInput")
    out_d = dram("out", [SEQ, D], F32, "ExternalOutput")
    mix_d = dram("mixd", [SEQ, D], BF16, "Internal")

    base = (nc.sbuf_base + 31) // 32 * 32
    top = nc.sbuf_top

    def sb(name, shape, dt, off):
        nb = int(np.prod(shape[1:])) * (2 if dt == BF16 else 4)
        assert off % 32 == 0 and base + off + nb <= top, (name, off, nb, top - base)
        return nc.alloc_sbuf_tensor_at(name, list(shape), dt, offset=base + off)

    ident = sb("ident", [128, 128], BF16, 0)
    tri = sb("tri", [128, 128], BF16, 256)
    mhalf = sb("mhalf", [128, 8], F32, 512)
    cosT = sb("cosT", [128, NSUB, 16], F32, 576)
    sinT = sb("sinT", [128, NSUB, 16], F32, 2624)
    gT = sb("gTs", [128, DEPTH * GW], F32, 4672)
    stt_ = sb("stats", [128, 64], F32, 5216)
    invf = sb("invfs", [128, 16], F32, 5472)
    posi = sb("posi", [128, NSUB], I32, 5536)
    posf = sb("posf", [128, NSUB], F32, 5664)
    stv = sb("stv", [128, 32], F32, 5792)
    CEND = 5920
    BIG = CEND
    MID = BIG + 174080
    assert base + MID + 30720 <= top, (base, MID, top)

    KT = sb("KT", [128, H, SEQ], BF16, BIG + 0)
    Vc = sb("Vc", [128, NSUB, H, 65], BF16, BIG + 65536)
    w_in = sb("w_in_s", [128, 8, INW], BF16, BIG + 98816)
    w_uq = sb("w_uq_s", [128, 3, 768], BF16, BIG + 130048)
    w_ukv = sb("w_ukv_s", [128, 2, 1024], BF16, BIG + 134656)
    qT = sb("qT", [128, H, 512], BF16, BIG + 138752)
    mix_tm = sb("mix_tm", [128, 4, D], BF16, BIG + 146944)
    SA = BIG + 155136
    z_sb = sb("z_sb", [128, INW], F32, SA)
    uv = sb("uv", [128, 512], F32, SA + 7808)
    ysh1 = sb("ysh1", [128, 256], F32, SA + 9856)
    ysh2 = sb("ysh2", [128, 256], F32, SA + 10880)
    acc = sb("acc", [128, 256], F32, SA + 11904)
    sq = sb("sq", [128, 256], F32, SA + 12928)
    vn32 = sb("vn32", [128, 256], F32, SA + 13952)
    cqkv = sb("cqkv", [128, 640], BF16, SA + 14976)
    ycv = [sb("ycv0", [128, 256], F32, SA + 16352), sb("ycv1", [128, 256], F32, SA + 17376)]
    assert SA + 18400 <= MID
    ya = sb("ya", [128, 4, 512], F32, SA)
    PT = [sb(f"PT{r}", [128, 512], BF16, SA + 8192 + 1024 * r) for r in range(4)]
    hT = [sb("hT0", [128, 8, 128], BF16, MID + 0), sb("hT1", [128, 8, 128], BF16, MID + 2048)]
    xs = sb("xs", [128, D], F32, MID + 4096)
    xn = sb("xn", [128, D], BF16, MID + 8192)
    sgln = sb("sgln", [128, 512], F32, MID + 10240)
    convw = sb("convw", [128, 768], F32, MID + 12288)
    wspT = sb("wspTs", [128, 4, 128], BF16, MID + 15360)
    cqT = sb("cqT", [128, 5, 128], BF16, MID + 16384)
    q_tm = sb("q_tm", [128, H, 96], BF16, MID + 17664)
    k_tm = sb("k_tm", [128, H, 96], BF16, MID + 19200)
    rp = sb("rp", [128, 9, 32], F32, MID + 20736)
    rt = [sb(f"rt{i}", [128, 9, 16], F32, MID + 21888 + 576 * i) for i in range(4)]
    rr = sb("rr", [128, 9, 32], F32, MID + 24192)
    vn = sb("vn", [128, 256], BF16, MID + 25344)
    rinv = sb("rinv", [128, 4], F32, MID + 25856)
    w_gate = sb("w_gate_s", [128, 8, DFF], BF16, BIG + 0)
    w_up = sb("w_up_s", [128, 8, DFF], BF16, BIG + 45056)
    w_down = sb("w_down_s", [128, NFF, D], BF16, BIG + 90112)
    w_out = sb("w_out_s", [128, 8, D], BF16, BIG + 135168)
    aT = sb("aT", [128, NFF, 512], BF16, BIG + 151552)
    h2T = sb("h2T", [128, 8, 512], BF16, MID + 0)
    xs_f = sb("xs_f", [128, D], F32, MID + 8192)
    xn_f = [sb("xn_f0", [128, D], BF16, MID + 12288), sb("xn_f1", [128, D], BF16, MID + 14336)]
    sg_f = [sb("sg_f0", [128, 512], F32, MID + 12288), sb("sg_f1", [128, 512], F32, MID + 14336)]
    mixT = sb("mixT", [128, 8, 128], BF16, MID + 16384)
    t_f = sb("t_f", [128, D], F32, MID + 18432)
    postg_m = sb("postg_m", [128, D], F32, MID + 22528)
    postg_f = sb("postg_f", [128, D], F32, MID + 26624)
    AT0 = BIG + 151552
    xs_f2 = sb("xs_f2", [128, D], F32, AT0 + 0)
    t_f2 = sb("t_f2", [128, D], F32, AT0 + 4096)
    xn_b = [sb("xn_b0", [128, D], BF16, AT0 + 8192), sb("xn_b1", [128, D], BF16, AT0 + 10240)]
    mixT2 = sb("mixT2", [128, 8, 128], BF16, AT0 + 12288)

    pT = [nc.alloc_psum_tensor(f"pT{i}", [128, 8, 128], BF16) for i in range(2)]
    bk = [nc.alloc_psum_tensor(f"bk{i}", [128, 512], F32) for i in range(6)]
    BpT = [Buf(f"pT{i}", psum=True) for i in range(2)]
    Bbk = [Buf(f"bk{i}", psum=True) for i in range(6)]
    tcount = [0]

    def next_pT():
        i = tcount[0] % 2
        tcount[0] += 1
        return pT[i], BpT[i]

    B = {}

    def bf(name):
        if name not in B:
            B[name] = Buf(name)
        return B[name]

    Xd = [Buf(f"xd{n}") for n in range(NSUB)]
    Md = [Buf(f"md{i}") for i in range(NTILE)]
    BKT = [Buf(f"kt{n}") for n in range(NSUB)]
    BV = [Buf(f"v{n}") for n in range(NSUB)]

    slot = [0]

    def new_slot():
        k = slot[0] % 16
        slot[0] += 1
        return stt_[:, 4 * k:4 * k + 4], bf(f"slot{k}")

    def rstd_from_ss(sl, Bsl, ncols, Dn):
        if ncols == 2:
            S.op("pool", lambda e: e.tensor_tensor(out=sl[:, 0:1], in0=sl[:, 0:1], in1=sl[:, 1:2], op=ALU.add), reads=[Bsl], writes=[Bsl])
        S.op("pool", lambda e: e.tensor_scalar(out=sl[:, 2:3], in0=sl[:, 0:1], scalar1=1.0 / Dn, scalar2=EPS, op0=ALU.mult, op1=ALU.add), reads=[Bsl], writes=[Bsl])
        S.op("pool", lambda e: e.tensor_tensor(out=sl[:, 3:4], in0=sl[:, 2:3], in1=mhalf[:, 0:1], op=ALU.pow), reads=[Bsl, bf("mhalf")], writes=[Bsl])
        return sl[:, 3:4]

    def transpose_to(src_ap_fn, nchunk, rows, Bsrc, dst_ap, Bdst, gcol0=None, eng="dve"):
        p, Bp = next_pT()
        for c in range(nchunk):
            S.op("pe", lambda e, c=c: e.transpose(out=p[0:rows, c, :], in_=src_ap_fn(c), identity=ident[:]),
                 reads=[Bsrc, bf("ident")], writes=[Bp], inc=(c == nchunk - 1))
        if gcol0 is None:
            if eng == "act":
                S.op("act", lambda e: e.activation(out=dst_ap, in_=p[0:rows, 0:nchunk, :], func=AF.Copy), reads=[Bp], writes=[Bdst])
            else:
                S.op(eng, lambda e: e.tensor_copy(out=dst_ap, in_=p[0:rows, 0:nchunk, :]), reads=[Bp], writes=[Bdst])
        else:
            g = gT[0:rows, gcol0:gcol0 + nchunk].unsqueeze(2).broadcast_to([rows, nchunk, 128])
            S.op(eng, lambda e: e.tensor_tensor(out=dst_ap, in0=p[0:rows, 0:nchunk, :], in1=g, op=ALU.mult), reads=[Bp, bf("gT")], writes=[Bdst])

    S.op("pool", lambda e: e.memset(ident[:], 1.0), writes=[bf("ident")])
    S.op("pool", lambda e: e.affine_select(out=ident[:], in_=ident[:], pattern=[[-1, 128]], compare_op=ALU.is_equal, fill=0.0, base=0, channel_multiplier=1),
         reads=[bf("ident")], writes=[bf("ident")])
    S.op("pool", lambda e: e.memset(tri[:], 1.0), writes=[bf("tri")])
    S.op("pool", lambda e: e.affine_select(out=tri[:], in_=tri[:], pattern=[[1, 128]], compare_op=ALU.is_ge, fill=0.0, base=0, channel_multiplier=-1),
         reads=[bf("tri")], writes=[bf("tri")])
    S.op("pool", lambda e: e.memset(mhalf[:], -0.5), writes=[bf("mhalf")])
    S.dma("sp", lambda e: e.dma_start(out=gT[:], in_=gT_d), writes=[bf("gT")])
    S.dma("sp", lambda e: e.dma_start(out=invf[:], in_=invf_d), writes=[bf("invf")])
    S.dma("sp", lambda e: e.dma_start(out=posi[:], in_=pos_d), writes=[bf("posi")])
    TWO_PI = 2.0 * math.pi
    C1 = float(np.float32(6.28125))
    C2 = float(np.float32(TWO_PI - 6.28125))
    MAGIC = 12582912.0
    ang = sb("ang", [128, NSUB, 16], F32, BIG + 0)
    kk = sb("kk", [128, NSUB, 16], F32, BIG + 2048)
    t2 = sb("t2p", [128, NSUB, 16], F32, BIG + 4096)
    S.op("dve", lambda e: e.tensor_copy(out=posf[:], in_=posi[:]), reads=[bf("posi")], writes=[bf("posf")])
    S.op("dve", lambda e: e.tensor_tensor(out=ang[:], in0=posf[:].unsqueeze(2).broadcast_to([128, NSUB, 16]),
                                          in1=invf[:].unsqueeze(1).broadcast_to([128, NSUB, 16]), op=ALU.mult),
         reads=[bf("posf"), bf("invf")], writes=[bf("ang")])
    S.op("dve", lambda e: e.tensor_scalar(out=t2[:], in0=ang[:], scalar1=1.0 / TWO_PI, scalar2=MAGIC, op0=ALU.mult, op1=ALU.add), reads=[bf("ang")], writes=[bf("t2")])
    S.op("dve", lambda e: e.tensor_scalar(out=kk[:], in0=t2[:], scalar1=-MAGIC, scalar2=None, op0=ALU.add), reads=[bf("t2")], writes=[bf("kk")])
    S.op("dve", lambda e: e.scalar_tensor_tensor(out=ang[:], in0=kk[:], scalar=-C1, in1=ang[:], op0=ALU.mult, op1=ALU.add), reads=[bf("kk"), bf("ang")], writes=[bf("ang")])
    S.op("dve", lambda e: e.scalar_tensor_tensor(out=ang[:], in0=kk[:], scalar=-C2, in1=ang[:], op0=ALU.mult, op1=ALU.add), reads=[bf("kk"), bf("ang")], writes=[bf("ang")])
    S.op("dve", lambda e: e.tensor_scalar(out=ang[:], in0=ang[:], scalar1=math.pi, scalar2=-math.pi, op0=ALU.min, op1=ALU.max), reads=[bf("ang")], writes=[bf("ang")])
    S.op("act", lambda e: e.activation(out=sinT[:], in_=ang[:], func=AF.Sin), reads=[bf("ang")], writes=[bf("sin")])
    S.op("dve", lambda e: e.tensor_scalar(out=t2[:], in0=ang[:], scalar1=-1.0, scalar2=None, op0=ALU.mult), reads=[bf("ang")], writes=[bf("t2")])
    S.op("dve", lambda e: e.tensor_tensor(out=t2[:], in0=t2[:], in1=ang[:], op=ALU.max), reads=[bf("ang"), bf("t2")], writes=[bf("t2")])
    S.op("dve", lambda e: e.tensor_scalar(out=t2[:], in0=t2[:], scalar1=-1.0, scalar2=math.pi / 2, op0=ALU.mult, op1=ALU.add), reads=[bf("t2")], writes=[bf("t2")])
    S.op("act", lambda e: e.activation(out=cosT[:], in_=t2[:], func=AF.Sin), reads=[bf("t2")], writes=[bf("cos")])

    stA = [bf("z0"), bf("z1"), bf("z2"), bf("z3"), bf("uv"), bf("ysh1"), bf("ysh2"), bf("acc"), bf("sq"), bf("vn32"), bf("cqkv")]
    stB = [bf("ya"), bf("PT0"), bf("PT1"), bf("PT2"), bf("PT3")]
    rot = [0]

    def mixer_pass(l):
        xsrc = x_d if l == 0 else out_d
        g0 = l * GW
        S.barrier()
        for k in range(8):
            S.dma("pool", lambda e, k=k: e.dma_start(out=w_in[:, k, :], in_=w_in_d[l, k * 128:(k + 1) * 128, :]), writes=[bf(f"w_in{k}")])
        for k in range(3):
            S.dma("pool", lambda e, k=k: e.dma_start(out=w_uq[:, k, :], in_=w_uq_d[l, k * 128:(k + 1) * 128, :]), writes=[bf("w_uq")])
        for k in range(2):
            S.dma("pool", lambda e, k=k: e.dma_start(out=w_ukv[:, k, :], in_=w_ukv_d[l, k * 128:(k + 1) * 128, :]), writes=[bf("w_ukv")])
        S.dma("pool", lambda e: e.dma_start(out=wspT[:].rearrange("p g t -> p (g t)"), in_=wsp_d[l]), writes=[bf("wspT")])
        S.dma("sp", lambda e: e.dma_start(out=sgln[:], in_=bc_d[l, :, 2048:2560]), writes=[bf("sgln")])
        S.dma("sp", lambda e: e.dma_start(out=convw[:], in_=bc_d[l, :, 2560:3328]), writes=[bf("convw")])
        S.op("dve", lambda e: e.tensor_tensor(out=wspT[:], in0=wspT[:], in1=tri[:].unsqueeze(1).broadcast_to([128, 4, 128]), op=ALU.mult),
             reads=[bf("wspT"), bf("tri")], writes=[bf("wspT")])
        S.op("pool", lambda e: e.memset(Vc[:, :, :, 64:65], 1.0), writes=BV)
        S.op("pool", lambda e: e.memset(ycv[1][:], 0.0), writes=[bf("ycv1")])

        for i in range(DBG_T):
            S.handoff(stB, stA)
            for s in range(4):
                n = 4 * i + s
                t0 = n * 128
                hb = hT[n % 2]
                Bh = bf(f"hT{n % 2}")
                if DBG_STOP <= 0:
                    return
                S.dma("sp", lambda e: e.dma_start(out=xs[:], in_=xsrc[t0:t0 + 128, :]), reads=[Xd[n]], writes=[bf("xs")])
                sl, Bsl = new_slot()
                S.op("act", lambda e: e.activation(out=xn[:], in_=xs[:], func=AF.Square, accum_out=sl[:, 0:1]), reads=[bf("xs")], writes=[bf("xn"), Bsl])
                r = rstd_from_ss(sl, Bsl, 1, D)
                S.op("dve", lambda e: e.tensor_scalar(out=xn[:], in0=xs[:], scalar1=r, scalar2=None, op0=ALU.mult), reads=[bf("xs"), Bsl], writes=[bf("xn")])
                transpose_to(lambda c: xn[:, c * 128:(c + 1) * 128], 8, 128, bf("xn"), hb[:], Bh, gcol0=g0 + 0)
                if DBG_STOP <= 1:
                    return
                cg = [(0, 512), (512, 1024), (1024, 1536), (1536, INW)]
                for k in range(8):
                    for q, (c0, c1) in enumerate(cg):
                        S.op("pe", lambda e, k=k, q=q, c0=c0, c1=c1: e.matmul(bk[q][:, 0:c1 - c0], lhsT=hb[:, k, :], rhs=w_in[:, k, c0:c1], start=(k == 0), stop=(k == 7)),
                             reads=[Bh, bf(f"w_in{k}")], writes=[Bbk[q]], inc=(k == 7 and q == 3))
                for q, (c0, c1) in enumerate(cg):
                    if q % 2 == 0:
                        S.op("act", lambda e, q=q, c0=c0, c1=c1: e.activation(out=z_sb[:, c0:c1], in_=bk[q][:, 0:c1 - c0], func=AF.Copy), reads=[Bbk[q]], writes=[bf(f"z{q}")])
                    else:
                        S.op("dve", lambda e, q=q, c0=c0, c1=c1: e.tensor_copy(out=z_sb[:, c0:c1], in_=bk[q][:, 0:c1 - c0]), reads=[Bbk[q]], writes=[bf(f"z{q}")])
                if DBG_STOP <= 2:
                    return
                slq, Bq = new_slot()
                slk, Bk_ = new_slot()
                S.op("act", lambda e: e.activation(out=cqkv[:, 0:384], in_=z_sb[:, 0:384], func=AF.Square, accum_out=slq[:, 0:1]), reads=[bf("z0")], writes=[bf("cqkv"), Bq])
                S.op("act", lambda e: e.activation(out=cqkv[:, 384:640], in_=z_sb[:, 384:640], func=AF.Square, accum_out=slk[:, 0:1]), reads=[bf("z0"), bf("z1")], writes=[bf("cqkv"), Bk_])
                rq = rstd_from_ss(slq, Bq, 1, 384)
                rk = rstd_from_ss(slk, Bk_, 1, 256)
                S.op("dve", lambda e: e.tensor_scalar(out=cqkv[:, 0:384], in0=z_sb[:, 0:384], scalar1=rq, scalar2=None, op0=ALU.mult), reads=[bf("z0"), Bq], writes=[bf("cqkv")])
                S.op("dve", lambda e: e.tensor_scalar(out=cqkv[:, 384:640], in0=z_sb[:, 384:640], scalar1=rk, scalar2=None, op0=ALU.mult), reads=[bf("z0"), bf("z1"), Bk_], writes=[bf("cqkv")])
                transpose_to(lambda c: cqkv[:, c * 128:(c + 1) * 128], 5, 128, bf("cqkv"), cqT[:], bf("cqT"), gcol0=g0 + 24)
                if DBG_STOP <= 3:
                    return
                for k in range(3):
                    S.op("pe", lambda e, k=k: e.matmul(bk[4][:, 0:480], lhsT=cqT[:, k, :], rhs=w_uq[:, k, 0:480], start=(k == 0), stop=(k == 2)),
                         reads=[bf("cqT"), bf("w_uq")], writes=[Bbk[4]], inc=False)
                    S.op("pe", lambda e, k=k: e.matmul(bk[5][:, 0:288], lhsT=cqT[:, k, :], rhs=w_uq[:, k, 480:768], start=(k == 0), stop=(k == 2)),
                         reads=[bf("cqT"), bf("w_uq")], writes=[Bbk[5]], inc=(k == 2))
                for k in range(2):
                    S.op("pe", lambda e, k=k: e.matmul(bk[0][:, :], lhsT=cqT[:, 3 + k, :], rhs=w_ukv[:, k, 0:512], start=(k == 0), stop=(k == 1)),
                         reads=[bf("cqT"), bf("w_ukv")], writes=[Bbk[0]], inc=False)
                    S.op("pe", lambda e, k=k: e.matmul(bk[1][:, :], lhsT=cqT[:, 3 + k, :], rhs=w_ukv[:, k, 512:1024], start=(k == 0), stop=(k == 1)),
                         reads=[bf("cqT"), bf("w_ukv")], writes=[Bbk[1]], inc=(k == 1))
                if DBG_STOP <= 3.1:
                    return
                q4 = bk[4][:, 0:480].rearrange("p (h c) -> p h c", c=96)
                q5 = bk[5][:, 0:288].rearrange("p (h c) -> p h c", c=96)
                k0 = bk[0][:, :].rearrange("p (h c) -> p h c", c=128)
                k1 = bk[1][:, :].rearrange("p (h c) -> p h c", c=128)
                S.op("act", lambda e: e.activation(out=q_tm[:, 0:5, 0:64], in_=q4[:, :, 0:64], func=AF.Copy), reads=[Bbk[4]], writes=[bf("q_tm")])
                S.op("act", lambda e: e.activation(out=q_tm[:, 5:8, 0:64], in_=q5[:, :, 0:64], func=AF.Copy), reads=[Bbk[5]], writes=[bf("q_tm")])
                S.op("dve", lambda e: e.tensor_copy(out=rp[:, 0:5, :], in_=q4[:, :, 64:96]), reads=[Bbk[4]], writes=[bf("rp")])
                S.op("dve", lambda e: e.tensor_copy(out=rp[:, 5:8, :], in_=q5[:, :, 64:96]), reads=[Bbk[5]], writes=[bf("rp")])
                S.op("dve", lambda e: e.tensor_copy(out=rp[:, 8, :], in_=z_sb[:, 640:672]), reads=[bf("z1")], writes=[bf("rp")])
                if DBG_STOP <= 3.2:
                    return
                cs = cosT[:, n, :].unsqueeze(1).broadcast_to([128, 9, 16])
                sn = sinT[:, n, :].unsqueeze(1).broadcast_to([128, 9, 16])
                S.op("dve", lambda e: e.tensor_tensor(out=rt[0][:], in0=rp[:, :, 0:16], in1=cs, op=ALU.mult), reads=[bf("rp"), bf("cos")], writes=[bf("rt0")])
                S.op("dve", lambda e: e.tensor_tensor(out=rt[1][:], in0=rp[:, :, 16:32], in1=sn, op=ALU.mult), reads=[bf("rp"), bf("sin")], writes=[bf("rt1")])
                S.op("dve", lambda e: e.tensor_tensor(out=rt[2][:], in0=rp[:, :, 16:32], in1=cs, op=ALU.mult), reads=[bf("rp"), bf("cos")], writes=[bf("rt2")])
                S.op("dve", lambda e: e.tensor_tensor(out=rt[3][:], in0=rp[:, :, 0:16], in1=sn, op=ALU.mult), reads=[bf("rp"), bf("sin")], writes=[bf("rt3")])
                S.op("dve", lambda e: e.tensor_tensor(out=rr[:, :, 0:16], in0=rt[0][:], in1=rt[1][:], op=ALU.subtract), reads=[bf("rt0"), bf("rt1")], writes=[bf("rr")])
                S.op("dve", lambda e: e.tensor_tensor(out=rr[:, :, 16:32], in0=rt[2][:], in1=rt[3][:], op=ALU.add), reads=[bf("rt2"), bf("rt3")], writes=[bf("rr")])
                S.op("dve", lambda e: e.tensor_copy(out=q_tm[:, :, 64:96], in_=rr[:, 0:8, :]), reads=[bf("rr")], writes=[bf("q_tm")])
                S.op("dve", lambda e: e.tensor_copy(out=k_tm[:, :, 64:96], in_=rr[:, 8:9, :].broadcast_to([128, 8, 32])), reads=[bf("rr")], writes=[bf("k_tm")])
                if DBG_STOP <= 3.3:
                    return
                S.op("act", lambda e: e.activation(out=k_tm[:, 0:4, 0:64], in_=k0[:, :, 0:64], func=AF.Copy), reads=[Bbk[0]], writes=[bf("k_tm")])
                S.op("dve", lambda e: e.tensor_copy(out=k_tm[:, 4:8, 0:64], in_=k1[:, :, 0:64]), reads=[Bbk[1]], writes=[bf("k_tm")])
                if DBG_STOP <= 3.4:
                    return
                S.op("act", lambda e: e.activation(out=Vc[:, n, 0:4, 0:64], in_=k0[:, :, 64:128], func=AF.Copy), reads=[Bbk[0]], writes=[BV[n]])
                S.op("dve", lambda e: e.tensor_copy(out=Vc[:, n, 4:8, 0:64], in_=k1[:, :, 64:128]), reads=[Bbk[1]], writes=[BV[n]])
                if DBG_STOP <= 4:
                    return
                transpose_to(lambda h: q_tm[:, h, :], 8, 96, bf("q_tm"), qT[0:96, :, s * 128:(s + 1) * 128], bf("qT"), eng="act")
                transpose_to(lambda h: k_tm[:, h, :], 8, 96, bf("k_tm"), KT[0:96, :, t0:t0 + 128], BKT[n], eng="dve")
                if DBG_STOP <= 5:
                    return
                S.op("act", lambda e: e.activation(out=uv[:], in_=z_sb[:, 672:1184], func=AF.Gelu_apprx_tanh), reads=[bf("z1"), bf("z2")], writes=[bf("uv")])
                v3 = uv[:, 256:512].rearrange("p (g e) -> p g e", g=4)
                Bsv = bf("stv")
                S.op("dve", lambda e: e.tensor_reduce(out=stv[:, 0:4], in_=v3, axis=AX.X, op=ALU.add), reads=[bf("uv")], writes=[Bsv])
                S.op("pool", lambda e: e.tensor_tensor(out=sq[:], in0=uv[:, 256:512], in1=uv[:, 256:512], op=ALU.mult), reads=[bf("uv")], writes=[bf("sq")])
                S.op("dve", lambda e: e.tensor_reduce(out=stv[:, 4:8], in_=sq[:].rearrange("p (g e) -> p g e", g=4), axis=AX.X, op=ALU.add), reads=[bf("sq")], writes=[Bsv])
                S.op("pool", lambda e: e.tensor_scalar(out=stv[:, 8:12], in0=stv[:, 0:4], scalar1=1.0 / 64, scalar2=None, op0=ALU.mult), reads=[Bsv], writes=[Bsv])
                S.op("pool", lambda e: e.tensor_tensor(out=stv[:, 12:16], in0=stv[:, 8:12], in1=stv[:, 8:12], op=ALU.mult), reads=[Bsv], writes=[Bsv])
                S.op("pool", lambda e: e.tensor_scalar(out=stv[:, 24:28], in0=stv[:, 4:8], scalar1=1.0 / 64, scalar2=EPS, op0=ALU.mult, op1=ALU.add), reads=[Bsv], writes=[Bsv])
                S.op("pool", lambda e: e.tensor_tensor(out=stv[:, 16:20], in0=stv[:, 24:28], in1=stv[:, 12:16], op=ALU.subtract), reads=[Bsv], writes=[Bsv])
                S.op("pool", lambda e: e.tensor_tensor(out=stv[:, 20:24], in0=stv[:, 16:20], in1=mhalf[:, 0:4], op=ALU.pow), reads=[Bsv, bf("mhalf")], writes=[Bsv])
                vn3 = vn32[:].rearrange("p (g e) -> p g e", g=4)
                S.op("dve", lambda e: e.tensor_tensor(out=vn3, in0=v3, in1=stv[:, 8:12].unsqueeze(2).broadcast_to([128, 4, 64]), op=ALU.subtract), reads=[bf("uv"), Bsv], writes=[bf("vn32")])
                S.op("dve", lambda e: e.tensor_tensor(out=vn3, in0=vn3, in1=stv[:, 20:24].unsqueeze(2).broadcast_to([128, 4, 64]), op=ALU.mult), reads=[bf("vn32"), Bsv], writes=[bf("vn32")])
                S.op("pool", lambda e: e.tensor_tensor(out=vn32[:], in0=vn32[:], in1=sgln[:, 0:256], op=ALU.mult), reads=[bf("vn32"), bf("sgln")], writes=[bf("vn32")])
                S.op("pool", lambda e: e.tensor_tensor(out=vn[:], in0=vn32[:], in1=sgln[:, 256:512], op=ALU.add), reads=[bf("vn32"), bf("sgln")], writes=[bf("vn")])
                for g in range(4):
                    S.op("pe", lambda e, g=g: e.matmul(bk[2][:, g * 64:(g + 1) * 64], lhsT=wspT[:, g, :], rhs=vn[:, g * 64:(g + 1) * 64], start=True, stop=True, skip_group_check=True),
                         reads=[bf("wspT"), bf("vn")], writes=[Bbk[2]], inc=(g == 3))
                for g in range(4):
                    S.op("dve", lambda e, g=g: e.scalar_tensor_tensor(out=sq[:, g * 64:(g + 1) * 64], in0=bk[2][:, g * 64:(g + 1) * 64], scalar=gT[:, g0 + 29 + g:g0 + 30 + g],
                                                                      in1=uv[:, g * 64:(g + 1) * 64], op0=ALU.add, op1=ALU.mult),
                         reads=[Bbk[2], bf("gT"), bf("uv")], writes=[bf("sq")])
                slb, Bb_ = new_slot()
                S.op("act", lambda e: e.activation(out=mix_tm[:, s, 512:768], in_=sq[:], func=AF.Square, accum_out=slb[:, 0:1]), reads=[bf("sq")], writes=[bf(f"mix{s}"), Bb_])
                rb = rstd_from_ss(slb, Bb_, 1, 256)
                S.op("dve", lambda e: e.tensor_scalar(out=mix_tm[:, s, 512:768], in0=sq[:], scalar1=rb, scalar2=None, op0=ALU.mult), reads=[bf("sq"), Bb_], writes=[bf(f"mix{s}")])
                if DBG_STOP <= 6:
                    return
                yc, yp = ycv[n % 2], ycv[(n + 1) % 2]
                Byc, Byp = bf(f"ycv{n % 2}"), bf(f"ycv{(n + 1) % 2}")
                S.op("pool", lambda e: e.tensor_tensor(out=yc[:], in0=z_sb[:, 1440:1696], in1=z_sb[:, 1696:1952], op=ALU.mult), reads=[bf("z2"), bf("z3")], writes=[Byc])
                S.dma("sp", lambda e: e.dma_start(out=ysh1[1:128, :], in_=yc[0:127, :]), reads=[Byc], writes=[bf("ysh1")])
                S.dma("sp", lambda e: e.dma_start(out=ysh1[0:1, :], in_=yp[127:128, :]), reads=[Byp], writes=[bf("ysh1")])
                S.dma("sp", lambda e: e.dma_start(out=ysh2[2:128, :], in_=yc[0:126, :]), reads=[Byc], writes=[bf("ysh2")])
                S.dma("sp", lambda e: e.dma_start(out=ysh2[0:2, :], in_=yp[126:128, :]), reads=[Byp], writes=[bf("ysh2")])
                S.op("pool", lambda e: e.tensor_tensor(out=acc[:], in0=yc[:], in1=convw[:, 512:768], op=ALU.mult), reads=[Byc, bf("convw")], writes=[bf("acc")])
                S.op("pool", lambda e: e.tensor_tensor(out=ysh1[:], in0=ysh1[:], in1=convw[:, 256:512], op=ALU.mult), reads=[bf("ysh1"), bf("convw")], writes=[bf("ysh1")])
                S.op("pool", lambda e: e.tensor_tensor(out=acc[:], in0=acc[:], in1=ysh1[:], op=ALU.add), reads=[bf("acc"), bf("ysh1")], writes=[bf("acc")])
                S.op("pool", lambda e: e.tensor_tensor(out=ysh2[:], in0=ysh2[:], in1=convw[:, 0:256], op=ALU.mult), reads=[bf("ysh2"), bf("convw")], writes=[bf("ysh2")])
                S.op("pool", lambda e: e.tensor_tensor(out=acc[:], in0=acc[:], in1=ysh2[:], op=ALU.add), reads=[bf("acc"), bf("ysh2")], writes=[bf("acc")])
                S.op("pool", lambda e: e.tensor_tensor(out=acc[:], in0=acc[:], in1=z_sb[:, 1184:1440], op=ALU.mult), reads=[bf("acc"), bf("z2")], writes=[bf("acc")])
                slc, Bc_ = new_slot()
                S.op("act", lambda e: e.activation(out=mix_tm[:, s, 768:1024], in_=acc[:], func=AF.Square, accum_out=slc[:, 0:1]), reads=[bf("acc")], writes=[bf(f"mix{s}"), Bc_])
                rc = rstd_from_ss(slc, Bc_, 1, 256)
                S.op("dve", lambda e: e.tensor_scalar(out=mix_tm[:, s, 768:1024], in0=acc[:], scalar1=rc, scalar2=None, op0=ALU.mult), reads=[bf("acc"), Bc_], writes=[bf(f"mix{s}")])

            if DBG_STAGE in ('A', 'A0'):
                continue
            S.handoff(stA, stB)
            nkb = 4 * i + 4
            for h in range(H):
                O = bk[4 + h % 2]
                BO = Bbk[4 + h % 2]
                O3 = O[:, :].rearrange("p (j c) -> p j c", j=4)
                for kb in range(nkb):
                    j0 = max(0, kb - 4 * i)
                    c0 = j0 * 128
                    r = rot[0] % 4
                    rot[0] += 1
                    sbk, Bs = bk[r], Bbk[r]
                    S.op("pe", lambda e, kb=kb, c0=c0, sbk=sbk: e.matmul(sbk[:, c0:512], lhsT=KT[0:96, h, kb * 128:(kb + 1) * 128], rhs=qT[0:96, h, c0:512], start=True, stop=True),
                         reads=[BKT[kb], bf("qT")], writes=[Bs])
                    S.op("act", lambda e, c0=c0, sbk=sbk, r=r: e.activation(out=PT[r][:, c0:512], in_=sbk[:, c0:512], func=AF.Exp, scale=SCALE), reads=[Bs], writes=[bf(f"PT{r}")])
                    if kb >= 4 * i:
                        S.op("pool", lambda e, c0=c0, r=r: e.tensor_tensor(out=PT[r][:, c0:c0 + 128], in0=PT[r][:, c0:c0 + 128], in1=tri[:], op=ALU.mult),
                             reads=[bf(f"PT{r}"), bf("tri")], writes=[bf(f"PT{r}")])
                    for j in range(j0, 4):
                        S.op("pe", lambda e, kb=kb, j=j, r=r: e.matmul(O3[:, j, 0:65], lhsT=PT[r][:, j * 128:(j + 1) * 128], rhs=Vc[:, kb, h, :],
                                                                       start=(kb == 0 and j == 0), stop=(kb == 4 * i + j), skip_group_check=True),
                             reads=[bf(f"PT{r}"), BV[kb]], writes=[BO], inc=(j == 3))
                S.op("dve", lambda e: e.reciprocal(out=rinv[:].unsqueeze(2), in_=O3[:, :, 64:65]), reads=[BO], writes=[bf("rinv")])
                S.op("dve", lambda e: e.tensor_tensor(out=ya[:, :, h * 64:(h + 1) * 64], in0=O3[:, :, 0:64], in1=rinv[:].unsqueeze(2).broadcast_to([128, 4, 64]), op=ALU.mult),
                     reads=[BO, bf("rinv")], writes=[bf("ya")])
            for j in range(4):
                sla, Ba_ = new_slot()
                S.op("act", lambda e, j=j: e.activation(out=mix_tm[:, j, 0:512], in_=ya[:, j, :], func=AF.Square, accum_out=sla[:, 0:1]), reads=[bf("ya")], writes=[bf(f"mix{j}"), Ba_])
                ra = rstd_from_ss(sla, Ba_, 1, 512)
                S.op("dve", lambda e, j=j: e.tensor_scalar(out=mix_tm[:, j, 0:512], in0=ya[:, j, :], scalar1=ra, scalar2=None, op0=ALU.mult), reads=[bf("ya"), Ba_], writes=[bf(f"mix{j}")])
            S.dma("sp", lambda e: e.dma_start(out=mix_d[i * 512:(i + 1) * 512, :].rearrange("(j p) d -> p j d", p=128), in_=mix_tm[:]),
                  reads=[bf("mix0"), bf("mix1"), bf("mix2"), bf("mix3")], writes=[Md[i]])

    def ffn_pass(l):
        xsrc = x_d if l == 0 else out_d
        g0 = l * GW
        S.barrier()
        for k in range(8):
            S.dma("pool", lambda e, k=k: e.dma_start(out=w_out[:, k, :], in_=w_out_d[l, k * 128:(k + 1) * 128, :]), writes=[bf("w_out")])
        S.dma("sp", lambda e: e.dma_start(out=postg_m[:], in_=bc_d[l, :, 0:1024]), writes=[bf("postg_m")])
        S.dma("sp", lambda e: e.dma_start(out=postg_f[:], in_=bc_d[l, :, 1024:2048]), writes=[bf("postg_f")])
        for k in range(8):
            S.dma("pool", lambda e, k=k: e.dma_start(out=w_gate[:, k, :], in_=w_gate_d[l, k * 128:(k + 1) * 128, :]), writes=[bf(f"w_gate{k}")])
            S.dma("pool", lambda e, k=k: e.dma_start(out=w_up[:, k, :], in_=w_up_d[l, k * 128:(k + 1) * 128, :]), writes=[bf(f"w_up{k}")])
        for c in range(NFF):
            S.dma("pool", lambda e, c=c: e.dma_start(out=w_down[:, c, :], in_=w_down_d[l, c * 128:(c + 1) * 128, :]), writes=[bf(f"w_down{c}")])

        def post_norm_residual(postg, Bpostg, ba, bb, xs_, Bxs, t_, Bt):
            sl, Bsl = new_slot()
            S.op("act", lambda e: e.activation(out=t_[:, 0:512], in_=bk[ba][:, :], func=AF.Square, accum_out=sl[:, 0:1]), reads=[Bbk[ba]], writes=[Bt, Bsl])
            S.op("act", lambda e: e.activation(out=t_[:, 512:1024], in_=bk[bb][:, :], func=AF.Square, accum_out=sl[:, 1:2]), reads=[Bbk[bb]], writes=[Bt, Bsl])
            r = rstd_from_ss(sl, Bsl, 2, D)
            S.op("dve", lambda e: e.scalar_tensor_tensor(out=t_[:, 0:512], in0=bk[ba][:, :], scalar=r, in1=postg[:, 0:512], op0=ALU.mult, op1=ALU.mult),
                 reads=[Bbk[ba], Bsl, Bpostg], writes=[Bt])
            S.op("dve", lambda e: e.scalar_tensor_tensor(out=t_[:, 512:1024], in0=bk[bb][:, :], scalar=r, in1=postg[:, 512:1024], op0=ALU.mult, op1=ALU.mult),
                 reads=[Bbk[bb], Bsl, Bpostg], writes=[Bt])
            S.op("pool", lambda e: e.tensor_tensor(out=xs_[:], in0=xs_[:], in1=t_[:], op=ALU.add), reads=[Bxs, Bt], writes=[Bxs])

        aT_bufs = [bf(f"aT{c}") for c in range(NFF)]
        set1_bufs = [bf("xs_f2"), bf("t_f2"), bf("xn_b0"), bf("xn_b1"), bf("mixT2")]
        sets = [dict(xs=xs_f, Bxs=bf("xs_f"), t=t_f, Bt=bf("t_f"), m0=xn_f[0], Bm0=bf("xn_f0"), m1=xn_f[1], Bm1=bf("xn_f1"), mixT=mixT, BmixT=bf("mixT"), ba=4, bb=5),
                dict(xs=xs_f2, Bxs=bf("xs_f2"), t=t_f2, Bt=bf("t_f2"), m0=xn_b[0], Bm0=bf("xn_b0"), m1=xn_b[1], Bm1=bf("xn_b1"), mixT=mixT2, BmixT=bf("mixT2"), ba=2, bb=3)]
        for i in range(DBG_T if DBG_STAGE == 'full' else 0):
            S.handoff(aT_bufs, set1_bufs)
            for s in range(4):
                n = 4 * i + s
                t0 = n * 128
                Q = sets[s % 2]
                m0, Bm0, m1, Bm1, xs_, Bxs, t_, Bt, mT, BmT, ba, bb = Q["m0"], Q["Bm0"], Q["m1"], Q["Bm1"], Q["xs"], Q["Bxs"], Q["t"], Q["Bt"], Q["mixT"], Q["BmixT"], Q["ba"], Q["bb"]
                S.dma("sp", lambda e: e.dma_start(out=m0[:], in_=mix_d[t0:t0 + 128, :]), reads=[Md[i]], writes=[Bm0])
                S.dma("sp", lambda e: e.dma_start(out=xs_[:], in_=xsrc[t0:t0 + 128, :]), reads=[Xd[n]], writes=[Bxs])
                transpose_to(lambda c: m0[:, c * 128:(c + 1) * 128], 8, 128, Bm0, mT[:], BmT, gcol0=g0 + 16)
                for k in range(8):
                    S.op("pe", lambda e, k=k: e.matmul(bk[ba][:, :], lhsT=mT[:, k, :], rhs=w_out[:, k, 0:512], start=(k == 0), stop=(k == 7)),
                         reads=[BmT, bf("w_out")], writes=[Bbk[ba]], inc=False)
                    S.op("pe", lambda e, k=k: e.matmul(bk[bb][:, :], lhsT=mT[:, k, :], rhs=w_out[:, k, 512:1024], start=(k == 0), stop=(k == 7)),
                         reads=[BmT, bf("w_out")], writes=[Bbk[bb]], inc=(k == 7))
                post_norm_residual(postg_m, bf("postg_m"), ba, bb, xs_, Bxs, t_, Bt)
                S.dma("sp", lambda e: e.dma_start(out=out_d[t0:t0 + 128, :], in_=xs_[:]), reads=[Bxs], writes=[Xd[n]])
                sl, Bsl = new_slot()
                S.op("act", lambda e: e.activation(out=m1[:], in_=xs_[:], func=AF.Square, accum_out=sl[:, 0:1]), reads=[Bxs], writes=[Bm1, Bsl])
                r = rstd_from_ss(sl, Bsl, 1, D)
                S.op("dve", lambda e: e.tensor_scalar(out=m1[:], in0=xs_[:], scalar1=r, scalar2=None, op0=ALU.mult), reads=[Bxs, Bsl], writes=[Bm1])
                transpose_to(lambda c: m1[:, c * 128:(c + 1) * 128], 8, 128, Bm1, h2T[:, :, s * 128:(s + 1) * 128], bf("h2T"), gcol0=g0 + 8)
            S.handoff(set1_bufs, aT_bufs)
            for c in range(NFF):
                gb, Bg = bk[c % 2], Bbk[c % 2]
                ub, Bu = bk[2 + c % 2], Bbk[2 + c % 2]
                for k in range(8):
                    S.op("pe", lambda e, k=k, c=c, gb=gb: e.matmul(gb[:, :], lhsT=w_gate[:, k, c * 128:(c + 1) * 128], rhs=h2T[:, k, :], start=(k == 0), stop=(k == 7)),
                         reads=[bf(f"w_gate{k}"), bf("h2T")], writes=[Bg], inc=(k == 7))
                for k in range(8):
                    S.op("pe", lambda e, k=k, c=c, ub=ub: e.matmul(ub[:, :], lhsT=w_up[:, k, c * 128:(c + 1) * 128], rhs=h2T[:, k, :], start=(k == 0), stop=(k == 7)),
                         reads=[bf(f"w_up{k}"), bf("h2T")], writes=[Bu], inc=(k == 7))
                sg, Bsg = sg_f[c % 2], bf(f"xn_f{c % 2}")
                S.op("act", lambda e, gb=gb, sg=sg: e.activation(out=sg[:], in_=gb[:, :], func=AF.Silu), reads=[Bg], writes=[Bsg])
                S.op("dve", lambda e, c=c, ub=ub, sg=sg: e.tensor_tensor(out=aT[:, c, :], in0=ub[:, :], in1=sg[:], op=ALU.mult), reads=[Bu, Bsg], writes=[bf(f"aT{c}")])
            for s in range(4):
                n = 4 * i + s
                t0 = n * 128
                ba, bb = [(4, 5), (0, 1), (2, 3), (4, 5)][s]
                for c in range(NFF):
                    S.op("pe", lambda e, c=c: e.matmul(bk[ba][:, :], lhsT=aT[:, c, s * 128:(s + 1) * 128], rhs=w_down[:, c, 0:512], start=(c == 0), stop=(c == NFF - 1)),
                         reads=[bf(f"aT{c}"), bf(f"w_down{c}")], writes=[Bbk[ba]], inc=False)
                    S.op("pe", lambda e, c=c: e.matmul(bk[bb][:, :], lhsT=aT[:, c, s * 128:(s + 1) * 128], rhs=w_down[:, c, 512:1024], start=(c == 0), stop=(c == NFF - 1)),
                         reads=[bf(f"aT{c}"), bf(f"w_down{c}")], writes=[Bbk[bb]], inc=(c == NFF - 1))
                S.dma("sp", lambda e: e.dma_start(out=xs_f[:], in_=out_d[t0:t0 + 128, :]), reads=[Xd[n]], writes=[bf("xs_f")])
                post_norm_residual(postg_f, bf("postg_f"), ba, bb, xs_f, bf("xs_f"), t_f, bf("t_f"))
                S.dma("sp", lambda e: e.dma_start(out=out_d[t0:t0 + 128, :], in_=xs_f[:]), reads=[bf("xs_f")], writes=[Xd[n]])

    for l in range(L if DBG_STAGE != 'P' else 0):
        mixer_pass(l)
        if DBG_STAGE != 'A0':
            ffn_pass(l)
    S.finish("sp", Xd)
    S.barrier()
    return nc


def _layouts(p):
    L = DEPTH
    gT = np.zeros((128, L * GW), np.float32)
    for l in range(L):
        o = l * GW
        gT[:, o + 0:o + 8] = p["mix_pre_g"][l].reshape(8, 128).T
        gT[:, o + 8:o + 16] = p["ffn_pre_g"][l].reshape(8, 128).T
        gT[:, o + 16:o + 24] = p["out_norm_g"][l].reshape(8, 128).T
        gT[:, o + 24:o + 27] = p["q_norm_g"][l].reshape(3, 128).T
        gT[:, o + 27:o + 29] = p["kv_norm_g"][l].reshape(2, 128).T
        gT[:, o + 29:o + 33] = p["b_sp"][l].T
    bc = np.zeros((L, 128, BCW), np.float32)
    bc[:, :, 0:1024] = p["mix_post_g"][:, None, :]
    bc[:, :, 1024:2048] = p["ffn_post_g"][:, None, :]
    bc[:, :, 2048:2304] = p["sg_ln_g"][:, None, :]
    bc[:, :, 2304:2560] = p["sg_ln_b"][:, None, :]
    bc[:, :, 2560:3328] = p["conv_w"].reshape(L, 1, 768)
    wspT = np.ascontiguousarray(np.transpose(p["w_sp"], (0, 3, 1, 2))).reshape(L, 128, 512)
    return gT, bc, wspT


_INVF = (np.float32(1.0) / (np.float32(10000.0) ** (np.arange(16, dtype=np.float32) / np.float32(16)))).astype(np.float32)


def kernel(x, positions, mix_pre_g, mix_post_g, ffn_pre_g, ffn_post_g, w_in, q_norm_g, w_uq, kv_norm_g, w_ukv,
           sg_ln_g, sg_ln_b, w_sp, b_sp, conv_w, out_norm_g, w_out, w_gate, w_up, w_down, _depth=DEPTH, _cores=8):
    p = dict(mix_pre_g=mix_pre_g, mix_post_g=mix_post_g, ffn_pre_g=ffn_pre_g, ffn_post_g=ffn_post_g, q_norm_g=q_norm_g,
             kv_norm_g=kv_norm_g, sg_ln_g=sg_ln_g, sg_ln_b=sg_ln_b, w_sp=w_sp, b_sp=b_sp, conv_w=conv_w, out_norm_g=out_norm_g)
    p = {k: np.asarray(v, np.float32) for k, v in p.items()}
    gT, bc, wspT = _layouts(p)
    f = lambda a: np.ascontiguousarray(np.asarray(a, np.float32))
    shared = {"invf": np.ascontiguousarray(np.broadcast_to(_INVF[None, :], (128, 16))), "w_in": f(w_in), "w_uq": f(w_uq), "w_ukv": f(w_ukv),
              "w_out": f(w_out), "w_gate": f(w_gate), "w_up": f(w_up), "w_down": f(w_down), "gT": gT, "bc": bc, "wspT": wspT}
    x = np.asarray(x, np.float32)
    positions = np.asarray(positions, np.int32)
    nc = build_program(_depth)
    in_maps = []
    for b in range(_cores):
        m = dict(shared)
        m["x"] = np.ascontiguousarray(x[b])
        m["pos"] = np.ascontiguousarray(positions[b].reshape(NSUB, 128).T)
        in_maps.append(m)
    res = run_bass_kernel_spmd(nc, in_maps, core_ids=list(range(_cores)))
    return np.stack([np.asarray(r["out"], np.float32) for r in res.results], axis=0)
```

```python
import math
import os
import numpy as np
import concourse.bass as bass
import concourse.mybir as mybir
from concourse.bass_utils import run_bass_kernel_spmd

F32 = mybir.dt.float32
BF16 = mybir.dt.bfloat16
I32 = mybir.dt.int32
AF = mybir.ActivationFunctionType
ALU = mybir.AluOpType
AX = mybir.AxisListType

D = 1024
SEQ = 4096
DEPTH = 4
NSUB = SEQ // 128
NTILE = SEQ // 512
INW = 1952
H = 8
DFF = 2816
NFF = DFF // 128
EPS = 1e-6
SCALE = 96.0 ** -0.5
GW = 33
BCW = 3328
ND = 8


class Buf:
    __slots__ = ("name", "w", "r", "psum")

    def __init__(self, name, psum=False):
        self.name = name
        self.w = None
        self.r = {}
        self.psum = psum


class Sched:
    def __init__(self, nc):
        self.nc = nc
        self.E = {"pe": nc.tensor, "act": nc.scalar, "dve": nc.vector, "pool": nc.gpsimd, "sp": nc.sync}
        self.sem = {k: nc.alloc_semaphore("s_" + k) for k in self.E}
        self.cnt = {k: 0 for k in self.E}
        self.seen = {k: {} for k in self.E}
        self.pend = {k: [] for k in self.E}
        self.dsem = {q: [nc.alloc_semaphore(f"d_{q}{i}") for i in range(ND)] for q in ("sp", "pool")}
        self.dcnt = {q: [0] * ND for q in self.dsem}
        self.drr = {q: 0 for q in self.dsem}

    def _collect(self, eng, reads, writes, skip_own_war=True):
        toks = {}
        own = self.sem.get(eng)

        def need(s, v, war=False):
            if s is own and (eng == "pe" or (war and skip_own_war)):
                return
            if toks.get(s, 0) < v:
                toks[s] = v

        for b in reads:
            if b.w is not None:
                need(*b.w)
            if b.psum:
                for s, v in b.r.items():
                    if s is not own:
                        need(s, v)
        for b in writes:
            if b.w is not None:
                need(*b.w)
            for s, v in b.r.items():
                need(s, v, war=True)
        return toks

    def _emit_waits(self, eng, toks):
        e = self.E[eng]
        seen = self.seen[eng]
        for s, v in toks.items():
            if seen.get(s, 0) < v:
                e.wait_ge(s, v)
                seen[s] = v

    def op(self, eng, fn, reads=(), writes=(), inc=True):
        self._emit_waits(eng, self._collect(eng, reads, writes))
        ins = fn(self.E[eng])
        self.pend[eng].append((tuple(reads), tuple(writes)))
        if inc:
            self.cnt[eng] += 1
            s = self.sem[eng]
            ins.then_inc(s, 1)
            v = self.cnt[eng]
            for rs, ws in self.pend[eng]:
                for b in rs:
                    b.r[s] = v
                for b in ws:
                    b.w = (s, v)
                    b.r = {}
            self.pend[eng] = []
        return ins

    def dma(self, q, fn, reads=(), writes=()):
        assert not self.pend[q]
        i = self.drr[q]
        self.drr[q] = (i + 1) % ND
        s = self.dsem[q][i]
        toks = self._collect(q, reads, writes, skip_own_war=False)
        prev = 16 * self.dcnt[q][i]
        if prev and toks.get(s, 0) < prev:
            toks[s] = prev
        self._emit_waits(q, toks)
        ins = fn(self.E[q])
        ins.then_inc(s, 16)
        self.dcnt[q][i] += 1
        v = 16 * self.dcnt[q][i]
        for b in reads:
            b.r[s] = v
        for b in writes:
            b.w = (s, v)
            b.r = {}
        return ins

    def handoff(self, src, dst):
        for d in dst:
            for b in src:
                if b.w is not None:
                    s, v = b.w
                    if d.r.get(s, 0) < v:
                        d.r[s] = v
                for s, v in b.r.items():
                    if d.r.get(s, 0) < v:
                        d.r[s] = v

    def barrier(self):
        for k in self.E:
            assert not self.pend[k]
        toks = {}
        for k in self.E:
            if self.cnt[k]:
                toks[self.sem[k]] = self.cnt[k]
        for q in self.dsem:
            for i in range(ND):
                if self.dcnt[q][i]:
                    toks[self.dsem[q][i]] = 16 * self.dcnt[q][i]
        for k in self.E:
            t = {s: v for s, v in toks.items() if s is not self.sem[k]}
            self._emit_waits(k, t)

    def finish(self, eng, bufs):
        toks = {}
        for b in bufs:
            if b.w is not None:
                s, v = b.w
                toks[s] = max(toks.get(s, 0), v)
            for s, v in b.r.items():
                toks[s] = max(toks.get(s, 0), v)
        self._emit_waits(eng, toks)


def build_program(L=DEPTH):
    DBG_T = int(os.environ.get('KDBG_TILES', NTILE))
    DBG_STAGE = os.environ.get('KDBG_STAGE', 'full')
    DBG_STOP = float(os.environ.get('KDBG_STOP', 99))
    nc = bass.Bass("TRN2", target_bir_lowering=False)
    S = Sched(nc)

    def dram(name, shape, dt, kind):
        return nc.dram_tensor(name, shape, dt, kind=kind).ap()

    x_d = dram("x", [SEQ, D], F32, "ExternalInput")
    pos_d = dram("pos", [128, NSUB], I32, "ExternalInput")
    invf_d = dram("invf", [128, 16], F32, "ExternalInput")
    w_in_d = dram("w_in", [DEPTH, D, INW], F32, "ExternalInput")
    w_uq_d = dram("w_uq", [DEPTH, 384, 768], F32, "ExternalInput")
    w_ukv_d = dram("w_ukv", [DEPTH, 256, 1024], F32, "ExternalInput")
    w_out_d = dram("w_out", [DEPTH, D, D], F32, "ExternalInput")
    w_gate_d = dram("w_gate", [DEPTH, D, DFF], F32, "ExternalInput")
    w_up_d = dram("w_up", [DEPTH, D, DFF], F32, "ExternalInput")
    w_down_d = dram("w_down", [DEPTH, DFF, D], F32, "ExternalInput")
    gT_d = dram("gT", [128, DEPTH * GW], F32, "ExternalInput")
    bc_d = dram("bc", [DEPTH, 128, BCW], F32, "ExternalInput")
    wsp_d = dram("wspT", [DEPTH, 128, 512], F32, "ExternalInput")
    out_d = dram("out", [SEQ, D], F32, "ExternalOutput")
    mix_d = dram("mixd", [SEQ, D], BF16, "Internal")

    base = (nc.sbuf_base + 31) // 32 * 32
    top = nc.sbuf_top

    def sb(name, shape, dt, off):
        nb = int(np.prod(shape[1:])) * (2 if dt == BF16 else 4)
        assert off % 32 == 0 and base + off + nb <= top, (name, off, nb, top - base)
        return nc.alloc_sbuf_tensor_at(name, list(shape), dt, offset=base + off)

    ident = sb("ident", [128, 128], BF16, 0)
    tri = sb("tri", [128, 128], BF16, 256)
    mhalf = sb("mhalf", [128, 8], F32, 512)
    cosT = sb("cosT", [128, NSUB, 16], F32, 576)
    sinT = sb("sinT", [128, NSUB, 16], F32, 2624)
    gT = sb("gTs", [128, DEPTH * GW], F32, 4672)
    stt_ = sb("stats", [128, 64], F32, 5216)
    invf = sb("invfs", [128, 16], F32, 5472)
    posi = sb("posi", [128, NSUB], I32, 5536)
    posf = sb("posf", [128, NSUB], F32, 5664)
    stv = sb("stv", [128, 32], F32, 5792)
    CEND = 5920
    BIG = CEND
    MID = BIG + 174080
    assert base + MID + 30720 <= top, (base, MID, top)

    KT = sb("KT", [128, H, SEQ], BF16, BIG + 0)
    Vc = sb("Vc", [128, NSUB, H, 65], BF16, BIG + 65536)
    w_in = sb("w_in_s", [128, 8, INW], BF16, BIG + 98816)
    w_uq = sb("w_uq_s", [128, 3, 768], BF16, BIG + 130048)
    w_ukv = sb("w_ukv_s", [128, 2, 1024], BF16, BIG + 134656)
    qT = sb("qT", [128, H, 512], BF16, BIG + 138752)
    mix_tm = sb("mix_tm", [128, 4, D], BF16, BIG + 146944)
    SA = BIG + 155136
    z_sb = sb("z_sb", [128, INW], F32, SA)
    uv = sb("uv", [128, 512], F32, SA + 7808)
    ysh1 = sb("ysh1", [128, 256], F32, SA + 9856)
    ysh2 = sb("ysh2", [128, 256], F32, SA + 10880)
    acc = sb("acc", [128, 256], F32, SA + 11904)
    sq = sb("sq", [128, 256], F32, SA + 12928)
    vn32 = sb("vn32", [128, 256], F32, SA + 13952)
    cqkv = sb("cqkv", [128, 640], BF16, SA + 14976)
    ycv = [sb("ycv0", [128, 256], F32, SA + 16352), sb("ycv1", [128, 256], F32, SA + 17376)]
    assert SA + 18400 <= MID
    ya = sb("ya", [128, 4, 512], F32, SA)
    PT = [sb(f"PT{r}", [128, 512], BF16, SA + 8192 + 1024 * r) for r in range(4)]
    hT = [sb("hT0", [128, 8, 128], BF16, MID + 0), sb("hT1", [128, 8, 128], BF16, MID + 2048)]
    xs = sb("xs", [128, D], F32, MID + 4096)
    xn = sb("xn", [128, D], BF16, MID + 8192)
    sgln = sb("sgln", [128, 512], F32, MID + 10240)
    convw = sb("convw", [128, 768], F32, MID + 12288)
    wspT = sb("wspTs", [128, 4, 128], BF16, MID + 15360)
    cqT = sb("cqT", [128, 5, 128], BF16, MID + 16384)
    q_tm = sb("q_tm", [128, H, 96], BF16, MID + 17664)
    k_tm = sb("k_tm", [128, H, 96], BF16, MID + 19200)
    rp = sb("rp", [128, 9, 32], F32, MID + 20736)
    rt = [sb(f"rt{i}", [128, 9, 16], F32, MID + 21888 + 576 * i) for i in range(4)]
    rr = sb("rr", [128, 9, 32], F32, MID + 24192)
    vn = sb("vn", [128, 256], BF16, MID + 25344)
    rinv = sb("rinv", [128, 4], F32, MID + 25856)
    w_gate = sb("w_gate_s", [128, 8, DFF], BF16, BIG + 0)
    w_up = sb("w_up_s", [128, 8, DFF], BF16, BIG + 45056)
    w_down = sb("w_down_s", [128, NFF, D], BF16, BIG + 90112)
    w_out = sb("w_out_s", [128, 8, D], BF16, BIG + 135168)
    aT = sb("aT", [128, NFF, 512], BF16, BIG + 151552)
    h2T = sb("h2T", [128, 8, 512], BF16, MID + 0)
    xs_f = sb("xs_f", [128, D], F32, MID + 8192)
    xn_f = [sb("xn_f0", [128, D], BF16, MID + 12288), sb("xn_f1", [128, D], BF16, MID + 14336)]
    sg_f = [sb("sg_f0", [128, 512], F32, MID + 12288), sb("sg_f1", [128, 512], F32, MID + 14336)]
    mixT = sb("mixT", [128, 8, 128], BF16, MID + 16384)
    t_f = sb("t_f", [128, D], F32, MID + 18432)
    postg_m = sb("postg_m", [128, D], F32, MID + 22528)
    postg_f = sb("postg_f", [128, D], F32, MID + 26624)
    AT0 = BIG + 151552
    xs_f2 = sb("xs_f2", [128, D], F32, AT0 + 0)
    t_f2 = sb("t_f2", [128, D], F32, AT0 + 4096)
    xn_b = [sb("xn_b0", [128, D], BF16, AT0 + 8192), sb("xn_b1", [128, D], BF16, AT0 + 10240)]
    mixT2 = sb("mixT2", [128, 8, 128], BF16, AT0 + 12288)

    pT = [nc.alloc_psum_tensor(f"pT{i}", [128, 8, 128], BF16) for i in range(2)]
    bk = [nc.alloc_psum_tensor(f"bk{i}", [128, 512], F32) for i in range(6)]
    BpT = [Buf(f"pT{i}", psum=True) for i in range(2)]
    Bbk = [Buf(f"bk{i}", psum=True) for i in range(6)]
    tcount = [0]

    def next_pT():
        i = tcount[0] % 2
        tcount[0] += 1
        return pT[i], BpT[i]

    B = {}

    def bf(name):
        if name not in B:
            B[name] = Buf(name)
        return B[name]

    Xd = [Buf(f"xd{n}") for n in range(NSUB)]
    Md = [Buf(f"md{i}") for i in range(NTILE)]
    BKT = [Buf(f"kt{n}") for n in range(NSUB)]
    BV = [Buf(f"v{n}") for n in range(NSUB)]

    slot = [0]

    def new_slot():
        k = slot[0] % 16
        slot[0] += 1
        return stt_[:, 4 * k:4 * k + 4], bf(f"slot{k}")

    def rstd_from_ss(sl, Bsl, ncols, Dn):
        if ncols == 2:
            S.op("pool", lambda e: e.tensor_tensor(out=sl[:, 0:1], in0=sl[:, 0:1], in1=sl[:, 1:2], op=ALU.add), reads=[Bsl], writes=[Bsl])
        S.op("pool", lambda e: e.tensor_scalar(out=sl[:, 2:3], in0=sl[:, 0:1], scalar1=1.0 / Dn, scalar2=EPS, op0=ALU.mult, op1=ALU.add), reads=[Bsl], writes=[Bsl])
        S.op("pool", lambda e: e.tensor_tensor(out=sl[:, 3:4], in0=sl[:, 2:3], in1=mhalf[:, 0:1], op=ALU.pow), reads=[Bsl, bf("mhalf")], writes=[Bsl])
        return sl[:, 3:4]

    def transpose_to(src_ap_fn, nchunk, rows, Bsrc, dst_ap, Bdst, gcol0=None, eng="dve"):
        p, Bp = next_pT()
        for c in range(nchunk):
            S.op("pe", lambda e, c=c: e.transpose(out=p[0:rows, c, :], in_=src_ap_fn(c), identity=ident[:]),
                 reads=[Bsrc, bf("ident")], writes=[Bp], inc=(c == nchunk - 1))
        if gcol0 is None:
            if eng == "act":
                S.op("act", lambda e: e.activation(out=dst_ap, in_=p[0:rows, 0:nchunk, :], func=AF.Copy), reads=[Bp], writes=[Bdst])
            else:
                S.op(eng, lambda e: e.tensor_copy(out=dst_ap, in_=p[0:rows, 0:nchunk, :]), reads=[Bp], writes=[Bdst])
        else:
            g = gT[0:rows, gcol0:gcol0 + nchunk].unsqueeze(2).broadcast_to([rows, nchunk, 128])
            S.op(eng, lambda e: e.tensor_tensor(out=dst_ap, in0=p[0:rows, 0:nchunk, :], in1=g, op=ALU.mult), reads=[Bp, bf("gT")], writes=[Bdst])

    S.op("pool", lambda e: e.memset(ident[:], 1.0), writes=[bf("ident")])
    S.op("pool", lambda e: e.affine_select(out=ident[:], in_=ident[:], pattern=[[-1, 128]], compare_op=ALU.is_equal, fill=0.0, base=0, channel_multiplier=1),
         reads=[bf("ident")], writes=[bf("ident")])
    S.op("pool", lambda e: e.memset(tri[:], 1.0), writes=[bf("tri")])
    S.op("pool", lambda e: e.affine_select(out=tri[:], in_=tri[:], pattern=[[1, 128]], compare_op=ALU.is_ge, fill=0.0, base=0, channel_multiplier=-1),
         reads=[bf("tri")], writes=[bf("tri")])
    S.op("pool", lambda e: e.memset(mhalf[:], -0.5), writes=[bf("mhalf")])
    S.dma("sp", lambda e: e.dma_start(out=gT[:], in_=gT_d), writes=[bf("gT")])
    S.dma("sp", lambda e: e.dma_start(out=invf[:], in_=invf_d), writes=[bf("invf")])
    S.dma("sp", lambda e: e.dma_start(out=posi[:], in_=pos_d), writes=[bf("posi")])
    TWO_PI = 2.0 * math.pi
    C1 = float(np.float32(6.28125))
    C2 = float(np.float32(TWO_PI - 6.28125))
    MAGIC = 12582912.0
    ang = sb("ang", [128, NSUB, 16], F32, BIG + 0)
    kk = sb("kk", [128, NSUB, 16], F32, BIG + 2048)
    t2 = sb("t2p", [128, NSUB, 16], F32, BIG + 4096)
    S.op("dve", lambda e: e.tensor_copy(out=posf[:], in_=posi[:]), reads=[bf("posi")], writes=[bf("posf")])
    S.op("dve", lambda e: e.tensor_tensor(out=ang[:], in0=posf[:].unsqueeze(2).broadcast_to([128, NSUB, 16]),
                                          in1=invf[:].unsqueeze(1).broadcast_to([128, NSUB, 16]), op=ALU.mult),
         reads=[bf("posf"), bf("invf")], writes=[bf("ang")])
    S.op("dve", lambda e: e.tensor_scalar(out=t2[:], in0=ang[:], scalar1=1.0 / TWO_PI, scalar2=MAGIC, op0=ALU.mult, op1=ALU.add), reads=[bf("ang")], writes=[bf("t2")])
    S.op("dve", lambda e: e.tensor_scalar(out=kk[:], in0=t2[:], scalar1=-MAGIC, scalar2=None, op0=ALU.add), reads=[bf("t2")], writes=[bf("kk")])
    S.op("dve", lambda e: e.scalar_tensor_tensor(out=ang[:], in0=kk[:], scalar=-C1, in1=ang[:], op0=ALU.mult, op1=ALU.add), reads=[bf("kk"), bf("ang")], writes=[bf("ang")])
    S.op("dve", lambda e: e.scalar_tensor_tensor(out=ang[:], in0=kk[:], scalar=-C2, in1=ang[:], op0=ALU.mult, op1=ALU.add), reads=[bf("kk"), bf("ang")], writes=[bf("ang")])
    S.op("dve", lambda e: e.tensor_scalar(out=ang[:], in0=ang[:], scalar1=math.pi, scalar2=-math.pi, op0=ALU.min, op1=ALU.max), reads=[bf("ang")], writes=[bf("ang")])
    S.op("act", lambda e: e.activation(out=sinT[:], in_=ang[:], func=AF.Sin), reads=[bf("ang")], writes=[bf("sin")])
    S.op("dve", lambda e: e.tensor_scalar(out=t2[:], in0=ang[:], scalar1=-1.0, scalar2=None, op0=ALU.mult), reads=[bf("ang")], writes=[bf("t2")])
    S.op("dve", lambda e: e.tensor_tensor(out=t2[:], in0=t2[:], in1=ang[:], op=ALU.max), reads=[bf("ang"), bf("t2")], writes=[bf("t2")])
    S.op("dve", lambda e: e.tensor_scalar(out=t2[:], in0=t2[:], scalar1=-1.0, scalar2=math.pi / 2, op0=ALU.mult, op1=ALU.add), reads=[bf("t2")], writes=[bf("t2")])
    S.op("act", lambda e: e.activation(out=cosT[:], in_=t2[:], func=AF.Sin), reads=[bf("t2")], writes=[bf("cos")])

    stA = [bf("z0"), bf("z1"), bf("z2"), bf("z3"), bf("uv"), bf("ysh1"), bf("ysh2"), bf("acc"), bf("sq"), bf("vn32"), bf("cqkv")]
    stB = [bf("ya"), bf("PT0"), bf("PT1"), bf("PT2"), bf("PT3")]
    rot = [0]

    def mixer_pass(l):
        xsrc = x_d if l == 0 else out_d
        g0 = l * GW
        S.barrier()
        for k in range(8):
            S.dma("pool", lambda e, k=k: e.dma_start(out=w_in[:, k, :], in_=w_in_d[l, k * 128:(k + 1) * 128, :]), writes=[bf(f"w_in{k}")])
        for k in range(3):
            S.dma("pool", lambda e, k=k: e.dma_start(out=w_uq[:, k, :], in_=w_uq_d[l, k * 128:(k + 1) * 128, :]), writes=[bf("w_uq")])
        for k in range(2):
            S.dma("pool", lambda e, k=k: e.dma_start(out=w_ukv[:, k, :], in_=w_ukv_d[l, k * 128:(k + 1) * 128, :]), writes=[bf("w_ukv")])
        S.dma("pool", lambda e: e.dma_start(out=wspT[:].rearrange("p g t -> p (g t)"), in_=wsp_d[l]), writes=[bf("wspT")])
        S.dma("sp", lambda e: e.dma_start(out=sgln[:], in_=bc_d[l, :, 2048:2560]), writes=[bf("sgln")])
        S.dma("sp", lambda e: e.dma_start(out=convw[:], in_=bc_d[l, :, 2560:3328]), writes=[bf("convw")])
        S.op("dve", lambda e: e.tensor_tensor(out=wspT[:], in0=wspT[:], in1=tri[:].unsqueeze(1).broadcast_to([128, 4, 128]), op=ALU.mult),
             reads=[bf("wspT"), bf("tri")], writes=[bf("wspT")])
        S.op("pool", lambda e: e.memset(Vc[:, :, :, 64:65], 1.0), writes=BV)
        S.op("pool", lambda e: e.memset(ycv[1][:], 0.0), writes=[bf("ycv1")])

        for i in range(DBG_T):
            S.handoff(stB, stA)
            for s in range(4):
                n = 4 * i + s
                t0 = n * 128
                hb = hT[n % 2]
                Bh = bf(f"hT{n % 2}")
                if DBG_STOP <= 0:
                    return
                S.dma("sp", lambda e: e.dma_start(out=xs[:], in_=xsrc[t0:t0 + 128, :]), reads=[Xd[n]], writes=[bf("xs")])
                sl, Bsl = new_slot()
                S.op("act", lambda e: e.activation(out=xn[:], in_=xs[:], func=AF.Square, accum_out=sl[:, 0:1]), reads=[bf("xs")], writes=[bf("xn"), Bsl])
                r = rstd_from_ss(sl, Bsl, 1, D)
                S.op("dve", lambda e: e.tensor_scalar(out=xn[:], in0=xs[:], scalar1=r, scalar2=None, op0=ALU.mult), reads=[bf("xs"), Bsl], writes=[bf("xn")])
                transpose_to(lambda c: xn[:, c * 128:(c + 1) * 128], 8, 128, bf("xn"), hb[:], Bh, gcol0=g0 + 0)
                if DBG_STOP <= 1:
                    return
                cg = [(0, 512), (512, 1024), (1024, 1536), (1536, INW)]
                for k in range(8):
                    for q, (c0, c1) in enumerate(cg):
                        S.op("pe", lambda e, k=k, q=q, c0=c0, c1=c1: e.matmul(bk[q][:, 0:c1 - c0], lhsT=hb[:, k, :], rhs=w_in[:, k, c0:c1], start=(k == 0), stop=(k == 7)),
                             reads=[Bh, bf(f"w_in{k}")], writes=[Bbk[q]], inc=(k == 7 and q == 3))
                for q, (c0, c1) in enumerate(cg):
                    if q % 2 == 0:
                        S.op("act", lambda e, q=q, c0=c0, c1=c1: e.activation(out=z_sb[:, c0:c1], in_=bk[q][:, 0:c1 - c0], func=AF.Copy), reads=[Bbk[q]], writes=[bf(f"z{q}")])
                    else:
                        S.op("dve", lambda e, q=q, c0=c0, c1=c1: e.tensor_copy(out=z_sb[:, c0:c1], in_=bk[q][:, 0:c1 - c0]), reads=[Bbk[q]], writes=[bf(f"z{q}")])
                if DBG_STOP <= 2:
                    return
                slq, Bq = new_slot()
                slk, Bk_ = new_slot()
                S.op("act", lambda e: e.activation(out=cqkv[:, 0:384], in_=z_sb[:, 0:384], func=AF.Square, accum_out=slq[:, 0:1]), reads=[bf("z0")], writes=[bf("cqkv"), Bq])
                S.op("act", lambda e: e.activation(out=cqkv[:, 384:640], in_=z_sb[:, 384:640], func=AF.Square, accum_out=slk[:, 0:1]), reads=[bf("z0"), bf("z1")], writes=[bf("cqkv"), Bk_])
                rq = rstd_from_ss(slq, Bq, 1, 384)
                rk = rstd_from_ss(slk, Bk_, 1, 256)
                S.op("dve", lambda e: e.tensor_scalar(out=cqkv[:, 0:384], in0=z_sb[:, 0:384], scalar1=rq, scalar2=None, op0=ALU.mult), reads=[bf("z0"), Bq], writes=[bf("cqkv")])
                S.op("dve", lambda e: e.tensor_scalar(out=cqkv[:, 384:640], in0=z_sb[:, 384:640], scalar1=rk, scalar2=None, op0=ALU.mult), reads=[bf("z0"), bf("z1"), Bk_], writes=[bf("cqkv")])
                transpose_to(lambda c: cqkv[:, c * 128:(c + 1) * 128], 5, 128, bf("cqkv"), cqT[:], bf("cqT"), gcol0=g0 + 24)
                yc, yp = ycv[n % 2], ycv[(n + 1) % 2]
                Byc, Byp = bf(f"ycv{n % 2}"), bf(f"ycv{(n + 1) % 2}")
                S.op("pool", lambda e: e.tensor_tensor(out=yc[:], in0=z_sb[:, 1440:1696], in1=z_sb[:, 1696:1952], op=ALU.mult), reads=[bf("z2"), bf("z3")], writes=[Byc])
                S.dma("sp", lambda e: e.dma_start(out=ysh1[1:128, :], in_=yc[0:127, :]), reads=[Byc], writes=[bf("ysh1")])
                S.dma("sp", lambda e: e.dma_start(out=ysh1[0:1, :], in_=yp[127:128, :]), reads=[Byp], writes=[bf("ysh1")])
                S.dma("sp", lambda e: e.dma_start(out=ysh2[2:128, :], in_=yc[0:126, :]), reads=[Byc], writes=[bf("ysh2")])
                S.dma("sp", lambda e: e.dma_start(out=ysh2[0:2, :], in_=yp[126:128, :]), reads=[Byp], writes=[bf("ysh2")])
                S.op("pool", lambda e: e.tensor_tensor(out=acc[:], in0=yc[:], in1=convw[:, 512:768], op=ALU.mult), reads=[Byc, bf("convw")], writes=[bf("acc")])
                S.op("pool", lambda e: e.tensor_tensor(out=ysh1[:], in0=ysh1[:], in1=convw[:, 256:512], op=ALU.mult), reads=[bf("ysh1"), bf("convw")], writes=[bf("ysh1")])
                S.op("pool", lambda e: e.tensor_tensor(out=acc[:], in0=acc[:], in1=ysh1[:], op=ALU.add), reads=[bf("acc"), bf("ysh1")], writes=[bf("acc")])
                S.op("pool", lambda e: e.tensor_tensor(out=ysh2[:], in0=ysh2[:], in1=convw[:, 0:256], op=ALU.mult), reads=[bf("ysh2"), bf("convw")], writes=[bf("ysh2")])
                S.op("pool", lambda e: e.tensor_tensor(out=acc[:], in0=acc[:], in1=ysh2[:], op=ALU.add), reads=[bf("acc"), bf("ysh2")], writes=[bf("acc")])
                S.op("pool", lambda e: e.tensor_tensor(out=acc[:], in0=acc[:], in1=z_sb[:, 1184:1440], op=ALU.mult), reads=[bf("acc"), bf("z2")], writes=[bf("acc")])
                if DBG_STOP <= 3:
                    return
                for k in range(3):
                    S.op("pe", lambda e, k=k: e.matmul(bk[4][:, 0:480], lhsT=cqT[:, k, :], rhs=w_uq[:, k, 0:480], start=(k == 0), stop=(k == 2)),
                         reads=[bf("cqT"), bf("w_uq")], writes=[Bbk[4]], inc=False)
                    S.op("pe", lambda e, k=k: e.matmul(bk[5][:, 0:288], lhsT=cqT[:, k, :], rhs=w_uq[:, k, 480:768], start=(k == 0), stop=(k == 2)),
                         reads=[bf("cqT"), bf("w_uq")], writes=[Bbk[5]], inc=(k == 2))
                for k in range(2):
                    S.op("pe", lambda e, k=k: e.matmul(bk[0][:, :], lhsT=cqT[:, 3 + k, :], rhs=w_ukv[:, k, 0:512], start=(k == 0), stop=(k == 1)),
                         reads=[bf("cqT"), bf("w_ukv")], writes=[Bbk[0]], inc=False)
                    S.op("pe", lambda e, k=k: e.matmul(bk[1][:, :], lhsT=cqT[:, 3 + k, :], rhs=w_ukv[:, k, 512:1024], start=(k == 0), stop=(k == 1)),
                         reads=[bf("cqT"), bf("w_ukv")], writes=[Bbk[1]], inc=(k == 1))
                if DBG_STOP <= 3.1:
                    return
                q4 = bk[4][:, 0:480].rearrange("p (h c) -> p h c", c=96)
                q5 = bk[5][:, 0:288].rearrange("p (h c) -> p h c", c=96)
                k0 = bk[0][:, :].rearrange("p (h c) -> p h c", c=128)
                k1 = bk[1][:, :].rearrange("p (h c) -> p h c", c=128)
                S.op("act", lambda e: e.activation(out=q_tm[:, 0:5, 0:64], in_=q4[:, :, 0:64], func=AF.Copy), reads=[Bbk[4]], writes=[bf("q_tm")])
                S.op("act", lambda e: e.activation(out=q_tm[:, 5:8, 0:64], in_=q5[:, :, 0:64], func=AF.Copy), reads=[Bbk[5]], writes=[bf("q_tm")])
                S.op("dve", lambda e: e.tensor_copy(out=rp[:, 0:5, :], in_=q4[:, :, 64:96]), reads=[Bbk[4]], writes=[bf("rp")])
                S.op("dve", lambda e: e.tensor_copy(out=rp[:, 5:8, :], in_=q5[:, :, 64:96]), reads=[Bbk[5]], writes=[bf("rp")])
                S.op("dve", lambda e: e.tensor_copy(out=rp[:, 8, :], in_=z_sb[:, 640:672]), reads=[bf("z1")], writes=[bf("rp")])
                if DBG_STOP <= 3.2:
                    return
                cs = cosT[:, n, :].unsqueeze(1).broadcast_to([128, 9, 16])
                sn = sinT[:, n, :].unsqueeze(1).broadcast_to([128, 9, 16])
                S.op("dve", lambda e: e.tensor_tensor(out=rt[0][:], in0=rp[:, :, 0:16], in1=cs, op=ALU.mult), reads=[bf("rp"), bf("cos")], writes=[bf("rt0")])
                S.op("dve", lambda e: e.tensor_tensor(out=rt[1][:], in0=rp[:, :, 16:32], in1=sn, op=ALU.mult), reads=[bf("rp"), bf("sin")], writes=[bf("rt1")])
                S.op("dve", lambda e: e.tensor_tensor(out=rt[2][:], in0=rp[:, :, 16:32], in1=cs, op=ALU.mult), reads=[bf("rp"), bf("cos")], writes=[bf("rt2")])
                S.op("dve", lambda e: e.tensor_tensor(out=rt[3][:], in0=rp[:, :, 0:16], in1=sn, op=ALU.mult), reads=[bf("rp"), bf("sin")], writes=[bf("rt3")])
                S.op("dve", lambda e: e.tensor_tensor(out=rr[:, :, 0:16], in0=rt[0][:], in1=rt[1][:], op=ALU.subtract), reads=[bf("rt0"), bf("rt1")], writes=[bf("rr")])
                S.op("dve", lambda e: e.tensor_tensor(out=rr[:, :, 16:32], in0=rt[2][:], in1=rt[3][:], op=ALU.add), reads=[bf("rt2"), bf("rt3")], writes=[bf("rr")])
                S.op("dve", lambda e: e.tensor_copy(out=q_tm[:, :, 64:96], in_=rr[:, 0:8, :]), reads=[bf("rr")], writes=[bf("q_tm")])
                S.op("dve", lambda e: e.tensor_copy(out=k_tm[:, :, 64:96], in_=rr[:, 8:9, :].broadcast_to([128, 8, 32])), reads=[bf("rr")], writes=[bf("k_tm")])
                if DBG_STOP <= 3.3:
                    return
                S.op("act", lambda e: e.activation(out=k_tm[:, 0:4, 0:64], in_=k0[:, :, 0:64], func=AF.Copy), reads=[Bbk[0]], writes=[bf("k_tm")])
                S.op("dve", lambda e: e.tensor_copy(out=k_tm[:, 4:8, 0:64], in_=k1[:, :, 0:64]), reads=[Bbk[1]], writes=[bf("k_tm")])
                if DBG_STOP <= 3.4:
                    return
                S.op("act", lambda e: e.activation(out=Vc[:, n, 0:4, 0:64], in_=k0[:, :, 64:128], func=AF.Copy), reads=[Bbk[0]], writes=[BV[n]])
                S.op("dve", lambda e: e.tensor_copy(out=Vc[:, n, 4:8, 0:64], in_=k1[:, :, 64:128]), reads=[Bbk[1]], writes=[BV[n]])
                if DBG_STOP <= 4:
                    return
                transpose_to(lambda h: q_tm[:, h, :], 8, 96, bf("q_tm"), qT[0:96, :, s * 128:(s + 1) * 128], bf("qT"), eng="act")
                transpose_to(lambda h: k_tm[:, h, :], 8, 96, bf("k_tm"), KT[0:96, :, t0:t0 + 128], BKT[n], eng="dve")
                if DBG_STOP <= 5:
                    return
                S.op("act", lambda e: e.activation(out=uv[:], in_=z_sb[:, 672:1184], func=AF.Gelu_apprx_tanh), reads=[bf("z1"), bf("z2")], writes=[bf("uv")])
                v3 = uv[:, 256:512].rearrange("p (g e) -> p g e", g=4)
                Bsv = bf("stv")
                S.op("dve", lambda e: e.tensor_reduce(out=stv[:, 0:4], in_=v3, axis=AX.X, op=ALU.add), reads=[bf("uv")], writes=[Bsv])
                S.op("pool", lambda e: e.tensor_tensor(out=sq[:], in0=uv[:, 256:512], in1=uv[:, 256:512], op=ALU.mult), reads=[bf("uv")], writes=[bf("sq")])
                S.op("dve", lambda e: e.tensor_reduce(out=stv[:, 4:8], in_=sq[:].rearrange("p (g e) -> p g e", g=4), axis=AX.X, op=ALU.add), reads=[bf("sq")], writes=[Bsv])
                S.op("pool", lambda e: e.tensor_scalar(out=stv[:, 8:12], in0=stv[:, 0:4], scalar1=1.0 / 64, scalar2=None, op0=ALU.mult), reads=[Bsv], writes=[Bsv])
                S.op("pool", lambda e: e.tensor_tensor(out=stv[:, 12:16], in0=stv[:, 8:12], in1=stv[:, 8:12], op=ALU.mult), reads=[Bsv], writes=[Bsv])
                S.op("pool", lambda e: e.tensor_scalar(out=stv[:, 24:28], in0=stv[:, 4:8], scalar1=1.0 / 64, scalar2=EPS, op0=ALU.mult, op1=ALU.add), reads=[Bsv], writes=[Bsv])
                S.op("pool", lambda e: e.tensor_tensor(out=stv[:, 16:20], in0=stv[:, 24:28], in1=stv[:, 12:16], op=ALU.subtract), reads=[Bsv], writes=[Bsv])
                S.op("pool", lambda e: e.tensor_tensor(out=stv[:, 20:24], in0=stv[:, 16:20], in1=mhalf[:, 0:4], op=ALU.pow), reads=[Bsv, bf("mhalf")], writes=[Bsv])
                vn3 = vn32[:].rearrange("p (g e) -> p g e", g=4)
                S.op("dve", lambda e: e.tensor_tensor(out=vn3, in0=v3, in1=stv[:, 8:12].unsqueeze(2).broadcast_to([128, 4, 64]), op=ALU.subtract), reads=[bf("uv"), Bsv], writes=[bf("vn32")])
                S.op("dve", lambda e: e.tensor_tensor(out=vn3, in0=vn3, in1=stv[:, 20:24].unsqueeze(2).broadcast_to([128, 4, 64]), op=ALU.mult), reads=[bf("vn32"), Bsv], writes=[bf("vn32")])
                S.op("pool", lambda e: e.tensor_tensor(out=vn32[:], in0=vn32[:], in1=sgln[:, 0:256], op=ALU.mult), reads=[bf("vn32"), bf("sgln")], writes=[bf("vn32")])
                S.op("pool", lambda e: e.tensor_tensor(out=vn[:], in0=vn32[:], in1=sgln[:, 256:512], op=ALU.add), reads=[bf("vn32"), bf("sgln")], writes=[bf("vn")])
                for g in range(4):
                    S.op("pe", lambda e, g=g: e.matmul(bk[2][:, g * 64:(g + 1) * 64], lhsT=wspT[:, g, :], rhs=vn[:, g * 64:(g + 1) * 64], start=True, stop=True, skip_group_check=True),
                         reads=[bf("wspT"), bf("vn")], writes=[Bbk[2]], inc=(g == 3))
                for g in range(4):
                    S.op("dve", lambda e, g=g: e.scalar_tensor_tensor(out=sq[:, g * 64:(g + 1) * 64], in0=bk[2][:, g * 64:(g + 1) * 64], scalar=gT[:, g0 + 29 + g:g0 + 30 + g],
                                                                      in1=uv[:, g * 64:(g + 1) * 64], op0=ALU.add, op1=ALU.mult),
                         reads=[Bbk[2], bf("gT"), bf("uv")], writes=[bf("sq")])
                slb, Bb_ = new_slot()
                S.op("act", lambda e: e.activation(out=mix_tm[:, s, 512:768], in_=sq[:], func=AF.Square, accum_out=slb[:, 0:1]), reads=[bf("sq")], writes=[bf(f"mix{s}"), Bb_])
                rb = rstd_from_ss(slb, Bb_, 1, 256)
                S.op("dve", lambda e: e.tensor_scalar(out=mix_tm[:, s, 512:768], in0=sq[:], scalar1=rb, scalar2=None, op0=ALU.mult), reads=[bf("sq"), Bb_], writes=[bf(f"mix{s}")])
                if DBG_STOP <= 6:
                    return
                slc, Bc_ = new_slot()
                S.op("act", lambda e: e.activation(out=mix_tm[:, s, 768:1024], in_=acc[:], func=AF.Square, accum_out=slc[:, 0:1]), reads=[bf("acc")], writes=[bf(f"mix{s}"), Bc_])
                rc = rstd_from_ss(slc, Bc_, 1, 256)
                S.op("dve", lambda e: e.tensor_scalar(out=mix_tm[:, s, 768:1024], in0=acc[:], scalar1=rc, scalar2=None, op0=ALU.mult), reads=[bf("acc"), Bc_], writes=[bf(f"mix{s}")])

            if DBG_STAGE in ('A', 'A0'):
                continue
            S.handoff(stA, stB)
            nkb = 4 * i + 4
            for h in range(H):
                O = bk[4 + h % 2]
                BO = Bbk[4 + h % 2]
                O3 = O[:, :].rearrange("p (j c) -> p j c", j=4)
                for kb in range(nkb):
                    j0 = max(0, kb - 4 * i)
                    c0 = j0 * 128
                    r = rot[0] % 4
                    rot[0] += 1
                    sbk, Bs = bk[r], Bbk[r]
                    S.op("pe", lambda e, kb=kb, c0=c0, sbk=sbk: e.matmul(sbk[:, c0:512], lhsT=KT[0:96, h, kb * 128:(kb + 1) * 128], rhs=qT[0:96, h, c0:512], start=True, stop=True),
                         reads=[BKT[kb], bf("qT")], writes=[Bs])
                    S.op("act", lambda e, c0=c0, sbk=sbk, r=r: e.activation(out=PT[r][:, c0:512], in_=sbk[:, c0:512], func=AF.Exp, scale=SCALE), reads=[Bs], writes=[bf(f"PT{r}")])
                    if kb >= 4 * i:
                        S.op("pool", lambda e, c0=c0, r=r: e.tensor_tensor(out=PT[r][:, c0:c0 + 128], in0=PT[r][:, c0:c0 + 128], in1=tri[:], op=ALU.mult),
                             reads=[bf(f"PT{r}"), bf("tri")], writes=[bf(f"PT{r}")])
                    for j in range(j0, 4):
                        S.op("pe", lambda e, kb=kb, j=j, r=r: e.matmul(O3[:, j, 0:65], lhsT=PT[r][:, j * 128:(j + 1) * 128], rhs=Vc[:, kb, h, :],
                                                                       start=(kb == 0 and j == 0), stop=(kb == 4 * i + j), skip_group_check=True),
                             reads=[bf(f"PT{r}"), BV[kb]], writes=[BO], inc=(j == 3))
                S.op("dve", lambda e: e.reciprocal(out=rinv[:].unsqueeze(2), in_=O3[:, :, 64:65]), reads=[BO], writes=[bf("rinv")])
                S.op("dve", lambda e: e.tensor_tensor(out=ya[:, :, h * 64:(h + 1) * 64], in0=O3[:, :, 0:64], in1=rinv[:].unsqueeze(2).broadcast_to([128, 4, 64]), op=ALU.mult),
                     reads=[BO, bf("rinv")], writes=[bf("ya")])
            for j in range(4):
                sla, Ba_ = new_slot()
                S.op("act", lambda e, j=j: e.activation(out=mix_tm[:, j, 0:512], in_=ya[:, j, :], func=AF.Square, accum_out=sla[:, 0:1]), reads=[bf("ya")], writes=[bf(f"mix{j}"), Ba_])
                ra = rstd_from_ss(sla, Ba_, 1, 512)
                S.op("dve", lambda e, j=j: e.tensor_scalar(out=mix_tm[:, j, 0:512], in0=ya[:, j, :], scalar1=ra, scalar2=None, op0=ALU.mult), reads=[bf("ya"), Ba_], writes=[bf(f"mix{j}")])
            S.dma("sp", lambda e: e.dma_start(out=mix_d[i * 512:(i + 1) * 512, :].rearrange("(j p) d -> p j d", p=128), in_=mix_tm[:]),
                  reads=[bf("mix0"), bf("mix1"), bf("mix2"), bf("mix3")], writes=[Md[i]])

    def ffn_pass(l):
        xsrc = x_d if l == 0 else out_d
        g0 = l * GW
        S.barrier()
        for k in range(8):
            S.dma("pool", lambda e, k=k: e.dma_start(out=w_out[:, k, :], in_=w_out_d[l, k * 128:(k + 1) * 128, :]), writes=[bf("w_out")])
        S.dma("sp", lambda e: e.dma_start(out=postg_m[:], in_=bc_d[l, :, 0:1024]), writes=[bf("postg_m")])
        S.dma("sp", lambda e: e.dma_start(out=postg_f[:], in_=bc_d[l, :, 1024:2048]), writes=[bf("postg_f")])
        for k in range(8):
            S.dma("pool", lambda e, k=k: e.dma_start(out=w_gate[:, k, :], in_=w_gate_d[l, k * 128:(k + 1) * 128, :]), writes=[bf(f"w_gate{k}")])
            S.dma("pool", lambda e, k=k: e.dma_start(out=w_up[:, k, :], in_=w_up_d[l, k * 128:(k + 1) * 128, :]), writes=[bf(f"w_up{k}")])
        for c in range(NFF):
            S.dma("pool", lambda e, c=c: e.dma_start(out=w_down[:, c, :], in_=w_down_d[l, c * 128:(c + 1) * 128, :]), writes=[bf(f"w_down{c}")])

        def post_norm_residual(postg, Bpostg, ba, bb, xs_, Bxs, t_, Bt):
            sl, Bsl = new_slot()
            S.op("act", lambda e: e.activation(out=t_[:, 0:512], in_=bk[ba][:, :], func=AF.Square, accum_out=sl[:, 0:1]), reads=[Bbk[ba]], writes=[Bt, Bsl])
            S.op("act", lambda e: e.activation(out=t_[:, 512:1024], in_=bk[bb][:, :], func=AF.Square, accum_out=sl[:, 1:2]), reads=[Bbk[bb]], writes=[Bt, Bsl])
            r = rstd_from_ss(sl, Bsl, 2, D)
            S.op("dve", lambda e: e.scalar_tensor_tensor(out=t_[:, 0:512], in0=bk[ba][:, :], scalar=r, in1=postg[:, 0:512], op0=ALU.mult, op1=ALU.mult),
                 reads=[Bbk[ba], Bsl, Bpostg], writes=[Bt])
            S.op("dve", lambda e: e.scalar_tensor_tensor(out=t_[:, 512:1024], in0=bk[bb][:, :], scalar=r, in1=postg[:, 512:1024], op0=ALU.mult, op1=ALU.mult),
                 reads=[Bbk[bb], Bsl, Bpostg], writes=[Bt])
            S.op("pool", lambda e: e.tensor_tensor(out=xs_[:], in0=xs_[:], in1=t_[:], op=ALU.add), reads=[Bxs, Bt], writes=[Bxs])

        aT_bufs = [bf(f"aT{c}") for c in range(NFF)]
        set1_bufs = [bf("xs_f2"), bf("t_f2"), bf("xn_b0"), bf("xn_b1"), bf("mixT2")]
        sets = [dict(xs=xs_f, Bxs=bf("xs_f"), t=t_f, Bt=bf("t_f"), m0=xn_f[0], Bm0=bf("xn_f0"), m1=xn_f[1], Bm1=bf("xn_f1"), mixT=mixT, BmixT=bf("mixT"), ba=4, bb=5),
                dict(xs=xs_f2, Bxs=bf("xs_f2"), t=t_f2, Bt=bf("t_f2"), m0=xn_b[0], Bm0=bf("xn_b0"), m1=xn_b[1], Bm1=bf("xn_b1"), mixT=mixT2, BmixT=bf("mixT2"), ba=2, bb=3)]
        for i in range(DBG_T if DBG_STAGE == 'full' else 0):
            S.handoff(aT_bufs, set1_bufs)
            for s in range(4):
                n = 4 * i + s
                t0 = n * 128
                Q = sets[s % 2]
                m0, Bm0, m1, Bm1, xs_, Bxs, t_, Bt, mT, BmT, ba, bb = Q["m0"], Q["Bm0"], Q["m1"], Q["Bm1"], Q["xs"], Q["Bxs"], Q["t"], Q["Bt"], Q["mixT"], Q["BmixT"], Q["ba"], Q["bb"]
                S.dma("sp", lambda e: e.dma_start(out=m0[:], in_=mix_d[t0:t0 + 128, :]), reads=[Md[i]], writes=[Bm0])
                S.dma("sp", lambda e: e.dma_start(out=xs_[:], in_=xsrc[t0:t0 + 128, :]), reads=[Xd[n]], writes=[Bxs])
                transpose_to(lambda c: m0[:, c * 128:(c + 1) * 128], 8, 128, Bm0, mT[:], BmT, gcol0=g0 + 16)
                for k in range(8):
                    S.op("pe", lambda e, k=k: e.matmul(bk[ba][:, :], lhsT=mT[:, k, :], rhs=w_out[:, k, 0:512], start=(k == 0), stop=(k == 7)),
                         reads=[BmT, bf("w_out")], writes=[Bbk[ba]], inc=False)
                    S.op("pe", lambda e, k=k: e.matmul(bk[bb][:, :], lhsT=mT[:, k, :], rhs=w_out[:, k, 512:1024], start=(k == 0), stop=(k == 7)),
                         reads=[BmT, bf("w_out")], writes=[Bbk[bb]], inc=(k == 7))
                post_norm_residual(postg_m, bf("postg_m"), ba, bb, xs_, Bxs, t_, Bt)
                S.dma("sp", lambda e: e.dma_start(out=out_d[t0:t0 + 128, :], in_=xs_[:]), reads=[Bxs], writes=[Xd[n]])
                sl, Bsl = new_slot()
                S.op("act", lambda e: e.activation(out=m1[:], in_=xs_[:], func=AF.Square, accum_out=sl[:, 0:1]), reads=[Bxs], writes=[Bm1, Bsl])
                r = rstd_from_ss(sl, Bsl, 1, D)
                S.op("dve", lambda e: e.tensor_scalar(out=m1[:], in0=xs_[:], scalar1=r, scalar2=None, op0=ALU.mult), reads=[Bxs, Bsl], writes=[Bm1])
                transpose_to(lambda c: m1[:, c * 128:(c + 1) * 128], 8, 128, Bm1, h2T[:, :, s * 128:(s + 1) * 128], bf("h2T"), gcol0=g0 + 8)
            S.handoff(set1_bufs, aT_bufs)
            for c in range(NFF):
                gb, Bg = bk[c % 2], Bbk[c % 2]
                ub, Bu = bk[2 + c % 2], Bbk[2 + c % 2]
                for k in range(8):
                    S.op("pe", lambda e, k=k, c=c, gb=gb: e.matmul(gb[:, :], lhsT=w_gate[:, k, c * 128:(c + 1) * 128], rhs=h2T[:, k, :], start=(k == 0), stop=(k == 7)),
                         reads=[bf(f"w_gate{k}"), bf("h2T")], writes=[Bg], inc=(k == 7))
                for k in range(8):
                    S.op("pe", lambda e, k=k, c=c, ub=ub: e.matmul(ub[:, :], lhsT=w_up[:, k, c * 128:(c + 1) * 128], rhs=h2T[:, k, :], start=(k == 0), stop=(k == 7)),
                         reads=[bf(f"w_up{k}"), bf("h2T")], writes=[Bu], inc=(k == 7))
                sg, Bsg = sg_f[c % 2], bf(f"xn_f{c % 2}")
                S.op("act", lambda e, gb=gb, sg=sg: e.activation(out=sg[:], in_=gb[:, :], func=AF.Silu), reads=[Bg], writes=[Bsg])
                S.op("dve", lambda e, c=c, ub=ub, sg=sg: e.tensor_tensor(out=aT[:, c, :], in0=ub[:, :], in1=sg[:], op=ALU.mult), reads=[Bu, Bsg], writes=[bf(f"aT{c}")])
            for s in range(4):
                n = 4 * i + s
                t0 = n * 128
                ba, bb = [(4, 5), (0, 1), (2, 3), (4, 5)][s]
                for c in range(NFF):
                    S.op("pe", lambda e, c=c: e.matmul(bk[ba][:, :], lhsT=aT[:, c, s * 128:(s + 1) * 128], rhs=w_down[:, c, 0:512], start=(c == 0), stop=(c == NFF - 1)),
                         reads=[bf(f"aT{c}"), bf(f"w_down{c}")], writes=[Bbk[ba]], inc=False)
                    S.op("pe", lambda e, c=c: e.matmul(bk[bb][:, :], lhsT=aT[:, c, s * 128:(s + 1) * 128], rhs=w_down[:, c, 512:1024], start=(c == 0), stop=(c == NFF - 1)),
                         reads=[bf(f"aT{c}"), bf(f"w_down{c}")], writes=[Bbk[bb]], inc=(c == NFF - 1))
                S.dma("sp", lambda e: e.dma_start(out=xs_f[:], in_=out_d[t0:t0 + 128, :]), reads=[Xd[n]], writes=[bf("xs_f")])
                post_norm_residual(postg_f, bf("postg_f"), ba, bb, xs_f, bf("xs_f"), t_f, bf("t_f"))
                S.dma("sp", lambda e: e.dma_start(out=out_d[t0:t0 + 128, :], in_=xs_f[:]), reads=[bf("xs_f")], writes=[Xd[n]])

    for l in range(L if DBG_STAGE != 'P' else 0):
        mixer_pass(l)
        if DBG_STAGE != 'A0':
            ffn_pass(l)
    S.finish("sp", Xd)
    S.barrier()
    return nc


def _layouts(p):
    L = DEPTH
    gT = np.zeros((128, L * GW), np.float32)
    for l in range(L):
        o = l * GW
        gT[:, o + 0:o + 8] = p["mix_pre_g"][l].reshape(8, 128).T
        gT[:, o + 8:o + 16] = p["ffn_pre_g"][l].reshape(8, 128).T
        gT[:, o + 16:o + 24] = p["out_norm_g"][l].reshape(8, 128).T
        gT[:, o + 24:o + 27] = p["q_norm_g"][l].reshape(3, 128).T
        gT[:, o + 27:o + 29] = p["kv_norm_g"][l].reshape(2, 128).T
        gT[:, o + 29:o + 33] = p["b_sp"][l].T
    bc = np.zeros((L, 128, BCW), np.float32)
    bc[:, :, 0:1024] = p["mix_post_g"][:, None, :]
    bc[:, :, 1024:2048] = p["ffn_post_g"][:, None, :]
    bc[:, :, 2048:2304] = p["sg_ln_g"][:, None, :]
    bc[:, :, 2304:2560] = p["sg_ln_b"][:, None, :]
    bc[:, :, 2560:3328] = p["conv_w"].reshape(L, 1, 768)
    wspT = np.ascontiguousarray(np.transpose(p["w_sp"], (0, 3, 1, 2))).reshape(L, 128, 512)
    return gT, bc, wspT


_INVF = (np.float32(1.0) / (np.float32(10000.0) ** (np.arange(16, dtype=np.float32) / np.float32(16)))).astype(np.float32)


def kernel(x, positions, mix_pre_g, mix_post_g, ffn_pre_g, ffn_post_g, w_in, q_norm_g, w_uq, kv_norm_g, w_ukv,
           sg_ln_g, sg_ln_b, w_sp, b_sp, conv_w, out_norm_g, w_out, w_gate, w_up, w_down, _depth=DEPTH, _cores=8):
    p = dict(mix_pre_g=mix_pre_g, mix_post_g=mix_post_g, ffn_pre_g=ffn_pre_g, ffn_post_g=ffn_post_g, q_norm_g=q_norm_g,
             kv_norm_g=kv_norm_g, sg_ln_g=sg_ln_g, sg_ln_b=sg_ln_b, w_sp=w_sp, b_sp=b_sp, conv_w=conv_w, out_norm_g=out_norm_g)
    p = {k: np.asarray(v, np.float32) for k, v in p.items()}
    gT, bc, wspT = _layouts(p)
    f = lambda a: np.ascontiguousarray(np.asarray(a, np.float32))
    shared = {"invf": np.ascontiguousarray(np.broadcast_to(_INVF[None, :], (128, 16))), "w_in": f(w_in), "w_uq": f(w_uq), "w_ukv": f(w_ukv),
              "w_out": f(w_out), "w_gate": f(w_gate), "w_up": f(w_up), "w_down": f(w_down), "gT": gT, "bc": bc, "wspT": wspT}
    x = np.asarray(x, np.float32)
    positions = np.asarray(positions, np.int32)
    nc = build_program(_depth)
    in_maps = []
    for b in range(_cores):
        m = dict(shared)
        m["x"] = np.ascontiguousarray(x[b])
        m["pos"] = np.ascontiguousarray(positions[b].reshape(NSUB, 128).T)
        in_maps.append(m)
    res = run_bass_kernel_spmd(nc, in_maps, core_ids=list(range(_cores)))
    return np.stack([np.asarray(r["out"], np.float32) for r in res.results], axis=0)
```

```python
import math
import os
import numpy as np
import concourse.bass as bass
import concourse.mybir as mybir
from concourse.bass_utils import run_bass_kernel_spmd

F32 = mybir.dt.float32
BF16 = mybir.dt.bfloat16
I32 = mybir.dt.int32
AF = mybir.ActivationFunctionType
ALU = mybir.AluOpType
AX = mybir.AxisListType

D = 1024
SEQ = 4096
DEPTH = 4
NSUB = SEQ // 128
NTILE = SEQ // 512
INW = 1952
H = 8
DFF = 2816
NFF = DFF // 128
EPS = 1e-6
SCALE = 96.0 ** -0.5
GW = 33
BCW = 3328
ND = 8


class Buf:
    __slots__ = ("name", "w", "r", "psum")

    def __init__(self, name, psum=False):
        self.name = name
        self.w = None
        self.r = {}
        self.psum = psum


class Sched:
    def __init__(self, nc):
        self.nc = nc
        self.E = {"pe": nc.tensor, "act": nc.scalar, "dve": nc.vector, "pool": nc.gpsimd, "sp": nc.sync}
        self.sem = {k: nc.alloc_semaphore("s_" + k) for k in self.E}
        self.cnt = {k: 0 for k in self.E}
        self.seen = {k: {} for k in self.E}
        self.pend = {k: [] for k in self.E}
        self.dsem = {q: [nc.alloc_semaphore(f"d_{q}{i}") for i in range(ND)] for q in ("sp", "pool")}
        self.dcnt = {q: [0] * ND for q in self.dsem}
        self.drr = {q: 0 for q in self.dsem}

    def _collect(self, eng, reads, writes, skip_own_war=True):
        toks = {}
        own = self.sem.get(eng)

        def need(s, v, war=False):
            if s is own and (eng == "pe" or (war and skip_own_war)):
                return
            if toks.get(s, 0) < v:
                toks[s] = v

        for b in reads:
            if b.w is not None:
                need(*b.w)
            if b.psum:
                for s, v in b.r.items():
                    if s is not own:
                        need(s, v)
        for b in writes:
            if b.w is not None:
                need(*b.w)
            for s, v in b.r.items():
                need(s, v, war=True)
        return toks

    def _emit_waits(self, eng, toks):
        e = self.E[eng]
        seen = self.seen[eng]
        for s, v in toks.items():
            if seen.get(s, 0) < v:
                e.wait_ge(s, v)
                seen[s] = v

    def op(self, eng, fn, reads=(), writes=(), inc=True):
        self._emit_waits(eng, self._collect(eng, reads, writes))
        ins = fn(self.E[eng])
        self.pend[eng].append((tuple(reads), tuple(writes)))
        if inc:
            self.cnt[eng] += 1
            s = self.sem[eng]
            ins.then_inc(s, 1)
            v = self.cnt[eng]
            for rs, ws in self.pend[eng]:
                for b in rs:
                    b.r[s] = v
                for b in ws:
                    b.w = (s, v)
                    b.r = {}
            self.pend[eng] = []
        return ins

    def dma(self, q, fn, reads=(), writes=()):
        assert not self.pend[q]
        i = self.drr[q]
        self.drr[q] = (i + 1) % ND
        s = self.dsem[q][i]
        toks = self._collect(q, reads, writes, skip_own_war=False)
        prev = 16 * self.dcnt[q][i]
        if prev and toks.get(s, 0) < prev:
            toks[s] = prev
        self._emit_waits(q, toks)
        ins = fn(self.E[q])
        ins.then_inc(s, 16)
        self.dcnt[q][i] += 1
        v = 16 * self.dcnt[q][i]
        for b in reads:
            b.r[s] = v
        for b in writes:
            b.w = (s, v)
            b.r = {}
        return ins

    def handoff(self, src, dst):
        for d in dst:
            for b in src:
                if b.w is not None:
                    s, v = b.w
                    if d.r.get(s, 0) < v:
                        d.r[s] = v
                for s, v in b.r.items():
                    if d.r.get(s, 0) < v:
                        d.r[s] = v

    def barrier(self):
        for k in self.E:
            assert not self.pend[k]
        toks = {}
        for k in self.E:
            if self.cnt[k]:
                toks[self.sem[k]] = self.cnt[k]
        for q in self.dsem:
            for i in range(ND):
                if self.dcnt[q][i]:
                    toks[self.dsem[q][i]] = 16 * self.dcnt[q][i]
        for k in self.E:
            t = {s: v for s, v in toks.items() if s is not self.sem[k]}
            self._emit_waits(k, t)

    def finish(self, eng, bufs):
        toks = {}
        for b in bufs:
            if b.w is not None:
                s, v = b.w
                toks[s] = max(toks.get(s, 0), v)
            for s, v in b.r.items():
                toks[s] = max(toks.get(s, 0), v)
        self._emit_waits(eng, toks)


def build_program(L=DEPTH):
    DBG_T = int(os.environ.get('KDBG_TILES', NTILE))
    DBG_STAGE = os.environ.get('KDBG_STAGE', 'full')
    DBG_STOP = float(os.environ.get('KDBG_STOP', 99))
    nc = bass.Bass("TRN2", target_bir_lowering=False)
    S = Sched(nc)

    def dram(name, shape, dt, kind):
        return nc.dram_tensor(name, shape, dt, kind=kind).ap()

    x_d = dram("x", [SEQ, D], F32, "ExternalInput")
    pos_d = dram("pos", [128, NSUB], I32, "ExternalInput")
    invf_d = dram("invf", [128, 16], F32, "ExternalInput")
    w_in_d = dram("w_in", [DEPTH, D, INW], F32, "ExternalInput")
    w_uq_d = dram("w_uq", [DEPTH, 384, 768], F32, "ExternalInput")
    w_ukv_d = dram("w_ukv", [DEPTH, 256, 1024], F32, "ExternalInput")
    w_out_d = dram("w_out", [DEPTH, D, D], F32, "ExternalInput")
    w_gate_d = dram("w_gate", [DEPTH, D, DFF], F32, "ExternalInput")
    w_up_d = dram("w_up", [DEPTH, D, DFF], F32, "ExternalInput")
    w_down_d = dram("w_down", [DEPTH, DFF, D], F32, "ExternalInput")
    gT_d = dram("gT", [128, DEPTH * GW], F32, "ExternalInput")
    bc_d = dram("bc", [DEPTH, 128, BCW], F32, "ExternalInput")
    wsp_d = dram("wspT", [DEPTH, 128, 512], F32, "ExternalInput")
    out_d = dram("out", [SEQ, D], F32, "ExternalOutput")
    mix_d = dram("mixd", [SEQ, D], BF16, "Internal")

    base = (nc.sbuf_base + 31) // 32 * 32
    top = nc.sbuf_top

    def sb(name, shape, dt, off):
        nb = int(np.prod(shape[1:])) * (2 if dt == BF16 else 4)
        assert off % 32 == 0 and base + off + nb <= top, (name, off, nb, top - base)
        return nc.alloc_sbuf_tensor_at(name, list(shape), dt, offset=base + off)

    ident = sb("ident", [128, 128], BF16, 0)
    tri = sb("tri", [128, 128], BF16, 256)
    mhalf = sb("mhalf", [128, 8], F32, 512)
    cosT = sb("cosT", [128, NSUB, 16], F32, 576)
    sinT = sb("sinT", [128, NSUB, 16], F32, 2624)
    gT = sb("gTs", [128, DEPTH * GW], F32, 4672)
    stt_ = sb("stats", [128, 64], F32, 5216)
    invf = sb("invfs", [128, 16], F32, 5472)
    posi = sb("posi", [128, NSUB], I32, 5536)
    posf = sb("posf", [128, NSUB], F32, 5664)
    stv = sb("stv", [128, 32], F32, 5792)
    CEND = 5920
    BIG = CEND
    MID = BIG + 174080
    assert base + MID + 30720 <= top, (base, MID, top)

    KT = sb("KT", [128, H, SEQ], BF16, BIG + 0)
    Vc = sb("Vc", [128, NSUB, H, 65], BF16, BIG + 65536)
    w_in = sb("w_in_s", [128, 8, INW], BF16, BIG + 98816)
    w_uq = sb("w_uq_s", [128, 3, 768], BF16, BIG + 130048)
    w_ukv = sb("w_ukv_s", [128, 2, 1024], BF16, BIG + 134656)
    qT = sb("qT", [128, H, 512], BF16, BIG + 138752)
    mix_tm = sb("mix_tm", [128, 4, D], BF16, BIG + 146944)
    SA = BIG + 155136
    z_sb = sb("z_sb", [128, INW], F32, SA)
    uv = sb("uv", [128, 512], F32, SA + 7808)
    ysh1 = sb("ysh1", [128, 256], F32, SA + 9856)
    ysh2 = sb("ysh2", [128, 256], F32, SA + 10880)
    acc = sb("acc", [128, 256], F32, SA + 11904)
    sq = sb("sq", [128, 256], F32, SA + 12928)
    vn32 = sb("vn32", [128, 256], F32, SA + 13952)
    cqkv = sb("cqkv", [128, 640], BF16, SA + 14976)
    ycv = [sb("ycv0", [128, 256], F32, SA + 16352), sb("ycv1", [128, 256], F32, SA + 17376)]
    assert SA + 18400 <= MID
    ya = sb("ya", [128, 4, 512], F32, SA)
    PT = [sb(f"PT{r}", [128, 512], BF16, SA + 8192 + 1024 * r) for r in range(4)]
    hT = [sb("hT0", [128, 8, 128], BF16, MID + 0), sb("hT1", [128, 8, 128], BF16, MID + 2048)]
    xs = sb("xs", [128, D], F32, MID + 4096)
    xn = sb("xn", [128, D], BF16, MID + 8192)
    sgln = sb("sgln", [128, 512], F32, MID + 10240)
    convw = sb("convw", [128, 768], F32, MID + 12288)
    wspT = sb("wspTs", [128, 4, 128], BF16, MID + 15360)
    cqT = sb("cqT", [128, 5, 128], BF16, MID + 16384)
    q_tm = sb("q_tm", [128, H, 96], BF16, MID + 17664)
    k_tm = sb("k_tm", [128, H, 96], BF16, MID + 19200)
    rp = sb("rp", [128, 9, 32], F32, MID + 20736)
    rt = [sb(f"rt{i}", [128, 9, 16], F32, MID + 21888 + 576 * i) for i in range(4)]
    rr = sb("rr", [128, 9, 32], F32, MID + 24192)
    vn = sb("vn", [128, 256], BF16, MID + 25344)
    rinv = sb("rinv", [128, 4], F32, MID + 25856)
    w_gate = sb("w_gate_s", [128, 8, DFF], BF16, BIG + 0)
    w_up = sb("w_up_s", [128, 8, DFF], BF16, BIG + 45056)
    w_down = sb("w_down_s", [128, NFF, D], BF16, BIG + 90112)
    w_out = sb("w_out_s", [128, 8, D], BF16, BIG + 135168)
    aT = sb("aT", [128, NFF, 512], BF16, BIG + 151552)
    h2T = sb("h2T", [128, 8, 512], BF16, MID + 0)
    xs_f = sb("xs_f", [128, D], F32, MID + 8192)
    xn_f = [sb("xn_f0", [128, D], BF16, MID + 12288), sb("xn_f1", [128, D], BF16, MID + 14336)]
    sg_f = [sb("sg_f0", [128, 512], F32, MID + 12288), sb("sg_f1", [128, 512], F32, MID + 14336)]
    mixT = sb("mixT", [128, 8, 128], BF16, MID + 16384)
    t_f = sb("t_f", [128, D], F32, MID + 18432)
    postg_m = sb("postg_m", [128, D], F32, MID + 22528)
    postg_f = sb("postg_f", [128, D], F32, MID + 26624)
    AT0 = BIG + 151552
    xs_f2 = sb("xs_f2", [128, D], F32, AT0 + 0)
    t_f2 = sb("t_f2", [128, D], F32, AT0 + 4096)
    xn_b = [sb("xn_b0", [128, D], BF16, AT0 + 8192), sb("xn_b1", [128, D], BF16, AT0 + 10240)]
    mixT2 = sb("mixT2", [128, 8, 128], BF16, AT0 + 12288)

    pT = [nc.alloc_psum_tensor(f"pT{i}", [128, 8, 128], BF16) for i in range(2)]
    bk = [nc.alloc_psum_tensor(f"bk{i}", [128, 512], F32) for i in range(6)]
    BpT = [Buf(f"pT{i}", psum=True) for i in range(2)]
    Bbk = [Buf(f"bk{i}", psum=True) for i in range(6)]
    tcount = [0]

    def next_pT():
        i = tcount[0] % 2
        tcount[0] += 1
        return pT[i], BpT[i]

    B = {}

    def bf(name):
        if name not in B:
            B[name] = Buf(name)
        return B[name]

    Xd = [Buf(f"xd{n}") for n in range(NSUB)]
    Md = [Buf(f"md{i}") for i in range(NTILE)]
    BKT = [Buf(f"kt{n}") for n in range(NSUB)]
    BV = [Buf(f"v{n}") for n in range(NSUB)]

    slot = [0]

    def new_slot():
        k = slot[0] % 16
        slot[0] += 1
        return stt_[:, 4 * k:4 * k + 4], bf(f"slot{k}")

    def rstd_from_ss(sl, Bsl, ncols, Dn):
        if ncols == 2:
            S.op("pool", lambda e: e.tensor_tensor(out=sl[:, 0:1], in0=sl[:, 0:1], in1=sl[:, 1:2], op=ALU.add), reads=[Bsl], writes=[Bsl])
        S.op("pool", lambda e: e.tensor_scalar(out=sl[:, 2:3], in0=sl[:, 0:1], scalar1=1.0 / Dn, scalar2=EPS, op0=ALU.mult, op1=ALU.add), reads=[Bsl], writes=[Bsl])
        S.op("pool", lambda e: e.tensor_tensor(out=sl[:, 3:4], in0=sl[:, 2:3], in1=mhalf[:, 0:1], op=ALU.pow), reads=[Bsl, bf("mhalf")], writes=[Bsl])
        return sl[:, 3:4]

    def transpose_to(src_ap_fn, nchunk, rows, Bsrc, dst_ap, Bdst, gcol0=None, eng="dve"):
        p, Bp = next_pT()
        for c in range(nchunk):
            S.op("pe", lambda e, c=c: e.transpose(out=p[0:rows, c, :], in_=src_ap_fn(c), identity=ident[:]),
                 reads=[Bsrc, bf("ident")], writes=[Bp], inc=(c == nchunk - 1))
        if gcol0 is None:
            if eng == "act":
                S.op("act", lambda e: e.activation(out=dst_ap, in_=p[0:rows, 0:nchunk, :], func=AF.Copy), reads=[Bp], writes=[Bdst])
            else:
                S.op(eng, lambda e: e.tensor_copy(out=dst_ap, in_=p[0:rows, 0:nchunk, :]), reads=[Bp], writes=[Bdst])
        else:
            g = gT[0:rows, gcol0:gcol0 + nchunk].unsqueeze(2).broadcast_to([rows, nchunk, 128])
            S.op(eng, lambda e: e.tensor_tensor(out=dst_ap, in0=p[0:rows, 0:nchunk, :], in1=g, op=ALU.mult), reads=[Bp, bf("gT")], writes=[Bdst])

    S.op("pool", lambda e: e.memset(ident[:], 1.0), writes=[bf("ident")])
    S.op("pool", lambda e: e.affine_select(out=ident[:], in_=ident[:], pattern=[[-1, 128]], compare_op=ALU.is_equal, fill=0.0, base=0, channel_multiplier=1),
         reads=[bf("ident")], writes=[bf("ident")])
    S.op("pool", lambda e: e.memset(tri[:], 1.0), writes=[bf("tri")])
    S.op("pool", lambda e: e.affine_select(out=tri[:], in_=tri[:], pattern=[[1, 128]], compare_op=ALU.is_ge, fill=0.0, base=0, channel_multiplier=-1),
         reads=[bf("tri")], writes=[bf("tri")])
    S.op("pool", lambda e: e.memset(mhalf[:], -0.5), writes=[bf("mhalf")])
    S.dma("sp", lambda e: e.dma_start(out=gT[:], in_=gT_d), writes=[bf("gT")])
    S.dma("sp", lambda e: e.dma_start(out=invf[:], in_=invf_d), writes=[bf("invf")])
    S.dma("sp", lambda e: e.dma_start(out=posi[:], in_=pos_d), writes=[bf("posi")])
    TWO_PI = 2.0 * math.pi
    C1 = float(np.float32(6.28125))
    C2 = float(np.float32(TWO_PI - 6.28125))
    MAGIC = 12582912.0
    ang = sb("ang", [128, NSUB, 16], F32, BIG + 0)
    kk = sb("kk", [128, NSUB, 16], F32, BIG + 2048)
    t2 = sb("t2p", [128, NSUB, 16], F32, BIG + 4096)
    S.op("dve", lambda e: e.tensor_copy(out=posf[:], in_=posi[:]), reads=[bf("posi")], writes=[bf("posf")])
    S.op("dve", lambda e: e.tensor_tensor(out=ang[:], in0=posf[:].unsqueeze(2).broadcast_to([128, NSUB, 16]),
                                          in1=invf[:].unsqueeze(1).broadcast_to([128, NSUB, 16]), op=ALU.mult),
         reads=[bf("posf"), bf("invf")], writes=[bf("ang")])
    S.op("dve", lambda e: e.tensor_scalar(out=t2[:], in0=ang[:], scalar1=1.0 / TWO_PI, scalar2=MAGIC, op0=ALU.mult, op1=ALU.add), reads=[bf("ang")], writes=[bf("t2")])
    S.op("dve", lambda e: e.tensor_scalar(out=kk[:], in0=t2[:], scalar1=-MAGIC, scalar2=None, op0=ALU.add), reads=[bf("t2")], writes=[bf("kk")])
    S.op("dve", lambda e: e.scalar_tensor_tensor(out=ang[:], in0=kk[:], scalar=-C1, in1=ang[:], op0=ALU.mult, op1=ALU.add), reads=[bf("kk"), bf("ang")], writes=[bf("ang")])
    S.op("dve", lambda e: e.scalar_tensor_tensor(out=ang[:], in0=kk[:], scalar=-C2, in1=ang[:], op0=ALU.mult, op1=ALU.add), reads=[bf("kk"), bf("ang")], writes=[bf("ang")])
    S.op("dve", lambda e: e.tensor_scalar(out=ang[:], in0=ang[:], scalar1=math.pi, scalar2=-math.pi, op0=ALU.min, op1=ALU.max), reads=[bf("ang")], writes=[bf("ang")])
    S.op("act", lambda e: e.activation(out=sinT[:], in_=ang[:], func=AF.Sin), reads=[bf("ang")], writes=[bf("sin")])
    S.op("dve", lambda e: e.tensor_scalar(out=t2[:], in0=ang[:], scalar1=-1.0, scalar2=None, op0=ALU.mult), reads=[bf("ang")], writes=[bf("t2")])
    S.op("dve", lambda e: e.tensor_tensor(out=t2[:], in0=t2[:], in1=ang[:], op=ALU.max), reads=[bf("ang"), bf("t2")], writes=[bf("t2")])
    S.op("dve", lambda e: e.tensor_scalar(out=t2[:], in0=t2[:], scalar1=-1.0, scalar2=math.pi / 2, op0=ALU.mult, op1=ALU.add), reads=[bf("t2")], writes=[bf("t2")])
    S.op("act", lambda e: e.activation(out=cosT[:], in_=t2[:], func=AF.Sin), reads=[bf("t2")], writes=[bf("cos")])

    stA = [bf("z0"), bf("z1"), bf("z2"), bf("z3"), bf("uv"), bf("ysh1"), bf("ysh2"), bf("acc"), bf("sq"), bf("vn32"), bf("cqkv")]
    stB = [bf("ya"), bf("PT0"), bf("PT1"), bf("PT2"), bf("PT3")]
    rot = [0]

    def mixer_pass(l):
        xsrc = x_d if l == 0 else out_d
        g0 = l * GW
        S.barrier()
        for k in range(8):
            S.dma("pool", lambda e, k=k: e.dma_start(out=w_in[:, k, :], in_=w_in_d[l, k * 128:(k + 1) * 128, :]), writes=[bf(f"w_in{k}")])
        for k in range(3):
            S.dma("pool", lambda e, k=k: e.dma_start(out=w_uq[:, k, :], in_=w_uq_d[l, k * 128:(k + 1) * 128, :]), writes=[bf("w_uq")])
        for k in range(2):
            S.dma("pool", lambda e, k=k: e.dma_start(out=w_ukv[:, k, :], in_=w_ukv_d[l, k * 128:(k + 1) * 128, :]), writes=[bf("w_ukv")])
        S.dma("pool", lambda e: e.dma_start(out=wspT[:].rearrange("p g t -> p (g t)"), in_=wsp_d[l]), writes=[bf("wspT")])
        S.dma("sp", lambda e: e.dma_start(out=sgln[:], in_=bc_d[l, :, 2048:2560]), writes=[bf("sgln")])
        S.dma("sp", lambda e: e.dma_start(out=convw[:], in_=bc_d[l, :, 2560:3328]), writes=[bf("convw")])
        S.op("dve", lambda e: e.tensor_tensor(out=wspT[:], in0=wspT[:], in1=tri[:].unsqueeze(1).broadcast_to([128, 4, 128]), op=ALU.mult),
             reads=[bf("wspT"), bf("tri")], writes=[bf("wspT")])
        S.op("pool", lambda e: e.memset(Vc[:, :, :, 64:65], 1.0), writes=BV)
        S.op("pool", lambda e: e.memset(ycv[1][:], 0.0), writes=[bf("ycv1")])

        for i in range(DBG_T):
            S.handoff(stB, stA)
            for s in range(4):
                n = 4 * i + s
                t0 = n * 128
                hb = hT[n % 2]
                Bh = bf(f"hT{n % 2}")
                if DBG_STOP <= 0:
                    return
                S.dma("sp", lambda e: e.dma_start(out=xs[:], in_=xsrc[t0:t0 + 128, :]), reads=[Xd[n]], writes=[bf("xs")])
                sl, Bsl = new_slot()
                S.op("act", lambda e: e.activation(out=xn[:], in_=xs[:], func=AF.Square, accum_out=sl[:, 0:1]), reads=[bf("xs")], writes=[bf("xn"), Bsl])
                r = rstd_from_ss(sl, Bsl, 1, D)
                S.op("dve", lambda e: e.tensor_scalar(out=xn[:], in0=xs[:], scalar1=r, scalar2=None, op0=ALU.mult), reads=[bf("xs"), Bsl], writes=[bf("xn")])
                transpose_to(lambda c: xn[:, c * 128:(c + 1) * 128], 8, 128, bf("xn"), hb[:], Bh, gcol0=g0 + 0)
                if DBG_STOP <= 1:
                    return
                cg = [(0, 512), (512, 1024), (1024, 1536), (1536, INW)]
                for k in range(8):
                    for q, (c0, c1) in enumerate(cg):
                        S.op("pe", lambda e, k=k, q=q, c0=c0, c1=c1: e.matmul(bk[q][:, 0:c1 - c0], lhsT=hb[:, k, :], rhs=w_in[:, k, c0:c1], start=(k == 0), stop=(k == 7)),
                             reads=[Bh, bf(f"w_in{k}")], writes=[Bbk[q]], inc=(k == 7 and q == 3))
                for q, (c0, c1) in enumerate(cg):
                    if q % 2 == 0:
                        S.op("act", lambda e, q=q, c0=c0, c1=c1: e.activation(out=z_sb[:, c0:c1], in_=bk[q][:, 0:c1 - c0], func=AF.Copy), reads=[Bbk[q]], writes=[bf(f"z{q}")])
                    else:
                        S.op("dve", lambda e, q=q, c0=c0, c1=c1: e.tensor_copy(out=z_sb[:, c0:c1], in_=bk[q][:, 0:c1 - c0]), reads=[Bbk[q]], writes=[bf(f"z{q}")])
                if DBG_STOP <= 2:
                    return
                slq, Bq = new_slot()
                slk, Bk_ = new_slot()
                S.op("act", lambda e: e.activation(out=cqkv[:, 0:384], in_=z_sb[:, 0:384], func=AF.Square, accum_out=slq[:, 0:1]), reads=[bf("z0")], writes=[bf("cqkv"), Bq])
                S.op("act", lambda e: e.activation(out=cqkv[:, 384:640], in_=z_sb[:, 384:640], func=AF.Square, accum_out=slk[:, 0:1]), reads=[bf("z0"), bf("z1")], writes=[bf("cqkv"), Bk_])
                rq = rstd_from_ss(slq, Bq, 1, 384)
                rk = rstd_from_ss(slk, Bk_, 1, 256)
                S.op("dve", lambda e: e.tensor_scalar(out=cqkv[:, 0:384], in0=z_sb[:, 0:384], scalar1=rq, scalar2=None, op0=ALU.mult), reads=[bf("z0"), Bq], writes=[bf("cqkv")])
                S.op("dve", lambda e: e.tensor_scalar(out=cqkv[:, 384:640], in0=z_sb[:, 384:640], scalar1=rk, scalar2=None, op0=ALU.mult), reads=[bf("z0"), bf("z1"), Bk_], writes=[bf("cqkv")])
                transpose_to(lambda c: cqkv[:, c * 128:(c + 1) * 128], 5, 128, bf("cqkv"), cqT[:], bf("cqT"), gcol0=g0 + 24)
                yc, yp = ycv[n % 2], ycv[(n + 1) % 2]
                Byc, Byp = bf(f"ycv{n % 2}"), bf(f"ycv{(n + 1) % 2}")
                S.op("pool", lambda e: e.tensor_tensor(out=yc[:], in0=z_sb[:, 1440:1696], in1=z_sb[:, 1696:1952], op=ALU.mult), reads=[bf("z2"), bf("z3")], writes=[Byc])
                S.dma("sp", lambda e: e.dma_start(out=ysh1[1:128, :], in_=yc[0:127, :]), reads=[Byc], writes=[bf("ysh1")])
                S.dma("sp", lambda e: e.dma_start(out=ysh1[0:1, :], in_=yp[127:128, :]), reads=[Byp], writes=[bf("ysh1")])
                S.dma("sp", lambda e: e.dma_start(out=ysh2[2:128, :], in_=yc[0:126, :]), reads=[Byc], writes=[bf("ysh2")])
                S.dma("sp", lambda e: e.dma_start(out=ysh2[0:2, :], in_=yp[126:128, :]), reads=[Byp], writes=[bf("ysh2")])
                S.op("pool", lambda e: e.tensor_tensor(out=acc[:], in0=yc[:], in1=convw[:, 512:768], op=ALU.mult), reads=[Byc, bf("convw")], writes=[bf("acc")])
                S.op("pool", lambda e: e.tensor_tensor(out=ysh1[:], in0=ysh1[:], in1=convw[:, 256:512], op=ALU.mult), reads=[bf("ysh1"), bf("convw")], writes=[bf("ysh1")])
                S.op("pool", lambda e: e.tensor_tensor(out=acc[:], in0=acc[:], in1=ysh1[:], op=ALU.add), reads=[bf("acc"), bf("ysh1")], writes=[bf("acc")])
                S.op("pool", lambda e: e.tensor_tensor(out=ysh2[:], in0=ysh2[:], in1=convw[:, 0:256], op=ALU.mult), reads=[bf("ysh2"), bf("convw")], writes=[bf("ysh2")])
                S.op("pool", lambda e: e.tensor_tensor(out=acc[:], in0=acc[:], in1=ysh2[:], op=ALU.add), reads=[bf("acc"), bf("ysh2")], writes=[bf("acc")])
                S.op("pool", lambda e: e.tensor_tensor(out=acc[:], in0=acc[:], in1=z_sb[:, 1184:1440], op=ALU.mult), reads=[bf("acc"), bf("z2")], writes=[bf("acc")])
                if DBG_STOP <= 3:
                    return
                for k in range(3):
                    S.op("pe", lambda e, k=k: e.matmul(bk[4][:, 0:480], lhsT=cqT[:, k, :], rhs=w_uq[:, k, 0:480], start=(k == 0), stop=(k == 2)),
                         reads=[bf("cqT"), bf("w_uq")], writes=[Bbk[4]], inc=False)
                    S.op("pe", lambda e, k=k: e.matmul(bk[5][:, 0:288], lhsT=cqT[:, k, :], rhs=w_uq[:, k, 480:768], start=(k == 0), stop=(k == 2)),
                         reads=[bf("cqT"), bf("w_uq")], writes=[Bbk[5]], inc=(k == 2))
                for k in range(2):
                    S.op("pe", lambda e, k=k: e.matmul(bk[0][:, :], lhsT=cqT[:, 3 + k, :], rhs=w_ukv[:, k, 0:512], start=(k == 0), stop=(k == 1)),
                         reads=[bf("cqT"), bf("w_ukv")], writes=[Bbk[0]], inc=False)
                    S.op("pe", lambda e, k=k: e.matmul(bk[1][:, :], lhsT=cqT[:, 3 + k, :], rhs=w_ukv[:, k, 512:1024], start=(k == 0), stop=(k == 1)),
                         reads=[bf("cqT"), bf("w_ukv")], writes=[Bbk[1]], inc=(k == 1))
                if DBG_STOP <= 3.1:
                    return
                q4 = bk[4][:, 0:480].rearrange("p (h c) -> p h c", c=96)
                q5 = bk[5][:, 0:288].rearrange("p (h c) -> p h c", c=96)
                k0 = bk[0][:, :].rearrange("p (h c) -> p h c", c=128)
                k1 = bk[1][:, :].rearrange("p (h c) -> p h c", c=128)
                S.op("act", lambda e: e.activation(out=q_tm[:, 0:5, 0:64], in_=q4[:, :, 0:64], func=AF.Copy), reads=[Bbk[4]], writes=[bf("q_tm")])
                S.op("act", lambda e: e.activation(out=q_tm[:, 5:8, 0:64], in_=q5[:, :, 0:64], func=AF.Copy), reads=[Bbk[5]], writes=[bf("q_tm")])
                S.op("dve", lambda e: e.tensor_copy(out=rp[:, 0:5, :], in_=q4[:, :, 64:96]), reads=[Bbk[4]], writes=[bf("rp")])
                S.op("dve", lambda e: e.tensor_copy(out=rp[:, 5:8, :], in_=q5[:, :, 64:96]), reads=[Bbk[5]], writes=[bf("rp")])
                S.op("dve", lambda e: e.tensor_copy(out=rp[:, 8, :], in_=z_sb[:, 640:672]), reads=[bf("z1")], writes=[bf("rp")])
                if DBG_STOP <= 3.2:
                    return
                cs = cosT[:, n, :].unsqueeze(1).broadcast_to([128, 9, 16])
                sn = sinT[:, n, :].unsqueeze(1).broadcast_to([128, 9, 16])
                S.op("dve", lambda e: e.tensor_tensor(out=rt[0][:], in0=rp[:, :, 0:16], in1=cs, op=ALU.mult), reads=[bf("rp"), bf("cos")], writes=[bf("rt0")])
                S.op("dve", lambda e: e.tensor_tensor(out=rt[1][:], in0=rp[:, :, 16:32], in1=sn, op=ALU.mult), reads=[bf("rp"), bf("sin")], writes=[bf("rt1")])
                S.op("dve", lambda e: e.tensor_tensor(out=rt[2][:], in0=rp[:, :, 16:32], in1=cs, op=ALU.mult), reads=[bf("rp"), bf("cos")], writes=[bf("rt2")])
                S.op("dve", lambda e: e.tensor_tensor(out=rt[3][:], in0=rp[:, :, 0:16], in1=sn, op=ALU.mult), reads=[bf("rp"), bf("sin")], writes=[bf("rt3")])
                S.op("dve", lambda e: e.tensor_tensor(out=rr[:, :, 0:16], in0=rt[0][:], in1=rt[1][:], op=ALU.subtract), reads=[bf("rt0"), bf("rt1")], writes=[bf("rr")])
                S.op("dve", lambda e: e.tensor_tensor(out=rr[:, :, 16:32], in0=rt[2][:], in1=rt[3][:], op=ALU.add), reads=[bf("rt2"), bf("rt3")], writes=[bf("rr")])
                S.op("dve", lambda e: e.tensor_copy(out=q_tm[:, :, 64:96], in_=rr[:, 0:8, :]), reads=[bf("rr")], writes=[bf("q_tm")])
                S.op("dve", lambda e: e.tensor_copy(out=k_tm[:, :, 64:96], in_=rr[:, 8:9, :].broadcast_to([128, 8, 32])), reads=[bf("rr")], writes=[bf("k_tm")])
                if DBG_STOP <= 3.3:
                    return
                S.op("act", lambda e: e.activation(out=k_tm[:, 0:4, 0:64], in_=k0[:, :, 0:64], func=AF.Copy), reads=[Bbk[0]], writes=[bf("k_tm")])
                S.op("dve", lambda e: e.tensor_copy(out=k_tm[:, 4:8, 0:64], in_=k1[:, :, 0:64]), reads=[Bbk[1]], writes=[bf("k_tm")])
                if DBG_STOP <= 3.4:
                    return
                S.op("act", lambda e: e.activation(out=Vc[:, n, 0:4, 0:64], in_=k0[:, :, 64:128], func=AF.Copy), reads=[Bbk[0]], writes=[BV[n]])
                S.op("dve", lambda e: e.tensor_copy(out=Vc[:, n, 4:8, 0:64], in_=k1[:, :, 64:128]), reads=[Bbk[1]], writes=[BV[n]])
                if DBG_STOP <= 4:
                    return
                transpose_to(lambda h: q_tm[:, h, :], 8, 96, bf("q_tm"), qT[0:96, :, s * 128:(s + 1) * 128], bf("qT"), eng="act")
                transpose_to(lambda h: k_tm[:, h, :], 8, 96, bf("k_tm"), KT[0:96, :, t0:t0 + 128], BKT[n], eng="dve")
                if DBG_STOP <= 5:
                    return
                S.op("act", lambda e: e.activation(out=uv[:], in_=z_sb[:, 672:1184], func=AF.Gelu_apprx_tanh), reads=[bf("z1"), bf("z2")], writes=[bf("uv")])
                v3 = uv[:, 256:512].rearrange("p (g e) -> p g e", g=4)
                Bsv = bf("stv")
                S.op("dve", lambda e: e.tensor_reduce(out=stv[:, 0:4], in_=v3, axis=AX.X, op=ALU.add), reads=[bf("uv")], writes=[Bsv])
                S.op("pool", lambda e: e.tensor_tensor(out=sq[:], in0=uv[:, 256:512], in1=uv[:, 256:512], op=ALU.mult), reads=[bf("uv")], writes=[bf("sq")])
                S.op("dve", lambda e: e.tensor_reduce(out=stv[:, 4:8], in_=sq[:].rearrange("p (g e) -> p g e", g=4), axis=AX.X, op=ALU.add), reads=[bf("sq")], writes=[Bsv])
                S.op("pool", lambda e: e.tensor_scalar(out=stv[:, 8:12], in0=stv[:, 0:4], scalar1=1.0 / 64, scalar2=None, op0=ALU.mult), reads=[Bsv], writes=[Bsv])
                S.op("pool", lambda e: e.tensor_tensor(out=stv[:, 12:16], in0=stv[:, 8:12], in1=stv[:, 8:12], op=ALU.mult), reads=[Bsv], writes=[Bsv])
                S.op("pool", lambda e: e.tensor_scalar(out=stv[:, 24:28], in0=stv[:, 4:8], scalar1=1.0 / 64, scalar2=EPS, op0=ALU.mult, op1=ALU.add), reads=[Bsv], writes=[Bsv])
                S.op("pool", lambda e: e.tensor_tensor(out=stv[:, 16:20], in0=stv[:, 24:28], in1=stv[:, 12:16], op=ALU.subtract), reads=[Bsv], writes=[Bsv])
                S.op("pool", lambda e: e.tensor_tensor(out=stv[:, 20:24], in0=stv[:, 16:20], in1=mhalf[:, 0:4], op=ALU.pow), reads=[Bsv, bf("mhalf")], writes=[Bsv])
                vn3 = vn32[:].rearrange("p (g e) -> p g e", g=4)
                S.op("dve", lambda e: e.tensor_tensor(out=vn3, in0=v3, in1=stv[:, 8:12].unsqueeze(2).broadcast_to([128, 4, 64]), op=ALU.subtract), reads=[bf("uv"), Bsv], writes=[bf("vn32")])
                S.op("dve", lambda e: e.tensor_tensor(out=vn3, in0=vn3, in1=stv[:, 20:24].unsqueeze(2).broadcast_to([128, 4, 64]), op=ALU.mult), reads=[bf("vn32"), Bsv], writes=[bf("vn32")])
                S.op("pool", lambda e: e.tensor_tensor(out=vn32[:], in0=vn32[:], in1=sgln[:, 0:256], op=ALU.mult), reads=[bf("vn32"), bf("sgln")], writes=[bf("vn32")])
                S.op("pool", lambda e: e.tensor_tensor(out=vn[:], in0=vn32[:], in1=sgln[:, 256:512], op=ALU.add), reads=[bf("vn32"), bf("sgln")], writes=[bf("vn")])
                for g in range(4):
                    S.op("pe", lambda e, g=g: e.matmul(bk[2][:, g * 64:(g + 1) * 64], lhsT=wspT[:, g, :], rhs=vn[:, g * 64:(g + 1) * 64], start=True, stop=True, skip_group_check=True),
                         reads=[bf("wspT"), bf("vn")], writes=[Bbk[2]], inc=(g == 3))
                for g in range(4):
                    S.op("dve", lambda e, g=g: e.scalar_tensor_tensor(out=sq[:, g * 64:(g + 1) * 64], in0=bk[2][:, g * 64:(g + 1) * 64], scalar=gT[:, g0 + 29 + g:g0 + 30 + g],
                                                                      in1=uv[:, g * 64:(g + 1) * 64], op0=ALU.add, op1=ALU.mult),
                         reads=[Bbk[2], bf("gT"), bf("uv")], writes=[bf("sq")])
                slb, Bb_ = new_slot()
                S.op("act", lambda e: e.activation(out=mix_tm[:, s, 512:768], in_=sq[:], func=AF.Square, accum_out=slb[:, 0:1]), reads=[bf("sq")], writes=[bf(f"mix{s}"), Bb_])
                rb = rstd_from_ss(slb, Bb_, 1, 256)
                S.op("dve", lambda e: e.tensor_scalar(out=mix_tm[:, s, 512:768], in0=sq[:], scalar1=rb, scalar2=None, op0=ALU.mult), reads=[bf("sq"), Bb_], writes=[bf(f"mix{s}")])
                if DBG_STOP <= 6:
                    return
                slc, Bc_ = new_slot()
                S.op("act", lambda e: e.activation(out=mix_tm[:, s, 768:1024], in_=acc[:], func=AF.Square, accum_out=slc[:, 0:1]), reads=[bf("acc")], writes=[bf(f"mix{s}"), Bc_])
                rc = rstd_from_ss(slc, Bc_, 1, 256)
                S.op("dve", lambda e: e.tensor_scalar(out=mix_tm[:, s, 768:1024], in0=acc[:], scalar1=rc, scalar2=None, op0=ALU.mult), reads=[bf("acc"), Bc_], writes=[bf(f"mix{s}")])

            if DBG_STAGE in ('A', 'A0'):
                continue
            S.handoff(stA, stB)
            nkb = 4 * i + 4
            LA = 3

            def att_front(h, kb):
                j0 = max(0, kb - 4 * i)
                c0 = j0 * 128
                r = rot[0] % 4
                rot[0] += 1
                sbk, Bs = bk[r], Bbk[r]
                S.op("pe", lambda e: e.matmul(sbk[:, c0:512], lhsT=KT[0:96, h, kb * 128:(kb + 1) * 128], rhs=qT[0:96, h, c0:512], start=True, stop=True),
                     reads=[BKT[kb], bf("qT")], writes=[Bs])
                S.op("act", lambda e: e.activation(out=PT[r][:, c0:512], in_=sbk[:, c0:512], func=AF.Exp, scale=SCALE), reads=[Bs], writes=[bf(f"PT{r}")])
                if kb >= 4 * i:
                    S.op("pool", lambda e: e.tensor_tensor(out=PT[r][:, c0:c0 + 128], in0=PT[r][:, c0:c0 + 128], in1=tri[:], op=ALU.mult),
                         reads=[bf(f"PT{r}"), bf("tri")], writes=[bf(f"PT{r}")])
                return r

            def att_back(h, kb, r):
                j0 = max(0, kb - 4 * i)
                O = bk[4 + h % 2]
                BO = Bbk[4 + h % 2]
                O3 = O[:, :].rearrange("p (j c) -> p j c", j=4)
                for j in range(j0, 4):
                    S.op("pe", lambda e, j=j: e.matmul(O3[:, j, 0:65], lhsT=PT[r][:, j * 128:(j + 1) * 128], rhs=Vc[:, kb, h, :],
                                                       start=(kb == 0 and j == 0), stop=(kb == 4 * i + j), skip_group_check=True),
                         reads=[bf(f"PT{r}"), BV[kb]], writes=[BO], inc=(j == 3))
                if kb == nkb - 1:
                    S.op("dve", lambda e: e.reciprocal(out=rinv[:].unsqueeze(2), in_=O3[:, :, 64:65]), reads=[BO], writes=[bf("rinv")])
                    S.op("dve", lambda e: e.tensor_tensor(out=ya[:, :, h * 64:(h + 1) * 64], in0=O3[:, :, 0:64], in1=rinv[:].unsqueeze(2).broadcast_to([128, 4, 64]), op=ALU.mult),
                         reads=[BO, bf("rinv")], writes=[bf("ya")])

            inflight = []
            for h in range(H):
                for kb in range(nkb):
                    inflight.append((h, kb, att_front(h, kb)))
                    if len(inflight) > LA:
                        att_back(*inflight.pop(0))
            while inflight:
                att_back(*inflight.pop(0))
            for j in range(4):
                sla, Ba_ = new_slot()
                S.op("act", lambda e, j=j: e.activation(out=mix_tm[:, j, 0:512], in_=ya[:, j, :], func=AF.Square, accum_out=sla[:, 0:1]), reads=[bf("ya")], writes=[bf(f"mix{j}"), Ba_])
                ra = rstd_from_ss(sla, Ba_, 1, 512)
                S.op("dve", lambda e, j=j: e.tensor_scalar(out=mix_tm[:, j, 0:512], in0=ya[:, j, :], scalar1=ra, scalar2=None, op0=ALU.mult), reads=[bf("ya"), Ba_], writes=[bf(f"mix{j}")])
            S.dma("sp", lambda e: e.dma_start(out=mix_d[i * 512:(i + 1) * 512, :].rearrange("(j p) d -> p j d", p=128), in_=mix_tm[:]),
                  reads=[bf("mix0"), bf("mix1"), bf("mix2"), bf("mix3")], writes=[Md[i]])

    def ffn_pass(l):
        xsrc = x_d if l == 0 else out_d
        g0 = l * GW
        S.barrier()
        for k in range(8):
            S.dma("pool", lambda e, k=k: e.dma_start(out=w_out[:, k, :], in_=w_out_d[l, k * 128:(k + 1) * 128, :]), writes=[bf("w_out")])
        S.dma("sp", lambda e: e.dma_start(out=postg_m[:], in_=bc_d[l, :, 0:1024]), writes=[bf("postg_m")])
        S.dma("sp", lambda e: e.dma_start(out=postg_f[:], in_=bc_d[l, :, 1024:2048]), writes=[bf("postg_f")])
        for k in range(8):
            S.dma("pool", lambda e, k=k: e.dma_start(out=w_gate[:, k, :], in_=w_gate_d[l, k * 128:(k + 1) * 128, :]), writes=[bf(f"w_gate{k}")])
            S.dma("pool", lambda e, k=k: e.dma_start(out=w_up[:, k, :], in_=w_up_d[l, k * 128:(k + 1) * 128, :]), writes=[bf(f"w_up{k}")])
        for c in range(NFF):
            S.dma("pool", lambda e, c=c: e.dma_start(out=w_down[:, c, :], in_=w_down_d[l, c * 128:(c + 1) * 128, :]), writes=[bf(f"w_down{c}")])

        def post_norm_residual(postg, Bpostg, ba, bb, xs_, Bxs, t_, Bt):
            sl, Bsl = new_slot()
            S.op("act", lambda e: e.activation(out=t_[:, 0:512], in_=bk[ba][:, :], func=AF.Square, accum_out=sl[:, 0:1]), reads=[Bbk[ba]], writes=[Bt, Bsl])
            S.op("act", lambda e: e.activation(out=t_[:, 512:1024], in_=bk[bb][:, :], func=AF.Square, accum_out=sl[:, 1:2]), reads=[Bbk[bb]], writes=[Bt, Bsl])
            r = rstd_from_ss(sl, Bsl, 2, D)
            S.op("dve", lambda e: e.scalar_tensor_tensor(out=t_[:, 0:512], in0=bk[ba][:, :], scalar=r, in1=postg[:, 0:512], op0=ALU.mult, op1=ALU.mult),
                 reads=[Bbk[ba], Bsl, Bpostg], writes=[Bt])
            S.op("dve", lambda e: e.scalar_tensor_tensor(out=t_[:, 512:1024], in0=bk[bb][:, :], scalar=r, in1=postg[:, 512:1024], op0=ALU.mult, op1=ALU.mult),
                 reads=[Bbk[bb], Bsl, Bpostg], writes=[Bt])
            S.op("pool", lambda e: e.tensor_tensor(out=xs_[:], in0=xs_[:], in1=t_[:], op=ALU.add), reads=[Bxs, Bt], writes=[Bxs])

        aT_bufs = [bf(f"aT{c}") for c in range(NFF)]
        set1_bufs = [bf("xs_f2"), bf("t_f2"), bf("xn_b0"), bf("xn_b1"), bf("mixT2")]
        sets = [dict(xs=xs_f, Bxs=bf("xs_f"), t=t_f, Bt=bf("t_f"), m0=xn_f[0], Bm0=bf("xn_f0"), m1=xn_f[1], Bm1=bf("xn_f1"), mixT=mixT, BmixT=bf("mixT"), ba=4, bb=5),
                dict(xs=xs_f2, Bxs=bf("xs_f2"), t=t_f2, Bt=bf("t_f2"), m0=xn_b[0], Bm0=bf("xn_b0"), m1=xn_b[1], Bm1=bf("xn_b1"), mixT=mixT2, BmixT=bf("mixT2"), ba=2, bb=3)]
        for i in range(DBG_T if DBG_STAGE == 'full' else 0):
            S.handoff(aT_bufs, set1_bufs)
            for s in range(4):
                n = 4 * i + s
                t0 = n * 128
                Q = sets[s % 2]
                m0, Bm0, m1, Bm1, xs_, Bxs, t_, Bt, mT, BmT, ba, bb = Q["m0"], Q["Bm0"], Q["m1"], Q["Bm1"], Q["xs"], Q["Bxs"], Q["t"], Q["Bt"], Q["mixT"], Q["BmixT"], Q["ba"], Q["bb"]
                S.dma("sp", lambda e: e.dma_start(out=m0[:], in_=mix_d[t0:t0 + 128, :]), reads=[Md[i]], writes=[Bm0])
                S.dma("sp", lambda e: e.dma_start(out=xs_[:], in_=xsrc[t0:t0 + 128, :]), reads=[Xd[n]], writes=[Bxs])
                transpose_to(lambda c: m0[:, c * 128:(c + 1) * 128], 8, 128, Bm0, mT[:], BmT, gcol0=g0 + 16)
                for k in range(8):
                    S.op("pe", lambda e, k=k: e.matmul(bk[ba][:, :], lhsT=mT[:, k, :], rhs=w_out[:, k, 0:512], start=(k == 0), stop=(k == 7)),
                         reads=[BmT, bf("w_out")], writes=[Bbk[ba]], inc=False)
                    S.op("pe", lambda e, k=k: e.matmul(bk[bb][:, :], lhsT=mT[:, k, :], rhs=w_out[:, k, 512:1024], start=(k == 0), stop=(k == 7)),
                         reads=[BmT, bf("w_out")], writes=[Bbk[bb]], inc=(k == 7))
                post_norm_residual(postg_m, bf("postg_m"), ba, bb, xs_, Bxs, t_, Bt)
                S.dma("sp", lambda e: e.dma_start(out=out_d[t0:t0 + 128, :], in_=xs_[:]), reads=[Bxs], writes=[Xd[n]])
                sl, Bsl = new_slot()
                S.op("act", lambda e: e.activation(out=m1[:], in_=xs_[:], func=AF.Square, accum_out=sl[:, 0:1]), reads=[Bxs], writes=[Bm1, Bsl])
                r = rstd_from_ss(sl, Bsl, 1, D)
                S.op("dve", lambda e: e.tensor_scalar(out=m1[:], in0=xs_[:], scalar1=r, scalar2=None, op0=ALU.mult), reads=[Bxs, Bsl], writes=[Bm1])
                transpose_to(lambda c: m1[:, c * 128:(c + 1) * 128], 8, 128, Bm1, h2T[:, :, s * 128:(s + 1) * 128], bf("h2T"), gcol0=g0 + 8)
            S.handoff(set1_bufs, aT_bufs)
            for c in range(NFF):
                gb, Bg = bk[c % 2], Bbk[c % 2]
                ub, Bu = bk[2 + c % 2], Bbk[2 + c % 2]
                for k in range(8):
                    S.op("pe", lambda e, k=k, c=c, gb=gb: e.matmul(gb[:, :], lhsT=w_gate[:, k, c * 128:(c + 1) * 128], rhs=h2T[:, k, :], start=(k == 0), stop=(k == 7)),
                         reads=[bf(f"w_gate{k}"), bf("h2T")], writes=[Bg], inc=(k == 7))
                for k in range(8):
                    S.op("pe", lambda e, k=k, c=c, ub=ub: e.matmul(ub[:, :], lhsT=w_up[:, k, c * 128:(c + 1) * 128], rhs=h2T[:, k, :], start=(k == 0), stop=(k == 7)),
                         reads=[bf(f"w_up{k}"), bf("h2T")], writes=[Bu], inc=(k == 7))
                sg, Bsg = sg_f[c % 2], bf(f"xn_f{c % 2}")
                S.op("act", lambda e, gb=gb, sg=sg: e.activation(out=sg[:], in_=gb[:, :], func=AF.Silu), reads=[Bg], writes=[Bsg])
                S.op("dve", lambda e, c=c, ub=ub, sg=sg: e.tensor_tensor(out=aT[:, c, :], in0=ub[:, :], in1=sg[:], op=ALU.mult), reads=[Bu, Bsg], writes=[bf(f"aT{c}")])
            for s in range(4):
                n = 4 * i + s
                t0 = n * 128
                ba, bb = [(4, 5), (0, 1), (2, 3), (4, 5)][s]
                for c in range(NFF):
                    S.op("pe", lambda e, c=c: e.matmul(bk[ba][:, :], lhsT=aT[:, c, s * 128:(s + 1) * 128], rhs=w_down[:, c, 0:512], start=(c == 0), stop=(c == NFF - 1)),
                         reads=[bf(f"aT{c}"), bf(f"w_down{c}")], writes=[Bbk[ba]], inc=False)
                    S.op("pe", lambda e, c=c: e.matmul(bk[bb][:, :], lhsT=aT[:, c, s * 128:(s + 1) * 128], rhs=w_down[:, c, 512:1024], start=(c == 0), stop=(c == NFF - 1)),
                         reads=[bf(f"aT{c}"), bf(f"w_down{c}")], writes=[Bbk[bb]], inc=(c == NFF - 1))
                S.dma("sp", lambda e: e.dma_start(out=xs_f[:], in_=out_d[t0:t0 + 128, :]), reads=[Xd[n]], writes=[bf("xs_f")])
                post_norm_residual(postg_f, bf("postg_f"), ba, bb, xs_f, bf("xs_f"), t_f, bf("t_f"))
                S.dma("sp", lambda e: e.dma_start(out=out_d[t0:t0 + 128, :], in_=xs_f[:]), reads=[bf("xs_f")], writes=[Xd[n]])

    for l in range(L if DBG_STAGE != 'P' else 0):
        mixer_pass(l)
        if DBG_STAGE not in ('A0', 'B0'):
            ffn_pass(l)
    S.finish("sp", Xd)
    S.barrier()
    return nc


def _layouts(p):
    L = DEPTH
    gT = np.zeros((128, L * GW), np.float32)
    for l in range(L):
        o = l * GW
        gT[:, o + 0:o + 8] = p["mix_pre_g"][l].reshape(8, 128).T
        gT[:, o + 8:o + 16] = p["ffn_pre_g"][l].reshape(8, 128).T
        gT[:, o + 16:o + 24] = p["out_norm_g"][l].reshape(8, 128).T
        gT[:, o + 24:o + 27] = p["q_norm_g"][l].reshape(3, 128).T
        gT[:, o + 27:o + 29] = p["kv_norm_g"][l].reshape(2, 128).T
        gT[:, o + 29:o + 33] = p["b_sp"][l].T
    bc = np.zeros((L, 128, BCW), np.float32)
    bc[:, :, 0:1024] = p["mix_post_g"][:, None, :]
    bc[:, :, 1024:2048] = p["ffn_post_g"][:, None, :]
    bc[:, :, 2048:2304] = p["sg_ln_g"][:, None, :]
    bc[:, :, 2304:2560] = p["sg_ln_b"][:, None, :]
    bc[:, :, 2560:3328] = p["conv_w"].reshape(L, 1, 768)
    wspT = np.ascontiguousarray(np.transpose(p["w_sp"], (0, 3, 1, 2))).reshape(L, 128, 512)
    return gT, bc, wspT


_INVF = (np.float32(1.0) / (np.float32(10000.0) ** (np.arange(16, dtype=np.float32) / np.float32(16)))).astype(np.float32)


def kernel(x, positions, mix_pre_g, mix_post_g, ffn_pre_g, ffn_post_g, w_in, q_norm_g, w_uq, kv_norm_g, w_ukv,
           sg_ln_g, sg_ln_b, w_sp, b_sp, conv_w, out_norm_g, w_out, w_gate, w_up, w_down, _depth=DEPTH, _cores=8):
    p = dict(mix_pre_g=mix_pre_g, mix_post_g=mix_post_g, ffn_pre_g=ffn_pre_g, ffn_post_g=ffn_post_g, q_norm_g=q_norm_g,
             kv_norm_g=kv_norm_g, sg_ln_g=sg_ln_g, sg_ln_b=sg_ln_b, w_sp=w_sp, b_sp=b_sp, conv_w=conv_w, out_norm_g=out_norm_g)
    p = {k: np.asarray(v, np.float32) for k, v in p.items()}
    gT, bc, wspT = _layouts(p)
    f = lambda a: np.ascontiguousarray(np.asarray(a, np.float32))
    shared = {"invf": np.ascontiguousarray(np.broadcast_to(_INVF[None, :], (128, 16))), "w_in": f(w_in), "w_uq": f(w_uq), "w_ukv": f(w_ukv),
              "w_out": f(w_out), "w_gate": f(w_gate), "w_up": f(w_up), "w_down": f(w_down), "gT": gT, "bc": bc, "wspT": wspT}
    x = np.asarray(x, np.float32)
    positions = np.asarray(positions, np.int32)
    nc = build_program(_depth)
    in_maps = []
    for b in range(_cores):
        m = dict(shared)
        m["x"] = np.ascontiguousarray(x[b])
        m["pos"] = np.ascontiguousarray(positions[b].reshape(NSUB, 128).T)
        in_maps.append(m)
    res = run_bass_kernel_spmd(nc, in_maps, core_ids=list(range(_cores)))
    return np.stack([np.asarray(r["out"], np.float32) for r in res.results], axis=0)
```

```python
import math
import os
import numpy as np
import concourse.bass as bass
import concourse.mybir as mybir
from concourse.bass_utils import run_bass_kernel_spmd

F32 = mybir.dt.float32
BF16 = mybir.dt.bfloat16
I32 = mybir.dt.int32
AF = mybir.ActivationFunctionType
ALU = mybir.AluOpType
AX = mybir.AxisListType

D = 1024
SEQ = 4096
DEPTH = 4
NSUB = SEQ // 128
NTILE = SEQ // 512
INW = 1952
H = 8
DFF = 2816
NFF = DFF // 128
EPS = 1e-6
SCALE = 96.0 ** -0.5
GW = 33
BCW = 3328
ND = 8


class Buf:
    __slots__ = ("name", "w", "r", "psum")

    def __init__(self, name, psum=False):
        self.name = name
        self.w = None
        self.r = {}
        self.psum = psum


class Sched:
    def __init__(self, nc):
        self.nc = nc
        self.E = {"pe": nc.tensor, "act": nc.scalar, "dve": nc.vector, "pool": nc.gpsimd, "sp": nc.sync}
        self.sem = {k: nc.alloc_semaphore("s_" + k) for k in self.E}
        self.cnt = {k: 0 for k in self.E}
        self.seen = {k: {} for k in self.E}
        self.pend = {k: [] for k in self.E}
        self.dsem = {q: [nc.alloc_semaphore(f"d_{q}{i}") for i in range(ND)] for q in ("sp", "pool")}
        self.dcnt = {q: [0] * ND for q in self.dsem}
        self.drr = {q: 0 for q in self.dsem}
        self.rec = None
        self.adepth = 0

    def _rec(self, item):
        if self.adepth > 0:
            if self._open is None:
                self._open = []
                self.rec.append(self._open)
            self._open.append(item)
        else:
            self.rec.append([item])

    def begin_atom(self):
        self.adepth += 1
        if self.adepth == 1:
            self._open = None

    def end_atom(self):
        self.adepth -= 1
        if self.adepth == 0:
            self._open = None

    def record(self, f):
        assert self.rec is None
        self.rec = []
        self._open = None
        f()
        atoms, self.rec = self.rec, None
        return atoms

    @staticmethod
    def interleave(*streams):
        streams = [st for st in streams if st]
        idx = [0] * len(streams)
        out = []
        total = sum(len(st) for st in streams)
        while len(out) < total:
            k = min((idx[j] / len(streams[j]), j) for j in range(len(streams)) if idx[j] < len(streams[j]))[1]
            out.append(streams[k][idx[k]])
            idx[k] += 1
        return out

    def replay(self, atoms):
        assert self.rec is None
        for atom in atoms:
            for kind, args, kw in atom:
                (self.op if kind == "op" else self.dma)(*args, **kw)

    def _collect(self, eng, reads, writes, skip_own_war=True):
        toks = {}
        own = self.sem.get(eng)

        def need(s, v, war=False):
            if s is own and (eng == "pe" or (war and skip_own_war)):
                return
            if toks.get(s, 0) < v:
                toks[s] = v

        for b in reads:
            if b.w is not None:
                need(*b.w)
            if b.psum:
                for s, v in b.r.items():
                    if s is not own:
                        need(s, v)
        for b in writes:
            if b.w is not None:
                need(*b.w)
            for s, v in b.r.items():
                need(s, v, war=True)
        return toks

    def _emit_waits(self, eng, toks):
        e = self.E[eng]
        seen = self.seen[eng]
        for s, v in toks.items():
            if seen.get(s, 0) < v:
                e.wait_ge(s, v)
                seen[s] = v

    def op(self, eng, fn, reads=(), writes=(), inc=True):
        if self.rec is not None:
            self._rec(("op", (eng, fn), dict(reads=tuple(reads), writes=tuple(writes), inc=inc)))
            return None
        self._emit_waits(eng, self._collect(eng, reads, writes))
        ins = fn(self.E[eng])
        self.pend[eng].append((tuple(reads), tuple(writes)))
        if inc:
            self.cnt[eng] += 1
            s = self.sem[eng]
            ins.then_inc(s, 1)
            v = self.cnt[eng]
            for rs, ws in self.pend[eng]:
                for b in rs:
                    b.r[s] = v
                for b in ws:
                    b.w = (s, v)
                    b.r = {}
            self.pend[eng] = []
        return ins

    def dma(self, q, fn, reads=(), writes=()):
        if self.rec is not None:
            self._rec(("dma", (q, fn), dict(reads=tuple(reads), writes=tuple(writes))))
            return None
        assert not self.pend[q]
        i = self.drr[q]
        self.drr[q] = (i + 1) % ND
        s = self.dsem[q][i]
        toks = self._collect(q, reads, writes, skip_own_war=False)
        prev = 16 * self.dcnt[q][i]
        if prev and toks.get(s, 0) < prev:
            toks[s] = prev
        self._emit_waits(q, toks)
        ins = fn(self.E[q])
        ins.then_inc(s, 16)
        self.dcnt[q][i] += 1
        v = 16 * self.dcnt[q][i]
        for b in reads:
            b.r[s] = v
        for b in writes:
            b.w = (s, v)
            b.r = {}
        return ins

    def handoff(self, src, dst):
        for d in dst:
            for b in src:
                if b.w is not None:
                    s, v = b.w
                    if d.r.get(s, 0) < v:
                        d.r[s] = v
                for s, v in b.r.items():
                    if d.r.get(s, 0) < v:
                        d.r[s] = v

    def barrier(self):
        for k in self.E:
            assert not self.pend[k]
        toks = {}
        for k in self.E:
            if self.cnt[k]:
                toks[self.sem[k]] = self.cnt[k]
        for q in self.dsem:
            for i in range(ND):
                if self.dcnt[q][i]:
                    toks[self.dsem[q][i]] = 16 * self.dcnt[q][i]
        for k in self.E:
            t = {s: v for s, v in toks.items() if s is not self.sem[k]}
            self._emit_waits(k, t)

    def finish(self, eng, bufs):
        toks = {}
        for b in bufs:
            if b.w is not None:
                s, v = b.w
                toks[s] = max(toks.get(s, 0), v)
            for s, v in b.r.items():
                toks[s] = max(toks.get(s, 0), v)
        self._emit_waits(eng, toks)


def build_program(L=DEPTH):
    DBG_T = int(os.environ.get('KDBG_TILES', NTILE))
    DBG_STAGE = os.environ.get('KDBG_STAGE', 'full')
    DBG_STOP = float(os.environ.get('KDBG_STOP', 99))
    nc = bass.Bass("TRN2", target_bir_lowering=False)
    S = Sched(nc)

    def dram(name, shape, dt, kind):
        return nc.dram_tensor(name, shape, dt, kind=kind).ap()

    x_d = dram("x", [SEQ, D], F32, "ExternalInput")
    pos_d = dram("pos", [128, NSUB], I32, "ExternalInput")
    invf_d = dram("invf", [128, 16], F32, "ExternalInput")
    w_in_d = dram("w_in", [DEPTH, D, INW], F32, "ExternalInput")
    w_uq_d = dram("w_uq", [DEPTH, 384, 768], F32, "ExternalInput")
    w_ukv_d = dram("w_ukv", [DEPTH, 256, 1024], F32, "ExternalInput")
    w_out_d = dram("w_out", [DEPTH, D, D], F32, "ExternalInput")
    w_gate_d = dram("w_gate", [DEPTH, D, DFF], F32, "ExternalInput")
    w_up_d = dram("w_up", [DEPTH, D, DFF], F32, "ExternalInput")
    w_down_d = dram("w_down", [DEPTH, DFF, D], F32, "ExternalInput")
    gT_d = dram("gT", [128, DEPTH * GW], F32, "ExternalInput")
    bc_d = dram("bc", [DEPTH, 128, BCW], F32, "ExternalInput")
    wsp_d = dram("wspT", [DEPTH, 128, 512], F32, "ExternalInput")
    out_d = dram("out", [SEQ, D], F32, "ExternalOutput")
    mix_d = dram("mixd", [SEQ, D], BF16, "Internal")

    base = (nc.sbuf_base + 31) // 32 * 32
    top = nc.sbuf_top

    def sb(name, shape, dt, off):
        nb = int(np.prod(shape[1:])) * (2 if dt == BF16 else 4)
        assert off % 32 == 0 and base + off + nb <= top, (name, off, nb, top - base)
        return nc.alloc_sbuf_tensor_at(name, list(shape), dt, offset=base + off)

    ident = sb("ident", [128, 128], BF16, 0)
    tri = sb("tri", [128, 128], BF16, 256)
    mhalf = sb("mhalf", [128, 8], F32, 512)
    cosT = sb("cosT", [128, NSUB, 16], F32, 576)
    sinT = sb("sinT", [128, NSUB, 16], F32, 2624)
    gT = sb("gTs", [128, DEPTH * GW], F32, 4672)
    stt_ = sb("stats", [128, 64], F32, 5216)
    invf = sb("invfs", [128, 16], F32, 5472)
    posi = sb("posi", [128, NSUB], I32, 5536)
    posf = sb("posf", [128, NSUB], F32, 5664)
    stv = sb("stv", [128, 32], F32, 5792)
    CEND = 5920
    BIG = CEND
    MID = BIG + 174080
    assert base + MID + 30720 <= top, (base, MID, top)

    KT = sb("KT", [128, H, SEQ], BF16, BIG + 0)
    Vc = sb("Vc", [128, NSUB, H, 65], BF16, BIG + 65536)
    w_in = sb("w_in_s", [128, 8, INW], BF16, BIG + 98816)
    w_uq = sb("w_uq_s", [128, 3, 768], BF16, BIG + 130048)
    w_ukv = sb("w_ukv_s", [128, 2, 1024], BF16, BIG + 134656)
    qT = sb("qT", [128, H, 512], BF16, BIG + 138752)
    mix_tm = sb("mix_tm", [128, 4, D], BF16, BIG + 146944)
    SA = BIG + 155136
    z_sb = sb("z_sb", [128, INW], F32, SA)
    uv = sb("uv", [128, 512], F32, SA + 7808)
    ysh1 = sb("ysh1", [128, 256], F32, SA + 9856)
    ysh2 = sb("ysh2", [128, 256], F32, SA + 10880)
    acc = sb("acc", [128, 256], F32, SA + 11904)
    sq = sb("sq", [128, 256], F32, SA + 12928)
    vn32 = sb("vn32", [128, 256], F32, SA + 13952)
    cqkv = sb("cqkv", [128, 640], BF16, SA + 14976)
    ycv = [sb("ycv0", [128, 256], F32, SA + 16352), sb("ycv1", [128, 256], F32, SA + 17376)]
    assert SA + 18400 <= MID
    ya = sb("ya", [128, 4, 512], F32, SA)
    PT = [sb(f"PT{r}", [128, 512], BF16, SA + 8192 + 1024 * r) for r in range(4)]
    hT = [sb("hT0", [128, 8, 128], BF16, MID + 0), sb("hT1", [128, 8, 128], BF16, MID + 2048)]
    xs = sb("xs", [128, D], F32, MID + 4096)
    xn = sb("xn", [128, D], BF16, MID + 8192)
    sgln = sb("sgln", [128, 512], F32, MID + 10240)
    convw = sb("convw", [128, 768], F32, MID + 12288)
    wspT = sb("wspTs", [128, 4, 128], BF16, MID + 15360)
    cqT = sb("cqT", [128, 5, 128], BF16, MID + 16384)
    q_tm = sb("q_tm", [128, H, 96], BF16, MID + 17664)
    k_tm = sb("k_tm", [128, H, 96], BF16, MID + 19200)
    rp = sb("rp", [128, 9, 32], F32, MID + 20736)
    rt = [sb(f"rt{i}", [128, 9, 16], F32, MID + 21888 + 576 * i) for i in range(4)]
    rr = sb("rr", [128, 9, 32], F32, MID + 24192)
    vn = sb("vn", [128, 256], BF16, MID + 25344)
    rinv = sb("rinv", [128, 4], F32, MID + 25856)
    w_gate = sb("w_gate_s", [128, 8, DFF], BF16, BIG + 0)
    w_up = sb("w_up_s", [128, 8, DFF], BF16, BIG + 45056)
    w_down = sb("w_down_s", [128, NFF, D], BF16, BIG + 90112)
    w_out = sb("w_out_s", [128, 8, D], BF16, BIG + 135168)
    aT = sb("aT", [128, NFF, 512], BF16, BIG + 151552)
    h2T = sb("h2T", [128, 8, 512], BF16, MID + 0)
    xs_f = sb("xs_f", [128, D], F32, MID + 8192)
    xn_f = [sb("xn_f0", [128, D], BF16, MID + 12288), sb("xn_f1", [128, D], BF16, MID + 14336)]
    sg_f = [sb("sg_f0", [128, 512], F32, MID + 12288), sb("sg_f1", [128, 512], F32, MID + 14336)]
    mixT = sb("mixT", [128, 8, 128], BF16, MID + 16384)
    t_f = sb("t_f", [128, D], F32, MID + 18432)
    postg_m = sb("postg_m", [128, D], F32, MID + 22528)
    postg_f = sb("postg_f", [128, D], F32, MID + 26624)
    AT0 = BIG + 151552
    xs_f2 = sb("xs_f2", [128, D], F32, AT0 + 0)
    t_f2 = sb("t_f2", [128, D], F32, AT0 + 4096)
    xn_b = [sb("xn_b0", [128, D], BF16, AT0 + 8192), sb("xn_b1", [128, D], BF16, AT0 + 10240)]
    mixT2 = sb("mixT2", [128, 8, 128], BF16, AT0 + 12288)

    pT = [nc.alloc_psum_tensor(f"pT{i}", [128, 8, 128], BF16) for i in range(2)]
    bk = [nc.alloc_psum_tensor(f"bk{i}", [128, 512], F32) for i in range(6)]
    BpT = [Buf(f"pT{i}", psum=True) for i in range(2)]
    Bbk = [Buf(f"bk{i}", psum=True) for i in range(6)]
    tcount = [0]

    def next_pT():
        i = tcount[0] % 2
        tcount[0] += 1
        return pT[i], BpT[i]

    B = {}

    def bf(name):
        if name not in B:
            B[name] = Buf(name)
        return B[name]

    Xd = [Buf(f"xd{n}") for n in range(NSUB)]
    Md = [Buf(f"md{i}") for i in range(NTILE)]
    BKT = [Buf(f"kt{n}") for n in range(NSUB)]
    BV = [Buf(f"v{n}") for n in range(NSUB)]

    slot = [0]

    def new_slot():
        k = slot[0] % 16
        slot[0] += 1
        return stt_[:, 4 * k:4 * k + 4], bf(f"slot{k}")

    def rstd_from_ss(sl, Bsl, ncols, Dn):
        if ncols == 2:
            S.op("pool", lambda e: e.tensor_tensor(out=sl[:, 0:1], in0=sl[:, 0:1], in1=sl[:, 1:2], op=ALU.add), reads=[Bsl], writes=[Bsl])
        S.op("pool", lambda e: e.tensor_scalar(out=sl[:, 2:3], in0=sl[:, 0:1], scalar1=1.0 / Dn, scalar2=EPS, op0=ALU.mult, op1=ALU.add), reads=[Bsl], writes=[Bsl])
        S.op("pool", lambda e: e.tensor_tensor(out=sl[:, 3:4], in0=sl[:, 2:3], in1=mhalf[:, 0:1], op=ALU.pow), reads=[Bsl, bf("mhalf")], writes=[Bsl])
        return sl[:, 3:4]

    def transpose_to(src_ap_fn, nchunk, rows, Bsrc, dst_ap, Bdst, gcol0=None, eng="dve"):
        S.begin_atom()
        p, Bp = next_pT()
        for c in range(nchunk):
            S.op("pe", lambda e, c=c: e.transpose(out=p[0:rows, c, :], in_=src_ap_fn(c), identity=ident[:]),
                 reads=[Bsrc, bf("ident")], writes=[Bp], inc=(c == nchunk - 1))
        if gcol0 is None:
            if eng == "act":
                S.op("act", lambda e: e.activation(out=dst_ap, in_=p[0:rows, 0:nchunk, :], func=AF.Copy), reads=[Bp], writes=[Bdst])
            else:
                S.op(eng, lambda e: e.tensor_copy(out=dst_ap, in_=p[0:rows, 0:nchunk, :]), reads=[Bp], writes=[Bdst])
        else:
            g = gT[0:rows, gcol0:gcol0 + nchunk].unsqueeze(2).broadcast_to([rows, nchunk, 128])
            S.op(eng, lambda e: e.tensor_tensor(out=dst_ap, in0=p[0:rows, 0:nchunk, :], in1=g, op=ALU.mult), reads=[Bp, bf("gT")], writes=[Bdst])
        S.end_atom()

    S.op("pool", lambda e: e.memset(ident[:], 1.0), writes=[bf("ident")])
    S.op("pool", lambda e: e.affine_select(out=ident[:], in_=ident[:], pattern=[[-1, 128]], compare_op=ALU.is_equal, fill=0.0, base=0, channel_multiplier=1),
         reads=[bf("ident")], writes=[bf("ident")])
    S.op("pool", lambda e: e.memset(tri[:], 1.0), writes=[bf("tri")])
    S.op("pool", lambda e: e.affine_select(out=tri[:], in_=tri[:], pattern=[[1, 128]], compare_op=ALU.is_ge, fill=0.0, base=0, channel_multiplier=-1),
         reads=[bf("tri")], writes=[bf("tri")])
    S.op("pool", lambda e: e.memset(mhalf[:], -0.5), writes=[bf("mhalf")])
    S.dma("sp", lambda e: e.dma_start(out=gT[:], in_=gT_d), writes=[bf("gT")])
    S.dma("sp", lambda e: e.dma_start(out=invf[:], in_=invf_d), writes=[bf("invf")])
    S.dma("sp", lambda e: e.dma_start(out=posi[:], in_=pos_d), writes=[bf("posi")])
    TWO_PI = 2.0 * math.pi
    C1 = float(np.float32(6.28125))
    C2 = float(np.float32(TWO_PI - 6.28125))
    MAGIC = 12582912.0
    ang = sb("ang", [128, NSUB, 16], F32, BIG + 0)
    kk = sb("kk", [128, NSUB, 16], F32, BIG + 2048)
    t2 = sb("t2p", [128, NSUB, 16], F32, BIG + 4096)
    S.op("dve", lambda e: e.tensor_copy(out=posf[:], in_=posi[:]), reads=[bf("posi")], writes=[bf("posf")])
    S.op("dve", lambda e: e.tensor_tensor(out=ang[:], in0=posf[:].unsqueeze(2).broadcast_to([128, NSUB, 16]),
                                          in1=invf[:].unsqueeze(1).broadcast_to([128, NSUB, 16]), op=ALU.mult),
         reads=[bf("posf"), bf("invf")], writes=[bf("ang")])
    S.op("dve", lambda e: e.tensor_scalar(out=t2[:], in0=ang[:], scalar1=1.0 / TWO_PI, scalar2=MAGIC, op0=ALU.mult, op1=ALU.add), reads=[bf("ang")], writes=[bf("t2")])
    S.op("dve", lambda e: e.tensor_scalar(out=kk[:], in0=t2[:], scalar1=-MAGIC, scalar2=None, op0=ALU.add), reads=[bf("t2")], writes=[bf("kk")])
    S.op("dve", lambda e: e.scalar_tensor_tensor(out=ang[:], in0=kk[:], scalar=-C1, in1=ang[:], op0=ALU.mult, op1=ALU.add), reads=[bf("kk"), bf("ang")], writes=[bf("ang")])
    S.op("dve", lambda e: e.scalar_tensor_tensor(out=ang[:], in0=kk[:], scalar=-C2, in1=ang[:], op0=ALU.mult, op1=ALU.add), reads=[bf("kk"), bf("ang")], writes=[bf("ang")])
    S.op("dve", lambda e: e.tensor_scalar(out=ang[:], in0=ang[:], scalar1=math.pi, scalar2=-math.pi, op0=ALU.min, op1=ALU.max), reads=[bf("ang")], writes=[bf("ang")])
    S.op("act", lambda e: e.activation(out=sinT[:], in_=ang[:], func=AF.Sin), reads=[bf("ang")], writes=[bf("sin")])
    S.op("dve", lambda e: e.tensor_scalar(out=t2[:], in0=ang[:], scalar1=-1.0, scalar2=None, op0=ALU.mult), reads=[bf("ang")], writes=[bf("t2")])
    S.op("dve", lambda e: e.tensor_tensor(out=t2[:], in0=t2[:], in1=ang[:], op=ALU.max), reads=[bf("ang"), bf("t2")], writes=[bf("t2")])
    S.op("dve", lambda e: e.tensor_scalar(out=t2[:], in0=t2[:], scalar1=-1.0, scalar2=math.pi / 2, op0=ALU.mult, op1=ALU.add), reads=[bf("t2")], writes=[bf("t2")])
    S.op("act", lambda e: e.activation(out=cosT[:], in_=t2[:], func=AF.Sin), reads=[bf("t2")], writes=[bf("cos")])

    stA = [bf("z0"), bf("z1"), bf("z2"), bf("z3"), bf("uv"), bf("ysh1"), bf("ysh2"), bf("acc"), bf("sq"), bf("vn32"), bf("cqkv")]
    stB = [bf("ya"), bf("PT0"), bf("PT1"), bf("PT2"), bf("PT3")]
    rot = [0]

    def mixer_pass(l):
        xsrc = x_d if l == 0 else out_d
        g0 = l * GW
        S.barrier()
        for k in range(8):
            S.dma("pool", lambda e, k=k: e.dma_start(out=w_in[:, k, :], in_=w_in_d[l, k * 128:(k + 1) * 128, :]), writes=[bf(f"w_in{k}")])
        for k in range(3):
            S.dma("pool", lambda e, k=k: e.dma_start(out=w_uq[:, k, :], in_=w_uq_d[l, k * 128:(k + 1) * 128, :]), writes=[bf("w_uq")])
        for k in range(2):
            S.dma("pool", lambda e, k=k: e.dma_start(out=w_ukv[:, k, :], in_=w_ukv_d[l, k * 128:(k + 1) * 128, :]), writes=[bf("w_ukv")])
        S.dma("pool", lambda e: e.dma_start(out=wspT[:].rearrange("p g t -> p (g t)"), in_=wsp_d[l]), writes=[bf("wspT")])
        S.dma("sp", lambda e: e.dma_start(out=sgln[:], in_=bc_d[l, :, 2048:2560]), writes=[bf("sgln")])
        S.dma("sp", lambda e: e.dma_start(out=convw[:], in_=bc_d[l, :, 2560:3328]), writes=[bf("convw")])
        S.op("dve", lambda e: e.tensor_tensor(out=wspT[:], in0=wspT[:], in1=tri[:].unsqueeze(1).broadcast_to([128, 4, 128]), op=ALU.mult),
             reads=[bf("wspT"), bf("tri")], writes=[bf("wspT")])
        S.op("pool", lambda e: e.memset(Vc[:, :, :, 64:65], 1.0), writes=BV)
        S.op("pool", lambda e: e.memset(ycv[1][:], 0.0), writes=[bf("ycv1")])

        for i in range(DBG_T):
            S.handoff(stB, stA)
            for s in range(4):
                n = 4 * i + s
                t0 = n * 128
                hb = hT[n % 2]
                Bh = bf(f"hT{n % 2}")
                S.dma("sp", lambda e: e.dma_start(out=xs[:], in_=xsrc[t0:t0 + 128, :]), reads=[Xd[n]], writes=[bf("xs")])
                sl, Bsl = new_slot()
                S.op("act", lambda e: e.activation(out=xn[:], in_=xs[:], func=AF.Square, accum_out=sl[:, 0:1]), reads=[bf("xs")], writes=[bf("xn"), Bsl])
                r = rstd_from_ss(sl, Bsl, 1, D)
                S.op("dve", lambda e: e.tensor_scalar(out=xn[:], in0=xs[:], scalar1=r, scalar2=None, op0=ALU.mult), reads=[bf("xs"), Bsl], writes=[bf("xn")])
                transpose_to(lambda c: xn[:, c * 128:(c + 1) * 128], 8, 128, bf("xn"), hb[:], Bh, gcol0=g0 + 0)
                cg = [(0, 512), (512, 1024), (1024, 1536), (1536, INW)]
                for k in range(8):
                    for q, (c0, c1) in enumerate(cg):
                        S.op("pe", lambda e, k=k, q=q, c0=c0, c1=c1: e.matmul(bk[q][:, 0:c1 - c0], lhsT=hb[:, k, :], rhs=w_in[:, k, c0:c1], start=(k == 0), stop=(k == 7)),
                             reads=[Bh, bf(f"w_in{k}")], writes=[Bbk[q]], inc=(k == 7 and q == 3))
                for q, (c0, c1) in enumerate(cg):
                    if q % 2 == 0:
                        S.op("act", lambda e, q=q, c0=c0, c1=c1: e.activation(out=z_sb[:, c0:c1], in_=bk[q][:, 0:c1 - c0], func=AF.Copy), reads=[Bbk[q]], writes=[bf(f"z{q}")])
                    else:
                        S.op("dve", lambda e, q=q, c0=c0, c1=c1: e.tensor_copy(out=z_sb[:, c0:c1], in_=bk[q][:, 0:c1 - c0]), reads=[Bbk[q]], writes=[bf(f"z{q}")])
                def br_mla():
                    slq, Bq = new_slot()
                    slk, Bk_ = new_slot()
                    S.op("act", lambda e: e.activation(out=cqkv[:, 0:384], in_=z_sb[:, 0:384], func=AF.Square, accum_out=slq[:, 0:1]), reads=[bf("z0")], writes=[bf("cqkv"), Bq])
                    S.op("act", lambda e: e.activation(out=cqkv[:, 384:640], in_=z_sb[:, 384:640], func=AF.Square, accum_out=slk[:, 0:1]), reads=[bf("z0"), bf("z1")], writes=[bf("cqkv"), Bk_])
                    rq = rstd_from_ss(slq, Bq, 1, 384)
                    rk = rstd_from_ss(slk, Bk_, 1, 256)
                    S.op("dve", lambda e: e.tensor_scalar(out=cqkv[:, 0:384], in0=z_sb[:, 0:384], scalar1=rq, scalar2=None, op0=ALU.mult), reads=[bf("z0"), Bq], writes=[bf("cqkv")])
                    S.op("dve", lambda e: e.tensor_scalar(out=cqkv[:, 384:640], in0=z_sb[:, 384:640], scalar1=rk, scalar2=None, op0=ALU.mult), reads=[bf("z0"), bf("z1"), Bk_], writes=[bf("cqkv")])
                    transpose_to(lambda c: cqkv[:, c * 128:(c + 1) * 128], 5, 128, bf("cqkv"), cqT[:], bf("cqT"), gcol0=g0 + 24)
                    S.begin_atom()
                    for k in range(3):
                        S.op("pe", lambda e, k=k: e.matmul(bk[4][:, 0:480], lhsT=cqT[:, k, :], rhs=w_uq[:, k, 0:480], start=(k == 0), stop=(k == 2)),
                             reads=[bf("cqT"), bf("w_uq")], writes=[Bbk[4]], inc=False)
                        S.op("pe", lambda e, k=k: e.matmul(bk[5][:, 0:288], lhsT=cqT[:, k, :], rhs=w_uq[:, k, 480:768], start=(k == 0), stop=(k == 2)),
                             reads=[bf("cqT"), bf("w_uq")], writes=[Bbk[5]], inc=(k == 2))
                    S.end_atom()
                    S.begin_atom()
                    for k in range(2):
                        S.op("pe", lambda e, k=k: e.matmul(bk[0][:, :], lhsT=cqT[:, 3 + k, :], rhs=w_ukv[:, k, 0:512], start=(k == 0), stop=(k == 1)),
                             reads=[bf("cqT"), bf("w_ukv")], writes=[Bbk[0]], inc=False)
                        S.op("pe", lambda e, k=k: e.matmul(bk[1][:, :], lhsT=cqT[:, 3 + k, :], rhs=w_ukv[:, k, 512:1024], start=(k == 0), stop=(k == 1)),
                             reads=[bf("cqT"), bf("w_ukv")], writes=[Bbk[1]], inc=(k == 1))
                    S.end_atom()
                    q4 = bk[4][:, 0:480].rearrange("p (h c) -> p h c", c=96)
                    q5 = bk[5][:, 0:288].rearrange("p (h c) -> p h c", c=96)
                    k0 = bk[0][:, :].rearrange("p (h c) -> p h c", c=128)
                    k1 = bk[1][:, :].rearrange("p (h c) -> p h c", c=128)
                    S.op("act", lambda e: e.activation(out=q_tm[:, 0:5, 0:64], in_=q4[:, :, 0:64], func=AF.Copy), reads=[Bbk[4]], writes=[bf("q_tm")])
                    S.op("act", lambda e: e.activation(out=q_tm[:, 5:8, 0:64], in_=q5[:, :, 0:64], func=AF.Copy), reads=[Bbk[5]], writes=[bf("q_tm")])
                    S.op("dve", lambda e: e.tensor_copy(out=rp[:, 0:5, :], in_=q4[:, :, 64:96]), reads=[Bbk[4]], writes=[bf("rp")])
                    S.op("dve", lambda e: e.tensor_copy(out=rp[:, 5:8, :], in_=q5[:, :, 64:96]), reads=[Bbk[5]], writes=[bf("rp")])
                    S.op("dve", lambda e: e.tensor_copy(out=rp[:, 8, :], in_=z_sb[:, 640:672]), reads=[bf("z1")], writes=[bf("rp")])
                    cs = cosT[:, n, :].unsqueeze(1).broadcast_to([128, 9, 16])
                    sn = sinT[:, n, :].unsqueeze(1).broadcast_to([128, 9, 16])
                    S.op("dve", lambda e: e.tensor_tensor(out=rt[0][:], in0=rp[:, :, 0:16], in1=cs, op=ALU.mult), reads=[bf("rp"), bf("cos")], writes=[bf("rt0")])
                    S.op("dve", lambda e: e.tensor_tensor(out=rt[1][:], in0=rp[:, :, 16:32], in1=sn, op=ALU.mult), reads=[bf("rp"), bf("sin")], writes=[bf("rt1")])
                    S.op("dve", lambda e: e.tensor_tensor(out=rt[2][:], in0=rp[:, :, 16:32], in1=cs, op=ALU.mult), reads=[bf("rp"), bf("cos")], writes=[bf("rt2")])
                    S.op("dve", lambda e: e.tensor_tensor(out=rt[3][:], in0=rp[:, :, 0:16], in1=sn, op=ALU.mult), reads=[bf("rp"), bf("sin")], writes=[bf("rt3")])
                    S.op("dve", lambda e: e.tensor_tensor(out=rr[:, :, 0:16], in0=rt[0][:], in1=rt[1][:], op=ALU.subtract), reads=[bf("rt0"), bf("rt1")], writes=[bf("rr")])
                    S.op("dve", lambda e: e.tensor_tensor(out=rr[:, :, 16:32], in0=rt[2][:], in1=rt[3][:], op=ALU.add), reads=[bf("rt2"), bf("rt3")], writes=[bf("rr")])
                    S.op("dve", lambda e: e.tensor_copy(out=q_tm[:, :, 64:96], in_=rr[:, 0:8, :]), reads=[bf("rr")], writes=[bf("q_tm")])
                    S.op("dve", lambda e: e.tensor_copy(out=k_tm[:, :, 64:96], in_=rr[:, 8:9, :].broadcast_to([128, 8, 32])), reads=[bf("rr")], writes=[bf("k_tm")])
                    S.op("act", lambda e: e.activation(out=k_tm[:, 0:4, 0:64], in_=k0[:, :, 0:64], func=AF.Copy), reads=[Bbk[0]], writes=[bf("k_tm")])
                    S.op("dve", lambda e: e.tensor_copy(out=k_tm[:, 4:8, 0:64], in_=k1[:, :, 0:64]), reads=[Bbk[1]], writes=[bf("k_tm")])
                    S.op("act", lambda e: e.activation(out=Vc[:, n, 0:4, 0:64], in_=k0[:, :, 64:128], func=AF.Copy), reads=[Bbk[0]], writes=[BV[n]])
                    S.op("dve", lambda e: e.tensor_copy(out=Vc[:, n, 4:8, 0:64], in_=k1[:, :, 64:128]), reads=[Bbk[1]], writes=[BV[n]])
                    transpose_to(lambda h: q_tm[:, h, :], 8, 96, bf("q_tm"), qT[0:96, :, s * 128:(s + 1) * 128], bf("qT"), eng="act")
                    transpose_to(lambda h: k_tm[:, h, :], 8, 96, bf("k_tm"), KT[0:96, :, t0:t0 + 128], BKT[n], eng="dve")

                def br_sgu():
                    S.op("act", lambda e: e.activation(out=uv[:], in_=z_sb[:, 672:1184], func=AF.Gelu_apprx_tanh), reads=[bf("z1"), bf("z2")], writes=[bf("uv")])
                    v3 = uv[:, 256:512].rearrange("p (g e) -> p g e", g=4)
                    Bsv = bf("stv")
                    S.op("dve", lambda e: e.tensor_reduce(out=stv[:, 0:4], in_=v3, axis=AX.X, op=ALU.add), reads=[bf("uv")], writes=[Bsv])
                    S.op("pool", lambda e: e.tensor_tensor(out=sq[:], in0=uv[:, 256:512], in1=uv[:, 256:512], op=ALU.mult), reads=[bf("uv")], writes=[bf("sq")])
                    S.op("dve", lambda e: e.tensor_reduce(out=stv[:, 4:8], in_=sq[:].rearrange("p (g e) -> p g e", g=4), axis=AX.X, op=ALU.add), reads=[bf("sq")], writes=[Bsv])
                    S.op("pool", lambda e: e.tensor_scalar(out=stv[:, 8:12], in0=stv[:, 0:4], scalar1=1.0 / 64, scalar2=None, op0=ALU.mult), reads=[Bsv], writes=[Bsv])
                    S.op("pool", lambda e: e.tensor_tensor(out=stv[:, 12:16], in0=stv[:, 8:12], in1=stv[:, 8:12], op=ALU.mult), reads=[Bsv], writes=[Bsv])
                    S.op("pool", lambda e: e.tensor_scalar(out=stv[:, 24:28], in0=stv[:, 4:8], scalar1=1.0 / 64, scalar2=EPS, op0=ALU.mult, op1=ALU.add), reads=[Bsv], writes=[Bsv])
                    S.op("pool", lambda e: e.tensor_tensor(out=stv[:, 16:20], in0=stv[:, 24:28], in1=stv[:, 12:16], op=ALU.subtract), reads=[Bsv], writes=[Bsv])
                    S.op("pool", lambda e: e.tensor_tensor(out=stv[:, 20:24], in0=stv[:, 16:20], in1=mhalf[:, 0:4], op=ALU.pow), reads=[Bsv, bf("mhalf")], writes=[Bsv])
                    vn3 = vn32[:].rearrange("p (g e) -> p g e", g=4)
                    S.op("dve", lambda e: e.tensor_tensor(out=vn3, in0=v3, in1=stv[:, 8:12].unsqueeze(2).broadcast_to([128, 4, 64]), op=ALU.subtract), reads=[bf("uv"), Bsv], writes=[bf("vn32")])
                    S.op("dve", lambda e: e.tensor_tensor(out=vn3, in0=vn3, in1=stv[:, 20:24].unsqueeze(2).broadcast_to([128, 4, 64]), op=ALU.mult), reads=[bf("vn32"), Bsv], writes=[bf("vn32")])
                    S.op("pool", lambda e: e.tensor_tensor(out=vn32[:], in0=vn32[:], in1=sgln[:, 0:256], op=ALU.mult), reads=[bf("vn32"), bf("sgln")], writes=[bf("vn32")])
                    S.op("pool", lambda e: e.tensor_tensor(out=vn[:], in0=vn32[:], in1=sgln[:, 256:512], op=ALU.add), reads=[bf("vn32"), bf("sgln")], writes=[bf("vn")])
                    S.begin_atom()
                    for g in range(4):
                        S.op("pe", lambda e, g=g: e.matmul(bk[2][:, g * 64:(g + 1) * 64], lhsT=wspT[:, g, :], rhs=vn[:, g * 64:(g + 1) * 64], start=True, stop=True, skip_group_check=True),
                             reads=[bf("wspT"), bf("vn")], writes=[Bbk[2]], inc=(g == 3))
                    S.end_atom()
                    for g in range(4):
                        S.op("dve", lambda e, g=g: e.scalar_tensor_tensor(out=sq[:, g * 64:(g + 1) * 64], in0=bk[2][:, g * 64:(g + 1) * 64], scalar=gT[:, g0 + 29 + g:g0 + 30 + g],
                                                                          in1=uv[:, g * 64:(g + 1) * 64], op0=ALU.add, op1=ALU.mult),
                             reads=[Bbk[2], bf("gT"), bf("uv")], writes=[bf("sq")])
                    slb, Bb_ = new_slot()
                    S.op("act", lambda e: e.activation(out=mix_tm[:, s, 512:768], in_=sq[:], func=AF.Square, accum_out=slb[:, 0:1]), reads=[bf("sq")], writes=[bf(f"mix{s}"), Bb_])
                    rb = rstd_from_ss(slb, Bb_, 1, 256)
                    S.op("dve", lambda e: e.tensor_scalar(out=mix_tm[:, s, 512:768], in0=sq[:], scalar1=rb, scalar2=None, op0=ALU.mult), reads=[bf("sq"), Bb_], writes=[bf(f"mix{s}")])

                def br_conv():
                    yc, yp = ycv[n % 2], ycv[(n + 1) % 2]
                    Byc, Byp = bf(f"ycv{n % 2}"), bf(f"ycv{(n + 1) % 2}")
                    S.op("pool", lambda e: e.tensor_tensor(out=yc[:], in0=z_sb[:, 1440:1696], in1=z_sb[:, 1696:1952], op=ALU.mult), reads=[bf("z2"), bf("z3")], writes=[Byc])
                    S.dma("sp", lambda e: e.dma_start(out=ysh1[1:128, :], in_=yc[0:127, :]), reads=[Byc], writes=[bf("ysh1")])
                    S.dma("sp", lambda e: e.dma_start(out=ysh1[0:1, :], in_=yp[127:128, :]), reads=[Byp], writes=[bf("ysh1")])
                    S.dma("sp", lambda e: e.dma_start(out=ysh2[2:128, :], in_=yc[0:126, :]), reads=[Byc], writes=[bf("ysh2")])
                    S.dma("sp", lambda e: e.dma_start(out=ysh2[0:2, :], in_=yp[126:128, :]), reads=[Byp], writes=[bf("ysh2")])
                    S.op("pool", lambda e: e.tensor_tensor(out=acc[:], in0=yc[:], in1=convw[:, 512:768], op=ALU.mult), reads=[Byc, bf("convw")], writes=[bf("acc")])
                    S.op("pool", lambda e: e.tensor_tensor(out=ysh1[:], in0=ysh1[:], in1=convw[:, 256:512], op=ALU.mult), reads=[bf("ysh1"), bf("convw")], writes=[bf("ysh1")])
                    S.op("pool", lambda e: e.tensor_tensor(out=acc[:], in0=acc[:], in1=ysh1[:], op=ALU.add), reads=[bf("acc"), bf("ysh1")], writes=[bf("acc")])
                    S.op("pool", lambda e: e.tensor_tensor(out=ysh2[:], in0=ysh2[:], in1=convw[:, 0:256], op=ALU.mult), reads=[bf("ysh2"), bf("convw")], writes=[bf("ysh2")])
                    S.op("pool", lambda e: e.tensor_tensor(out=acc[:], in0=acc[:], in1=ysh2[:], op=ALU.add), reads=[bf("acc"), bf("ysh2")], writes=[bf("acc")])
                    S.op("pool", lambda e: e.tensor_tensor(out=acc[:], in0=acc[:], in1=z_sb[:, 1184:1440], op=ALU.mult), reads=[bf("acc"), bf("z2")], writes=[bf("acc")])
                    slc, Bc_ = new_slot()
                    S.op("act", lambda e: e.activation(out=mix_tm[:, s, 768:1024], in_=acc[:], func=AF.Square, accum_out=slc[:, 0:1]), reads=[bf("acc")], writes=[bf(f"mix{s}"), Bc_])
                    rc = rstd_from_ss(slc, Bc_, 1, 256)
                    S.op("dve", lambda e: e.tensor_scalar(out=mix_tm[:, s, 768:1024], in0=acc[:], scalar1=rc, scalar2=None, op0=ALU.mult), reads=[bf("acc"), Bc_], writes=[bf(f"mix{s}")])


                S.replay(S.interleave(S.record(br_mla), S.record(br_sgu), S.record(br_conv)))
            if DBG_STAGE in ('A', 'A0'):
                continue
            S.handoff(stA, stB)
            nkb = 4 * i + 4
            LA = 3

            def att_front(h, kb):
                j0 = max(0, kb - 4 * i)
                c0 = j0 * 128
                r = rot[0] % 4
                rot[0] += 1
                sbk, Bs = bk[r], Bbk[r]
                S.op("pe", lambda e: e.matmul(sbk[:, c0:512], lhsT=KT[0:96, h, kb * 128:(kb + 1) * 128], rhs=qT[0:96, h, c0:512], start=True, stop=True),
                     reads=[BKT[kb], bf("qT")], writes=[Bs])
                S.op("act", lambda e: e.activation(out=PT[r][:, c0:512], in_=sbk[:, c0:512], func=AF.Exp, scale=SCALE), reads=[Bs], writes=[bf(f"PT{r}")])
                if kb >= 4 * i:
                    S.op("pool", lambda e: e.tensor_tensor(out=PT[r][:, c0:c0 + 128], in0=PT[r][:, c0:c0 + 128], in1=tri[:], op=ALU.mult),
                         reads=[bf(f"PT{r}"), bf("tri")], writes=[bf(f"PT{r}")])
                return r

            def att_back(h, kb, r):
                j0 = max(0, kb - 4 * i)
                O = bk[4 + h % 2]
                BO = Bbk[4 + h % 2]
                O3 = O[:, :].rearrange("p (j c) -> p j c", j=4)
                for j in range(j0, 4):
                    S.op("pe", lambda e, j=j: e.matmul(O3[:, j, 0:65], lhsT=PT[r][:, j * 128:(j + 1) * 128], rhs=Vc[:, kb, h, :],
                                                       start=(kb == 0 and j == 0), stop=(kb == 4 * i + j), skip_group_check=True),
                         reads=[bf(f"PT{r}"), BV[kb]], writes=[BO], inc=(j == 3))
                if kb == nkb - 1:
                    S.op("dve", lambda e: e.reciprocal(out=rinv[:].unsqueeze(2), in_=O3[:, :, 64:65]), reads=[BO], writes=[bf("rinv")])
                    S.op("dve", lambda e: e.tensor_tensor(out=ya[:, :, h * 64:(h + 1) * 64], in0=O3[:, :, 0:64], in1=rinv[:].unsqueeze(2).broadcast_to([128, 4, 64]), op=ALU.mult),
                         reads=[BO, bf("rinv")], writes=[bf("ya")])

            inflight = []
            for h in range(H):
                for kb in range(nkb):
                    inflight.append((h, kb, att_front(h, kb)))
                    if len(inflight) > LA:
                        att_back(*inflight.pop(0))
            while inflight:
                att_back(*inflight.pop(0))
            for j in range(4):
                sla, Ba_ = new_slot()
                S.op("act", lambda e, j=j: e.activation(out=mix_tm[:, j, 0:512], in_=ya[:, j, :], func=AF.Square, accum_out=sla[:, 0:1]), reads=[bf("ya")], writes=[bf(f"mix{j}"), Ba_])
                ra = rstd_from_ss(sla, Ba_, 1, 512)
                S.op("dve", lambda e, j=j: e.tensor_scalar(out=mix_tm[:, j, 0:512], in0=ya[:, j, :], scalar1=ra, scalar2=None, op0=ALU.mult), reads=[bf("ya"), Ba_], writes=[bf(f"mix{j}")])
            S.dma("sp", lambda e: e.dma_start(out=mix_d[i * 512:(i + 1) * 512, :].rearrange("(j p) d -> p j d", p=128), in_=mix_tm[:]),
                  reads=[bf("mix0"), bf("mix1"), bf("mix2"), bf("mix3")], writes=[Md[i]])

    def ffn_pass(l):
        xsrc = x_d if l == 0 else out_d
        g0 = l * GW
        S.barrier()
        for k in range(8):
            S.dma("pool", lambda e, k=k: e.dma_start(out=w_out[:, k, :], in_=w_out_d[l, k * 128:(k + 1) * 128, :]), writes=[bf("w_out")])
        S.dma("sp", lambda e: e.dma_start(out=postg_m[:], in_=bc_d[l, :, 0:1024]), writes=[bf("postg_m")])
        S.dma("sp", lambda e: e.dma_start(out=postg_f[:], in_=bc_d[l, :, 1024:2048]), writes=[bf("postg_f")])
        for k in range(8):
            S.dma("pool", lambda e, k=k: e.dma_start(out=w_gate[:, k, :], in_=w_gate_d[l, k * 128:(k + 1) * 128, :]), writes=[bf(f"w_gate{k}")])
            S.dma("pool", lambda e, k=k: e.dma_start(out=w_up[:, k, :], in_=w_up_d[l, k * 128:(k + 1) * 128, :]), writes=[bf(f"w_up{k}")])
        for c in range(NFF):
            S.dma("pool", lambda e, c=c: e.dma_start(out=w_down[:, c, :], in_=w_down_d[l, c * 128:(c + 1) * 128, :]), writes=[bf(f"w_down{c}")])

        def post_norm_residual(postg, Bpostg, ba, bb, xs_, Bxs, t_, Bt):
            sl, Bsl = new_slot()
            S.op("act", lambda e: e.activation(out=t_[:, 0:512], in_=bk[ba][:, :], func=AF.Square, accum_out=sl[:, 0:1]), reads=[Bbk[ba]], writes=[Bt, Bsl])
            S.op("act", lambda e: e.activation(out=t_[:, 512:1024], in_=bk[bb][:, :], func=AF.Square, accum_out=sl[:, 1:2]), reads=[Bbk[bb]], writes=[Bt, Bsl])
            r = rstd_from_ss(sl, Bsl, 2, D)
            S.op("dve", lambda e: e.scalar_tensor_tensor(out=t_[:, 0:512], in0=bk[ba][:, :], scalar=r, in1=postg[:, 0:512], op0=ALU.mult, op1=ALU.mult),
                 reads=[Bbk[ba], Bsl, Bpostg], writes=[Bt])
            S.op("dve", lambda e: e.scalar_tensor_tensor(out=t_[:, 512:1024], in0=bk[bb][:, :], scalar=r, in1=postg[:, 512:1024], op0=ALU.mult, op1=ALU.mult),
                 reads=[Bbk[bb], Bsl, Bpostg], writes=[Bt])
            S.op("pool", lambda e: e.tensor_tensor(out=xs_[:], in0=xs_[:], in1=t_[:], op=ALU.add), reads=[Bxs, Bt], writes=[Bxs])

        aT_bufs = [bf(f"aT{c}") for c in range(NFF)]
        set1_bufs = [bf("xs_f2"), bf("t_f2"), bf("xn_b0"), bf("xn_b1"), bf("mixT2")]
        sets = [dict(xs=xs_f, Bxs=bf("xs_f"), t=t_f, Bt=bf("t_f"), m0=xn_f[0], Bm0=bf("xn_f0"), m1=xn_f[1], Bm1=bf("xn_f1"), mixT=mixT, BmixT=bf("mixT"), ba=4, bb=5),
                dict(xs=xs_f2, Bxs=bf("xs_f2"), t=t_f2, Bt=bf("t_f2"), m0=xn_b[0], Bm0=bf("xn_b0"), m1=xn_b[1], Bm1=bf("xn_b1"), mixT=mixT2, BmixT=bf("mixT2"), ba=2, bb=3)]
        for i in range(DBG_T if DBG_STAGE == 'full' else 0):
            S.handoff(aT_bufs, set1_bufs)

            def ffn_sub(s):
                n = 4 * i + s
                t0 = n * 128
                Q = sets[s % 2]
                m0, Bm0, m1, Bm1, xs_, Bxs, t_, Bt, mT, BmT, ba, bb = Q["m0"], Q["Bm0"], Q["m1"], Q["Bm1"], Q["xs"], Q["Bxs"], Q["t"], Q["Bt"], Q["mixT"], Q["BmixT"], Q["ba"], Q["bb"]
                S.dma("sp", lambda e: e.dma_start(out=m0[:], in_=mix_d[t0:t0 + 128, :]), reads=[Md[i]], writes=[Bm0])
                S.dma("sp", lambda e: e.dma_start(out=xs_[:], in_=xsrc[t0:t0 + 128, :]), reads=[Xd[n]], writes=[Bxs])
                transpose_to(lambda c: m0[:, c * 128:(c + 1) * 128], 8, 128, Bm0, mT[:], BmT, gcol0=g0 + 16)
                S.begin_atom()
                for k in range(8):
                    S.op("pe", lambda e, k=k: e.matmul(bk[ba][:, :], lhsT=mT[:, k, :], rhs=w_out[:, k, 0:512], start=(k == 0), stop=(k == 7)),
                         reads=[BmT, bf("w_out")], writes=[Bbk[ba]], inc=False)
                    S.op("pe", lambda e, k=k: e.matmul(bk[bb][:, :], lhsT=mT[:, k, :], rhs=w_out[:, k, 512:1024], start=(k == 0), stop=(k == 7)),
                         reads=[BmT, bf("w_out")], writes=[Bbk[bb]], inc=(k == 7))
                S.end_atom()
                post_norm_residual(postg_m, bf("postg_m"), ba, bb, xs_, Bxs, t_, Bt)
                S.dma("sp", lambda e: e.dma_start(out=out_d[t0:t0 + 128, :], in_=xs_[:]), reads=[Bxs], writes=[Xd[n]])
                sl, Bsl = new_slot()
                S.op("act", lambda e: e.activation(out=m1[:], in_=xs_[:], func=AF.Square, accum_out=sl[:, 0:1]), reads=[Bxs], writes=[Bm1, Bsl])
                r = rstd_from_ss(sl, Bsl, 1, D)
                S.op("dve", lambda e: e.tensor_scalar(out=m1[:], in0=xs_[:], scalar1=r, scalar2=None, op0=ALU.mult), reads=[Bxs, Bsl], writes=[Bm1])
                transpose_to(lambda c: m1[:, c * 128:(c + 1) * 128], 8, 128, Bm1, h2T[:, :, s * 128:(s + 1) * 128], bf("h2T"), gcol0=g0 + 8)

            for s2 in (0, 2):
                S.replay(S.interleave(S.record(lambda: ffn_sub(s2)), S.record(lambda: ffn_sub(s2 + 1))))
            S.handoff(set1_bufs, aT_bufs)
            for c in range(NFF):
                gb, Bg = bk[c % 2], Bbk[c % 2]
                ub, Bu = bk[2 + c % 2], Bbk[2 + c % 2]
                for k in range(8):
                    S.op("pe", lambda e, k=k, c=c, gb=gb: e.matmul(gb[:, :], lhsT=w_gate[:, k, c * 128:(c + 1) * 128], rhs=h2T[:, k, :], start=(k == 0), stop=(k == 7)),
                         reads=[bf(f"w_gate{k}"), bf("h2T")], writes=[Bg], inc=(k == 7))
                for k in range(8):
                    S.op("pe", lambda e, k=k, c=c, ub=ub: e.matmul(ub[:, :], lhsT=w_up[:, k, c * 128:(c + 1) * 128], rhs=h2T[:, k, :], start=(k == 0), stop=(k == 7)),
                         reads=[bf(f"w_up{k}"), bf("h2T")], writes=[Bu], inc=(k == 7))
                sg, Bsg = sg_f[c % 2], bf(f"xn_f{c % 2}")
                S.op("act", lambda e, gb=gb, sg=sg: e.activation(out=sg[:], in_=gb[:, :], func=AF.Silu), reads=[Bg], writes=[Bsg])
                S.op("dve", lambda e, c=c, ub=ub, sg=sg: e.tensor_tensor(out=aT[:, c, :], in0=ub[:, :], in1=sg[:], op=ALU.mult), reads=[Bu, Bsg], writes=[bf(f"aT{c}")])
            for s in range(4):
                n = 4 * i + s
                t0 = n * 128
                ba, bb = [(4, 5), (0, 1), (2, 3), (4, 5)][s]
                for c in range(NFF):
                    S.op("pe", lambda e, c=c: e.matmul(bk[ba][:, :], lhsT=aT[:, c, s * 128:(s + 1) * 128], rhs=w_down[:, c, 0:512], start=(c == 0), stop=(c == NFF - 1)),
                         reads=[bf(f"aT{c}"), bf(f"w_down{c}")], writes=[Bbk[ba]], inc=False)
                    S.op("pe", lambda e, c=c: e.matmul(bk[bb][:, :], lhsT=aT[:, c, s * 128:(s + 1) * 128], rhs=w_down[:, c, 512:1024], start=(c == 0), stop=(c == NFF - 1)),
                         reads=[bf(f"aT{c}"), bf(f"w_down{c}")], writes=[Bbk[bb]], inc=(c == NFF - 1))
                S.dma("sp", lambda e: e.dma_start(out=xs_f[:], in_=out_d[t0:t0 + 128, :]), reads=[Xd[n]], writes=[bf("xs_f")])
                post_norm_residual(postg_f, bf("postg_f"), ba, bb, xs_f, bf("xs_f"), t_f, bf("t_f"))
                S.dma("sp", lambda e: e.dma_start(out=out_d[t0:t0 + 128, :], in_=xs_f[:]), reads=[bf("xs_f")], writes=[Xd[n]])

    for l in range(L if DBG_STAGE != 'P' else 0):
        mixer_pass(l)
        if DBG_STAGE not in ('A0', 'B0'):
            ffn_pass(l)
    S.finish("sp", Xd)
    S.barrier()
    return nc


def _layouts(p):
    L = DEPTH
    gT = np.zeros((128, L * GW), np.float32)
    for l in range(L):
        o = l * GW
        gT[:, o + 0:o + 8] = p["mix_pre_g"][l].reshape(8, 128).T
        gT[:, o + 8:o + 16] = p["ffn_pre_g"][l].reshape(8, 128).T
        gT[:, o + 16:o + 24] = p["out_norm_g"][l].reshape(8, 128).T
        gT[:, o + 24:o + 27] = p["q_norm_g"][l].reshape(3, 128).T
        gT[:, o + 27:o + 29] = p["kv_norm_g"][l].reshape(2, 128).T
        gT[:, o + 29:o + 33] = p["b_sp"][l].T
    bc = np.zeros((L, 128, BCW), np.float32)
    bc[:, :, 0:1024] = p["mix_post_g"][:, None, :]
    bc[:, :, 1024:2048] = p["ffn_post_g"][:, None, :]
    bc[:, :, 2048:2304] = p["sg_ln_g"][:, None, :]
    bc[:, :, 2304:2560] = p["sg_ln_b"][:, None, :]
    bc[:, :, 2560:3328] = p["conv_w"].reshape(L, 1, 768)
    wspT = np.ascontiguousarray(np.transpose(p["w_sp"], (0, 3, 1, 2))).reshape(L, 128, 512)
    return gT, bc, wspT


_INVF = (np.float32(1.0) / (np.float32(10000.0) ** (np.arange(16, dtype=np.float32) / np.float32(16)))).astype(np.float32)


def kernel(x, positions, mix_pre_g, mix_post_g, ffn_pre_g, ffn_post_g, w_in, q_norm_g, w_uq, kv_norm_g, w_ukv,
           sg_ln_g, sg_ln_b, w_sp, b_sp, conv_w, out_norm_g, w_out, w_gate, w_up, w_down, _depth=DEPTH, _cores=8):
    p = dict(mix_pre_g=mix_pre_g, mix_post_g=mix_post_g, ffn_pre_g=ffn_pre_g, ffn_post_g=ffn_post_g, q_norm_g=q_norm_g,
             kv_norm_g=kv_norm_g, sg_ln_g=sg_ln_g, sg_ln_b=sg_ln_b, w_sp=w_sp, b_sp=b_sp, conv_w=conv_w, out_norm_g=out_norm_g)
    p = {k: np.asarray(v, np.float32) for k, v in p.items()}
    gT, bc, wspT = _layouts(p)
    f = lambda a: np.ascontiguousarray(np.asarray(a, np.float32))
    shared = {"invf": np.ascontiguousarray(np.broadcast_to(_INVF[None, :], (128, 16))), "w_in": f(w_in), "w_uq": f(w_uq), "w_ukv": f(w_ukv),
              "w_out": f(w_out), "w_gate": f(w_gate), "w_up": f(w_up), "w_down": f(w_down), "gT": gT, "bc": bc, "wspT": wspT}
    x = np.asarray(x, np.float32)
    positions = np.asarray(positions, np.int32)
    nc = build_program(_depth)
    in_maps = []
    for b in range(_cores):
        m = dict(shared)
        m["x"] = np.ascontiguousarray(x[b])
        m["pos"] = np.ascontiguousarray(positions[b].reshape(NSUB, 128).T)
        in_maps.append(m)
    res = run_bass_kernel_spmd(nc, in_maps, core_ids=list(range(_cores)))
    return np.stack([np.asarray(r["out"], np.float32) for r in res.results], axis=0)
```

```python
import math
import os
import numpy as np
import concourse.bass as bass
import concourse.mybir as mybir
from concourse.bass_utils import run_bass_kernel_spmd

F32 = mybir.dt.float32
BF16 = mybir.dt.bfloat16
I32 = mybir.dt.int32
AF = mybir.ActivationFunctionType
ALU = mybir.AluOpType
AX = mybir.AxisListType

D = 1024
SEQ = 4096
DEPTH = 4
NSUB = SEQ // 128
NTILE = SEQ // 512
INW = 1952
H = 8
DFF = 2816
NFF = DFF // 128
EPS = 1e-6
SCALE = 96.0 ** -0.5
GW = 33
BCW = 3328
ND = 8


class Buf:
    __slots__ = ("name", "w", "r", "psum")

    def __init__(self, name, psum=False):
        self.name = name
        self.w = None
        self.r = {}
        self.psum = psum


class Sched:
    def __init__(self, nc):
        self.nc = nc
        self.E = {"pe": nc.tensor, "act": nc.scalar, "dve": nc.vector, "pool": nc.gpsimd, "sp": nc.sync}
        self.sem = {k: nc.alloc_semaphore("s_" + k) for k in self.E}
        self.cnt = {k: 0 for k in self.E}
        self.seen = {k: {} for k in self.E}
        self.pend = {k: [] for k in self.E}
        self.dsem = {q: [nc.alloc_semaphore(f"d_{q}{i}") for i in range(ND)] for q in ("sp", "pool")}
        self.dcnt = {q: [0] * ND for q in self.dsem}
        self.drr = {q: 0 for q in self.dsem}
        self.rec = None
        self.adepth = 0

    def _rec(self, item):
        if self.adepth > 0:
            if self._open is None:
                self._open = []
                self.rec.append(self._open)
            self._open.append(item)
        else:
            self.rec.append([item])

    def begin_atom(self):
        self.adepth += 1
        if self.adepth == 1:
            self._open = None

    def end_atom(self):
        self.adepth -= 1
        if self.adepth == 0:
            self._open = None

    def record(self, f):
        assert self.rec is None
        self.rec = []
        self._open = None
        f()
        atoms, self.rec = self.rec, None
        return atoms

    @staticmethod
    def interleave(*streams):
        streams = [st for st in streams if st]
        idx = [0] * len(streams)
        out = []
        total = sum(len(st) for st in streams)
        while len(out) < total:
            k = min((idx[j] / len(streams[j]), j) for j in range(len(streams)) if idx[j] < len(streams[j]))[1]
            out.append(streams[k][idx[k]])
            idx[k] += 1
        return out

    def replay(self, atoms):
        assert self.rec is None
        for atom in atoms:
            for kind, args, kw in atom:
                (self.op if kind == "op" else self.dma)(*args, **kw)

    def _collect(self, eng, reads, writes, skip_own_war=True):
        toks = {}
        own = self.sem.get(eng)

        def need(s, v, war=False):
            if s is own and (eng == "pe" or (war and skip_own_war)):
                return
            if toks.get(s, 0) < v:
                toks[s] = v

        for b in reads:
            if b.w is not None:
                need(*b.w)
            if b.psum:
                for s, v in b.r.items():
                    if s is not own:
                        need(s, v)
        for b in writes:
            if b.w is not None:
                need(*b.w)
            for s, v in b.r.items():
                need(s, v, war=True)
        return toks

    def _emit_waits(self, eng, toks):
        e = self.E[eng]
        seen = self.seen[eng]
        for s, v in toks.items():
            if seen.get(s, 0) < v:
                e.wait_ge(s, v)
                seen[s] = v

    def op(self, eng, fn, reads=(), writes=(), inc=True):
        if self.rec is not None:
            self._rec(("op", (eng, fn), dict(reads=tuple(reads), writes=tuple(writes), inc=inc)))
            return None
        self._emit_waits(eng, self._collect(eng, reads, writes))
        ins = fn(self.E[eng])
        self.pend[eng].append((tuple(reads), tuple(writes)))
        if inc:
            self.cnt[eng] += 1
            s = self.sem[eng]
            ins.then_inc(s, 1)
            v = self.cnt[eng]
            for rs, ws in self.pend[eng]:
                for b in rs:
                    b.r[s] = v
                for b in ws:
                    b.w = (s, v)
                    b.r = {}
            self.pend[eng] = []
        return ins

    def dma(self, q, fn, reads=(), writes=()):
        if self.rec is not None:
            self._rec(("dma", (q, fn), dict(reads=tuple(reads), writes=tuple(writes))))
            return None
        assert not self.pend[q]
        i = self.drr[q]
        self.drr[q] = (i + 1) % ND
        s = self.dsem[q][i]
        toks = self._collect(q, reads, writes, skip_own_war=False)
        prev = 16 * self.dcnt[q][i]
        if prev and toks.get(s, 0) < prev:
            toks[s] = prev
        self._emit_waits(q, toks)
        ins = fn(self.E[q])
        ins.then_inc(s, 16)
        self.dcnt[q][i] += 1
        v = 16 * self.dcnt[q][i]
        for b in reads:
            b.r[s] = v
        for b in writes:
            b.w = (s, v)
            b.r = {}
        return ins

    def handoff(self, src, dst):
        for d in dst:
            for b in src:
                if b.w is not None:
                    s, v = b.w
                    if d.r.get(s, 0) < v:
                        d.r[s] = v
                for s, v in b.r.items():
                    if d.r.get(s, 0) < v:
                        d.r[s] = v

    def barrier(self):
        for k in self.E:
            assert not self.pend[k]
        toks = {}
        for k in self.E:
            if self.cnt[k]:
                toks[self.sem[k]] = self.cnt[k]
        for q in self.dsem:
            for i in range(ND):
                if self.dcnt[q][i]:
                    toks[self.dsem[q][i]] = 16 * self.dcnt[q][i]
        for k in self.E:
            t = {s: v for s, v in toks.items() if s is not self.sem[k]}
            self._emit_waits(k, t)

    def finish(self, eng, bufs):
        toks = {}
        for b in bufs:
            if b.w is not None:
                s, v = b.w
                toks[s] = max(toks.get(s, 0), v)
            for s, v in b.r.items():
                toks[s] = max(toks.get(s, 0), v)
        self._emit_waits(eng, toks)


def build_program(L=DEPTH):
    DBG_T = int(os.environ.get('KDBG_TILES', NTILE))
    DBG_STAGE = os.environ.get('KDBG_STAGE', 'full')
    DBG_STOP = float(os.environ.get('KDBG_STOP', 99))
    nc = bass.Bass("TRN2", target_bir_lowering=False)
    S = Sched(nc)

    def dram(name, shape, dt, kind):
        return nc.dram_tensor(name, shape, dt, kind=kind).ap()

    x_d = dram("x", [SEQ, D], F32, "ExternalInput")
    pos_d = dram("pos", [128, NSUB], I32, "ExternalInput")
    invf_d = dram("invf", [128, 16], F32, "ExternalInput")
    w_in_d = dram("w_in", [DEPTH, D, INW], F32, "ExternalInput")
    w_uq_d = dram("w_uq", [DEPTH, 384, 768], F32, "ExternalInput")
    w_ukv_d = dram("w_ukv", [DEPTH, 256, 1024], F32, "ExternalInput")
    w_out_d = dram("w_out", [DEPTH, D, D], F32, "ExternalInput")
    w_gate_d = dram("w_gate", [DEPTH, D, DFF], F32, "ExternalInput")
    w_up_d = dram("w_up", [DEPTH, D, DFF], F32, "ExternalInput")
    w_down_d = dram("w_down", [DEPTH, DFF, D], F32, "ExternalInput")
    gT_d = dram("gT", [128, DEPTH * GW], F32, "ExternalInput")
    bc_d = dram("bc", [DEPTH, 128, BCW], F32, "ExternalInput")
    wsp_d = dram("wspT", [DEPTH, 128, 512], F32, "ExternalInput")
    out_d = dram("out", [SEQ, D], F32, "ExternalOutput")
    mix_d = dram("mixd", [SEQ, D], BF16, "Internal")

    base = (nc.sbuf_base + 31) // 32 * 32
    top = nc.sbuf_top

    def sb(name, shape, dt, off):
        nb = int(np.prod(shape[1:])) * (2 if dt == BF16 else 4)
        assert off % 32 == 0 and base + off + nb <= top, (name, off, nb, top - base)
        return nc.alloc_sbuf_tensor_at(name, list(shape), dt, offset=base + off)

    ident = sb("ident", [128, 128], BF16, 0)
    tri = sb("tri", [128, 128], BF16, 256)
    mhalf = sb("mhalf", [128, 8], F32, 512)
    cosT = sb("cosT", [128, NSUB, 16], F32, 576)
    sinT = sb("sinT", [128, NSUB, 16], F32, 2624)
    gT = sb("gTs", [128, DEPTH * GW], F32, 4672)
    stt_ = sb("stats", [128, 64], F32, 5216)
    invf = sb("invfs", [128, 16], F32, 5472)
    posi = sb("posi", [128, NSUB], I32, 5536)
    posf = sb("posf", [128, NSUB], F32, 5664)
    stv = sb("stv", [128, 32], F32, 5792)
    CEND = 5920
    BIG = CEND
    MID = BIG + 174080
    assert base + MID + 30720 <= top, (base, MID, top)

    KT = sb("KT", [128, H, SEQ], BF16, BIG + 0)
    Vc = sb("Vc", [128, NSUB, H, 65], BF16, BIG + 65536)
    w_in = sb("w_in_s", [128, 8, INW], BF16, BIG + 98816)
    w_uq = sb("w_uq_s", [128, 3, 768], BF16, BIG + 130048)
    w_ukv = sb("w_ukv_s", [128, 2, 1024], BF16, BIG + 134656)
    qT = sb("qT", [128, H, 512], BF16, BIG + 138752)
    mix_tm = sb("mix_tm", [128, 4, D], BF16, BIG + 146944)
    SA = BIG + 155136
    z_sb = sb("z_sb", [128, INW], F32, SA)
    uv = sb("uv", [128, 512], F32, SA + 7808)
    ysh1 = sb("ysh1", [128, 256], F32, SA + 9856)
    ysh2 = sb("ysh2", [128, 256], F32, SA + 10880)
    acc = sb("acc", [128, 256], F32, SA + 11904)
    sq = sb("sq", [128, 256], F32, SA + 12928)
    vn32 = sb("vn32", [128, 256], F32, SA + 13952)
    cqkv = sb("cqkv", [128, 640], BF16, SA + 14976)
    ycv = [sb("ycv0", [128, 256], F32, SA + 16352), sb("ycv1", [128, 256], F32, SA + 17376)]
    assert SA + 18400 <= MID
    ya = sb("ya", [128, 4, 512], F32, SA)
    PT = [sb(f"PT{r}", [128, 512], BF16, SA + 8192 + 1024 * r) for r in range(4)]
    hT = [sb("hT0", [128, 8, 128], BF16, MID + 0), sb("hT1", [128, 8, 128], BF16, MID + 2048)]
    xs = sb("xs", [128, D], F32, MID + 4096)
    xn = sb("xn", [128, D], BF16, MID + 8192)
    sgln = sb("sgln", [128, 512], F32, MID + 10240)
    convw = sb("convw", [128, 768], F32, MID + 12288)
    wspT = sb("wspTs", [128, 4, 128], BF16, MID + 15360)
    cqT = sb("cqT", [128, 5, 128], BF16, MID + 16384)
    q_tm = sb("q_tm", [128, H, 96], BF16, MID + 17664)
    k_tm = sb("k_tm", [128, H, 96], BF16, MID + 19200)
    rp = sb("rp", [128, 9, 32], F32, MID + 20736)
    rt = [sb(f"rt{i}", [128, 9, 16], F32, MID + 21888 + 576 * i) for i in range(4)]
    rr = sb("rr", [128, 9, 32], F32, MID + 24192)
    vn = sb("vn", [128, 256], BF16, MID + 25344)
    rinv = sb("rinv", [128, 4], F32, MID + 25856)
    w_gate = sb("w_gate_s", [128, 8, DFF], BF16, BIG + 0)
    w_up = sb("w_up_s", [128, 8, DFF], BF16, BIG + 45056)
    w_down = sb("w_down_s", [128, NFF, D], BF16, BIG + 90112)
    w_out = sb("w_out_s", [128, 8, D], BF16, BIG + 135168)
    aT = sb("aT", [128, NFF, 512], BF16, BIG + 151552)
    h2T = sb("h2T", [128, 8, 512], BF16, MID + 0)
    xs_f = sb("xs_f", [128, D], F32, MID + 8192)
    xn_f = [sb("xn_f0", [128, D], BF16, MID + 12288), sb("xn_f1", [128, D], BF16, MID + 14336)]
    sg_f = [sb("sg_f0", [128, 512], F32, MID + 12288), sb("sg_f1", [128, 512], F32, MID + 14336)]
    mixT = sb("mixT", [128, 8, 128], BF16, MID + 16384)
    t_f = sb("t_f", [128, D], F32, MID + 18432)
    postg_m = sb("postg_m", [128, D], F32, MID + 22528)
    postg_f = sb("postg_f", [128, D], F32, MID + 26624)
    m0P = sb("m0P", [128, D], BF16, MID + 30720)

    pT = [nc.alloc_psum_tensor(f"pT{i}", [128, 8, 128], BF16) for i in range(2)]
    bk = [nc.alloc_psum_tensor(f"bk{i}", [128, 512], F32) for i in range(6)]
    BpT = [Buf(f"pT{i}", psum=True) for i in range(2)]
    Bbk = [Buf(f"bk{i}", psum=True) for i in range(6)]
    tcount = [0]

    def next_pT():
        i = tcount[0] % 2
        tcount[0] += 1
        return pT[i], BpT[i]

    B = {}

    def bf(name):
        if name not in B:
            B[name] = Buf(name)
        return B[name]

    Xd = [Buf(f"xd{n}") for n in range(NSUB)]
    Md = [Buf(f"md{i}") for i in range(NTILE)]
    BKT = [Buf(f"kt{n}") for n in range(NSUB)]
    BV = [Buf(f"v{n}") for n in range(NSUB)]

    slot = [0]

    def new_slot():
        k = slot[0] % 16
        slot[0] += 1
        return stt_[:, 4 * k:4 * k + 4], bf(f"slot{k}")

    def rstd_from_ss(sl, Bsl, ncols, Dn):
        if ncols == 2:
            S.op("pool", lambda e: e.tensor_tensor(out=sl[:, 0:1], in0=sl[:, 0:1], in1=sl[:, 1:2], op=ALU.add), reads=[Bsl], writes=[Bsl])
        S.op("pool", lambda e: e.tensor_scalar(out=sl[:, 2:3], in0=sl[:, 0:1], scalar1=1.0 / Dn, scalar2=EPS, op0=ALU.mult, op1=ALU.add), reads=[Bsl], writes=[Bsl])
        S.op("pool", lambda e: e.tensor_tensor(out=sl[:, 3:4], in0=sl[:, 2:3], in1=mhalf[:, 0:1], op=ALU.pow), reads=[Bsl, bf("mhalf")], writes=[Bsl])
        return sl[:, 3:4]

    def transpose_to(src_ap_fn, nchunk, rows, Bsrc, dst_ap, Bdst, gcol0=None, eng="dve"):
        S.begin_atom()
        p, Bp = next_pT()
        for c in range(nchunk):
            S.op("pe", lambda e, c=c: e.transpose(out=p[0:rows, c, :], in_=src_ap_fn(c), identity=ident[:]),
                 reads=[Bsrc, bf("ident")], writes=[Bp], inc=(c == nchunk - 1))
        if gcol0 is None:
            if eng == "act":
                S.op("act", lambda e: e.activation(out=dst_ap, in_=p[0:rows, 0:nchunk, :], func=AF.Copy), reads=[Bp], writes=[Bdst])
            else:
                S.op(eng, lambda e: e.tensor_copy(out=dst_ap, in_=p[0:rows, 0:nchunk, :]), reads=[Bp], writes=[Bdst])
        else:
            g = gT[0:rows, gcol0:gcol0 + nchunk].unsqueeze(2).broadcast_to([rows, nchunk, 128])
            S.op(eng, lambda e: e.tensor_tensor(out=dst_ap, in0=p[0:rows, 0:nchunk, :], in1=g, op=ALU.mult), reads=[Bp, bf("gT")], writes=[Bdst])
        S.end_atom()

    S.op("pool", lambda e: e.memset(ident[:], 1.0), writes=[bf("ident")])
    S.op("pool", lambda e: e.affine_select(out=ident[:], in_=ident[:], pattern=[[-1, 128]], compare_op=ALU.is_equal, fill=0.0, base=0, channel_multiplier=1),
         reads=[bf("ident")], writes=[bf("ident")])
    S.op("pool", lambda e: e.memset(tri[:], 1.0), writes=[bf("tri")])
    S.op("pool", lambda e: e.affine_select(out=tri[:], in_=tri[:], pattern=[[1, 128]], compare_op=ALU.is_ge, fill=0.0, base=0, channel_multiplier=-1),
         reads=[bf("tri")], writes=[bf("tri")])
    S.op("pool", lambda e: e.memset(mhalf[:], -0.5), writes=[bf("mhalf")])
    S.dma("sp", lambda e: e.dma_start(out=gT[:], in_=gT_d), writes=[bf("gT")])
    S.dma("sp", lambda e: e.dma_start(out=invf[:], in_=invf_d), writes=[bf("invf")])
    S.dma("sp", lambda e: e.dma_start(out=posi[:], in_=pos_d), writes=[bf("posi")])
    TWO_PI = 2.0 * math.pi
    C1 = float(np.float32(6.28125))
    C2 = float(np.float32(TWO_PI - 6.28125))
    MAGIC = 12582912.0
    ang = sb("ang", [128, NSUB, 16], F32, BIG + 0)
    kk = sb("kk", [128, NSUB, 16], F32, BIG + 2048)
    t2 = sb("t2p", [128, NSUB, 16], F32, BIG + 4096)
    S.op("dve", lambda e: e.tensor_copy(out=posf[:], in_=posi[:]), reads=[bf("posi")], writes=[bf("posf")])
    S.op("dve", lambda e: e.tensor_tensor(out=ang[:], in0=posf[:].unsqueeze(2).broadcast_to([128, NSUB, 16]),
                                          in1=invf[:].unsqueeze(1).broadcast_to([128, NSUB, 16]), op=ALU.mult),
         reads=[bf("posf"), bf("invf")], writes=[bf("ang")])
    S.op("dve", lambda e: e.tensor_scalar(out=t2[:], in0=ang[:], scalar1=1.0 / TWO_PI, scalar2=MAGIC, op0=ALU.mult, op1=ALU.add), reads=[bf("ang")], writes=[bf("t2")])
    S.op("dve", lambda e: e.tensor_scalar(out=kk[:], in0=t2[:], scalar1=-MAGIC, scalar2=None, op0=ALU.add), reads=[bf("t2")], writes=[bf("kk")])
    S.op("dve", lambda e: e.scalar_tensor_tensor(out=ang[:], in0=kk[:], scalar=-C1, in1=ang[:], op0=ALU.mult, op1=ALU.add), reads=[bf("kk"), bf("ang")], writes=[bf("ang")])
    S.op("dve", lambda e: e.scalar_tensor_tensor(out=ang[:], in0=kk[:], scalar=-C2, in1=ang[:], op0=ALU.mult, op1=ALU.add), reads=[bf("kk"), bf("ang")], writes=[bf("ang")])
    S.op("dve", lambda e: e.tensor_scalar(out=ang[:], in0=ang[:], scalar1=math.pi, scalar2=-math.pi, op0=ALU.min, op1=ALU.max), reads=[bf("ang")], writes=[bf("ang")])
    S.op("act", lambda e: e.activation(out=sinT[:], in_=ang[:], func=AF.Sin), reads=[bf("ang")], writes=[bf("sin")])
    S.op("dve", lambda e: e.tensor_scalar(out=t2[:], in0=ang[:], scalar1=-1.0, scalar2=None, op0=ALU.mult), reads=[bf("ang")], writes=[bf("t2")])
    S.op("dve", lambda e: e.tensor_tensor(out=t2[:], in0=t2[:], in1=ang[:], op=ALU.max), reads=[bf("ang"), bf("t2")], writes=[bf("t2")])
    S.op("dve", lambda e: e.tensor_scalar(out=t2[:], in0=t2[:], scalar1=-1.0, scalar2=math.pi / 2, op0=ALU.mult, op1=ALU.add), reads=[bf("t2")], writes=[bf("t2")])
    S.op("act", lambda e: e.activation(out=cosT[:], in_=t2[:], func=AF.Sin), reads=[bf("t2")], writes=[bf("cos")])

    stA = [bf("z0"), bf("z1"), bf("z2"), bf("z3"), bf("uv"), bf("ysh1"), bf("ysh2"), bf("acc"), bf("sq"), bf("vn32"), bf("cqkv")]
    stB = [bf("ya"), bf("PT0"), bf("PT1"), bf("PT2"), bf("PT3")]
    rot = [0]

    def mixer_pass(l):
        xsrc = x_d if l == 0 else out_d
        g0 = l * GW
        S.barrier()
        for k in range(8):
            S.dma("pool", lambda e, k=k: e.dma_start(out=w_in[:, k, :], in_=w_in_d[l, k * 128:(k + 1) * 128, :]), writes=[bf(f"w_in{k}")])
        for k in range(3):
            S.dma("pool", lambda e, k=k: e.dma_start(out=w_uq[:, k, :], in_=w_uq_d[l, k * 128:(k + 1) * 128, :]), writes=[bf("w_uq")])
        for k in range(2):
            S.dma("pool", lambda e, k=k: e.dma_start(out=w_ukv[:, k, :], in_=w_ukv_d[l, k * 128:(k + 1) * 128, :]), writes=[bf("w_ukv")])
        S.dma("pool", lambda e: e.dma_start(out=wspT[:].rearrange("p g t -> p (g t)"), in_=wsp_d[l]), writes=[bf("wspT")])
        S.dma("sp", lambda e: e.dma_start(out=sgln[:], in_=bc_d[l, :, 2048:2560]), writes=[bf("sgln")])
        S.dma("sp", lambda e: e.dma_start(out=convw[:], in_=bc_d[l, :, 2560:3328]), writes=[bf("convw")])
        S.op("dve", lambda e: e.tensor_tensor(out=wspT[:], in0=wspT[:], in1=tri[:].unsqueeze(1).broadcast_to([128, 4, 128]), op=ALU.mult),
             reads=[bf("wspT"), bf("tri")], writes=[bf("wspT")])
        S.op("pool", lambda e: e.memset(Vc[:, :, :, 64:65], 1.0), writes=BV)
        S.op("pool", lambda e: e.memset(ycv[1][:], 0.0), writes=[bf("ycv1")])

        for i in range(DBG_T):
            S.handoff(stB, stA)
            for s in range(4):
                n = 4 * i + s
                t0 = n * 128
                hb = hT[n % 2]
                Bh = bf(f"hT{n % 2}")
                S.dma("sp", lambda e: e.dma_start(out=xs[:], in_=xsrc[t0:t0 + 128, :]), reads=[Xd[n]], writes=[bf("xs")])
                sl, Bsl = new_slot()
                S.op("act", lambda e: e.activation(out=xn[:], in_=xs[:], func=AF.Square, accum_out=sl[:, 0:1]), reads=[bf("xs")], writes=[bf("xn"), Bsl])
                r = rstd_from_ss(sl, Bsl, 1, D)
                S.op("dve", lambda e: e.tensor_scalar(out=xn[:], in0=xs[:], scalar1=r, scalar2=None, op0=ALU.mult), reads=[bf("xs"), Bsl], writes=[bf("xn")])
                transpose_to(lambda c: xn[:, c * 128:(c + 1) * 128], 8, 128, bf("xn"), hb[:], Bh, gcol0=g0 + 0)
                cg = [(0, 512), (512, 1024), (1024, 1536), (1536, INW)]
                for k in range(8):
                    for q, (c0, c1) in enumerate(cg):
                        S.op("pe", lambda e, k=k, q=q, c0=c0, c1=c1: e.matmul(bk[q][:, 0:c1 - c0], lhsT=hb[:, k, :], rhs=w_in[:, k, c0:c1], start=(k == 0), stop=(k == 7)),
                             reads=[Bh, bf(f"w_in{k}")], writes=[Bbk[q]], inc=(k == 7 and q == 3))
                for q, (c0, c1) in enumerate(cg):
                    if q % 2 == 0:
                        S.op("act", lambda e, q=q, c0=c0, c1=c1: e.activation(out=z_sb[:, c0:c1], in_=bk[q][:, 0:c1 - c0], func=AF.Copy), reads=[Bbk[q]], writes=[bf(f"z{q}")])
                    else:
                        S.op("dve", lambda e, q=q, c0=c0, c1=c1: e.tensor_copy(out=z_sb[:, c0:c1], in_=bk[q][:, 0:c1 - c0]), reads=[Bbk[q]], writes=[bf(f"z{q}")])
                def br_mla():
                    slq, Bq = new_slot()
                    slk, Bk_ = new_slot()
                    S.op("act", lambda e: e.activation(out=cqkv[:, 0:384], in_=z_sb[:, 0:384], func=AF.Square, accum_out=slq[:, 0:1]), reads=[bf("z0")], writes=[bf("cqkv"), Bq])
                    S.op("act", lambda e: e.activation(out=cqkv[:, 384:640], in_=z_sb[:, 384:640], func=AF.Square, accum_out=slk[:, 0:1]), reads=[bf("z0"), bf("z1")], writes=[bf("cqkv"), Bk_])
                    rq = rstd_from_ss(slq, Bq, 1, 384)
                    rk = rstd_from_ss(slk, Bk_, 1, 256)
                    S.op("dve", lambda e: e.tensor_scalar(out=cqkv[:, 0:384], in0=z_sb[:, 0:384], scalar1=rq, scalar2=None, op0=ALU.mult), reads=[bf("z0"), Bq], writes=[bf("cqkv")])
                    S.op("dve", lambda e: e.tensor_scalar(out=cqkv[:, 384:640], in0=z_sb[:, 384:640], scalar1=rk, scalar2=None, op0=ALU.mult), reads=[bf("z0"), bf("z1"), Bk_], writes=[bf("cqkv")])
                    transpose_to(lambda c: cqkv[:, c * 128:(c + 1) * 128], 5, 128, bf("cqkv"), cqT[:], bf("cqT"), gcol0=g0 + 24)
                    S.begin_atom()
                    for k in range(3):
                        S.op("pe", lambda e, k=k: e.matmul(bk[4][:, 0:480], lhsT=cqT[:, k, :], rhs=w_uq[:, k, 0:480], start=(k == 0), stop=(k == 2)),
                             reads=[bf("cqT"), bf("w_uq")], writes=[Bbk[4]], inc=False)
                        S.op("pe", lambda e, k=k: e.matmul(bk[5][:, 0:288], lhsT=cqT[:, k, :], rhs=w_uq[:, k, 480:768], start=(k == 0), stop=(k == 2)),
                             reads=[bf("cqT"), bf("w_uq")], writes=[Bbk[5]], inc=(k == 2))
                    S.end_atom()
                    S.begin_atom()
                    for k in range(2):
                        S.op("pe", lambda e, k=k: e.matmul(bk[0][:, :], lhsT=cqT[:, 3 + k, :], rhs=w_ukv[:, k, 0:512], start=(k == 0), stop=(k == 1)),
                             reads=[bf("cqT"), bf("w_ukv")], writes=[Bbk[0]], inc=False)
                        S.op("pe", lambda e, k=k: e.matmul(bk[1][:, :], lhsT=cqT[:, 3 + k, :], rhs=w_ukv[:, k, 512:1024], start=(k == 0), stop=(k == 1)),
                             reads=[bf("cqT"), bf("w_ukv")], writes=[Bbk[1]], inc=(k == 1))
                    S.end_atom()
                    q4 = bk[4][:, 0:480].rearrange("p (h c) -> p h c", c=96)
                    q5 = bk[5][:, 0:288].rearrange("p (h c) -> p h c", c=96)
                    k0 = bk[0][:, :].rearrange("p (h c) -> p h c", c=128)
                    k1 = bk[1][:, :].rearrange("p (h c) -> p h c", c=128)
                    S.op("act", lambda e: e.activation(out=q_tm[:, 0:5, 0:64], in_=q4[:, :, 0:64], func=AF.Copy), reads=[Bbk[4]], writes=[bf("q_tm")])
                    S.op("act", lambda e: e.activation(out=q_tm[:, 5:8, 0:64], in_=q5[:, :, 0:64], func=AF.Copy), reads=[Bbk[5]], writes=[bf("q_tm")])
                    S.op("dve", lambda e: e.tensor_copy(out=rp[:, 0:5, :], in_=q4[:, :, 64:96]), reads=[Bbk[4]], writes=[bf("rp")])
                    S.op("dve", lambda e: e.tensor_copy(out=rp[:, 5:8, :], in_=q5[:, :, 64:96]), reads=[Bbk[5]], writes=[bf("rp")])
                    S.op("dve", lambda e: e.tensor_copy(out=rp[:, 8, :], in_=z_sb[:, 640:672]), reads=[bf("z1")], writes=[bf("rp")])
                    cs = cosT[:, n, :].unsqueeze(1).broadcast_to([128, 9, 16])
                    sn = sinT[:, n, :].unsqueeze(1).broadcast_to([128, 9, 16])
                    S.op("dve", lambda e: e.tensor_tensor(out=rt[0][:], in0=rp[:, :, 0:16], in1=cs, op=ALU.mult), reads=[bf("rp"), bf("cos")], writes=[bf("rt0")])
                    S.op("dve", lambda e: e.tensor_tensor(out=rt[1][:], in0=rp[:, :, 16:32], in1=sn, op=ALU.mult), reads=[bf("rp"), bf("sin")], writes=[bf("rt1")])
                    S.op("dve", lambda e: e.tensor_tensor(out=rt[2][:], in0=rp[:, :, 16:32], in1=cs, op=ALU.mult), reads=[bf("rp"), bf("cos")], writes=[bf("rt2")])
                    S.op("dve", lambda e: e.tensor_tensor(out=rt[3][:], in0=rp[:, :, 0:16], in1=sn, op=ALU.mult), reads=[bf("rp"), bf("sin")], writes=[bf("rt3")])
                    S.op("dve", lambda e: e.tensor_tensor(out=rr[:, :, 0:16], in0=rt[0][:], in1=rt[1][:], op=ALU.subtract), reads=[bf("rt0"), bf("rt1")], writes=[bf("rr")])
                    S.op("dve", lambda e: e.tensor_tensor(out=rr[:, :, 16:32], in0=rt[2][:], in1=rt[3][:], op=ALU.add), reads=[bf("rt2"), bf("rt3")], writes=[bf("rr")])
                    S.op("dve", lambda e: e.tensor_copy(out=q_tm[:, :, 64:96], in_=rr[:, 0:8, :]), reads=[bf("rr")], writes=[bf("q_tm")])
                    S.op("dve", lambda e: e.tensor_copy(out=k_tm[:, :, 64:96], in_=rr[:, 8:9, :].broadcast_to([128, 8, 32])), reads=[bf("rr")], writes=[bf("k_tm")])
                    S.op("act", lambda e: e.activation(out=k_tm[:, 0:4, 0:64], in_=k0[:, :, 0:64], func=AF.Copy), reads=[Bbk[0]], writes=[bf("k_tm")])
                    S.op("dve", lambda e: e.tensor_copy(out=k_tm[:, 4:8, 0:64], in_=k1[:, :, 0:64]), reads=[Bbk[1]], writes=[bf("k_tm")])
                    S.op("act", lambda e: e.activation(out=Vc[:, n, 0:4, 0:64], in_=k0[:, :, 64:128], func=AF.Copy), reads=[Bbk[0]], writes=[BV[n]])
                    S.op("dve", lambda e: e.tensor_copy(out=Vc[:, n, 4:8, 0:64], in_=k1[:, :, 64:128]), reads=[Bbk[1]], writes=[BV[n]])
                    transpose_to(lambda h: q_tm[:, h, :], 8, 96, bf("q_tm"), qT[0:96, :, s * 128:(s + 1) * 128], bf("qT"), eng="act")
                    transpose_to(lambda h: k_tm[:, h, :], 8, 96, bf("k_tm"), KT[0:96, :, t0:t0 + 128], BKT[n], eng="dve")

                def br_sgu():
                    S.op("act", lambda e: e.activation(out=uv[:], in_=z_sb[:, 672:1184], func=AF.Gelu_apprx_tanh), reads=[bf("z1"), bf("z2")], writes=[bf("uv")])
                    v3 = uv[:, 256:512].rearrange("p (g e) -> p g e", g=4)
                    Bsv = bf("stv")
                    S.op("dve", lambda e: e.tensor_reduce(out=stv[:, 0:4], in_=v3, axis=AX.X, op=ALU.add), reads=[bf("uv")], writes=[Bsv])
                    S.op("pool", lambda e: e.tensor_tensor(out=sq[:], in0=uv[:, 256:512], in1=uv[:, 256:512], op=ALU.mult), reads=[bf("uv")], writes=[bf("sq")])
                    S.op("dve", lambda e: e.tensor_reduce(out=stv[:, 4:8], in_=sq[:].rearrange("p (g e) -> p g e", g=4), axis=AX.X, op=ALU.add), reads=[bf("sq")], writes=[Bsv])
                    S.op("pool", lambda e: e.tensor_scalar(out=stv[:, 8:12], in0=stv[:, 0:4], scalar1=1.0 / 64, scalar2=None, op0=ALU.mult), reads=[Bsv], writes=[Bsv])
                    S.op("pool", lambda e: e.tensor_tensor(out=stv[:, 12:16], in0=stv[:, 8:12], in1=stv[:, 8:12], op=ALU.mult), reads=[Bsv], writes=[Bsv])
                    S.op("pool", lambda e: e.tensor_scalar(out=stv[:, 24:28], in0=stv[:, 4:8], scalar1=1.0 / 64, scalar2=EPS, op0=ALU.mult, op1=ALU.add), reads=[Bsv], writes=[Bsv])
                    S.op("pool", lambda e: e.tensor_tensor(out=stv[:, 16:20], in0=stv[:, 24:28], in1=stv[:, 12:16], op=ALU.subtract), reads=[Bsv], writes=[Bsv])
                    S.op("pool", lambda e: e.tensor_tensor(out=stv[:, 20:24], in0=stv[:, 16:20], in1=mhalf[:, 0:4], op=ALU.pow), reads=[Bsv, bf("mhalf")], writes=[Bsv])
                    vn3 = vn32[:].rearrange("p (g e) -> p g e", g=4)
                    S.op("dve", lambda e: e.tensor_tensor(out=vn3, in0=v3, in1=stv[:, 8:12].unsqueeze(2).broadcast_to([128, 4, 64]), op=ALU.subtract), reads=[bf("uv"), Bsv], writes=[bf("vn32")])
                    S.op("dve", lambda e: e.tensor_tensor(out=vn3, in0=vn3, in1=stv[:, 20:24].unsqueeze(2).broadcast_to([128, 4, 64]), op=ALU.mult), reads=[bf("vn32"), Bsv], writes=[bf("vn32")])
                    S.op("pool", lambda e: e.tensor_tensor(out=vn32[:], in0=vn32[:], in1=sgln[:, 0:256], op=ALU.mult), reads=[bf("vn32"), bf("sgln")], writes=[bf("vn32")])
                    S.op("pool", lambda e: e.tensor_tensor(out=vn[:], in0=vn32[:], in1=sgln[:, 256:512], op=ALU.add), reads=[bf("vn32"), bf("sgln")], writes=[bf("vn")])
                    S.begin_atom()
                    for g in range(4):
                        S.op("pe", lambda e, g=g: e.matmul(bk[2][:, g * 64:(g + 1) * 64], lhsT=wspT[:, g, :], rhs=vn[:, g * 64:(g + 1) * 64], start=True, stop=True, skip_group_check=True),
                             reads=[bf("wspT"), bf("vn")], writes=[Bbk[2]], inc=(g == 3))
                    S.end_atom()
                    for g in range(4):
                        S.op("dve", lambda e, g=g: e.scalar_tensor_tensor(out=sq[:, g * 64:(g + 1) * 64], in0=bk[2][:, g * 64:(g + 1) * 64], scalar=gT[:, g0 + 29 + g:g0 + 30 + g],
                                                                          in1=uv[:, g * 64:(g + 1) * 64], op0=ALU.add, op1=ALU.mult),
                             reads=[Bbk[2], bf("gT"), bf("uv")], writes=[bf("sq")])
                    slb, Bb_ = new_slot()
                    S.op("act", lambda e: e.activation(out=mix_tm[:, s, 512:768], in_=sq[:], func=AF.Square, accum_out=slb[:, 0:1]), reads=[bf("sq")], writes=[bf(f"mix{s}"), Bb_])
                    rb = rstd_from_ss(slb, Bb_, 1, 256)
                    S.op("dve", lambda e: e.tensor_scalar(out=mix_tm[:, s, 512:768], in0=sq[:], scalar1=rb, scalar2=None, op0=ALU.mult), reads=[bf("sq"), Bb_], writes=[bf(f"mix{s}")])

                def br_conv():
                    yc, yp = ycv[n % 2], ycv[(n + 1) % 2]
                    Byc, Byp = bf(f"ycv{n % 2}"), bf(f"ycv{(n + 1) % 2}")
                    S.op("pool", lambda e: e.tensor_tensor(out=yc[:], in0=z_sb[:, 1440:1696], in1=z_sb[:, 1696:1952], op=ALU.mult), reads=[bf("z2"), bf("z3")], writes=[Byc])
                    S.dma("sp", lambda e: e.dma_start(out=ysh1[1:128, :], in_=yc[0:127, :]), reads=[Byc], writes=[bf("ysh1")])
                    S.dma("sp", lambda e: e.dma_start(out=ysh1[0:1, :], in_=yp[127:128, :]), reads=[Byp], writes=[bf("ysh1")])
                    S.dma("sp", lambda e: e.dma_start(out=ysh2[2:128, :], in_=yc[0:126, :]), reads=[Byc], writes=[bf("ysh2")])
                    S.dma("sp", lambda e: e.dma_start(out=ysh2[0:2, :], in_=yp[126:128, :]), reads=[Byp], writes=[bf("ysh2")])
                    S.op("pool", lambda e: e.tensor_tensor(out=acc[:], in0=yc[:], in1=convw[:, 512:768], op=ALU.mult), reads=[Byc, bf("convw")], writes=[bf("acc")])
                    S.op("pool", lambda e: e.tensor_tensor(out=ysh1[:], in0=ysh1[:], in1=convw[:, 256:512], op=ALU.mult), reads=[bf("ysh1"), bf("convw")], writes=[bf("ysh1")])
                    S.op("pool", lambda e: e.tensor_tensor(out=acc[:], in0=acc[:], in1=ysh1[:], op=ALU.add), reads=[bf("acc"), bf("ysh1")], writes=[bf("acc")])
                    S.op("pool", lambda e: e.tensor_tensor(out=ysh2[:], in0=ysh2[:], in1=convw[:, 0:256], op=ALU.mult), reads=[bf("ysh2"), bf("convw")], writes=[bf("ysh2")])
                    S.op("pool", lambda e: e.tensor_tensor(out=acc[:], in0=acc[:], in1=ysh2[:], op=ALU.add), reads=[bf("acc"), bf("ysh2")], writes=[bf("acc")])
                    S.op("pool", lambda e: e.tensor_tensor(out=acc[:], in0=acc[:], in1=z_sb[:, 1184:1440], op=ALU.mult), reads=[bf("acc"), bf("z2")], writes=[bf("acc")])
                    slc, Bc_ = new_slot()
                    S.op("act", lambda e: e.activation(out=mix_tm[:, s, 768:1024], in_=acc[:], func=AF.Square, accum_out=slc[:, 0:1]), reads=[bf("acc")], writes=[bf(f"mix{s}"), Bc_])
                    rc = rstd_from_ss(slc, Bc_, 1, 256)
                    S.op("dve", lambda e: e.tensor_scalar(out=mix_tm[:, s, 768:1024], in0=acc[:], scalar1=rc, scalar2=None, op0=ALU.mult), reads=[bf("acc"), Bc_], writes=[bf(f"mix{s}")])


                S.replay(S.interleave(S.record(br_mla), S.record(br_sgu), S.record(br_conv)))
            if DBG_STAGE in ('A', 'A0'):
                continue
            S.handoff(stA, stB)
            nkb = 4 * i + 4
            LA = 3

            def att_front(h, kb):
                j0 = max(0, kb - 4 * i)
                c0 = j0 * 128
                r = rot[0] % 4
                rot[0] += 1
                sbk, Bs = bk[r], Bbk[r]
                S.op("pe", lambda e: e.matmul(sbk[:, c0:512], lhsT=KT[0:96, h, kb * 128:(kb + 1) * 128], rhs=qT[0:96, h, c0:512], start=True, stop=True),
                     reads=[BKT[kb], bf("qT")], writes=[Bs])
                S.op("act", lambda e: e.activation(out=PT[r][:, c0:512], in_=sbk[:, c0:512], func=AF.Exp, scale=SCALE), reads=[Bs], writes=[bf(f"PT{r}")])
                if kb >= 4 * i:
                    S.op("pool", lambda e: e.tensor_tensor(out=PT[r][:, c0:c0 + 128], in0=PT[r][:, c0:c0 + 128], in1=tri[:], op=ALU.mult),
                         reads=[bf(f"PT{r}"), bf("tri")], writes=[bf(f"PT{r}")])
                return r

            def att_back(h, kb, r):
                j0 = max(0, kb - 4 * i)
                O = bk[4 + h % 2]
                BO = Bbk[4 + h % 2]
                O3 = O[:, :].rearrange("p (j c) -> p j c", j=4)
                for j in range(j0, 4):
                    S.op("pe", lambda e, j=j: e.matmul(O3[:, j, 0:65], lhsT=PT[r][:, j * 128:(j + 1) * 128], rhs=Vc[:, kb, h, :],
                                                       start=(kb == 0 and j == 0), stop=(kb == 4 * i + j), skip_group_check=True),
                         reads=[bf(f"PT{r}"), BV[kb]], writes=[BO], inc=(j == 3))
                if kb == nkb - 1:
                    S.op("dve", lambda e: e.reciprocal(out=rinv[:].unsqueeze(2), in_=O3[:, :, 64:65]), reads=[BO], writes=[bf("rinv")])
                    S.op("dve", lambda e: e.tensor_tensor(out=ya[:, :, h * 64:(h + 1) * 64], in0=O3[:, :, 0:64], in1=rinv[:].unsqueeze(2).broadcast_to([128, 4, 64]), op=ALU.mult),
                         reads=[BO, bf("rinv")], writes=[bf("ya")])

            inflight = []
            for h in range(H):
                for kb in range(nkb):
                    inflight.append((h, kb, att_front(h, kb)))
                    if len(inflight) > LA:
                        att_back(*inflight.pop(0))
            while inflight:
                att_back(*inflight.pop(0))
            for j in range(4):
                sla, Ba_ = new_slot()
                S.op("act", lambda e, j=j: e.activation(out=mix_tm[:, j, 0:512], in_=ya[:, j, :], func=AF.Square, accum_out=sla[:, 0:1]), reads=[bf("ya")], writes=[bf(f"mix{j}"), Ba_])
                ra = rstd_from_ss(sla, Ba_, 1, 512)
                S.op("dve", lambda e, j=j: e.tensor_scalar(out=mix_tm[:, j, 0:512], in0=ya[:, j, :], scalar1=ra, scalar2=None, op0=ALU.mult), reads=[bf("ya"), Ba_], writes=[bf(f"mix{j}")])
            S.dma("sp", lambda e: e.dma_start(out=mix_d[i * 512:(i + 1) * 512, :].rearrange("(j p) d -> p j d", p=128), in_=mix_tm[:]),
                  reads=[bf("mix0"), bf("mix1"), bf("mix2"), bf("mix3")], writes=[Md[i]])

    def ffn_pass(l):
        xsrc = x_d if l == 0 else out_d
        g0 = l * GW
        S.barrier()
        for k in range(8):
            S.dma("pool", lambda e, k=k: e.dma_start(out=w_out[:, k, :], in_=w_out_d[l, k * 128:(k + 1) * 128, :]), writes=[bf("w_out")])
        S.dma("sp", lambda e: e.dma_start(out=postg_m[:], in_=bc_d[l, :, 0:1024]), writes=[bf("postg_m")])
        S.dma("sp", lambda e: e.dma_start(out=postg_f[:], in_=bc_d[l, :, 1024:2048]), writes=[bf("postg_f")])
        for k in range(8):
            S.dma("pool", lambda e, k=k: e.dma_start(out=w_gate[:, k, :], in_=w_gate_d[l, k * 128:(k + 1) * 128, :]), writes=[bf(f"w_gate{k}")])
            S.dma("pool", lambda e, k=k: e.dma_start(out=w_up[:, k, :], in_=w_up_d[l, k * 128:(k + 1) * 128, :]), writes=[bf(f"w_up{k}")])
        for c in range(NFF):
            S.dma("pool", lambda e, c=c: e.dma_start(out=w_down[:, c, :], in_=w_down_d[l, c * 128:(c + 1) * 128, :]), writes=[bf(f"w_down{c}")])

        xsP, BxsP, Bm0P = t_f, bf("xsP"), bf("m0P")

        def post_norm_inplace(postg, Bpostg, ba, bb, junk, Bjunk):
            sl, Bsl = new_slot()
            S.op("act", lambda e: e.activation(out=junk[:, 0:512], in_=bk[ba][:, :], func=AF.Square, accum_out=sl[:, 0:1]), reads=[Bbk[ba]], writes=[Bjunk, Bsl])
            S.op("act", lambda e: e.activation(out=junk[:, 512:1024], in_=bk[bb][:, :], func=AF.Square, accum_out=sl[:, 1:2]), reads=[Bbk[bb]], writes=[Bjunk, Bsl])
            r = rstd_from_ss(sl, Bsl, 2, D)
            S.op("dve", lambda e: e.scalar_tensor_tensor(out=bk[ba][:, :], in0=bk[ba][:, :], scalar=r, in1=postg[:, 0:512], op0=ALU.mult, op1=ALU.mult),
                 reads=[Bsl, Bpostg], writes=[Bbk[ba]])
            S.op("dve", lambda e: e.scalar_tensor_tensor(out=bk[bb][:, :], in0=bk[bb][:, :], scalar=r, in1=postg[:, 512:1024], op0=ALU.mult, op1=ALU.mult),
                 reads=[Bsl, Bpostg], writes=[Bbk[bb]])

        def add_banks(xt, Bxt, ba, bb):
            S.op("dve", lambda e: e.tensor_tensor(out=xt[:, 0:512], in0=bk[ba][:, :], in1=xt[:, 0:512], op=ALU.add), reads=[Bbk[ba], Bxt], writes=[Bxt])
            S.op("dve", lambda e: e.tensor_tensor(out=xt[:, 512:1024], in0=bk[bb][:, :], in1=xt[:, 512:1024], op=ALU.add), reads=[Bbk[bb], Bxt], writes=[Bxt])

        def P1(i, s):
            n = 4 * i + s
            t0 = n * 128
            S.dma("sp", lambda e: e.dma_start(out=m0P[:], in_=mix_d[t0:t0 + 128, :]), reads=[Md[i]], writes=[Bm0P])
            S.dma("sp", lambda e: e.dma_start(out=xsP[:], in_=xsrc[t0:t0 + 128, :]), reads=[Xd[n]], writes=[BxsP])
            transpose_to(lambda c: m0P[:, c * 128:(c + 1) * 128], 8, 128, Bm0P, mixT[:], bf("mixT"), gcol0=g0 + 16)
            S.begin_atom()
            for k in range(8):
                S.op("pe", lambda e, k=k: e.matmul(bk[4][:, :], lhsT=mixT[:, k, :], rhs=w_out[:, k, 0:512], start=(k == 0), stop=(k == 7)),
                     reads=[bf("mixT"), bf("w_out")], writes=[Bbk[4]], inc=False)
                S.op("pe", lambda e, k=k: e.matmul(bk[5][:, :], lhsT=mixT[:, k, :], rhs=w_out[:, k, 512:1024], start=(k == 0), stop=(k == 7)),
                     reads=[bf("mixT"), bf("w_out")], writes=[Bbk[5]], inc=(k == 7))
            S.end_atom()
            post_norm_inplace(postg_m, bf("postg_m"), 4, 5, m0P, Bm0P)
            add_banks(xsP, BxsP, 4, 5)
            S.dma("sp", lambda e: e.dma_start(out=out_d[t0:t0 + 128, :], in_=xsP[:]), reads=[BxsP], writes=[Xd[n]])

        def P2(i, s):
            n = 4 * i + s
            t0 = n * 128
            S.dma("sp", lambda e: e.dma_start(out=xsP[:], in_=out_d[t0:t0 + 128, :]), reads=[Xd[n]], writes=[BxsP])
            sl, Bsl = new_slot()
            S.op("act", lambda e: e.activation(out=m0P[:], in_=xsP[:], func=AF.Square, accum_out=sl[:, 0:1]), reads=[BxsP], writes=[Bm0P, Bsl])
            r = rstd_from_ss(sl, Bsl, 1, D)
            S.op("dve", lambda e: e.tensor_scalar(out=m0P[:], in0=xsP[:], scalar1=r, scalar2=None, op0=ALU.mult), reads=[BxsP, Bsl], writes=[Bm0P])
            transpose_to(lambda c: m0P[:, c * 128:(c + 1) * 128], 8, 128, Bm0P, h2T[:, :, s * 128:(s + 1) * 128], bf("h2T"), gcol0=g0 + 8)

        def G(i):
            for c in range(NFF):
                gb, Bg = bk[c % 2], Bbk[c % 2]
                ub, Bu = bk[2 + c % 2], Bbk[2 + c % 2]
                S.begin_atom()
                for k in range(8):
                    S.op("pe", lambda e, k=k, c=c, gb=gb: e.matmul(gb[:, :], lhsT=w_gate[:, k, c * 128:(c + 1) * 128], rhs=h2T[:, k, :], start=(k == 0), stop=(k == 7)),
                         reads=[bf(f"w_gate{k}"), bf("h2T")], writes=[Bg], inc=(k == 7))
                S.end_atom()
                S.begin_atom()
                for k in range(8):
                    S.op("pe", lambda e, k=k, c=c, ub=ub: e.matmul(ub[:, :], lhsT=w_up[:, k, c * 128:(c + 1) * 128], rhs=h2T[:, k, :], start=(k == 0), stop=(k == 7)),
                         reads=[bf(f"w_up{k}"), bf("h2T")], writes=[Bu], inc=(k == 7))
                S.end_atom()
                sg, Bsg = sg_f[c % 2], bf(f"xn_f{c % 2}")
                S.op("act", lambda e, gb=gb, sg=sg: e.activation(out=sg[:], in_=gb[:, :], func=AF.Silu), reads=[Bg], writes=[Bsg])
                S.op("dve", lambda e, c=c, ub=ub, sg=sg: e.tensor_tensor(out=aT[:, c, :], in0=ub[:, :], in1=sg[:], op=ALU.mult), reads=[Bu, Bsg], writes=[bf(f"aT{c}")])

        def Dn(i):
            for s in range(4):
                n = 4 * i + s
                t0 = n * 128
                ba, bb = [(0, 1), (2, 3)][s % 2]
                S.begin_atom()
                for c in range(NFF):
                    S.op("pe", lambda e, c=c, s=s, ba=ba: e.matmul(bk[ba][:, :], lhsT=aT[:, c, s * 128:(s + 1) * 128], rhs=w_down[:, c, 0:512], start=(c == 0), stop=(c == NFF - 1)),
                         reads=[bf(f"aT{c}"), bf(f"w_down{c}")], writes=[Bbk[ba]], inc=False)
                    S.op("pe", lambda e, c=c, s=s, bb=bb: e.matmul(bk[bb][:, :], lhsT=aT[:, c, s * 128:(s + 1) * 128], rhs=w_down[:, c, 512:1024], start=(c == 0), stop=(c == NFF - 1)),
                         reads=[bf(f"aT{c}"), bf(f"w_down{c}")], writes=[Bbk[bb]], inc=(c == NFF - 1))
                S.end_atom()
                post_norm_inplace(postg_f, bf("postg_f"), ba, bb, xn_f[s % 2], bf(f"xn_f{s % 2}"))
                S.dma("sp", lambda e, t0=t0: e.dma_start(out=xs_f[:], in_=out_d[t0:t0 + 128, :]), reads=[Xd[n]], writes=[bf("xs_f")])
                add_banks(xs_f, bf("xs_f"), ba, bb)
                S.dma("sp", lambda e, t0=t0: e.dma_start(out=out_d[t0:t0 + 128, :], in_=xs_f[:]), reads=[bf("xs_f")], writes=[Xd[n]])

        ntl = DBG_T if DBG_STAGE == 'full' else 0
        if ntl:
            for s in range(4):
                P1(0, s)
            for s in range(4):
                P2(0, s)
        for i in range(ntl):
            sG = S.record(lambda: G(i))
            sP1 = S.record(lambda: [P1(i + 1, s) for s in range(4)]) if i + 1 < ntl else []
            S.replay(S.interleave(sG, sP1))
            sD = S.record(lambda: Dn(i))
            sP2 = S.record(lambda: [P2(i + 1, s) for s in range(4)]) if i + 1 < ntl else []
            S.replay(S.interleave(sD, sP2))

    for l in range(L if DBG_STAGE != 'P' else 0):
        mixer_pass(l)
        if DBG_STAGE not in ('A0', 'B0'):
            ffn_pass(l)
    S.finish("sp", Xd)
    S.barrier()
    return nc


def _layouts(p):
    L = DEPTH
    gT = np.zeros((128, L * GW), np.float32)
    for l in range(L):
        o = l * GW
        gT[:, o + 0:o + 8] = p["mix_pre_g"][l].reshape(8, 128).T
        gT[:, o + 8:o + 16] = p["ffn_pre_g"][l].reshape(8, 128).T
        gT[:, o + 16:o + 24] = p["out_norm_g"][l].reshape(8, 128).T
        gT[:, o + 24:o + 27] = p["q_norm_g"][l].reshape(3, 128).T
        gT[:, o + 27:o + 29] = p["kv_norm_g"][l].reshape(2, 128).T
        gT[:, o + 29:o + 33] = p["b_sp"][l].T
    bc = np.zeros((L, 128, BCW), np.float32)
    bc[:, :, 0:1024] = p["mix_post_g"][:, None, :]
    bc[:, :, 1024:2048] = p["ffn_post_g"][:, None, :]
    bc[:, :, 2048:2304] = p["sg_ln_g"][:, None, :]
    bc[:, :, 2304:2560] = p["sg_ln_b"][:, None, :]
    bc[:, :, 2560:3328] = p["conv_w"].reshape(L, 1, 768)
    wspT = np.ascontiguousarray(np.transpose(p["w_sp"], (0, 3, 1, 2))).reshape(L, 128, 512)
    return gT, bc, wspT


_INVF = (np.float32(1.0) / (np.float32(10000.0) ** (np.arange(16, dtype=np.float32) / np.float32(16)))).astype(np.float32)


def kernel(x, positions, mix_pre_g, mix_post_g, ffn_pre_g, ffn_post_g, w_in, q_norm_g, w_uq, kv_norm_g, w_ukv,
           sg_ln_g, sg_ln_b, w_sp, b_sp, conv_w, out_norm_g, w_out, w_gate, w_up, w_down, _depth=DEPTH, _cores=8):
    p = dict(mix_pre_g=mix_pre_g, mix_post_g=mix_post_g, ffn_pre_g=ffn_pre_g, ffn_post_g=ffn_post_g, q_norm_g=q_norm_g,
             kv_norm_g=kv_norm_g, sg_ln_g=sg_ln_g, sg_ln_b=sg_ln_b, w_sp=w_sp, b_sp=b_sp, conv_w=conv_w, out_norm_g=out_norm_g)
    p = {k: np.asarray(v, np.float32) for k, v in p.items()}
    gT, bc, wspT = _layouts(p)
    f = lambda a: np.ascontiguousarray(np.asarray(a, np.float32))
    shared = {"invf": np.ascontiguousarray(np.broadcast_to(_INVF[None, :], (128, 16))), "w_in": f(w_in), "w_uq": f(w_uq), "w_ukv": f(w_ukv),
              "w_out": f(w_out), "w_gate": f(w_gate), "w_up": f(w_up), "w_down": f(w_down), "gT": gT, "bc": bc, "wspT": wspT}
    x = np.asarray(x, np.float32)
    positions = np.asarray(positions, np.int32)
    nc = build_program(_depth)
    in_maps = []
    for b in range(_cores):
        m = dict(shared)
        m["x"] = np.ascontiguousarray(x[b])
        m["pos"] = np.ascontiguousarray(positions[b].reshape(NSUB, 128).T)
        in_maps.append(m)
    res = run_bass_kernel_spmd(nc, in_maps, core_ids=list(range(_cores)))
    return np.stack([np.asarray(r["out"], np.float32) for r in res.results], axis=0)
```

```python
import math
import os
import numpy as np
import concourse.bass as bass
import concourse.mybir as mybir
from concourse.bass_utils import run_bass_kernel_spmd

F32 = mybir.dt.float32
BF16 = mybir.dt.bfloat16
I32 = mybir.dt.int32
AF = mybir.ActivationFunctionType
ALU = mybir.AluOpType
AX = mybir.AxisListType

D = 1024
SEQ = 4096
DEPTH = 4
NSUB = SEQ // 128
NTILE = SEQ // 512
INW = 1952
H = 8
DFF = 2816
NFF = DFF // 128
EPS = 1e-6
SCALE = 96.0 ** -0.5
GW = 33
BCW = 3328
ND = 8


class Buf:
    __slots__ = ("name", "w", "r", "psum")

    def __init__(self, name, psum=False):
        self.name = name
        self.w = None
        self.r = {}
        self.psum = psum


class Sched:
    def __init__(self, nc):
        self.nc = nc
        self.E = {"pe": nc.tensor, "act": nc.scalar, "dve": nc.vector, "pool": nc.gpsimd, "sp": nc.sync}
        self.sem = {k: nc.alloc_semaphore("s_" + k) for k in self.E}
        self.cnt = {k: 0 for k in self.E}
        self.seen = {k: {} for k in self.E}
        self.pend = {k: [] for k in self.E}
        self.dsem = {q: [nc.alloc_semaphore(f"d_{q}{i}") for i in range(ND)] for q in ("sp", "pool")}
        self.dcnt = {q: [0] * ND for q in self.dsem}
        self.drr = {q: 0 for q in self.dsem}
        self.rec = None
        self.adepth = 0

    def _rec(self, item):
        if self.adepth > 0:
            if self._open is None:
                self._open = []
                self.rec.append(self._open)
            self._open.append(item)
        else:
            self.rec.append([item])

    def begin_atom(self):
        self.adepth += 1
        if self.adepth == 1:
            self._open = None

    def end_atom(self):
        self.adepth -= 1
        if self.adepth == 0:
            self._open = None

    def record(self, f):
        assert self.rec is None
        self.rec = []
        self._open = None
        f()
        atoms, self.rec = self.rec, None
        return atoms

    @staticmethod
    def interleave(*streams):
        streams = [st for st in streams if st]
        idx = [0] * len(streams)
        out = []
        total = sum(len(st) for st in streams)
        while len(out) < total:
            k = min((idx[j] / len(streams[j]), j) for j in range(len(streams)) if idx[j] < len(streams[j]))[1]
            out.append(streams[k][idx[k]])
            idx[k] += 1
        return out

    def replay(self, atoms):
        assert self.rec is None
        for atom in atoms:
            for kind, args, kw in atom:
                (self.op if kind == "op" else self.dma)(*args, **kw)

    def _collect(self, eng, reads, writes, skip_own_war=True):
        toks = {}
        own = self.sem.get(eng)

        def need(s, v, war=False):
            if s is own and (eng == "pe" or (war and skip_own_war)):
                return
            if toks.get(s, 0) < v:
                toks[s] = v

        for b in reads:
            if b.w is not None:
                need(*b.w)
            if b.psum:
                for s, v in b.r.items():
                    if s is not own:
                        need(s, v)
        for b in writes:
            if b.w is not None:
                need(*b.w)
            for s, v in b.r.items():
                need(s, v, war=True)
        return toks

    def _emit_waits(self, eng, toks):
        e = self.E[eng]
        seen = self.seen[eng]
        for s, v in toks.items():
            if seen.get(s, 0) < v:
                e.wait_ge(s, v)
                seen[s] = v

    def op(self, eng, fn, reads=(), writes=(), inc=True):
        if self.rec is not None:
            self._rec(("op", (eng, fn), dict(reads=tuple(reads), writes=tuple(writes), inc=inc)))
            return None
        self._emit_waits(eng, self._collect(eng, reads, writes))
        ins = fn(self.E[eng])
        self.pend[eng].append((tuple(reads), tuple(writes)))
        if inc:
            self.cnt[eng] += 1
            s = self.sem[eng]
            ins.then_inc(s, 1)
            v = self.cnt[eng]
            for rs, ws in self.pend[eng]:
                for b in rs:
                    b.r[s] = v
                for b in ws:
                    b.w = (s, v)
                    b.r = {}
            self.pend[eng] = []
        return ins

    def dma(self, q, fn, reads=(), writes=()):
        if self.rec is not None:
            self._rec(("dma", (q, fn), dict(reads=tuple(reads), writes=tuple(writes))))
            return None
        assert not self.pend[q]
        i = self.drr[q]
        self.drr[q] = (i + 1) % ND
        s = self.dsem[q][i]
        toks = self._collect(q, reads, writes, skip_own_war=False)
        prev = 16 * self.dcnt[q][i]
        if prev and toks.get(s, 0) < prev:
            toks[s] = prev
        self._emit_waits(q, toks)
        ins = fn(self.E[q])
        ins.then_inc(s, 16)
        self.dcnt[q][i] += 1
        v = 16 * self.dcnt[q][i]
        for b in reads:
            b.r[s] = v
        for b in writes:
            b.w = (s, v)
            b.r = {}
        return ins

    def handoff(self, src, dst):
        for d in dst:
            for b in src:
                if b.w is not None:
                    s, v = b.w
                    if d.r.get(s, 0) < v:
                        d.r[s] = v
                for s, v in b.r.items():
                    if d.r.get(s, 0) < v:
                        d.r[s] = v

    def barrier(self):
        for k in self.E:
            assert not self.pend[k]
        toks = {}
        for k in self.E:
            if self.cnt[k]:
                toks[self.sem[k]] = self.cnt[k]
        for q in self.dsem:
            for i in range(ND):
                if self.dcnt[q][i]:
                    toks[self.dsem[q][i]] = 16 * self.dcnt[q][i]
        for k in self.E:
            t = {s: v for s, v in toks.items() if s is not self.sem[k]}
            self._emit_waits(k, t)

    def finish(self, eng, bufs):
        toks = {}
        for b in bufs:
            if b.w is not None:
                s, v = b.w
                toks[s] = max(toks.get(s, 0), v)
            for s, v in b.r.items():
                toks[s] = max(toks.get(s, 0), v)
        self._emit_waits(eng, toks)


def build_program(L=DEPTH):
    DBG_T = int(os.environ.get('KDBG_TILES', NTILE))
    DBG_STAGE = os.environ.get('KDBG_STAGE', 'full')
    DBG_STOP = float(os.environ.get('KDBG_STOP', 99))
    nc = bass.Bass("TRN2", target_bir_lowering=False)
    S = Sched(nc)

    def dram(name, shape, dt, kind):
        return nc.dram_tensor(name, shape, dt, kind=kind).ap()

    x_d = dram("x", [SEQ, D], F32, "ExternalInput")
    pos_d = dram("pos", [128, NSUB], I32, "ExternalInput")
    invf_d = dram("invf", [128, 16], F32, "ExternalInput")
    w_in_d = dram("w_in", [DEPTH, D, INW], F32, "ExternalInput")
    w_uq_d = dram("w_uq", [DEPTH, 384, 768], F32, "ExternalInput")
    w_ukv_d = dram("w_ukv", [DEPTH, 256, 1024], F32, "ExternalInput")
    w_out_d = dram("w_out", [DEPTH, D, D], F32, "ExternalInput")
    w_gate_d = dram("w_gate", [DEPTH, D, DFF], F32, "ExternalInput")
    w_up_d = dram("w_up", [DEPTH, D, DFF], F32, "ExternalInput")
    w_down_d = dram("w_down", [DEPTH, DFF, D], F32, "ExternalInput")
    gT_d = dram("gT", [128, DEPTH * GW], F32, "ExternalInput")
    bc_d = dram("bc", [DEPTH, 128, BCW], F32, "ExternalInput")
    wsp_d = dram("wspT", [DEPTH, 128, 512], F32, "ExternalInput")
    out_d = dram("out", [SEQ, D], F32, "ExternalOutput")
    mix_d = dram("mixd", [SEQ, D], BF16, "Internal")

    base = (nc.sbuf_base + 31) // 32 * 32
    top = nc.sbuf_top

    def sb(name, shape, dt, off):
        nb = int(np.prod(shape[1:])) * (2 if dt == BF16 else 4)
        assert off % 32 == 0 and base + off + nb <= top, (name, off, nb, top - base)
        return nc.alloc_sbuf_tensor_at(name, list(shape), dt, offset=base + off)

    ident = sb("ident", [128, 128], BF16, 0)
    tri = sb("tri", [128, 128], BF16, 256)
    mhalf = sb("mhalf", [128, 8], F32, 512)
    cosT = sb("cosT", [128, NSUB, 16], F32, 576)
    sinT = sb("sinT", [128, NSUB, 16], F32, 2624)
    gT = sb("gTs", [128, DEPTH * GW], F32, 4672)
    stt_ = sb("stats", [128, 64], F32, 5216)
    invf = sb("invfs", [128, 16], F32, 5472)
    posi = sb("posi", [128, NSUB], I32, 5536)
    posf = sb("posf", [128, NSUB], F32, 5664)
    stv = sb("stv", [128, 32], F32, 5792)
    CEND = 5920
    BIG = CEND
    MID = BIG + 174080
    assert base + MID + 30720 <= top, (base, MID, top)

    KT = sb("KT", [128, H, SEQ], BF16, BIG + 0)
    Vc = sb("Vc", [128, NSUB, H, 65], BF16, BIG + 65536)
    w_in = sb("w_in_s", [128, 8, INW], BF16, BIG + 98816)
    w_uq = sb("w_uq_s", [128, 3, 768], BF16, BIG + 130048)
    w_ukv = sb("w_ukv_s", [128, 2, 1024], BF16, BIG + 134656)
    qT = sb("qT", [128, H, 512], BF16, BIG + 138752)
    mix_tm = sb("mix_tm", [128, 4, D], BF16, BIG + 146944)
    SA = BIG + 155136
    z_sb = sb("z_sb", [128, INW], F32, SA)
    uv = sb("uv", [128, 512], F32, SA + 7808)
    ysh1 = sb("ysh1", [128, 256], F32, SA + 9856)
    ysh2 = sb("ysh2", [128, 256], F32, SA + 10880)
    acc = sb("acc", [128, 256], F32, SA + 11904)
    sq = sb("sq", [128, 256], F32, SA + 12928)
    vn32 = sb("vn32", [128, 256], F32, SA + 13952)
    cqkv = sb("cqkv", [128, 640], BF16, SA + 14976)
    ycv = [sb("ycv0", [128, 256], F32, SA + 16352), sb("ycv1", [128, 256], F32, SA + 17376)]
    assert SA + 18400 <= MID
    ya = sb("ya", [128, 4, 512], F32, SA)
    PT = [sb(f"PT{r}", [128, 512], BF16, SA + 8192 + 1024 * r) for r in range(4)]
    hT = [sb("hT0", [128, 8, 128], BF16, MID + 0), sb("hT1", [128, 8, 128], BF16, MID + 2048)]
    xs = sb("xs", [128, D], F32, MID + 4096)
    xn = sb("xn", [128, D], BF16, MID + 8192)
    sgln = sb("sgln", [128, 512], F32, MID + 10240)
    convw = sb("convw", [128, 768], F32, MID + 12288)
    wspT = sb("wspTs", [128, 4, 128], BF16, MID + 15360)
    cqT = sb("cqT", [128, 5, 128], BF16, MID + 16384)
    q_tm = sb("q_tm", [128, H, 96], BF16, MID + 17664)
    k_tm = sb("k_tm", [128, H, 96], BF16, MID + 19200)
    rp = sb("rp", [128, 9, 32], F32, MID + 20736)
    rt = [sb(f"rt{i}", [128, 9, 16], F32, MID + 21888 + 576 * i) for i in range(4)]
    rr = sb("rr", [128, 9, 32], F32, MID + 24192)
    vn = sb("vn", [128, 256], BF16, MID + 25344)
    rinv = sb("rinv", [128, 4], F32, MID + 25856)
    w_gate = sb("w_gate_s", [128, 8, DFF], BF16, BIG + 0)
    w_up = sb("w_up_s", [128, 8, DFF], BF16, BIG + 45056)
    w_down = sb("w_down_s", [128, NFF, D], BF16, BIG + 90112)
    w_out = sb("w_out_s", [128, 8, D], BF16, BIG + 135168)
    aT = sb("aT", [128, NFF, 512], BF16, BIG + 151552)
    h2T = sb("h2T", [128, 8, 512], BF16, MID + 0)
    xs_f = sb("xs_f", [128, D], F32, MID + 8192)
    xn_f = [sb("xn_f0", [128, D], BF16, MID + 12288), sb("xn_f1", [128, D], BF16, MID + 14336)]
    sg_f = [sb("sg_f0", [128, 512], F32, MID + 12288), sb("sg_f1", [128, 512], F32, MID + 14336)]
    mixT = sb("mixT", [128, 8, 128], BF16, MID + 16384)
    t_f = sb("t_f", [128, D], F32, MID + 18432)
    postg_m = sb("postg_m", [128, D], F32, MID + 22528)
    postg_f = sb("postg_f", [128, D], F32, MID + 26624)
    m0P = sb("m0P", [128, D], BF16, MID + 30720)

    pT = [nc.alloc_psum_tensor(f"pT{i}", [128, 8, 128], BF16) for i in range(2)]
    bk = [nc.alloc_psum_tensor(f"bk{i}", [128, 512], F32) for i in range(6)]
    BpT = [Buf(f"pT{i}", psum=True) for i in range(2)]
    Bbk = [Buf(f"bk{i}", psum=True) for i in range(6)]
    tcount = [0]

    def next_pT():
        i = tcount[0] % 2
        tcount[0] += 1
        return pT[i], BpT[i]

    B = {}

    def bf(name):
        if name not in B:
            B[name] = Buf(name)
        return B[name]

    Xd = [Buf(f"xd{n}") for n in range(NSUB)]
    Md = [Buf(f"md{i}") for i in range(NTILE)]
    BKT = [Buf(f"kt{n}") for n in range(NSUB)]
    BV = [Buf(f"v{n}") for n in range(NSUB)]

    slot = [0]

    def new_slot():
        k = slot[0] % 16
        slot[0] += 1
        return stt_[:, 4 * k:4 * k + 4], bf(f"slot{k}")

    def rstd_from_ss(sl, Bsl, ncols, Dn):
        if ncols == 2:
            S.op("pool", lambda e: e.tensor_tensor(out=sl[:, 0:1], in0=sl[:, 0:1], in1=sl[:, 1:2], op=ALU.add), reads=[Bsl], writes=[Bsl])
        S.op("pool", lambda e: e.tensor_scalar(out=sl[:, 2:3], in0=sl[:, 0:1], scalar1=1.0 / Dn, scalar2=EPS, op0=ALU.mult, op1=ALU.add), reads=[Bsl], writes=[Bsl])
        S.op("pool", lambda e: e.tensor_tensor(out=sl[:, 3:4], in0=sl[:, 2:3], in1=mhalf[:, 0:1], op=ALU.pow), reads=[Bsl, bf("mhalf")], writes=[Bsl])
        return sl[:, 3:4]

    def transpose_to(src_ap_fn, nchunk, rows, Bsrc, dst_ap, Bdst, gcol0=None, eng="dve"):
        S.begin_atom()
        p, Bp = next_pT()
        for c in range(nchunk):
            S.op("pe", lambda e, c=c: e.transpose(out=p[0:rows, c, :], in_=src_ap_fn(c), identity=ident[:]),
                 reads=[Bsrc, bf("ident")], writes=[Bp], inc=(c == nchunk - 1))
        if gcol0 is None:
            if eng == "act":
                S.op("act", lambda e: e.activation(out=dst_ap, in_=p[0:rows, 0:nchunk, :], func=AF.Copy), reads=[Bp], writes=[Bdst])
            else:
                S.op(eng, lambda e: e.tensor_copy(out=dst_ap, in_=p[0:rows, 0:nchunk, :]), reads=[Bp], writes=[Bdst])
        else:
            g = gT[0:rows, gcol0:gcol0 + nchunk].unsqueeze(2).broadcast_to([rows, nchunk, 128])
            S.op(eng, lambda e: e.tensor_tensor(out=dst_ap, in0=p[0:rows, 0:nchunk, :], in1=g, op=ALU.mult), reads=[Bp, bf("gT")], writes=[Bdst])
        S.end_atom()

    S.op("pool", lambda e: e.memset(ident[:], 1.0), writes=[bf("ident")])
    S.op("pool", lambda e: e.affine_select(out=ident[:], in_=ident[:], pattern=[[-1, 128]], compare_op=ALU.is_equal, fill=0.0, base=0, channel_multiplier=1),
         reads=[bf("ident")], writes=[bf("ident")])
    S.op("pool", lambda e: e.memset(tri[:], 1.0), writes=[bf("tri")])
    S.op("pool", lambda e: e.affine_select(out=tri[:], in_=tri[:], pattern=[[1, 128]], compare_op=ALU.is_ge, fill=0.0, base=0, channel_multiplier=-1),
         reads=[bf("tri")], writes=[bf("tri")])
    S.op("pool", lambda e: e.memset(mhalf[:], -0.5), writes=[bf("mhalf")])
    S.dma("sp", lambda e: e.dma_start(out=gT[:], in_=gT_d), writes=[bf("gT")])
    S.dma("sp", lambda e: e.dma_start(out=invf[:], in_=invf_d), writes=[bf("invf")])
    S.dma("sp", lambda e: e.dma_start(out=posi[:], in_=pos_d), writes=[bf("posi")])
    TWO_PI = 2.0 * math.pi
    C1 = float(np.float32(6.28125))
    C2 = float(np.float32(TWO_PI - 6.28125))
    MAGIC = 12582912.0
    ang = sb("ang", [128, NSUB, 16], F32, BIG + 0)
    kk = sb("kk", [128, NSUB, 16], F32, BIG + 2048)
    t2 = sb("t2p", [128, NSUB, 16], F32, BIG + 4096)
    S.op("dve", lambda e: e.tensor_copy(out=posf[:], in_=posi[:]), reads=[bf("posi")], writes=[bf("posf")])
    S.op("dve", lambda e: e.tensor_tensor(out=ang[:], in0=posf[:].unsqueeze(2).broadcast_to([128, NSUB, 16]),
                                          in1=invf[:].unsqueeze(1).broadcast_to([128, NSUB, 16]), op=ALU.mult),
         reads=[bf("posf"), bf("invf")], writes=[bf("ang")])
    S.op("dve", lambda e: e.tensor_scalar(out=t2[:], in0=ang[:], scalar1=1.0 / TWO_PI, scalar2=MAGIC, op0=ALU.mult, op1=ALU.add), reads=[bf("ang")], writes=[bf("t2")])
    S.op("dve", lambda e: e.tensor_scalar(out=kk[:], in0=t2[:], scalar1=-MAGIC, scalar2=None, op0=ALU.add), reads=[bf("t2")], writes=[bf("kk")])
    S.op("dve", lambda e: e.scalar_tensor_tensor(out=ang[:], in0=kk[:], scalar=-C1, in1=ang[:], op0=ALU.mult, op1=ALU.add), reads=[bf("kk"), bf("ang")], writes=[bf("ang")])
    S.op("dve", lambda e: e.scalar_tensor_tensor(out=ang[:], in0=kk[:], scalar=-C2, in1=ang[:], op0=ALU.mult, op1=ALU.add), reads=[bf("kk"), bf("ang")], writes=[bf("ang")])
    S.op("dve", lambda e: e.tensor_scalar(out=ang[:], in0=ang[:], scalar1=math.pi, scalar2=-math.pi, op0=ALU.min, op1=ALU.max), reads=[bf("ang")], writes=[bf("ang")])
    S.op("act", lambda e: e.activation(out=sinT[:], in_=ang[:], func=AF.Sin), reads=[bf("ang")], writes=[bf("sin")])
    S.op("dve", lambda e: e.tensor_scalar(out=t2[:], in0=ang[:], scalar1=-1.0, scalar2=None, op0=ALU.mult), reads=[bf("ang")], writes=[bf("t2")])
    S.op("dve", lambda e: e.tensor_tensor(out=t2[:], in0=t2[:], in1=ang[:], op=ALU.max), reads=[bf("ang"), bf("t2")], writes=[bf("t2")])
    S.op("dve", lambda e: e.tensor_scalar(out=t2[:], in0=t2[:], scalar1=-1.0, scalar2=math.pi / 2, op0=ALU.mult, op1=ALU.add), reads=[bf("t2")], writes=[bf("t2")])
    S.op("act", lambda e: e.activation(out=cosT[:], in_=t2[:], func=AF.Sin), reads=[bf("t2")], writes=[bf("cos")])

    stA = [bf("z0"), bf("z1"), bf("z2"), bf("z3"), bf("uv"), bf("ysh1"), bf("ysh2"), bf("acc"), bf("sq"), bf("vn32"), bf("cqkv")]
    stB = [bf("ya"), bf("PT0"), bf("PT1"), bf("PT2"), bf("PT3")]
    rot = [0]

    def mixer_pass(l):
        xsrc = x_d if l == 0 else out_d
        g0 = l * GW
        S.barrier()
        for k in range(8):
            S.dma("pool", lambda e, k=k: e.dma_start(out=w_in[:, k, :], in_=w_in_d[l, k * 128:(k + 1) * 128, :]), writes=[bf(f"w_in{k}")])
        for k in range(3):
            S.dma("pool", lambda e, k=k: e.dma_start(out=w_uq[:, k, :], in_=w_uq_d[l, k * 128:(k + 1) * 128, :]), writes=[bf("w_uq")])
        for k in range(2):
            S.dma("pool", lambda e, k=k: e.dma_start(out=w_ukv[:, k, :], in_=w_ukv_d[l, k * 128:(k + 1) * 128, :]), writes=[bf("w_ukv")])
        S.dma("pool", lambda e: e.dma_start(out=wspT[:].rearrange("p g t -> p (g t)"), in_=wsp_d[l]), writes=[bf("wspT")])
        S.dma("sp", lambda e: e.dma_start(out=sgln[:], in_=bc_d[l, :, 2048:2560]), writes=[bf("sgln")])
        S.dma("sp", lambda e: e.dma_start(out=convw[:], in_=bc_d[l, :, 2560:3328]), writes=[bf("convw")])
        S.op("dve", lambda e: e.tensor_tensor(out=wspT[:], in0=wspT[:], in1=tri[:].unsqueeze(1).broadcast_to([128, 4, 128]), op=ALU.mult),
             reads=[bf("wspT"), bf("tri")], writes=[bf("wspT")])
        S.op("pool", lambda e: e.memset(Vc[:, :, :, 64:65], 1.0), writes=BV)
        S.op("pool", lambda e: e.memset(ycv[1][:], 0.0), writes=[bf("ycv1")])

        nsub_run = 4 * DBG_T

        def f_early(n_):
            t0_ = n_ * 128
            S.dma("sp", lambda e: e.dma_start(out=xs[:], in_=xsrc[t0_:t0_ + 128, :]), reads=[Xd[n_]], writes=[bf("xs")])
            sl_, Bsl_ = new_slot()
            S.op("act", lambda e: e.activation(out=xn[:], in_=xs[:], func=AF.Square, accum_out=sl_[:, 0:1]), reads=[bf("xs")], writes=[bf("xn"), Bsl_])
            r_ = rstd_from_ss(sl_, Bsl_, 1, D)
            S.op("dve", lambda e: e.tensor_scalar(out=xn[:], in0=xs[:], scalar1=r_, scalar2=None, op0=ALU.mult), reads=[bf("xs"), Bsl_], writes=[bf("xn")])
            transpose_to(lambda c: xn[:, c * 128:(c + 1) * 128], 8, 128, bf("xn"), hT[n_ % 2][:], bf(f"hT{n_ % 2}"), gcol0=g0 + 0)

        for i in range(DBG_T):
            S.handoff(stB, stA)
            for s in range(4):
                n = 4 * i + s
                t0 = n * 128
                hb = hT[n % 2]
                Bh = bf(f"hT{n % 2}")
                if n == 0:
                    f_early(0)
                cg = [(0, 512), (512, 1024), (1024, 1536), (1536, INW)]
                for k in range(8):
                    for q, (c0, c1) in enumerate(cg):
                        S.op("pe", lambda e, k=k, q=q, c0=c0, c1=c1: e.matmul(bk[q][:, 0:c1 - c0], lhsT=hb[:, k, :], rhs=w_in[:, k, c0:c1], start=(k == 0), stop=(k == 7)),
                             reads=[Bh, bf(f"w_in{k}")], writes=[Bbk[q]], inc=(k == 7 and q == 3))
                for q, (c0, c1) in enumerate(cg):
                    if q % 2 == 0:
                        S.op("act", lambda e, q=q, c0=c0, c1=c1: e.activation(out=z_sb[:, c0:c1], in_=bk[q][:, 0:c1 - c0], func=AF.Copy), reads=[Bbk[q]], writes=[bf(f"z{q}")])
                    else:
                        S.op("dve", lambda e, q=q, c0=c0, c1=c1: e.tensor_copy(out=z_sb[:, c0:c1], in_=bk[q][:, 0:c1 - c0]), reads=[Bbk[q]], writes=[bf(f"z{q}")])
                def br_mla():
                    slq, Bq = new_slot()
                    slk, Bk_ = new_slot()
                    S.op("act", lambda e: e.activation(out=cqkv[:, 0:384], in_=z_sb[:, 0:384], func=AF.Square, accum_out=slq[:, 0:1]), reads=[bf("z0")], writes=[bf("cqkv"), Bq])
                    S.op("act", lambda e: e.activation(out=cqkv[:, 384:640], in_=z_sb[:, 384:640], func=AF.Square, accum_out=slk[:, 0:1]), reads=[bf("z0"), bf("z1")], writes=[bf("cqkv"), Bk_])
                    rq = rstd_from_ss(slq, Bq, 1, 384)
                    rk = rstd_from_ss(slk, Bk_, 1, 256)
                    S.op("dve", lambda e: e.tensor_scalar(out=cqkv[:, 0:384], in0=z_sb[:, 0:384], scalar1=rq, scalar2=None, op0=ALU.mult), reads=[bf("z0"), Bq], writes=[bf("cqkv")])
                    S.op("dve", lambda e: e.tensor_scalar(out=cqkv[:, 384:640], in0=z_sb[:, 384:640], scalar1=rk, scalar2=None, op0=ALU.mult), reads=[bf("z0"), bf("z1"), Bk_], writes=[bf("cqkv")])
                    transpose_to(lambda c: cqkv[:, c * 128:(c + 1) * 128], 5, 128, bf("cqkv"), cqT[:], bf("cqT"), gcol0=g0 + 24)
                    S.begin_atom()
                    for k in range(3):
                        S.op("pe", lambda e, k=k: e.matmul(bk[4][:, 0:480], lhsT=cqT[:, k, :], rhs=w_uq[:, k, 0:480], start=(k == 0), stop=(k == 2)),
                             reads=[bf("cqT"), bf("w_uq")], writes=[Bbk[4]], inc=False)
                        S.op("pe", lambda e, k=k: e.matmul(bk[5][:, 0:288], lhsT=cqT[:, k, :], rhs=w_uq[:, k, 480:768], start=(k == 0), stop=(k == 2)),
                             reads=[bf("cqT"), bf("w_uq")], writes=[Bbk[5]], inc=(k == 2))
                    S.end_atom()
                    S.begin_atom()
                    for k in range(2):
                        S.op("pe", lambda e, k=k: e.matmul(bk[0][:, :], lhsT=cqT[:, 3 + k, :], rhs=w_ukv[:, k, 0:512], start=(k == 0), stop=(k == 1)),
                             reads=[bf("cqT"), bf("w_ukv")], writes=[Bbk[0]], inc=False)
                        S.op("pe", lambda e, k=k: e.matmul(bk[1][:, :], lhsT=cqT[:, 3 + k, :], rhs=w_ukv[:, k, 512:1024], start=(k == 0), stop=(k == 1)),
                             reads=[bf("cqT"), bf("w_ukv")], writes=[Bbk[1]], inc=(k == 1))
                    S.end_atom()
                    q4 = bk[4][:, 0:480].rearrange("p (h c) -> p h c", c=96)
                    q5 = bk[5][:, 0:288].rearrange("p (h c) -> p h c", c=96)
                    k0 = bk[0][:, :].rearrange("p (h c) -> p h c", c=128)
                    k1 = bk[1][:, :].rearrange("p (h c) -> p h c", c=128)
                    S.op("act", lambda e: e.activation(out=q_tm[:, 0:5, 0:64], in_=q4[:, :, 0:64], func=AF.Copy), reads=[Bbk[4]], writes=[bf("q_tm")])
                    S.op("act", lambda e: e.activation(out=q_tm[:, 5:8, 0:64], in_=q5[:, :, 0:64], func=AF.Copy), reads=[Bbk[5]], writes=[bf("q_tm")])
                    S.op("dve", lambda e: e.tensor_copy(out=rp[:, 0:5, :], in_=q4[:, :, 64:96]), reads=[Bbk[4]], writes=[bf("rp")])
                    S.op("dve", lambda e: e.tensor_copy(out=rp[:, 5:8, :], in_=q5[:, :, 64:96]), reads=[Bbk[5]], writes=[bf("rp")])
                    S.op("dve", lambda e: e.tensor_copy(out=rp[:, 8, :], in_=z_sb[:, 640:672]), reads=[bf("z1")], writes=[bf("rp")])
                    cs = cosT[:, n, :].unsqueeze(1).broadcast_to([128, 9, 16])
                    sn = sinT[:, n, :].unsqueeze(1).broadcast_to([128, 9, 16])
                    S.op("dve", lambda e: e.tensor_tensor(out=rt[0][:], in0=rp[:, :, 0:16], in1=cs, op=ALU.mult), reads=[bf("rp"), bf("cos")], writes=[bf("rt0")])
                    S.op("dve", lambda e: e.tensor_tensor(out=rt[1][:], in0=rp[:, :, 16:32], in1=sn, op=ALU.mult), reads=[bf("rp"), bf("sin")], writes=[bf("rt1")])
                    S.op("dve", lambda e: e.tensor_tensor(out=rt[2][:], in0=rp[:, :, 16:32], in1=cs, op=ALU.mult), reads=[bf("rp"), bf("cos")], writes=[bf("rt2")])
                    S.op("dve", lambda e: e.tensor_tensor(out=rt[3][:], in0=rp[:, :, 0:16], in1=sn, op=ALU.mult), reads=[bf("rp"), bf("sin")], writes=[bf("rt3")])
                    S.op("dve", lambda e: e.tensor_tensor(out=rr[:, :, 0:16], in0=rt[0][:], in1=rt[1][:], op=ALU.subtract), reads=[bf("rt0"), bf("rt1")], writes=[bf("rr")])
                    S.op("dve", lambda e: e.tensor_tensor(out=rr[:, :, 16:32], in0=rt[2][:], in1=rt[3][:], op=ALU.add), reads=[bf("rt2"), bf("rt3")], writes=[bf("rr")])
                    S.op("dve", lambda e: e.tensor_copy(out=q_tm[:, :, 64:96], in_=rr[:, 0:8, :]), reads=[bf("rr")], writes=[bf("q_tm")])
                    S.op("dve", lambda e: e.tensor_copy(out=k_tm[:, :, 64:96], in_=rr[:, 8:9, :].broadcast_to([128, 8, 32])), reads=[bf("rr")], writes=[bf("k_tm")])
                    S.op("act", lambda e: e.activation(out=k_tm[:, 0:4, 0:64], in_=k0[:, :, 0:64], func=AF.Copy), reads=[Bbk[0]], writes=[bf("k_tm")])
                    S.op("dve", lambda e: e.tensor_copy(out=k_tm[:, 4:8, 0:64], in_=k1[:, :, 0:64]), reads=[Bbk[1]], writes=[bf("k_tm")])
                    S.op("act", lambda e: e.activation(out=Vc[:, n, 0:4, 0:64], in_=k0[:, :, 64:128], func=AF.Copy), reads=[Bbk[0]], writes=[BV[n]])
                    S.op("dve", lambda e: e.tensor_copy(out=Vc[:, n, 4:8, 0:64], in_=k1[:, :, 64:128]), reads=[Bbk[1]], writes=[BV[n]])
                    transpose_to(lambda h: q_tm[:, h, :], 8, 96, bf("q_tm"), qT[0:96, :, s * 128:(s + 1) * 128], bf("qT"), eng="act")
                    transpose_to(lambda h: k_tm[:, h, :], 8, 96, bf("k_tm"), KT[0:96, :, t0:t0 + 128], BKT[n], eng="dve")

                def br_sgu():
                    S.op("act", lambda e: e.activation(out=uv[:], in_=z_sb[:, 672:1184], func=AF.Gelu_apprx_tanh), reads=[bf("z1"), bf("z2")], writes=[bf("uv")])
                    v3 = uv[:, 256:512].rearrange("p (g e) -> p g e", g=4)
                    Bsv = bf("stv")
                    S.op("dve", lambda e: e.tensor_reduce(out=stv[:, 0:4], in_=v3, axis=AX.X, op=ALU.add), reads=[bf("uv")], writes=[Bsv])
                    S.op("pool", lambda e: e.tensor_tensor(out=sq[:], in0=uv[:, 256:512], in1=uv[:, 256:512], op=ALU.mult), reads=[bf("uv")], writes=[bf("sq")])
                    S.op("dve", lambda e: e.tensor_reduce(out=stv[:, 4:8], in_=sq[:].rearrange("p (g e) -> p g e", g=4), axis=AX.X, op=ALU.add), reads=[bf("sq")], writes=[Bsv])
                    S.op("pool", lambda e: e.tensor_scalar(out=stv[:, 8:12], in0=stv[:, 0:4], scalar1=1.0 / 64, scalar2=None, op0=ALU.mult), reads=[Bsv], writes=[Bsv])
                    S.op("pool", lambda e: e.tensor_tensor(out=stv[:, 12:16], in0=stv[:, 8:12], in1=stv[:, 8:12], op=ALU.mult), reads=[Bsv], writes=[Bsv])
                    S.op("pool", lambda e: e.tensor_scalar(out=stv[:, 24:28], in0=stv[:, 4:8], scalar1=1.0 / 64, scalar2=EPS, op0=ALU.mult, op1=ALU.add), reads=[Bsv], writes=[Bsv])
                    S.op("pool", lambda e: e.tensor_tensor(out=stv[:, 16:20], in0=stv[:, 24:28], in1=stv[:, 12:16], op=ALU.subtract), reads=[Bsv], writes=[Bsv])
                    S.op("pool", lambda e: e.tensor_tensor(out=stv[:, 20:24], in0=stv[:, 16:20], in1=mhalf[:, 0:4], op=ALU.pow), reads=[Bsv, bf("mhalf")], writes=[Bsv])
                    vn3 = vn32[:].rearrange("p (g e) -> p g e", g=4)
                    S.op("dve", lambda e: e.tensor_tensor(out=vn3, in0=v3, in1=stv[:, 8:12].unsqueeze(2).broadcast_to([128, 4, 64]), op=ALU.subtract), reads=[bf("uv"), Bsv], writes=[bf("vn32")])
                    S.op("dve", lambda e: e.tensor_tensor(out=vn3, in0=vn3, in1=stv[:, 20:24].unsqueeze(2).broadcast_to([128, 4, 64]), op=ALU.mult), reads=[bf("vn32"), Bsv], writes=[bf("vn32")])
                    S.op("pool", lambda e: e.tensor_tensor(out=vn32[:], in0=vn32[:], in1=sgln[:, 0:256], op=ALU.mult), reads=[bf("vn32"), bf("sgln")], writes=[bf("vn32")])
                    S.op("pool", lambda e: e.tensor_tensor(out=vn[:], in0=vn32[:], in1=sgln[:, 256:512], op=ALU.add), reads=[bf("vn32"), bf("sgln")], writes=[bf("vn")])
                    S.begin_atom()
                    for g in range(4):
                        S.op("pe", lambda e, g=g: e.matmul(bk[2][:, g * 64:(g + 1) * 64], lhsT=wspT[:, g, :], rhs=vn[:, g * 64:(g + 1) * 64], start=True, stop=True, skip_group_check=True),
                             reads=[bf("wspT"), bf("vn")], writes=[Bbk[2]], inc=(g == 3))
                    S.end_atom()
                    for g in range(4):
                        S.op("dve", lambda e, g=g: e.scalar_tensor_tensor(out=sq[:, g * 64:(g + 1) * 64], in0=bk[2][:, g * 64:(g + 1) * 64], scalar=gT[:, g0 + 29 + g:g0 + 30 + g],
                                                                          in1=uv[:, g * 64:(g + 1) * 64], op0=ALU.add, op1=ALU.mult),
                             reads=[Bbk[2], bf("gT"), bf("uv")], writes=[bf("sq")])
                    slb, Bb_ = new_slot()
                    S.op("act", lambda e: e.activation(out=mix_tm[:, s, 512:768], in_=sq[:], func=AF.Square, accum_out=slb[:, 0:1]), reads=[bf("sq")], writes=[bf(f"mix{s}"), Bb_])
                    rb = rstd_from_ss(slb, Bb_, 1, 256)
                    S.op("dve", lambda e: e.tensor_scalar(out=mix_tm[:, s, 512:768], in0=sq[:], scalar1=rb, scalar2=None, op0=ALU.mult), reads=[bf("sq"), Bb_], writes=[bf(f"mix{s}")])

                def br_conv():
                    yc, yp = ycv[n % 2], ycv[(n + 1) % 2]
                    Byc, Byp = bf(f"ycv{n % 2}"), bf(f"ycv{(n + 1) % 2}")
                    S.op("pool", lambda e: e.tensor_tensor(out=yc[:], in0=z_sb[:, 1440:1696], in1=z_sb[:, 1696:1952], op=ALU.mult), reads=[bf("z2"), bf("z3")], writes=[Byc])
                    S.dma("sp", lambda e: e.dma_start(out=ysh1[1:128, :], in_=yc[0:127, :]), reads=[Byc], writes=[bf("ysh1")])
                    S.dma("sp", lambda e: e.dma_start(out=ysh1[0:1, :], in_=yp[127:128, :]), reads=[Byp], writes=[bf("ysh1")])
                    S.dma("sp", lambda e: e.dma_start(out=ysh2[2:128, :], in_=yc[0:126, :]), reads=[Byc], writes=[bf("ysh2")])
                    S.dma("sp", lambda e: e.dma_start(out=ysh2[0:2, :], in_=yp[126:128, :]), reads=[Byp], writes=[bf("ysh2")])
                    S.op("pool", lambda e: e.tensor_tensor(out=acc[:], in0=yc[:], in1=convw[:, 512:768], op=ALU.mult), reads=[Byc, bf("convw")], writes=[bf("acc")])
                    S.op("pool", lambda e: e.tensor_tensor(out=ysh1[:], in0=ysh1[:], in1=convw[:, 256:512], op=ALU.mult), reads=[bf("ysh1"), bf("convw")], writes=[bf("ysh1")])
                    S.op("pool", lambda e: e.tensor_tensor(out=acc[:], in0=acc[:], in1=ysh1[:], op=ALU.add), reads=[bf("acc"), bf("ysh1")], writes=[bf("acc")])
                    S.op("pool", lambda e: e.tensor_tensor(out=ysh2[:], in0=ysh2[:], in1=convw[:, 0:256], op=ALU.mult), reads=[bf("ysh2"), bf("convw")], writes=[bf("ysh2")])
                    S.op("pool", lambda e: e.tensor_tensor(out=acc[:], in0=acc[:], in1=ysh2[:], op=ALU.add), reads=[bf("acc"), bf("ysh2")], writes=[bf("acc")])
                    S.op("pool", lambda e: e.tensor_tensor(out=acc[:], in0=acc[:], in1=z_sb[:, 1184:1440], op=ALU.mult), reads=[bf("acc"), bf("z2")], writes=[bf("acc")])
                    slc, Bc_ = new_slot()
                    S.op("act", lambda e: e.activation(out=mix_tm[:, s, 768:1024], in_=acc[:], func=AF.Square, accum_out=slc[:, 0:1]), reads=[bf("acc")], writes=[bf(f"mix{s}"), Bc_])
                    rc = rstd_from_ss(slc, Bc_, 1, 256)
                    S.op("dve", lambda e: e.tensor_scalar(out=mix_tm[:, s, 768:1024], in0=acc[:], scalar1=rc, scalar2=None, op0=ALU.mult), reads=[bf("acc"), Bc_], writes=[bf(f"mix{s}")])


                merged = S.interleave(S.record(br_mla), S.record(br_sgu), S.record(br_conv))
                if n + 1 < nsub_run:
                    for j_, atom_ in enumerate(S.record(lambda: f_early(n + 1))):
                        merged.insert(min(len(merged), 2 + 7 * j_), atom_)
                S.replay(merged)
            if DBG_STAGE in ('A', 'A0'):
                continue
            S.handoff(stA, stB)
            nkb = 4 * i + 4
            LA = 3

            def att_front(h, kb):
                j0 = max(0, kb - 4 * i)
                c0 = j0 * 128
                r = rot[0] % 4
                rot[0] += 1
                sbk, Bs = bk[r], Bbk[r]
                S.op("pe", lambda e: e.matmul(sbk[:, c0:512], lhsT=KT[0:96, h, kb * 128:(kb + 1) * 128], rhs=qT[0:96, h, c0:512], start=True, stop=True),
                     reads=[BKT[kb], bf("qT")], writes=[Bs])
                S.op("act", lambda e: e.activation(out=PT[r][:, c0:512], in_=sbk[:, c0:512], func=AF.Exp, scale=SCALE), reads=[Bs], writes=[bf(f"PT{r}")])
                if kb >= 4 * i:
                    S.op("pool", lambda e: e.tensor_tensor(out=PT[r][:, c0:c0 + 128], in0=PT[r][:, c0:c0 + 128], in1=tri[:], op=ALU.mult),
                         reads=[bf(f"PT{r}"), bf("tri")], writes=[bf(f"PT{r}")])
                return r

            def att_back(h, kb, r):
                j0 = max(0, kb - 4 * i)
                O = bk[4 + h % 2]
                BO = Bbk[4 + h % 2]
                O3 = O[:, :].rearrange("p (j c) -> p j c", j=4)
                for j in range(j0, 4):
                    S.op("pe", lambda e, j=j: e.matmul(O3[:, j, 0:65], lhsT=PT[r][:, j * 128:(j + 1) * 128], rhs=Vc[:, kb, h, :],
                                                       start=(kb == 0 and j == 0), stop=(kb == 4 * i + j), skip_group_check=True),
                         reads=[bf(f"PT{r}"), BV[kb]], writes=[BO], inc=(j == 3))
                if kb == nkb - 1:
                    S.op("dve", lambda e: e.reciprocal(out=rinv[:].unsqueeze(2), in_=O3[:, :, 64:65]), reads=[BO], writes=[bf("rinv")])
                    S.op("dve", lambda e: e.tensor_tensor(out=ya[:, :, h * 64:(h + 1) * 64], in0=O3[:, :, 0:64], in1=rinv[:].unsqueeze(2).broadcast_to([128, 4, 64]), op=ALU.mult),
                         reads=[BO, bf("rinv")], writes=[bf("ya")])

            inflight = []
            for h in range(H):
                for kb in range(nkb):
                    inflight.append((h, kb, att_front(h, kb)))
                    if len(inflight) > LA:
                        att_back(*inflight.pop(0))
            while inflight:
                att_back(*inflight.pop(0))
            for j in range(4):
                sla, Ba_ = new_slot()
                S.op("act", lambda e, j=j: e.activation(out=mix_tm[:, j, 0:512], in_=ya[:, j, :], func=AF.Square, accum_out=sla[:, 0:1]), reads=[bf("ya")], writes=[bf(f"mix{j}"), Ba_])
                ra = rstd_from_ss(sla, Ba_, 1, 512)
                S.op("dve", lambda e, j=j: e.tensor_scalar(out=mix_tm[:, j, 0:512], in0=ya[:, j, :], scalar1=ra, scalar2=None, op0=ALU.mult), reads=[bf("ya"), Ba_], writes=[bf(f"mix{j}")])
            S.dma("sp", lambda e: e.dma_start(out=mix_d[i * 512:(i + 1) * 512, :].rearrange("(j p) d -> p j d", p=128), in_=mix_tm[:]),
                  reads=[bf("mix0"), bf("mix1"), bf("mix2"), bf("mix3")], writes=[Md[i]])

    def ffn_pass(l):
        xsrc = x_d if l == 0 else out_d
        g0 = l * GW
        S.barrier()
        for k in range(8):
            S.dma("pool", lambda e, k=k: e.dma_start(out=w_out[:, k, :], in_=w_out_d[l, k * 128:(k + 1) * 128, :]), writes=[bf("w_out")])
        S.dma("sp", lambda e: e.dma_start(out=postg_m[:], in_=bc_d[l, :, 0:1024]), writes=[bf("postg_m")])
        S.dma("sp", lambda e: e.dma_start(out=postg_f[:], in_=bc_d[l, :, 1024:2048]), writes=[bf("postg_f")])
        HC = DFF // 2
        for hf in range(2):
            for k in range(8):
                S.dma("pool", lambda e, k=k, hf=hf: e.dma_start(out=w_gate[:, k, hf * HC:(hf + 1) * HC], in_=w_gate_d[l, k * 128:(k + 1) * 128, hf * HC:(hf + 1) * HC]),
                      writes=[bf(f"w_gate{k}_{hf}")])
                S.dma("pool", lambda e, k=k, hf=hf: e.dma_start(out=w_up[:, k, hf * HC:(hf + 1) * HC], in_=w_up_d[l, k * 128:(k + 1) * 128, hf * HC:(hf + 1) * HC]),
                      writes=[bf(f"w_up{k}_{hf}")])
        for c in range(NFF):
            S.dma("pool", lambda e, c=c: e.dma_start(out=w_down[:, c, :], in_=w_down_d[l, c * 128:(c + 1) * 128, :]), writes=[bf(f"w_down{c}")])

        xsP, BxsP, Bm0P = t_f, bf("xsP"), bf("m0P")

        def post_norm_inplace(postg, Bpostg, ba, bb, junk, Bjunk):
            sl, Bsl = new_slot()
            S.op("act", lambda e: e.activation(out=junk[:, 0:512], in_=bk[ba][:, :], func=AF.Square, accum_out=sl[:, 0:1]), reads=[Bbk[ba]], writes=[Bjunk, Bsl])
            S.op("act", lambda e: e.activation(out=junk[:, 512:1024], in_=bk[bb][:, :], func=AF.Square, accum_out=sl[:, 1:2]), reads=[Bbk[bb]], writes=[Bjunk, Bsl])
            r = rstd_from_ss(sl, Bsl, 2, D)
            S.op("dve", lambda e: e.scalar_tensor_tensor(out=bk[ba][:, :], in0=bk[ba][:, :], scalar=r, in1=postg[:, 0:512], op0=ALU.mult, op1=ALU.mult),
                 reads=[Bsl, Bpostg], writes=[Bbk[ba]])
            S.op("dve", lambda e: e.scalar_tensor_tensor(out=bk[bb][:, :], in0=bk[bb][:, :], scalar=r, in1=postg[:, 512:1024], op0=ALU.mult, op1=ALU.mult),
                 reads=[Bsl, Bpostg], writes=[Bbk[bb]])

        def add_banks(xt, Bxt, ba, bb):
            S.op("dve", lambda e: e.tensor_tensor(out=xt[:, 0:512], in0=bk[ba][:, :], in1=xt[:, 0:512], op=ALU.add), reads=[Bbk[ba], Bxt], writes=[Bxt])
            S.op("dve", lambda e: e.tensor_tensor(out=xt[:, 512:1024], in0=bk[bb][:, :], in1=xt[:, 512:1024], op=ALU.add), reads=[Bbk[bb], Bxt], writes=[Bxt])

        def P1(i, s):
            n = 4 * i + s
            t0 = n * 128
            S.dma("sp", lambda e: e.dma_start(out=m0P[:], in_=mix_d[t0:t0 + 128, :]), reads=[Md[i]], writes=[Bm0P])
            S.dma("sp", lambda e: e.dma_start(out=xsP[:], in_=xsrc[t0:t0 + 128, :]), reads=[Xd[n]], writes=[BxsP])
            transpose_to(lambda c: m0P[:, c * 128:(c + 1) * 128], 8, 128, Bm0P, mixT[:], bf("mixT"), gcol0=g0 + 16)
            S.begin_atom()
            for k in range(8):
                S.op("pe", lambda e, k=k: e.matmul(bk[4][:, :], lhsT=mixT[:, k, :], rhs=w_out[:, k, 0:512], start=(k == 0), stop=(k == 7)),
                     reads=[bf("mixT"), bf("w_out")], writes=[Bbk[4]], inc=False)
                S.op("pe", lambda e, k=k: e.matmul(bk[5][:, :], lhsT=mixT[:, k, :], rhs=w_out[:, k, 512:1024], start=(k == 0), stop=(k == 7)),
                     reads=[bf("mixT"), bf("w_out")], writes=[Bbk[5]], inc=(k == 7))
            S.end_atom()
            post_norm_inplace(postg_m, bf("postg_m"), 4, 5, m0P, Bm0P)
            add_banks(xsP, BxsP, 4, 5)
            S.dma("sp", lambda e: e.dma_start(out=out_d[t0:t0 + 128, :], in_=xsP[:]), reads=[BxsP], writes=[Xd[n]])

        def P2(i, s):
            n = 4 * i + s
            t0 = n * 128
            S.dma("sp", lambda e: e.dma_start(out=xsP[:], in_=out_d[t0:t0 + 128, :]), reads=[Xd[n]], writes=[BxsP])
            sl, Bsl = new_slot()
            S.op("act", lambda e: e.activation(out=m0P[:], in_=xsP[:], func=AF.Square, accum_out=sl[:, 0:1]), reads=[BxsP], writes=[Bm0P, Bsl])
            r = rstd_from_ss(sl, Bsl, 1, D)
            S.op("dve", lambda e: e.tensor_scalar(out=m0P[:], in0=xsP[:], scalar1=r, scalar2=None, op0=ALU.mult), reads=[BxsP, Bsl], writes=[Bm0P])
            transpose_to(lambda c: m0P[:, c * 128:(c + 1) * 128], 8, 128, Bm0P, h2T[:, :, s * 128:(s + 1) * 128], bf("h2T"), gcol0=g0 + 8)

        def G(i):
            for c in range(NFF):
                gb, Bg = bk[c % 2], Bbk[c % 2]
                ub, Bu = bk[2 + c % 2], Bbk[2 + c % 2]
                S.begin_atom()
                for k in range(8):
                    S.op("pe", lambda e, k=k, c=c, gb=gb: e.matmul(gb[:, :], lhsT=w_gate[:, k, c * 128:(c + 1) * 128], rhs=h2T[:, k, :], start=(k == 0), stop=(k == 7)),
                         reads=[bf(f"w_gate{k}_{c // 11}"), bf("h2T")], writes=[Bg], inc=(k == 7))
                S.end_atom()
                S.begin_atom()
                for k in range(8):
                    S.op("pe", lambda e, k=k, c=c, ub=ub: e.matmul(ub[:, :], lhsT=w_up[:, k, c * 128:(c + 1) * 128], rhs=h2T[:, k, :], start=(k == 0), stop=(k == 7)),
                         reads=[bf(f"w_up{k}_{c // 11}"), bf("h2T")], writes=[Bu], inc=(k == 7))
                S.end_atom()
                sg, Bsg = sg_f[c % 2], bf(f"xn_f{c % 2}")
                S.op("act", lambda e, gb=gb, sg=sg: e.activation(out=sg[:], in_=gb[:, :], func=AF.Silu), reads=[Bg], writes=[Bsg])
                S.op("dve", lambda e, c=c, ub=ub, sg=sg: e.tensor_tensor(out=aT[:, c, :], in0=ub[:, :], in1=sg[:], op=ALU.mult), reads=[Bu, Bsg], writes=[bf(f"aT{c}")])

        def Dn(i):
            for s in range(4):
                n = 4 * i + s
                t0 = n * 128
                ba, bb = [(0, 1), (2, 3)][s % 2]
                S.begin_atom()
                for c in range(NFF):
                    S.op("pe", lambda e, c=c, s=s, ba=ba: e.matmul(bk[ba][:, :], lhsT=aT[:, c, s * 128:(s + 1) * 128], rhs=w_down[:, c, 0:512], start=(c == 0), stop=(c == NFF - 1)),
                         reads=[bf(f"aT{c}"), bf(f"w_down{c}")], writes=[Bbk[ba]], inc=False)
                    S.op("pe", lambda e, c=c, s=s, bb=bb: e.matmul(bk[bb][:, :], lhsT=aT[:, c, s * 128:(s + 1) * 128], rhs=w_down[:, c, 512:1024], start=(c == 0), stop=(c == NFF - 1)),
                         reads=[bf(f"aT{c}"), bf(f"w_down{c}")], writes=[Bbk[bb]], inc=(c == NFF - 1))
                S.end_atom()
                post_norm_inplace(postg_f, bf("postg_f"), ba, bb, xn_f[s % 2], bf(f"xn_f{s % 2}"))
                S.dma("sp", lambda e, t0=t0: e.dma_start(out=xs_f[:], in_=out_d[t0:t0 + 128, :]), reads=[Xd[n]], writes=[bf("xs_f")])
                add_banks(xs_f, bf("xs_f"), ba, bb)
                S.dma("sp", lambda e, t0=t0: e.dma_start(out=out_d[t0:t0 + 128, :], in_=xs_f[:]), reads=[bf("xs_f")], writes=[Xd[n]])

        ntl = DBG_T if DBG_STAGE == 'full' else 0
        if ntl:
            for s in range(4):
                P1(0, s)
            for s in range(4):
                P2(0, s)
        for i in range(ntl):
            sG = S.record(lambda: G(i))
            sP1 = S.record(lambda: [P1(i + 1, s) for s in range(4)]) if i + 1 < ntl else []
            S.replay(S.interleave(sG, sP1))
            sD = S.record(lambda: Dn(i))
            sP2 = S.record(lambda: [P2(i + 1, s) for s in range(4)]) if i + 1 < ntl else []
            S.replay(S.interleave(sD, sP2))

    for l in range(L if DBG_STAGE != 'P' else 0):
        mixer_pass(l)
        if DBG_STAGE not in ('A0', 'B0'):
            ffn_pass(l)
    S.finish("sp", Xd)
    S.barrier()
    return nc


def _layouts(p):
    L = DEPTH
    gT = np.zeros((128, L * GW), np.float32)
    for l in range(L):
        o = l * GW
        gT[:, o + 0:o + 8] = p["mix_pre_g"][l].reshape(8, 128).T
        gT[:, o + 8:o + 16] = p["ffn_pre_g"][l].reshape(8, 128).T
        gT[:, o + 16:o + 24] = p["out_norm_g"][l].reshape(8, 128).T
        gT[:, o + 24:o + 27] = p["q_norm_g"][l].reshape(3, 128).T
        gT[:, o + 27:o + 29] = p["kv_norm_g"][l].reshape(2, 128).T
        gT[:, o + 29:o + 33] = p["b_sp"][l].T
    bc = np.zeros((L, 128, BCW), np.float32)
    bc[:, :, 0:1024] = p["mix_post_g"][:, None, :]
    bc[:, :, 1024:2048] = p["ffn_post_g"][:, None, :]
    bc[:, :, 2048:2304] = p["sg_ln_g"][:, None, :]
    bc[:, :, 2304:2560] = p["sg_ln_b"][:, None, :]
    bc[:, :, 2560:3328] = p["conv_w"].reshape(L, 1, 768)
    wspT = np.ascontiguousarray(np.transpose(p["w_sp"], (0, 3, 1, 2))).reshape(L, 128, 512)
    return gT, bc, wspT


_INVF = (np.float32(1.0) / (np.float32(10000.0) ** (np.arange(16, dtype=np.float32) / np.float32(16)))).astype(np.float32)


def kernel(x, positions, mix_pre_g, mix_post_g, ffn_pre_g, ffn_post_g, w_in, q_norm_g, w_uq, kv_norm_g, w_ukv,
           sg_ln_g, sg_ln_b, w_sp, b_sp, conv_w, out_norm_g, w_out, w_gate, w_up, w_down, _depth=DEPTH, _cores=8):
    p = dict(mix_pre_g=mix_pre_g, mix_post_g=mix_post_g, ffn_pre_g=ffn_pre_g, ffn_post_g=ffn_post_g, q_norm_g=q_norm_g,
             kv_norm_g=kv_norm_g, sg_ln_g=sg_ln_g, sg_ln_b=sg_ln_b, w_sp=w_sp, b_sp=b_sp, conv_w=conv_w, out_norm_g=out_norm_g)
    p = {k: np.asarray(v, np.float32) for k, v in p.items()}
    gT, bc, wspT = _layouts(p)
    f = lambda a: np.ascontiguousarray(np.asarray(a, np.float32))
    shared = {"invf": np.ascontiguousarray(np.broadcast_to(_INVF[None, :], (128, 16))), "w_in": f(w_in), "w_uq": f(w_uq), "w_ukv": f(w_ukv),
              "w_out": f(w_out), "w_gate": f(w_gate), "w_up": f(w_up), "w_down": f(w_down), "gT": gT, "bc": bc, "wspT": wspT}
    x = np.asarray(x, np.float32)
    positions = np.asarray(positions, np.int32)
    nc = build_program(_depth)
    in_maps = []
    for b in range(_cores):
        m = dict(shared)
        m["x"] = np.ascontiguousarray(x[b])
        m["pos"] = np.ascontiguousarray(positions[b].reshape(NSUB, 128).T)
        in_maps.append(m)
    res = run_bass_kernel_spmd(nc, in_maps, core_ids=list(range(_cores)))
    return np.stack([np.asarray(r["out"], np.float32) for r in res.results], axis=0)
```

```python
import math
import os
import numpy as np
import concourse.bass as bass
import concourse.mybir as mybir
from concourse.bass_utils import run_bass_kernel_spmd

F32 = mybir.dt.float32
BF16 = mybir.dt.bfloat16
I32 = mybir.dt.int32
AF = mybir.ActivationFunctionType
ALU = mybir.AluOpType
AX = mybir.AxisListType

D = 1024
SEQ = 4096
DEPTH = 4
NSUB = SEQ // 128
NTILE = SEQ // 512
INW = 1952
H = 8
DFF = 2816
NFF = DFF // 128
EPS = 1e-6
SCALE = 96.0 ** -0.5
GW = 33
BCW = 3328
ND = 8


class Buf:
    __slots__ = ("name", "w", "r", "psum")

    def __init__(self, name, psum=False):
        self.name = name
        self.w = None
        self.r = {}
        self.psum = psum


class Sched:
    def __init__(self, nc):
        self.nc = nc
        self.E = {"pe": nc.tensor, "act": nc.scalar, "dve": nc.vector, "pool": nc.gpsimd, "sp": nc.sync}
        self.sem = {k: nc.alloc_semaphore("s_" + k) for k in self.E}
        self.cnt = {k: 0 for k in self.E}
        self.seen = {k: {} for k in self.E}
        self.pend = {k: [] for k in self.E}
        self.dsem = {q: [nc.alloc_semaphore(f"d_{q}{i}") for i in range(ND)] for q in ("sp", "pool")}
        self.dcnt = {q: [0] * ND for q in self.dsem}
        self.drr = {q: 0 for q in self.dsem}
        self.rec = None
        self.adepth = 0

    def _rec(self, item):
        if self.adepth > 0:
            if self._open is None:
                self._open = []
                self.rec.append(self._open)
            self._open.append(item)
        else:
            self.rec.append([item])

    def begin_atom(self):
        self.adepth += 1
        if self.adepth == 1:
            self._open = None

    def end_atom(self):
        self.adepth -= 1
        if self.adepth == 0:
            self._open = None

    def record(self, f):
        assert self.rec is None
        self.rec = []
        self._open = None
        f()
        atoms, self.rec = self.rec, None
        return atoms

    @staticmethod
    def interleave(*streams):
        streams = [st for st in streams if st]
        idx = [0] * len(streams)
        out = []
        total = sum(len(st) for st in streams)
        while len(out) < total:
            k = min((idx[j] / len(streams[j]), j) for j in range(len(streams)) if idx[j] < len(streams[j]))[1]
            out.append(streams[k][idx[k]])
            idx[k] += 1
        return out

    def replay(self, atoms):
        assert self.rec is None
        for atom in atoms:
            for kind, args, kw in atom:
                (self.op if kind == "op" else self.dma)(*args, **kw)

    def _collect(self, eng, reads, writes, skip_own_war=True):
        toks = {}
        own = self.sem.get(eng)

        def need(s, v, war=False):
            if s is own and (eng == "pe" or (war and skip_own_war)):
                return
            if toks.get(s, 0) < v:
                toks[s] = v

        for b in reads:
            if b.w is not None:
                need(*b.w)
            if b.psum:
                for s, v in b.r.items():
                    if s is not own:
                        need(s, v)
        for b in writes:
            if b.w is not None:
                need(*b.w)
            for s, v in b.r.items():
                need(s, v, war=True)
        return toks

    def _emit_waits(self, eng, toks):
        e = self.E[eng]
        seen = self.seen[eng]
        for s, v in toks.items():
            if seen.get(s, 0) < v:
                e.wait_ge(s, v)
                seen[s] = v

    def op(self, eng, fn, reads=(), writes=(), inc=True):
        if self.rec is not None:
            self._rec(("op", (eng, fn), dict(reads=tuple(reads), writes=tuple(writes), inc=inc)))
            return None
        self._emit_waits(eng, self._collect(eng, reads, writes))
        ins = fn(self.E[eng])
        self.pend[eng].append((tuple(reads), tuple(writes)))
        if inc:
            self.cnt[eng] += 1
            s = self.sem[eng]
            ins.then_inc(s, 1)
            v = self.cnt[eng]
            for rs, ws in self.pend[eng]:
                for b in rs:
                    b.r[s] = v
                for b in ws:
                    b.w = (s, v)
                    b.r = {}
            self.pend[eng] = []
        return ins

    def dma(self, q, fn, reads=(), writes=()):
        if self.rec is not None:
            self._rec(("dma", (q, fn), dict(reads=tuple(reads), writes=tuple(writes))))
            return None
        assert not self.pend[q]
        i = self.drr[q]
        self.drr[q] = (i + 1) % ND
        s = self.dsem[q][i]
        toks = self._collect(q, reads, writes, skip_own_war=False)
        prev = 16 * self.dcnt[q][i]
        if prev and toks.get(s, 0) < prev:
            toks[s] = prev
        self._emit_waits(q, toks)
        ins = fn(self.E[q])
        ins.then_inc(s, 16)
        self.dcnt[q][i] += 1
        v = 16 * self.dcnt[q][i]
        for b in reads:
            b.r[s] = v
        for b in writes:
            b.w = (s, v)
            b.r = {}
        return ins

    def handoff(self, src, dst):
        for d in dst:
            for b in src:
                if b.w is not None:
                    s, v = b.w
                    if d.r.get(s, 0) < v:
                        d.r[s] = v
                for s, v in b.r.items():
                    if d.r.get(s, 0) < v:
                        d.r[s] = v

    def barrier(self):
        for k in self.E:
            assert not self.pend[k]
        toks = {}
        for k in self.E:
            if self.cnt[k]:
                toks[self.sem[k]] = self.cnt[k]
        for q in self.dsem:
            for i in range(ND):
                if self.dcnt[q][i]:
                    toks[self.dsem[q][i]] = 16 * self.dcnt[q][i]
        for k in self.E:
            t = {s: v for s, v in toks.items() if s is not self.sem[k]}
            self._emit_waits(k, t)

    def finish(self, eng, bufs):
        toks = {}
        for b in bufs:
            if b.w is not None:
                s, v = b.w
                toks[s] = max(toks.get(s, 0), v)
            for s, v in b.r.items():
                toks[s] = max(toks.get(s, 0), v)
        self._emit_waits(eng, toks)


def build_program(L=DEPTH):
    DBG_T = int(os.environ.get('KDBG_TILES', NTILE))
    DBG_STAGE = os.environ.get('KDBG_STAGE', 'full')
    DBG_STOP = float(os.environ.get('KDBG_STOP', 99))
    nc = bass.Bass("TRN2", target_bir_lowering=False)
    S = Sched(nc)

    def dram(name, shape, dt, kind):
        return nc.dram_tensor(name, shape, dt, kind=kind).ap()

    x_d = dram("x", [SEQ, D], F32, "ExternalInput")
    pos_d = dram("pos", [128, NSUB], I32, "ExternalInput")
    invf_d = dram("invf", [128, 16], F32, "ExternalInput")
    w_in_d = dram("w_in", [DEPTH, D, INW], F32, "ExternalInput")
    w_uq_d = dram("w_uq", [DEPTH, 384, 768], F32, "ExternalInput")
    w_ukv_d = dram("w_ukv", [DEPTH, 256, 1024], F32, "ExternalInput")
    w_out_d = dram("w_out", [DEPTH, D, D], F32, "ExternalInput")
    w_gate_d = dram("w_gate", [DEPTH, D, DFF], F32, "ExternalInput")
    w_up_d = dram("w_up", [DEPTH, D, DFF], F32, "ExternalInput")
    w_down_d = dram("w_down", [DEPTH, DFF, D], F32, "ExternalInput")
    gT_d = dram("gT", [128, DEPTH * GW], F32, "ExternalInput")
    bc_d = dram("bc", [DEPTH, 128, BCW], F32, "ExternalInput")
    wsp_d = dram("wspT", [DEPTH, 128, 512], F32, "ExternalInput")
    out_d = dram("out", [SEQ, D], F32, "ExternalOutput")
    mix_d = dram("mixd", [SEQ, D], BF16, "Internal")

    base = (nc.sbuf_base + 31) // 32 * 32
    top = nc.sbuf_top

    def sb(name, shape, dt, off):
        nb = int(np.prod(shape[1:])) * (2 if dt == BF16 else 4)
        assert off % 32 == 0 and base + off + nb <= top, (name, off, nb, top - base)
        return nc.alloc_sbuf_tensor_at(name, list(shape), dt, offset=base + off)

    ident = sb("ident", [128, 128], BF16, 0)
    tri = sb("tri", [128, 128], BF16, 256)
    mhalf = sb("mhalf", [128, 8], F32, 512)
    cosT = sb("cosT", [128, NSUB, 16], F32, 576)
    sinT = sb("sinT", [128, NSUB, 16], F32, 2624)
    gT = sb("gTs", [128, DEPTH * GW], F32, 4672)
    stt_ = sb("stats", [128, 64], F32, 5216)
    invf = sb("invfs", [128, 16], F32, 5472)
    posi = sb("posi", [128, NSUB], I32, 5536)
    posf = sb("posf", [128, NSUB], F32, 5664)
    stv = sb("stv", [128, 32], F32, 5792)
    CEND = 5920
    BIG = CEND
    MID = BIG + 174080
    assert base + MID + 30720 <= top, (base, MID, top)

    KT = sb("KT", [128, H, SEQ], BF16, BIG + 0)
    Vc = sb("Vc", [128, NSUB, H, 65], BF16, BIG + 65536)
    w_in = sb("w_in_s", [128, 8, INW], BF16, BIG + 98816)
    w_uq = sb("w_uq_s", [128, 3, 768], BF16, BIG + 130048)
    w_ukv = sb("w_ukv_s", [128, 2, 1024], BF16, BIG + 134656)
    qT = sb("qT", [128, H, 512], BF16, BIG + 138752)
    mix_tm = sb("mix_tm", [128, 4, D], BF16, BIG + 146944)
    SA = BIG + 155136
    z_sb = sb("z_sb", [128, INW], F32, SA)
    uv = sb("uv", [128, 512], F32, SA + 7808)
    ysh1 = sb("ysh1", [128, 256], F32, SA + 9856)
    ysh2 = sb("ysh2", [128, 256], F32, SA + 10880)
    acc = sb("acc", [128, 256], F32, SA + 11904)
    sq = sb("sq", [128, 256], F32, SA + 12928)
    vn32 = sb("vn32", [128, 256], F32, SA + 13952)
    cqkv = sb("cqkv", [128, 640], BF16, SA + 14976)
    ycv = [sb("ycv0", [128, 256], F32, SA + 16352), sb("ycv1", [128, 256], F32, SA + 17376)]
    assert SA + 18400 <= MID
    ya = sb("ya", [128, 4, 512], F32, SA)
    PT = [sb(f"PT{r}", [128, 512], BF16, SA + 8192 + 1024 * r) for r in range(4)]
    hT = [sb("hT0", [128, 8, 128], BF16, MID + 0), sb("hT1", [128, 8, 128], BF16, MID + 2048)]
    xs = sb("xs", [128, D], F32, MID + 4096)
    xn = sb("xn", [128, D], BF16, MID + 8192)
    sgln = sb("sgln", [128, 512], F32, MID + 10240)
    convw = sb("convw", [128, 768], F32, MID + 12288)
    wspT = sb("wspTs", [128, 4, 128], BF16, MID + 15360)
    cqT = sb("cqT", [128, 5, 128], BF16, MID + 16384)
    q_tm = sb("q_tm", [128, H, 96], BF16, MID + 17664)
    k_tm = sb("k_tm", [128, H, 96], BF16, MID + 19200)
    rp = sb("rp", [128, 9, 32], F32, MID + 20736)
    rt = [sb(f"rt{i}", [128, 9, 16], F32, MID + 21888 + 576 * i) for i in range(4)]
    rr = sb("rr", [128, 9, 32], F32, MID + 24192)
    vn = sb("vn", [128, 256], BF16, MID + 25344)
    rinv = sb("rinv", [128, 4], F32, MID + 25856)
    w_gate = sb("w_gate_s", [128, 8, DFF], BF16, BIG + 0)
    w_up = sb("w_up_s", [128, 8, DFF], BF16, BIG + 45056)
    w_down = sb("w_down_s", [128, NFF, D], BF16, BIG + 90112)
    w_out = sb("w_out_s", [128, 8, D], BF16, BIG + 135168)
    aT = sb("aT", [128, NFF, 512], BF16, BIG + 151552)
    h2T = sb("h2T", [128, 8, 512], BF16, MID + 0)
    xs_f = sb("xs_f", [128, D], F32, MID + 8192)
    xn_f = [sb("xn_f0", [128, D], BF16, MID + 12288), sb("xn_f1", [128, D], BF16, MID + 14336)]
    sg_f = [sb("sg_f0", [128, 512], F32, MID + 12288), sb("sg_f1", [128, 512], F32, MID + 14336)]
    mixT = sb("mixT", [128, 8, 128], BF16, MID + 16384)
    t_f = sb("t_f", [128, D], F32, MID + 18432)
    postg_m = sb("postg_m", [128, D], F32, MID + 22528)
    postg_f = sb("postg_f", [128, D], F32, MID + 26624)
    m0P = sb("m0P", [128, D], BF16, MID + 30720)

    pT = [nc.alloc_psum_tensor(f"pT{i}", [128, 8, 128], BF16) for i in range(2)]
    bk = [nc.alloc_psum_tensor(f"bk{i}", [128, 512], F32) for i in range(6)]
    BpT = [Buf(f"pT{i}", psum=True) for i in range(2)]
    Bbk = [Buf(f"bk{i}", psum=True) for i in range(6)]
    tcount = [0]

    def next_pT():
        i = tcount[0] % 2
        tcount[0] += 1
        return pT[i], BpT[i]

    B = {}

    def bf(name):
        if name not in B:
            B[name] = Buf(name)
        return B[name]

    Xd = [Buf(f"xd{n}") for n in range(NSUB)]
    Md = [Buf(f"md{i}") for i in range(NTILE)]
    BKT = [Buf(f"kt{n}") for n in range(NSUB)]
    BV = [Buf(f"v{n}") for n in range(NSUB)]

    slot = [0]

    def new_slot():
        k = slot[0] % 16
        slot[0] += 1
        return stt_[:, 4 * k:4 * k + 4], bf(f"slot{k}")

    def rstd_from_ss(sl, Bsl, ncols, Dn):
        if ncols == 2:
            S.op("pool", lambda e: e.tensor_tensor(out=sl[:, 0:1], in0=sl[:, 0:1], in1=sl[:, 1:2], op=ALU.add), reads=[Bsl], writes=[Bsl])
        S.op("pool", lambda e: e.tensor_scalar(out=sl[:, 2:3], in0=sl[:, 0:1], scalar1=1.0 / Dn, scalar2=EPS, op0=ALU.mult, op1=ALU.add), reads=[Bsl], writes=[Bsl])
        S.op("pool", lambda e: e.tensor_tensor(out=sl[:, 3:4], in0=sl[:, 2:3], in1=mhalf[:, 0:1], op=ALU.pow), reads=[Bsl, bf("mhalf")], writes=[Bsl])
        return sl[:, 3:4]

    def transpose_to(src_ap_fn, nchunk, rows, Bsrc, dst_ap, Bdst, gcol0=None, eng="dve"):
        S.begin_atom()
        p, Bp = next_pT()
        for c in range(nchunk):
            S.op("pe", lambda e, c=c: e.transpose(out=p[0:rows, c, :], in_=src_ap_fn(c), identity=ident[:]),
                 reads=[Bsrc, bf("ident")], writes=[Bp], inc=(c == nchunk - 1))
        if gcol0 is None:
            if eng == "act":
                S.op("act", lambda e: e.activation(out=dst_ap, in_=p[0:rows, 0:nchunk, :], func=AF.Copy), reads=[Bp], writes=[Bdst])
            else:
                S.op(eng, lambda e: e.tensor_copy(out=dst_ap, in_=p[0:rows, 0:nchunk, :]), reads=[Bp], writes=[Bdst])
        else:
            g = gT[0:rows, gcol0:gcol0 + nchunk].unsqueeze(2).broadcast_to([rows, nchunk, 128])
            S.op(eng, lambda e: e.tensor_tensor(out=dst_ap, in0=p[0:rows, 0:nchunk, :], in1=g, op=ALU.mult), reads=[Bp, bf("gT")], writes=[Bdst])
        S.end_atom()

    S.op("pool", lambda e: e.memset(ident[:], 1.0), writes=[bf("ident")])
    S.op("pool", lambda e: e.affine_select(out=ident[:], in_=ident[:], pattern=[[-1, 128]], compare_op=ALU.is_equal, fill=0.0, base=0, channel_multiplier=1),
         reads=[bf("ident")], writes=[bf("ident")])
    S.op("pool", lambda e: e.memset(tri[:], 1.0), writes=[bf("tri")])
    S.op("pool", lambda e: e.affine_select(out=tri[:], in_=tri[:], pattern=[[1, 128]], compare_op=ALU.is_ge, fill=0.0, base=0, channel_multiplier=-1),
         reads=[bf("tri")], writes=[bf("tri")])
    S.op("pool", lambda e: e.memset(mhalf[:], -0.5), writes=[bf("mhalf")])
    S.dma("sp", lambda e: e.dma_start(out=gT[:], in_=gT_d), writes=[bf("gT")])
    S.dma("sp", lambda e: e.dma_start(out=invf[:], in_=invf_d), writes=[bf("invf")])
    S.dma("sp", lambda e: e.dma_start(out=posi[:], in_=pos_d), writes=[bf("posi")])
    TWO_PI = 2.0 * math.pi
    C1 = float(np.float32(6.28125))
    C2 = float(np.float32(TWO_PI - 6.28125))
    MAGIC = 12582912.0
    ang = sb("ang", [128, NSUB, 16], F32, BIG + 0)
    kk = sb("kk", [128, NSUB, 16], F32, BIG + 2048)
    t2 = sb("t2p", [128, NSUB, 16], F32, BIG + 4096)
    S.op("dve", lambda e: e.tensor_copy(out=posf[:], in_=posi[:]), reads=[bf("posi")], writes=[bf("posf")])
    S.op("dve", lambda e: e.tensor_tensor(out=ang[:], in0=posf[:].unsqueeze(2).broadcast_to([128, NSUB, 16]),
                                          in1=invf[:].unsqueeze(1).broadcast_to([128, NSUB, 16]), op=ALU.mult),
         reads=[bf("posf"), bf("invf")], writes=[bf("ang")])
    S.op("dve", lambda e: e.tensor_scalar(out=t2[:], in0=ang[:], scalar1=1.0 / TWO_PI, scalar2=MAGIC, op0=ALU.mult, op1=ALU.add), reads=[bf("ang")], writes=[bf("t2")])
    S.op("dve", lambda e: e.tensor_scalar(out=kk[:], in0=t2[:], scalar1=-MAGIC, scalar2=None, op0=ALU.add), reads=[bf("t2")], writes=[bf("kk")])
    S.op("dve", lambda e: e.scalar_tensor_tensor(out=ang[:], in0=kk[:], scalar=-C1, in1=ang[:], op0=ALU.mult, op1=ALU.add), reads=[bf("kk"), bf("ang")], writes=[bf("ang")])
    S.op("dve", lambda e: e.scalar_tensor_tensor(out=ang[:], in0=kk[:], scalar=-C2, in1=ang[:], op0=ALU.mult, op1=ALU.add), reads=[bf("kk"), bf("ang")], writes=[bf("ang")])
    S.op("dve", lambda e: e.tensor_scalar(out=ang[:], in0=ang[:], scalar1=math.pi, scalar2=-math.pi, op0=ALU.min, op1=ALU.max), reads=[bf("ang")], writes=[bf("ang")])
    S.op("act", lambda e: e.activation(out=sinT[:], in_=ang[:], func=AF.Sin), reads=[bf("ang")], writes=[bf("sin")])
    S.op("dve", lambda e: e.tensor_scalar(out=t2[:], in0=ang[:], scalar1=-1.0, scalar2=None, op0=ALU.mult), reads=[bf("ang")], writes=[bf("t2")])
    S.op("dve", lambda e: e.tensor_tensor(out=t2[:], in0=t2[:], in1=ang[:], op=ALU.max), reads=[bf("ang"), bf("t2")], writes=[bf("t2")])
    S.op("dve", lambda e: e.tensor_scalar(out=t2[:], in0=t2[:], scalar1=-1.0, scalar2=math.pi / 2, op0=ALU.mult, op1=ALU.add), reads=[bf("t2")], writes=[bf("t2")])
    S.op("act", lambda e: e.activation(out=cosT[:], in_=t2[:], func=AF.Sin), reads=[bf("t2")], writes=[bf("cos")])

    stA = [bf("z0"), bf("z1"), bf("z2"), bf("z3"), bf("uv"), bf("ysh1"), bf("ysh2"), bf("acc"), bf("sq"), bf("vn32"), bf("cqkv")]
    stB = [bf("ya"), bf("PT0"), bf("PT1"), bf("PT2"), bf("PT3")]
    rot = [0]

    def mixer_pass(l):
        xsrc = x_d if l == 0 else out_d
        g0 = l * GW
        S.barrier()
        for k in range(8):
            S.dma("pool", lambda e, k=k: e.dma_start(out=w_in[:, k, :], in_=w_in_d[l, k * 128:(k + 1) * 128, :]), writes=[bf(f"w_in{k}")])
        for k in range(3):
            S.dma("pool", lambda e, k=k: e.dma_start(out=w_uq[:, k, :], in_=w_uq_d[l, k * 128:(k + 1) * 128, :]), writes=[bf("w_uq")])
        for k in range(2):
            S.dma("pool", lambda e, k=k: e.dma_start(out=w_ukv[:, k, :], in_=w_ukv_d[l, k * 128:(k + 1) * 128, :]), writes=[bf("w_ukv")])
        S.dma("pool", lambda e: e.dma_start(out=wspT[:].rearrange("p g t -> p (g t)"), in_=wsp_d[l]), writes=[bf("wspT")])
        S.dma("sp", lambda e: e.dma_start(out=sgln[:], in_=bc_d[l, :, 2048:2560]), writes=[bf("sgln")])
        S.dma("sp", lambda e: e.dma_start(out=convw[:], in_=bc_d[l, :, 2560:3328]), writes=[bf("convw")])
        S.op("dve", lambda e: e.tensor_tensor(out=wspT[:], in0=wspT[:], in1=tri[:].unsqueeze(1).broadcast_to([128, 4, 128]), op=ALU.mult),
             reads=[bf("wspT"), bf("tri")], writes=[bf("wspT")])
        S.op("pool", lambda e: e.memset(Vc[:, :, :, 64:65], 1.0), writes=BV)
        S.op("pool", lambda e: e.memset(ycv[1][:], 0.0), writes=[bf("ycv1")])

        nsub_run = 4 * DBG_T

        def f_early(n_):
            t0_ = n_ * 128
            S.dma("sp", lambda e: e.dma_start(out=xs[:], in_=xsrc[t0_:t0_ + 128, :]), reads=[Xd[n_]], writes=[bf("xs")])
            sl_, Bsl_ = new_slot()
            S.op("act", lambda e: e.activation(out=xn[:], in_=xs[:], func=AF.Square, accum_out=sl_[:, 0:1]), reads=[bf("xs")], writes=[bf("xn"), Bsl_])
            r_ = rstd_from_ss(sl_, Bsl_, 1, D)
            S.op("dve", lambda e: e.tensor_scalar(out=xn[:], in0=xs[:], scalar1=r_, scalar2=None, op0=ALU.mult), reads=[bf("xs"), Bsl_], writes=[bf("xn")])
            transpose_to(lambda c: xn[:, c * 128:(c + 1) * 128], 8, 128, bf("xn"), hT[n_ % 2][:], bf(f"hT{n_ % 2}"), gcol0=g0 + 0)

        CG = [(0, 512), (512, 1024), (1024, 1536), (1536, INW)]

        def inproj(n_, groups):
            hb_, Bh_ = hT[n_ % 2], bf(f"hT{n_ % 2}")
            S.begin_atom()
            for k in range(8):
                for q in groups:
                    c0, c1 = CG[q]
                    S.op("pe", lambda e, k=k, q=q, c0=c0, c1=c1: e.matmul(bk[q][:, 0:c1 - c0], lhsT=hb_[:, k, :], rhs=w_in[:, k, c0:c1], start=(k == 0), stop=(k == 7)),
                         reads=[Bh_, bf(f"w_in{k}")], writes=[Bbk[q]], inc=(k == 7 and q == groups[-1]))
            S.end_atom()

        for i in range(DBG_T):
            S.handoff(stB, stA)
            for s in range(4):
                n = 4 * i + s
                t0 = n * 128
                hb = hT[n % 2]
                Bh = bf(f"hT{n % 2}")
                if n == 0:
                    f_early(0)
                cg = [(0, 512), (512, 1024), (1024, 1536), (1536, INW)]
                inproj(n, [2] if (n > 0 and s != 0) else [0, 1, 2, 3])
                for q, (c0, c1) in enumerate(cg):
                    if q % 2 == 0:
                        S.op("act", lambda e, q=q, c0=c0, c1=c1: e.activation(out=z_sb[:, c0:c1], in_=bk[q][:, 0:c1 - c0], func=AF.Copy), reads=[Bbk[q]], writes=[bf(f"z{q}")])
                    else:
                        S.op("dve", lambda e, q=q, c0=c0, c1=c1: e.tensor_copy(out=z_sb[:, c0:c1], in_=bk[q][:, 0:c1 - c0]), reads=[Bbk[q]], writes=[bf(f"z{q}")])
                def br_mla():
                    slq, Bq = new_slot()
                    slk, Bk_ = new_slot()
                    S.op("act", lambda e: e.activation(out=cqkv[:, 0:384], in_=z_sb[:, 0:384], func=AF.Square, accum_out=slq[:, 0:1]), reads=[bf("z0")], writes=[bf("cqkv"), Bq])
                    S.op("act", lambda e: e.activation(out=cqkv[:, 384:640], in_=z_sb[:, 384:640], func=AF.Square, accum_out=slk[:, 0:1]), reads=[bf("z0"), bf("z1")], writes=[bf("cqkv"), Bk_])
                    rq = rstd_from_ss(slq, Bq, 1, 384)
                    rk = rstd_from_ss(slk, Bk_, 1, 256)
                    S.op("dve", lambda e: e.tensor_scalar(out=cqkv[:, 0:384], in0=z_sb[:, 0:384], scalar1=rq, scalar2=None, op0=ALU.mult), reads=[bf("z0"), Bq], writes=[bf("cqkv")])
                    S.op("dve", lambda e: e.tensor_scalar(out=cqkv[:, 384:640], in0=z_sb[:, 384:640], scalar1=rk, scalar2=None, op0=ALU.mult), reads=[bf("z0"), bf("z1"), Bk_], writes=[bf("cqkv")])
                    transpose_to(lambda c: cqkv[:, c * 128:(c + 1) * 128], 5, 128, bf("cqkv"), cqT[:], bf("cqT"), gcol0=g0 + 24)
                    S.begin_atom()
                    for k in range(3):
                        S.op("pe", lambda e, k=k: e.matmul(bk[4][:, 0:480], lhsT=cqT[:, k, :], rhs=w_uq[:, k, 0:480], start=(k == 0), stop=(k == 2)),
                             reads=[bf("cqT"), bf("w_uq")], writes=[Bbk[4]], inc=False)
                        S.op("pe", lambda e, k=k: e.matmul(bk[5][:, 0:288], lhsT=cqT[:, k, :], rhs=w_uq[:, k, 480:768], start=(k == 0), stop=(k == 2)),
                             reads=[bf("cqT"), bf("w_uq")], writes=[Bbk[5]], inc=(k == 2))
                    S.end_atom()
                    q4 = bk[4][:, 0:480].rearrange("p (h c) -> p h c", c=96)
                    q5 = bk[5][:, 0:288].rearrange("p (h c) -> p h c", c=96)
                    S.op("act", lambda e: e.activation(out=q_tm[:, 0:5, 0:64], in_=q4[:, :, 0:64], func=AF.Copy), reads=[Bbk[4]], writes=[bf("q_tm")])
                    S.op("act", lambda e: e.activation(out=q_tm[:, 5:8, 0:64], in_=q5[:, :, 0:64], func=AF.Copy), reads=[Bbk[5]], writes=[bf("q_tm")])
                    S.op("dve", lambda e: e.tensor_copy(out=rp[:, 0:5, :], in_=q4[:, :, 64:96]), reads=[Bbk[4]], writes=[bf("rp")])
                    S.op("dve", lambda e: e.tensor_copy(out=rp[:, 5:8, :], in_=q5[:, :, 64:96]), reads=[Bbk[5]], writes=[bf("rp")])
                    S.begin_atom()
                    for k in range(2):
                        S.op("pe", lambda e, k=k: e.matmul(bk[4][:, :], lhsT=cqT[:, 3 + k, :], rhs=w_ukv[:, k, 0:512], start=(k == 0), stop=(k == 1)),
                             reads=[bf("cqT"), bf("w_ukv")], writes=[Bbk[4]], inc=False)
                        S.op("pe", lambda e, k=k: e.matmul(bk[5][:, :], lhsT=cqT[:, 3 + k, :], rhs=w_ukv[:, k, 512:1024], start=(k == 0), stop=(k == 1)),
                             reads=[bf("cqT"), bf("w_ukv")], writes=[Bbk[5]], inc=(k == 1))
                    S.end_atom()
                    k0 = bk[4][:, :].rearrange("p (h c) -> p h c", c=128)
                    k1 = bk[5][:, :].rearrange("p (h c) -> p h c", c=128)
                    S.op("dve", lambda e: e.tensor_copy(out=rp[:, 8, :], in_=z_sb[:, 640:672]), reads=[bf("z1")], writes=[bf("rp")])
                    cs = cosT[:, n, :].unsqueeze(1).broadcast_to([128, 9, 16])
                    sn = sinT[:, n, :].unsqueeze(1).broadcast_to([128, 9, 16])
                    S.op("dve", lambda e: e.tensor_tensor(out=rt[0][:], in0=rp[:, :, 0:16], in1=cs, op=ALU.mult), reads=[bf("rp"), bf("cos")], writes=[bf("rt0")])
                    S.op("dve", lambda e: e.tensor_tensor(out=rt[1][:], in0=rp[:, :, 16:32], in1=sn, op=ALU.mult), reads=[bf("rp"), bf("sin")], writes=[bf("rt1")])
                    S.op("dve", lambda e: e.tensor_tensor(out=rt[2][:], in0=rp[:, :, 16:32], in1=cs, op=ALU.mult), reads=[bf("rp"), bf("cos")], writes=[bf("rt2")])
                    S.op("dve", lambda e: e.tensor_tensor(out=rt[3][:], in0=rp[:, :, 0:16], in1=sn, op=ALU.mult), reads=[bf("rp"), bf("sin")], writes=[bf("rt3")])
                    S.op("dve", lambda e: e.tensor_tensor(out=rr[:, :, 0:16], in0=rt[0][:], in1=rt[1][:], op=ALU.subtract), reads=[bf("rt0"), bf("rt1")], writes=[bf("rr")])
                    S.op("dve", lambda e: e.tensor_tensor(out=rr[:, :, 16:32], in0=rt[2][:], in1=rt[3][:], op=ALU.add), reads=[bf("rt2"), bf("rt3")], writes=[bf("rr")])
                    S.op("dve", lambda e: e.tensor_copy(out=q_tm[:, :, 64:96], in_=rr[:, 0:8, :]), reads=[bf("rr")], writes=[bf("q_tm")])
                    S.op("dve", lambda e: e.tensor_copy(out=k_tm[:, :, 64:96], in_=rr[:, 8:9, :].broadcast_to([128, 8, 32])), reads=[bf("rr")], writes=[bf("k_tm")])
                    S.op("act", lambda e: e.activation(out=k_tm[:, 0:4, 0:64], in_=k0[:, :, 0:64], func=AF.Copy), reads=[Bbk[4]], writes=[bf("k_tm")])
                    S.op("dve", lambda e: e.tensor_copy(out=k_tm[:, 4:8, 0:64], in_=k1[:, :, 0:64]), reads=[Bbk[5]], writes=[bf("k_tm")])
                    S.op("act", lambda e: e.activation(out=Vc[:, n, 0:4, 0:64], in_=k0[:, :, 64:128], func=AF.Copy), reads=[Bbk[4]], writes=[BV[n]])
                    S.op("dve", lambda e: e.tensor_copy(out=Vc[:, n, 4:8, 0:64], in_=k1[:, :, 64:128]), reads=[Bbk[5]], writes=[BV[n]])
                    transpose_to(lambda h: q_tm[:, h, :], 8, 96, bf("q_tm"), qT[0:96, :, s * 128:(s + 1) * 128], bf("qT"), eng="act")
                    transpose_to(lambda h: k_tm[:, h, :], 8, 96, bf("k_tm"), KT[0:96, :, t0:t0 + 128], BKT[n], eng="dve")

                def br_sgu():
                    S.op("act", lambda e: e.activation(out=uv[:], in_=z_sb[:, 672:1184], func=AF.Gelu_apprx_tanh), reads=[bf("z1"), bf("z2")], writes=[bf("uv")])
                    v3 = uv[:, 256:512].rearrange("p (g e) -> p g e", g=4)
                    Bsv = bf("stv")
                    S.op("dve", lambda e: e.tensor_reduce(out=stv[:, 0:4], in_=v3, axis=AX.X, op=ALU.add), reads=[bf("uv")], writes=[Bsv])
                    S.op("dve", lambda e: e.tensor_tensor(out=sq[:], in0=uv[:, 256:512], in1=uv[:, 256:512], op=ALU.mult), reads=[bf("uv")], writes=[bf("sq")])
                    S.op("dve", lambda e: e.tensor_reduce(out=stv[:, 4:8], in_=sq[:].rearrange("p (g e) -> p g e", g=4), axis=AX.X, op=ALU.add), reads=[bf("sq")], writes=[Bsv])
                    S.op("pool", lambda e: e.tensor_scalar(out=stv[:, 8:12], in0=stv[:, 0:4], scalar1=1.0 / 64, scalar2=None, op0=ALU.mult), reads=[Bsv], writes=[Bsv])
                    S.op("pool", lambda e: e.tensor_tensor(out=stv[:, 12:16], in0=stv[:, 8:12], in1=stv[:, 8:12], op=ALU.mult), reads=[Bsv], writes=[Bsv])
                    S.op("pool", lambda e: e.tensor_scalar(out=stv[:, 24:28], in0=stv[:, 4:8], scalar1=1.0 / 64, scalar2=EPS, op0=ALU.mult, op1=ALU.add), reads=[Bsv], writes=[Bsv])
                    S.op("pool", lambda e: e.tensor_tensor(out=stv[:, 16:20], in0=stv[:, 24:28], in1=stv[:, 12:16], op=ALU.subtract), reads=[Bsv], writes=[Bsv])
                    S.op("pool", lambda e: e.tensor_tensor(out=stv[:, 20:24], in0=stv[:, 16:20], in1=mhalf[:, 0:4], op=ALU.pow), reads=[Bsv, bf("mhalf")], writes=[Bsv])
                    vn3 = vn32[:].rearrange("p (g e) -> p g e", g=4)
                    S.op("dve", lambda e: e.tensor_tensor(out=vn3, in0=v3, in1=stv[:, 8:12].unsqueeze(2).broadcast_to([128, 4, 64]), op=ALU.subtract), reads=[bf("uv"), Bsv], writes=[bf("vn32")])
                    S.op("dve", lambda e: e.tensor_tensor(out=vn3, in0=vn3, in1=stv[:, 20:24].unsqueeze(2).broadcast_to([128, 4, 64]), op=ALU.mult), reads=[bf("vn32"), Bsv], writes=[bf("vn32")])
                    S.op("dve", lambda e: e.tensor_tensor(out=vn32[:], in0=vn32[:], in1=sgln[:, 0:256], op=ALU.mult), reads=[bf("vn32"), bf("sgln")], writes=[bf("vn32")])
                    S.op("dve", lambda e: e.tensor_tensor(out=vn[:], in0=vn32[:], in1=sgln[:, 256:512], op=ALU.add), reads=[bf("vn32"), bf("sgln")], writes=[bf("vn")])
                    S.begin_atom()
                    for g in range(4):
                        S.op("pe", lambda e, g=g: e.matmul(bk[2][:, g * 64:(g + 1) * 64], lhsT=wspT[:, g, :], rhs=vn[:, g * 64:(g + 1) * 64], start=True, stop=True, skip_group_check=True),
                             reads=[bf("wspT"), bf("vn")], writes=[Bbk[2]], inc=(g == 3))
                    S.end_atom()
                    for g in range(4):
                        S.op("dve", lambda e, g=g: e.scalar_tensor_tensor(out=sq[:, g * 64:(g + 1) * 64], in0=bk[2][:, g * 64:(g + 1) * 64], scalar=gT[:, g0 + 29 + g:g0 + 30 + g],
                                                                          in1=uv[:, g * 64:(g + 1) * 64], op0=ALU.add, op1=ALU.mult),
                             reads=[Bbk[2], bf("gT"), bf("uv")], writes=[bf("sq")])
                    slb, Bb_ = new_slot()
                    S.op("act", lambda e: e.activation(out=mix_tm[:, s, 512:768], in_=sq[:], func=AF.Square, accum_out=slb[:, 0:1]), reads=[bf("sq")], writes=[bf(f"mix{s}"), Bb_])
                    rb = rstd_from_ss(slb, Bb_, 1, 256)
                    S.op("dve", lambda e: e.tensor_scalar(out=mix_tm[:, s, 512:768], in0=sq[:], scalar1=rb, scalar2=None, op0=ALU.mult), reads=[bf("sq"), Bb_], writes=[bf(f"mix{s}")])

                def br_conv():
                    yc, yp = ycv[n % 2], ycv[(n + 1) % 2]
                    Byc, Byp = bf(f"ycv{n % 2}"), bf(f"ycv{(n + 1) % 2}")
                    S.op("dve", lambda e: e.tensor_tensor(out=yc[:], in0=z_sb[:, 1440:1696], in1=z_sb[:, 1696:1952], op=ALU.mult), reads=[bf("z2"), bf("z3")], writes=[Byc])
                    S.dma("sp", lambda e: e.dma_start(out=ysh1[1:128, :], in_=yc[0:127, :]), reads=[Byc], writes=[bf("ysh1")])
                    S.dma("sp", lambda e: e.dma_start(out=ysh1[0:1, :], in_=yp[127:128, :]), reads=[Byp], writes=[bf("ysh1")])
                    S.dma("sp", lambda e: e.dma_start(out=ysh2[2:128, :], in_=yc[0:126, :]), reads=[Byc], writes=[bf("ysh2")])
                    S.dma("sp", lambda e: e.dma_start(out=ysh2[0:2, :], in_=yp[126:128, :]), reads=[Byp], writes=[bf("ysh2")])
                    S.op("dve", lambda e: e.tensor_tensor(out=acc[:], in0=yc[:], in1=convw[:, 512:768], op=ALU.mult), reads=[Byc, bf("convw")], writes=[bf("acc")])
                    S.op("dve", lambda e: e.tensor_tensor(out=ysh1[:], in0=ysh1[:], in1=convw[:, 256:512], op=ALU.mult), reads=[bf("ysh1"), bf("convw")], writes=[bf("ysh1")])
                    S.op("dve", lambda e: e.tensor_tensor(out=acc[:], in0=acc[:], in1=ysh1[:], op=ALU.add), reads=[bf("acc"), bf("ysh1")], writes=[bf("acc")])
                    S.op("dve", lambda e: e.tensor_tensor(out=ysh2[:], in0=ysh2[:], in1=convw[:, 0:256], op=ALU.mult), reads=[bf("ysh2"), bf("convw")], writes=[bf("ysh2")])
                    S.op("dve", lambda e: e.tensor_tensor(out=acc[:], in0=acc[:], in1=ysh2[:], op=ALU.add), reads=[bf("acc"), bf("ysh2")], writes=[bf("acc")])
                    S.op("dve", lambda e: e.tensor_tensor(out=acc[:], in0=acc[:], in1=z_sb[:, 1184:1440], op=ALU.mult), reads=[bf("acc"), bf("z2")], writes=[bf("acc")])
                    slc, Bc_ = new_slot()
                    S.op("act", lambda e: e.activation(out=mix_tm[:, s, 768:1024], in_=acc[:], func=AF.Square, accum_out=slc[:, 0:1]), reads=[bf("acc")], writes=[bf(f"mix{s}"), Bc_])
                    rc = rstd_from_ss(slc, Bc_, 1, 256)
                    S.op("dve", lambda e: e.tensor_scalar(out=mix_tm[:, s, 768:1024], in0=acc[:], scalar1=rc, scalar2=None, op0=ALU.mult), reads=[bf("acc"), Bc_], writes=[bf(f"mix{s}")])


                merged = S.interleave(S.record(br_mla), S.record(br_sgu), S.record(br_conv))
                if n + 1 < nsub_run:
                    pre = S.record(lambda: f_early(n + 1))
                    if s != 3:
                        pre += S.record(lambda: inproj(n + 1, [0, 1, 3]))
                    for j_, atom_ in enumerate(pre):
                        merged.insert(min(len(merged), 2 + 7 * j_), atom_)
                S.replay(merged)
            if DBG_STAGE in ('A', 'A0'):
                continue
            S.handoff(stA, stB)
            nkb = 4 * i + 4
            LA = 3

            def att_front(h, kb):
                j0 = max(0, kb - 4 * i)
                c0 = j0 * 128
                r = rot[0] % 4
                rot[0] += 1
                sbk, Bs = bk[r], Bbk[r]
                S.op("pe", lambda e: e.matmul(sbk[:, c0:512], lhsT=KT[0:96, h, kb * 128:(kb + 1) * 128], rhs=qT[0:96, h, c0:512], start=True, stop=True),
                     reads=[BKT[kb], bf("qT")], writes=[Bs])
                S.op("act", lambda e: e.activation(out=PT[r][:, c0:512], in_=sbk[:, c0:512], func=AF.Exp, scale=SCALE), reads=[Bs], writes=[bf(f"PT{r}")])
                if kb >= 4 * i:
                    S.op("pool", lambda e: e.tensor_tensor(out=PT[r][:, c0:c0 + 128], in0=PT[r][:, c0:c0 + 128], in1=tri[:], op=ALU.mult),
                         reads=[bf(f"PT{r}"), bf("tri")], writes=[bf(f"PT{r}")])
                return r

            def att_back(h, kb, r):
                j0 = max(0, kb - 4 * i)
                O = bk[4 + h % 2]
                BO = Bbk[4 + h % 2]
                O3 = O[:, :].rearrange("p (j c) -> p j c", j=4)
                for j in range(j0, 4):
                    S.op("pe", lambda e, j=j: e.matmul(O3[:, j, 0:65], lhsT=PT[r][:, j * 128:(j + 1) * 128], rhs=Vc[:, kb, h, :],
                                                       start=(kb == 0 and j == 0), stop=(kb == 4 * i + j), skip_group_check=True),
                         reads=[bf(f"PT{r}"), BV[kb]], writes=[BO], inc=(j == 3))
                if kb == nkb - 1:
                    S.op("dve", lambda e: e.reciprocal(out=rinv[:].unsqueeze(2), in_=O3[:, :, 64:65]), reads=[BO], writes=[bf("rinv")])
                    S.op("dve", lambda e: e.tensor_tensor(out=ya[:, :, h * 64:(h + 1) * 64], in0=O3[:, :, 0:64], in1=rinv[:].unsqueeze(2).broadcast_to([128, 4, 64]), op=ALU.mult),
                         reads=[BO, bf("rinv")], writes=[bf("ya")])

            inflight = []
            for h in range(H):
                for kb in range(nkb):
                    inflight.append((h, kb, att_front(h, kb)))
                    if len(inflight) > LA:
                        att_back(*inflight.pop(0))
            while inflight:
                att_back(*inflight.pop(0))
            for j in range(4):
                sla, Ba_ = new_slot()
                S.op("act", lambda e, j=j: e.activation(out=mix_tm[:, j, 0:512], in_=ya[:, j, :], func=AF.Square, accum_out=sla[:, 0:1]), reads=[bf("ya")], writes=[bf(f"mix{j}"), Ba_])
                ra = rstd_from_ss(sla, Ba_, 1, 512)
                S.op("dve", lambda e, j=j: e.tensor_scalar(out=mix_tm[:, j, 0:512], in0=ya[:, j, :], scalar1=ra, scalar2=None, op0=ALU.mult), reads=[bf("ya"), Ba_], writes=[bf(f"mix{j}")])
            S.dma("sp", lambda e: e.dma_start(out=mix_d[i * 512:(i + 1) * 512, :].rearrange("(j p) d -> p j d", p=128), in_=mix_tm[:]),
                  reads=[bf("mix0"), bf("mix1"), bf("mix2"), bf("mix3")], writes=[Md[i]])

    def ffn_pass(l):
        xsrc = x_d if l == 0 else out_d
        g0 = l * GW
        S.barrier()
        for k in range(8):
            S.dma("pool", lambda e, k=k: e.dma_start(out=w_out[:, k, :], in_=w_out_d[l, k * 128:(k + 1) * 128, :]), writes=[bf("w_out")])
        S.dma("sp", lambda e: e.dma_start(out=postg_m[:], in_=bc_d[l, :, 0:1024]), writes=[bf("postg_m")])
        S.dma("sp", lambda e: e.dma_start(out=postg_f[:], in_=bc_d[l, :, 1024:2048]), writes=[bf("postg_f")])
        HC = DFF // 2
        for hf in range(2):
            for k in range(8):
                S.dma("pool", lambda e, k=k, hf=hf: e.dma_start(out=w_gate[:, k, hf * HC:(hf + 1) * HC], in_=w_gate_d[l, k * 128:(k + 1) * 128, hf * HC:(hf + 1) * HC]),
                      writes=[bf(f"w_gate{k}_{hf}")])
                S.dma("pool", lambda e, k=k, hf=hf: e.dma_start(out=w_up[:, k, hf * HC:(hf + 1) * HC], in_=w_up_d[l, k * 128:(k + 1) * 128, hf * HC:(hf + 1) * HC]),
                      writes=[bf(f"w_up{k}_{hf}")])
        for c in range(NFF):
            S.dma("pool", lambda e, c=c: e.dma_start(out=w_down[:, c, :], in_=w_down_d[l, c * 128:(c + 1) * 128, :]), writes=[bf(f"w_down{c}")])

        xsP, BxsP, Bm0P = t_f, bf("xsP"), bf("m0P")

        def post_norm_inplace(postg, Bpostg, ba, bb, junk, Bjunk):
            sl, Bsl = new_slot()
            S.op("act", lambda e: e.activation(out=junk[:, 0:512], in_=bk[ba][:, :], func=AF.Square, accum_out=sl[:, 0:1]), reads=[Bbk[ba]], writes=[Bjunk, Bsl])
            S.op("act", lambda e: e.activation(out=junk[:, 512:1024], in_=bk[bb][:, :], func=AF.Square, accum_out=sl[:, 1:2]), reads=[Bbk[bb]], writes=[Bjunk, Bsl])
            r = rstd_from_ss(sl, Bsl, 2, D)
            S.op("dve", lambda e: e.scalar_tensor_tensor(out=bk[ba][:, :], in0=bk[ba][:, :], scalar=r, in1=postg[:, 0:512], op0=ALU.mult, op1=ALU.mult),
                 reads=[Bsl, Bpostg], writes=[Bbk[ba]])
            S.op("dve", lambda e: e.scalar_tensor_tensor(out=bk[bb][:, :], in0=bk[bb][:, :], scalar=r, in1=postg[:, 512:1024], op0=ALU.mult, op1=ALU.mult),
                 reads=[Bsl, Bpostg], writes=[Bbk[bb]])

        def add_banks(xt, Bxt, ba, bb):
            S.op("dve", lambda e: e.tensor_tensor(out=xt[:, 0:512], in0=bk[ba][:, :], in1=xt[:, 0:512], op=ALU.add), reads=[Bbk[ba], Bxt], writes=[Bxt])
            S.op("dve", lambda e: e.tensor_tensor(out=xt[:, 512:1024], in0=bk[bb][:, :], in1=xt[:, 512:1024], op=ALU.add), reads=[Bbk[bb], Bxt], writes=[Bxt])

        def P1(i, s):
            n = 4 * i + s
            t0 = n * 128
            S.dma("sp", lambda e: e.dma_start(out=m0P[:], in_=mix_d[t0:t0 + 128, :]), reads=[Md[i]], writes=[Bm0P])
            S.dma("sp", lambda e: e.dma_start(out=xsP[:], in_=xsrc[t0:t0 + 128, :]), reads=[Xd[n]], writes=[BxsP])
            transpose_to(lambda c: m0P[:, c * 128:(c + 1) * 128], 8, 128, Bm0P, mixT[:], bf("mixT"), gcol0=g0 + 16)
            S.begin_atom()
            for k in range(8):
                S.op("pe", lambda e, k=k: e.matmul(bk[4][:, :], lhsT=mixT[:, k, :], rhs=w_out[:, k, 0:512], start=(k == 0), stop=(k == 7)),
                     reads=[bf("mixT"), bf("w_out")], writes=[Bbk[4]], inc=False)
                S.op("pe", lambda e, k=k: e.matmul(bk[5][:, :], lhsT=mixT[:, k, :], rhs=w_out[:, k, 512:1024], start=(k == 0), stop=(k == 7)),
                     reads=[bf("mixT"), bf("w_out")], writes=[Bbk[5]], inc=(k == 7))
            S.end_atom()
            post_norm_inplace(postg_m, bf("postg_m"), 4, 5, m0P, Bm0P)
            add_banks(xsP, BxsP, 4, 5)
            S.dma("sp", lambda e: e.dma_start(out=out_d[t0:t0 + 128, :], in_=xsP[:]), reads=[BxsP], writes=[Xd[n]])

        def P2(i, s):
            n = 4 * i + s
            t0 = n * 128
            S.dma("sp", lambda e: e.dma_start(out=xsP[:], in_=out_d[t0:t0 + 128, :]), reads=[Xd[n]], writes=[BxsP])
            sl, Bsl = new_slot()
            S.op("act", lambda e: e.activation(out=m0P[:], in_=xsP[:], func=AF.Square, accum_out=sl[:, 0:1]), reads=[BxsP], writes=[Bm0P, Bsl])
            r = rstd_from_ss(sl, Bsl, 1, D)
            S.op("dve", lambda e: e.tensor_scalar(out=m0P[:], in0=xsP[:], scalar1=r, scalar2=None, op0=ALU.mult), reads=[BxsP, Bsl], writes=[Bm0P])
            transpose_to(lambda c: m0P[:, c * 128:(c + 1) * 128], 8, 128, Bm0P, h2T[:, :, s * 128:(s + 1) * 128], bf("h2T"), gcol0=g0 + 8)

        def G(i):
            for c in range(NFF):
                gb, Bg = bk[c % 2], Bbk[c % 2]
                ub, Bu = bk[2 + c % 2], Bbk[2 + c % 2]
                S.begin_atom()
                for k in range(8):
                    S.op("pe", lambda e, k=k, c=c, gb=gb: e.matmul(gb[:, :], lhsT=w_gate[:, k, c * 128:(c + 1) * 128], rhs=h2T[:, k, :], start=(k == 0), stop=(k == 7)),
                         reads=[bf(f"w_gate{k}_{c // 11}"), bf("h2T")], writes=[Bg], inc=(k == 7))
                S.end_atom()
                S.begin_atom()
                for k in range(8):
                    S.op("pe", lambda e, k=k, c=c, ub=ub: e.matmul(ub[:, :], lhsT=w_up[:, k, c * 128:(c + 1) * 128], rhs=h2T[:, k, :], start=(k == 0), stop=(k == 7)),
                         reads=[bf(f"w_up{k}_{c // 11}"), bf("h2T")], writes=[Bu], inc=(k == 7))
                S.end_atom()
                sg, Bsg = sg_f[c % 2], bf(f"xn_f{c % 2}")
                S.op("act", lambda e, gb=gb, sg=sg: e.activation(out=sg[:], in_=gb[:, :], func=AF.Silu), reads=[Bg], writes=[Bsg])
                S.op("dve", lambda e, c=c, ub=ub, sg=sg: e.tensor_tensor(out=aT[:, c, :], in0=ub[:, :], in1=sg[:], op=ALU.mult), reads=[Bu, Bsg], writes=[bf(f"aT{c}")])

        def Dn(i):
            for s in range(4):
                n = 4 * i + s
                t0 = n * 128
                ba, bb = [(0, 1), (2, 3)][s % 2]
                S.begin_atom()
                for c in range(NFF):
                    S.op("pe", lambda e, c=c, s=s, ba=ba: e.matmul(bk[ba][:, :], lhsT=aT[:, c, s * 128:(s + 1) * 128], rhs=w_down[:, c, 0:512], start=(c == 0), stop=(c == NFF - 1)),
                         reads=[bf(f"aT{c}"), bf(f"w_down{c}")], writes=[Bbk[ba]], inc=False)
                    S.op("pe", lambda e, c=c, s=s, bb=bb: e.matmul(bk[bb][:, :], lhsT=aT[:, c, s * 128:(s + 1) * 128], rhs=w_down[:, c, 512:1024], start=(c == 0), stop=(c == NFF - 1)),
                         reads=[bf(f"aT{c}"), bf(f"w_down{c}")], writes=[Bbk[bb]], inc=(c == NFF - 1))
                S.end_atom()
                post_norm_inplace(postg_f, bf("postg_f"), ba, bb, xn_f[s % 2], bf(f"xn_f{s % 2}"))
                S.dma("sp", lambda e, t0=t0: e.dma_start(out=xs_f[:], in_=out_d[t0:t0 + 128, :]), reads=[Xd[n]], writes=[bf("xs_f")])
                add_banks(xs_f, bf("xs_f"), ba, bb)
                S.dma("sp", lambda e, t0=t0: e.dma_start(out=out_d[t0:t0 + 128, :], in_=xs_f[:]), reads=[bf("xs_f")], writes=[Xd[n]])

        ntl = DBG_T if DBG_STAGE == 'full' else 0
        if ntl:
            for s in range(4):
                P1(0, s)
            for s in range(4):
                P2(0, s)
        for i in range(ntl):
            sG = S.record(lambda: G(i))
            sP1 = S.record(lambda: [P1(i + 1, s) for s in range(4)]) if i + 1 < ntl else []
            S.replay(S.interleave(sG, sP1))
            sD = S.record(lambda: Dn(i))
            sP2 = S.record(lambda: [P2(i + 1, s) for s in range(4)]) if i + 1 < ntl else []
            S.replay(S.interleave(sD, sP2))

    for l in range(L if DBG_STAGE != 'P' else 0):
        mixer_pass(l)
        if DBG_STAGE not in ('A0', 'B0'):
            ffn_pass(l)
    S.finish("sp", Xd)
    S.barrier()
    return nc


def _layouts(p):
    L = DEPTH
    gT = np.zeros((128, L * GW), np.float32)
    for l in range(L):
        o = l * GW
        gT[:, o + 0:o + 8] = p["mix_pre_g"][l].reshape(8, 128).T
        gT[:, o + 8:o + 16] = p["ffn_pre_g"][l].reshape(8, 128).T
        gT[:, o + 16:o + 24] = p["out_norm_g"][l].reshape(8, 128).T
        gT[:, o + 24:o + 27] = p["q_norm_g"][l].reshape(3, 128).T
        gT[:, o + 27:o + 29] = p["kv_norm_g"][l].reshape(2, 128).T
        gT[:, o + 29:o + 33] = p["b_sp"][l].T
    bc = np.zeros((L, 128, BCW), np.float32)
    bc[:, :, 0:1024] = p["mix_post_g"][:, None, :]
    bc[:, :, 1024:2048] = p["ffn_post_g"][:, None, :]
    bc[:, :, 2048:2304] = p["sg_ln_g"][:, None, :]
    bc[:, :, 2304:2560] = p["sg_ln_b"][:, None, :]
    bc[:, :, 2560:3328] = p["conv_w"].reshape(L, 1, 768)
    wspT = np.ascontiguousarray(np.transpose(p["w_sp"], (0, 3, 1, 2))).reshape(L, 128, 512)
    return gT, bc, wspT


_INVF = (np.float32(1.0) / (np.float32(10000.0) ** (np.arange(16, dtype=np.float32) / np.float32(16)))).astype(np.float32)


def kernel(x, positions, mix_pre_g, mix_post_g, ffn_pre_g, ffn_post_g, w_in, q_norm_g, w_uq, kv_norm_g, w_ukv,
           sg_ln_g, sg_ln_b, w_sp, b_sp, conv_w, out_norm_g, w_out, w_gate, w_up, w_down, _depth=DEPTH, _cores=8):
    p = dict(mix_pre_g=mix_pre_g, mix_post_g=mix_post_g, ffn_pre_g=ffn_pre_g, ffn_post_g=ffn_post_g, q_norm_g=q_norm_g,
             kv_norm_g=kv_norm_g, sg_ln_g=sg_ln_g, sg_ln_b=sg_ln_b, w_sp=w_sp, b_sp=b_sp, conv_w=conv_w, out_norm_g=out_norm_g)
    p = {k: np.asarray(v, np.float32) for k, v in p.items()}
    gT, bc, wspT = _layouts(p)
    f = lambda a: np.ascontiguousarray(np.asarray(a, np.float32))
    shared = {"invf": np.ascontiguousarray(np.broadcast_to(_INVF[None, :], (128, 16))), "w_in": f(w_in), "w_uq": f(w_uq), "w_ukv": f(w_ukv),
              "w_out": f(w_out), "w_gate": f(w_gate), "w_up": f(w_up), "w_down": f(w_down), "gT": gT, "bc": bc, "wspT": wspT}
    x = np.asarray(x, np.float32)
    positions = np.asarray(positions, np.int32)
    nc = build_program(_depth)
    in_maps = []
    for b in range(_cores):
        m = dict(shared)
        m["x"] = np.ascontiguousarray(x[b])
        m["pos"] = np.ascontiguousarray(positions[b].reshape(NSUB, 128).T)
        in_maps.append(m)
    res = run_bass_kernel_spmd(nc, in_maps, core_ids=list(range(_cores)))
    return np.stack([np.asarray(r["out"], np.float32) for r in res.results], axis=0)
```
